# Optimizing a Trainium2 kernel written in Bass

```python
import math
import jax
import jax.numpy as jnp
from jax import lax
import numpy as np

D_MODEL = 1024
BATCH = 16
SEQ = 2048
DEPTH = 2

GRID_W = 64
CTX_LEN = 256
HEAD_DIM = 64
MIX_WIDTH = D_MODEL
GROUP_WIDTH = MIX_WIDTH // 2
FFN_HIDDEN = 4 * D_MODEL
ROPE_BASE = 10000.0
LN_EPS = 1e-5

RWKV_HEADS = GROUP_WIDTH // HEAD_DIM
RWKV_DECAY_LORA = 64
RWKV_ICLR_LORA = 64
RWKV_GATE_LORA = 128
RWKV_GN_EPS = 64e-5
RWKV_COLS = 3 * GROUP_WIDTH + RWKV_DECAY_LORA + RWKV_ICLR_LORA + RWKV_GATE_LORA
RWKV_SPLITS = [GROUP_WIDTH, 2 * GROUP_WIDTH, 3 * GROUP_WIDTH,
               3 * GROUP_WIDTH + RWKV_DECAY_LORA, 3 * GROUP_WIDTH + RWKV_DECAY_LORA + RWKV_ICLR_LORA]

DIFF_HEADS = GROUP_WIDTH // (2 * HEAD_DIM)
DIFF_COLS = 3 * GROUP_WIDTH
ATTN_BLOCK = 128

SSD_HEADS = GROUP_WIDTH // HEAD_DIM
SSD_GROUPS = 2
SSD_STATE = 128
SSD_CONV = 5
SSD_CHUNK = 128
SSD_CONV_DIM = GROUP_WIDTH + 2 * SSD_GROUPS * SSD_STATE
SSD_COLS = GROUP_WIDTH + SSD_CONV_DIM + 2 * SSD_HEADS

SWA_HEADS = GROUP_WIDTH // HEAD_DIM
SWA_KV_HEADS = 2
SWA_WINDOW = 128
SWA_BLOCK = 128
SWA_COLS = GROUP_WIDTH + 2 * SWA_KV_HEADS * HEAD_DIM

AB_COLS = RWKV_COLS + DIFF_COLS
CD_COLS = SSD_COLS + SWA_COLS
N_AB = (DEPTH + 1) // 2
N_CD = DEPTH // 2
DEEPNORM_ALPHA = (2 * DEPTH) ** 0.25
DEEPNORM_BETA = (8 * DEPTH) ** -0.25

kernel_name = 'hybrid_rwkv7_diffattn_ssd_swa_prefix_dit'


def layer_norm(x, g, b, eps=LN_EPS):
    xf = x.astype(jnp.float32)
    mu = jnp.mean(xf, -1, keepdims=True)
    var = jnp.mean(jnp.square(xf - mu), -1, keepdims=True)
    return ((xf - mu) * lax.rsqrt(var + eps)).astype(x.dtype) * g + b


def rms_norm(x, g, eps=LN_EPS):
    xf = x.astype(jnp.float32)
    return (xf * lax.rsqrt(jnp.mean(xf * xf, -1, keepdims=True) + eps)).astype(x.dtype) * g


def modulate(h, shift, scale):
    return h * (1.0 + scale) + shift


def rope_1d(x, pos):
    half = x.shape[-1] // 2
    inv = ROPE_BASE ** (-jnp.arange(half, dtype=jnp.float32) / half)
    ang = pos.astype(jnp.float32)[:, None] * inv
    shape = (pos.shape[0],) + (1,) * (x.ndim - 3) + (half,)
    cos = jnp.cos(ang).reshape(shape).astype(x.dtype)
    sin = jnp.sin(ang).reshape(shape).astype(x.dtype)
    x1, x2 = x[..., :half], x[..., half:]
    return jnp.concatenate([x1 * cos - x2 * sin, x2 * cos + x1 * sin], -1)


def rope_2d(x, rows, cols):
    h = x.shape[-1] // 2
    return jnp.concatenate([rope_1d(x[..., :h], rows), rope_1d(x[..., h:], cols)], -1)


def bi_token_shift(p, mu):
    prev = jnp.pad(p, ((0, 0), (1, 0), (0, 0)))[:, :-1]
    nxt = jnp.pad(p, ((0, 0), (0, 1), (0, 0)))[:, 1:]
    return p + mu * (0.5 * (prev + nxt) - p)


def dw_conv_centred(x, w, b):
    k, ch = w.shape
    out = lax.conv_general_dilated(x, w[:, None, :], window_strides=(1,), padding=[(k // 2, k // 2)],
                                   dimension_numbers=('NWC', 'WIO', 'NWC'), feature_group_count=ch)
    return out + b


def segsum(a):
    t = a.shape[-1]
    cs = jnp.cumsum(a, -1)
    diff = cs[..., :, None] - cs[..., None, :]
    return jnp.where(jnp.tril(jnp.ones((t, t), bool)), diff, -jnp.inf)


def rwkv_scan(r, decay, k, v, a_vec, b_vec, s0, reverse, with_y):
    xs = tuple(jnp.moveaxis(t, 1, 0) for t in (r, decay, k, v, a_vec, b_vec))

    def step(s, inp):
        r_t, w_t, k_t, v_t, a_t, b_t = inp
        sa = jnp.einsum('bhvk,bhk->bhv', s, a_t)
        s = s * w_t[:, :, None, :] + sa[..., None] * b_t[:, :, None, :] + v_t[..., None] * k_t[:, :, None, :]
        y = jnp.einsum('bhvk,bhk->bhv', s, r_t) if with_y else None
        return s, y

    s_fin, ys = lax.scan(step, s0, xs, reverse=reverse)
    return (jnp.moveaxis(ys, 0, 1) if with_y else None), s_fin


def rwkv7_mixer(p_ctx, p_lat, mu, w0, w2, a0, a2, g2, k_k, k_a, r_k, gn_g, gn_b, need_ctx):
    def prep(p):
        b_, l_ = p.shape[:2]
        heads = lambda t: t.reshape(b_, l_, RWKV_HEADS, HEAD_DIM)
        p = bi_token_shift(p, mu)
        r, k, v, xw, xa, xg = jnp.split(p, RWKV_SPLITS, axis=-1)
        kk = heads(k * k_k)
        kk = kk * lax.rsqrt(jnp.maximum(jnp.sum(jnp.square(kk.astype(jnp.float32)), -1, keepdims=True),
                                        1e-24)).astype(kk.dtype)
        tw = jnp.tanh(xw)
        dirs = []
        for d in range(2):
            w_log = -jax.nn.softplus(-(w0[d] + tw @ w2[d])) - 0.5
            decay = jnp.exp(-jnp.exp(w_log.astype(jnp.float32))).astype(p.dtype)
            a = jax.nn.sigmoid(a0[d] + xa @ a2[d])
            dirs.append((heads(decay), heads(k * (1.0 + (a - 1.0) * k_a)), -kk, kk * heads(a)))
        g = jax.nn.sigmoid(xg) @ g2
        bonus = jnp.sum(heads(r * k) * r_k, -1, keepdims=True) * heads(v)
        return heads(r), heads(v), g, bonus, dirs

    def finish(y, bonus, g):
        y = layer_norm(y, gn_g.reshape(RWKV_HEADS, HEAD_DIM), gn_b.reshape(RWKV_HEADS, HEAD_DIM), RWKV_GN_EPS)
        return (y + bonus).reshape(g.shape) * g

    rc, vc, gc, bc, dc = prep(p_ctx)
    rl, vl, gl, bl, dl = prep(p_lat)
    s0 = jnp.zeros((p_lat.shape[0], RWKV_HEADS, HEAD_DIM, HEAD_DIM), p_lat.dtype)
    y_ctx, y_lat = 0.0, 0.0
    for d, rev in enumerate((False, True)):
        dec_c, k_c, av_c, bv_c = dc[d]
        dec_l, k_l, av_l, bv_l = dl[d]
        yc, s_ctx = rwkv_scan(rc, dec_c, k_c, vc, av_c, bv_c, s0, rev, need_ctx)
        yl, _ = rwkv_scan(rl, dec_l, k_l, vl, av_l, bv_l, s_ctx, rev, True)
        y_lat = y_lat + yl
        if need_ctx:
            y_ctx = y_ctx + yc
    o_ctx = finish(y_ctx, bc, gc) if need_ctx else None
    return o_ctx, finish(y_lat, bl, gl)


def diff_attend(q, k, v, lam):
    s = jnp.einsum('bqhmd,bkhmd->bhmqk', q, k).astype(jnp.float32) * (HEAD_DIM ** -0.5)
    p = jax.nn.softmax(s, axis=-1)
    w = (p[:, :, 0] - lam * p[:, :, 1]).astype(v.dtype)
    return jnp.einsum('bhqk,bkhe->bqhe', w, v)


def diff_attn_mixer(p_ctx, p_lat, rows, cols, lq1, lk1, lq2, lk2, subln_g, lam_init, need_ctx):
    def split(p):
        b_, l_ = p.shape[:2]
        q, k, v = jnp.split(p, 3, axis=-1)
        return (q.reshape(b_, l_, DIFF_HEADS, 2, HEAD_DIM), k.reshape(b_, l_, DIFF_HEADS, 2, HEAD_DIM),
                v.reshape(b_, l_, DIFF_HEADS, 2 * HEAD_DIM))

    qc, kc, vc = split(p_ctx)
    ql, kl, vl = split(p_lat)
    ql = rope_2d(ql, rows, cols)
    kl = rope_2d(kl, rows, cols)
    lam = (jnp.exp(jnp.sum(lq1 * lk1).astype(jnp.float32)) - jnp.exp(jnp.sum(lq2 * lk2).astype(jnp.float32))
           + lam_init)
    k_all = jnp.concatenate([kc, kl], 1)
    v_all = jnp.concatenate([vc, vl], 1)
    b_, l_ = p_lat.shape[:2]
    nb = l_ // ATTN_BLOCK
    qb = jnp.moveaxis(ql.reshape(b_, nb, ATTN_BLOCK, DIFF_HEADS, 2, HEAD_DIM), 1, 0)
    ob = lax.map(lambda qq: diff_attend(qq, k_all, v_all, lam), qb)
    o_lat = jnp.moveaxis(ob, 0, 1).reshape(b_, l_, DIFF_HEADS, 2 * HEAD_DIM)

    def finish(o):
        return (rms_norm(o, subln_g) * (1.0 - lam_init)).reshape(o.shape[:2] + (GROUP_WIDTH,))

    o_ctx = finish(diff_attend(qc, kc, vc, lam)) if need_ctx else None
    return o_ctx, finish(o_lat)


def ssd_chunked(xdt, adt, bm, cm, init, with_y):
    b, l, h, p = xdt.shape
    g, n = bm.shape[2], bm.shape[3]
    nc, q, e = l // SSD_CHUNK, SSD_CHUNK, h // g
    dt = xdt.dtype
    x = xdt.reshape(b, nc, q, g, e, p)
    a = adt.astype(jnp.float32).reshape(b, nc, q, g, e).transpose(0, 1, 3, 4, 2)
    bc = bm.reshape(b, nc, q, g, n)
    cc = cm.reshape(b, nc, q, g, n)
    a_cs = jnp.cumsum(a, -1)
    decay_to_end = jnp.exp(a_cs[..., -1:] - a_cs).astype(dt)
    states = jnp.einsum('bcsgn,bcges,bcsgep->bcgepn', bc, decay_to_end, x)
    states = jnp.concatenate([init.reshape(b, g, e, p, n)[:, None], states], 1)
    a_chunks = jnp.pad(jnp.moveaxis(a_cs[..., -1], 1, -1), ((0, 0), (0, 0), (0, 0), (1, 0)))
    chunk_decay = jnp.exp(segsum(a_chunks)).astype(dt)
    new_states = jnp.einsum('bgezc,bcgepn->bzgepn', chunk_decay, states)
    final = new_states[:, -1].reshape(b, h, p, n)
    if not with_y:
        return None, final
    prev = new_states[:, :-1]
    lmat = jnp.exp(segsum(a)).astype(dt)
    cb = jnp.einsum('bclgn,bcsgn->bcgls', cc, bc)
    y_diag = jnp.einsum('bcgls,bcgels,bcsgep->bclgep', cb, lmat, x)
    y_off = jnp.einsum('bclgn,bcgepn,bcgel->bclgep', cc, prev, jnp.exp(a_cs).astype(dt))
    return (y_diag + y_off).reshape(b, l, h, p), final


def flip_if(t, rev):
    return jnp.flip(t, 1) if rev else t


def ssd_mixer(p_ctx, p_lat, conv_w, conv_b, dt_bias, a_log, d_skip, norm_g, need_ctx):
    a_neg = -jnp.exp(a_log.astype(jnp.float32))

    def prep(p):
        b_, l_ = p.shape[:2]
        z, xbc, dt = jnp.split(p, [GROUP_WIDTH, GROUP_WIDTH + SSD_CONV_DIM], axis=-1)
        xbc = jax.nn.silu(dw_conv_centred(xbc, conv_w, conv_b))
        xs, bm, cm = jnp.split(xbc, [GROUP_WIDTH, GROUP_WIDTH + SSD_GROUPS * SSD_STATE], axis=-1)
        dt = jax.nn.softplus(dt.reshape(b_, l_, 2, SSD_HEADS) + dt_bias)
        return (z, xs.reshape(b_, l_, SSD_HEADS, HEAD_DIM), bm.reshape(b_, l_, SSD_GROUPS, SSD_STATE),
                cm.reshape(b_, l_, SSD_GROUPS, SSD_STATE), dt)

    zc, xc, bc, cc, dtc = prep(p_ctx)
    zl, xl, bl, cl, dtl = prep(p_lat)
    h0 = jnp.zeros((p_lat.shape[0], SSD_HEADS, HEAD_DIM, SSD_STATE), p_lat.dtype)
    y_ctx, y_lat = 0.0, 0.0
    for d, rev in enumerate((False, True)):
        yc, h_ctx = ssd_chunked(flip_if(xc * dtc[:, :, d, :, None], rev), flip_if(dtc[:, :, d] * a_neg[d], rev),
                                flip_if(bc, rev), flip_if(cc, rev), h0, need_ctx)
        yl, _ = ssd_chunked(flip_if(xl * dtl[:, :, d, :, None], rev), flip_if(dtl[:, :, d] * a_neg[d], rev),
                            flip_if(bl, rev), flip_if(cl, rev), h_ctx, True)
        y_lat = y_lat + flip_if(yl, rev)
        if need_ctx:
            y_ctx = y_ctx + flip_if(yc, rev)

    def finish(y, xs, z):
        y = (y + d_skip[:, None] * xs).reshape(z.shape) * jax.nn.silu(z)
        y = rms_norm(y.reshape(z.shape[:2] + (SSD_GROUPS, GROUP_WIDTH // SSD_GROUPS)),
                     norm_g.reshape(SSD_GROUPS, GROUP_WIDTH // SSD_GROUPS))
        return y.reshape(z.shape)

    o_ctx = finish(y_ctx, xc, zc) if need_ctx else None
    return o_ctx, finish(y_lat, xl, zl)


def sink_attend(q, k, v, mask, sink):
    s = jnp.einsum('bqhgd,bkhd->bhgqk', q, k).astype(jnp.float32) * (HEAD_DIM ** -0.5)
    s = jnp.where(mask, s, -jnp.inf)
    sink_col = jnp.broadcast_to(sink.astype(jnp.float32)[None, :, :, None, None], s.shape[:-1] + (1,))
    p = jax.nn.softmax(jnp.concatenate([sink_col, s], -1), axis=-1)[..., 1:]
    return jnp.einsum('bhgqk,bkhd->bqhgd', p.astype(v.dtype), v)


def swa_mixer(p_ctx, p_lat, rows, cols, sink, need_ctx):
    group = SWA_HEADS // SWA_KV_HEADS
    sink = sink.reshape(SWA_KV_HEADS, group)
    kv_w = SWA_KV_HEADS * HEAD_DIM

    def split(p):
        b_, l_ = p.shape[:2]
        q, k, v = jnp.split(p, [GROUP_WIDTH, GROUP_WIDTH + kv_w], axis=-1)
        return (q.reshape(b_, l_, SWA_KV_HEADS, group, HEAD_DIM), k.reshape(b_, l_, SWA_KV_HEADS, HEAD_DIM),
                v.reshape(b_, l_, SWA_KV_HEADS, HEAD_DIM))

    qc, kc, vc = split(p_ctx)
    ql, kl, vl = split(p_lat)
    ql = rope_2d(ql, rows, cols)
    kl = rope_2d(kl, rows, cols)
    b_, l_ = p_lat.shape[:2]
    n_ctx = kc.shape[1]
    nb = l_ // SWA_BLOCK
    span = SWA_BLOCK + 2 * SWA_WINDOW
    pad = ((0, 0), (SWA_WINDOW, SWA_WINDOW), (0, 0), (0, 0))
    kp, vp = jnp.pad(kl, pad), jnp.pad(vl, pad)
    ctx_mask = jnp.ones((SWA_BLOCK, n_ctx), bool)

    def block(args):
        i, qq = args
        start = i * SWA_BLOCK
        kw = lax.dynamic_slice_in_dim(kp, start, span, axis=1)
        vw = lax.dynamic_slice_in_dim(vp, start, span, axis=1)
        qpos = start + jnp.arange(SWA_BLOCK)
        kpos = start - SWA_WINDOW + jnp.arange(span)
        win = (jnp.abs(qpos[:, None] - kpos[None, :]) <= SWA_WINDOW) & ((kpos >= 0) & (kpos < l_))[None, :]
        return sink_attend(qq, jnp.concatenate([kc, kw], 1), jnp.concatenate([vc, vw], 1),
                           jnp.concatenate([ctx_mask, win], 1), sink)

    qb = jnp.moveaxis(ql.reshape(b_, nb, SWA_BLOCK, SWA_KV_HEADS, group, HEAD_DIM), 1, 0)
    ob = lax.map(block, (jnp.arange(nb), qb))
    o_lat = jnp.moveaxis(ob, 0, 1).reshape(b_, l_, GROUP_WIDTH)
    if need_ctx:
        o_ctx = sink_attend(qc, kc, vc, jnp.ones((n_ctx, n_ctx), bool), sink).reshape(b_, n_ctx, GROUP_WIDTH)
    else:
        o_ctx = None
    return o_ctx, o_lat


def ffn_sublayer(h, shift, scale, gate, w1, w2, ln_g, ln_b):
    f = jnp.square(jax.nn.relu(modulate(h, shift, scale) @ w1)) @ w2
    return layer_norm(DEEPNORM_ALPHA * h + gate * f, ln_g, ln_b)


def setup_inputs(seed: int = 0) -> dict:
    key = jax.random.key(seed)
    ks = iter(jax.random.split(key, 48))

    def nrm(shape, scale=1.0):
        return jax.random.normal(next(ks), shape, jnp.float32) * scale

    def unif(shape, lo, hi):
        return jax.random.uniform(next(ks), shape, jnp.float32, lo, hi)

    D, GW = D_MODEL, GROUP_WIDTH
    dt0 = jnp.exp(unif((N_CD, 2, SSD_HEADS), math.log(1e-3), math.log(1e-1)))
    return {
        'x': nrm((BATCH, SEQ, D)),
        'c': nrm((BATCH, D)),
        'ctx': nrm((BATCH, CTX_LEN, D)),
        'c_ctx': nrm((D,)),
        'mod_w': nrm((DEPTH, D, 6 * D), 0.5 * D ** -0.5),
        'mod_b': nrm((DEPTH, 6 * D), 0.02),
        'ln_mix_g': 1.0 + nrm((DEPTH, D), 0.02),
        'ln_mix_b': nrm((DEPTH, D), 0.02),
        'ln_ffn_g': 1.0 + nrm((DEPTH, D), 0.02),
        'ln_ffn_b': nrm((DEPTH, D), 0.02),
        'ffn_w1': nrm((DEPTH, D, FFN_HIDDEN), D ** -0.5),
        'ffn_w2': nrm((DEPTH, FFN_HIDDEN, D), DEEPNORM_BETA * FFN_HIDDEN ** -0.5),
        'w_out': nrm((DEPTH, MIX_WIDTH, D), DEEPNORM_BETA * MIX_WIDTH ** -0.5),
        'ab_w_in': nrm((N_AB, D, AB_COLS), D ** -0.5),
        'rwkv_mu': unif((N_AB, RWKV_COLS), 0.0, 1.0),
        'rwkv_w0': unif((N_AB, 2, GW), -5.0, 1.0),
        'rwkv_w2': nrm((N_AB, 2, RWKV_DECAY_LORA, GW), 0.1 * RWKV_DECAY_LORA ** -0.5),
        'rwkv_a0': nrm((N_AB, 2, GW), 0.5),
        'rwkv_a2': nrm((N_AB, 2, RWKV_ICLR_LORA, GW), 0.1 * RWKV_ICLR_LORA ** -0.5),
        'rwkv_g2': nrm((N_AB, RWKV_GATE_LORA, GW), RWKV_GATE_LORA ** -0.5),
        'rwkv_k_k': 0.85 + nrm((N_AB, GW), 0.05),
        'rwkv_k_a': 1.0 + nrm((N_AB, GW), 0.05),
        'rwkv_r_k': nrm((N_AB, RWKV_HEADS, HEAD_DIM), 0.1),
        'rwkv_gn_g': 1.0 + nrm((N_AB, GW), 0.02),
        'rwkv_gn_b': nrm((N_AB, GW), 0.02),
        'diff_lq1': nrm((N_AB, HEAD_DIM), 0.1),
        'diff_lk1': nrm((N_AB, HEAD_DIM), 0.1),
        'diff_lq2': nrm((N_AB, HEAD_DIM), 0.1),
        'diff_lk2': nrm((N_AB, HEAD_DIM), 0.1),
        'diff_subln_g': 1.0 + nrm((N_AB, 2 * HEAD_DIM), 0.02),
        'cd_w_in': nrm((N_CD, D, CD_COLS), D ** -0.5),
        'ssd_conv_w': nrm((N_CD, SSD_CONV, SSD_CONV_DIM), SSD_CONV ** -0.5),
        'ssd_conv_b': nrm((N_CD, SSD_CONV_DIM), 0.02),
        'ssd_dt_bias': dt0 + jnp.log(-jnp.expm1(-dt0)),
        'ssd_a_log': jnp.log(unif((N_CD, 2, SSD_HEADS), 1.0, 16.0)),
        'ssd_d': 1.0 + nrm((N_CD, SSD_HEADS), 0.1),
        'ssd_norm_g': 1.0 + nrm((N_CD, GW), 0.02),
        'swa_sink': nrm((N_CD, SWA_HEADS), 0.5),
    }


def reference(x, c, ctx, c_ctx, mod_w, mod_b, ln_mix_g, ln_mix_b, ln_ffn_g, ln_ffn_b, ffn_w1, ffn_w2, w_out,
              ab_w_in, rwkv_mu, rwkv_w0, rwkv_w2, rwkv_a0, rwkv_a2, rwkv_g2, rwkv_k_k, rwkv_k_a, rwkv_r_k,
              rwkv_gn_g, rwkv_gn_b, diff_lq1, diff_lk1, diff_lq2, diff_lk2, diff_subln_g,
              cd_w_in, ssd_conv_w, ssd_conv_b, ssd_dt_bias, ssd_a_log, ssd_d, ssd_norm_g, swa_sink):
    n_lat = x.shape[1]
    n_rows = n_lat // GRID_W
    rows = jnp.repeat(jnp.arange(n_rows, dtype=jnp.int32), GRID_W)
    cols = jnp.tile(jnp.arange(GRID_W, dtype=jnp.int32), n_rows)
    silu_c = jax.nn.silu(c)
    silu_cc = jax.nn.silu(c_ctx)
    h_lat, h_ctx = x, ctx
    for i in range(DEPTH):
        need_ctx = i < DEPTH - 1
        m_lat = jnp.split((silu_c @ mod_w[i] + mod_b[i])[:, None, :], 6, axis=-1)
        m_ctx = jnp.split((silu_cc @ mod_w[i] + mod_b[i])[None, None, :], 6, axis=-1)
        j = i // 2
        w_in = ab_w_in[j] if i % 2 == 0 else cd_w_in[j]
        p_lat = modulate(h_lat, m_lat[0], m_lat[1]) @ w_in
        p_ctx = modulate(h_ctx, m_ctx[0], m_ctx[1]) @ w_in
        if i % 2 == 0:
            o1_ctx, o1_lat = rwkv7_mixer(p_ctx[..., :RWKV_COLS], p_lat[..., :RWKV_COLS], rwkv_mu[j], rwkv_w0[j],
                                         rwkv_w2[j], rwkv_a0[j], rwkv_a2[j], rwkv_g2[j], rwkv_k_k[j], rwkv_k_a[j],
                                         rwkv_r_k[j], rwkv_gn_g[j], rwkv_gn_b[j], need_ctx)
            o2_ctx, o2_lat = diff_attn_mixer(p_ctx[..., RWKV_COLS:], p_lat[..., RWKV_COLS:], rows, cols,
                                             diff_lq1[j], diff_lk1[j], diff_lq2[j], diff_lk2[j], diff_subln_g[j],
                                             0.8 - 0.6 * math.exp(-0.3 * i), need_ctx)
        else:
            o1_ctx, o1_lat = ssd_mixer(p_ctx[..., :SSD_COLS], p_lat[..., :SSD_COLS], ssd_conv_w[j], ssd_conv_b[j],
                                       ssd_dt_bias[j], ssd_a_log[j], ssd_d[j], ssd_norm_g[j], need_ctx)
            o2_ctx, o2_lat = swa_mixer(p_ctx[..., SSD_COLS:], p_lat[..., SSD_COLS:], rows, cols, swa_sink[j], need_ctx)
        o_lat = jnp.concatenate([o1_lat, o2_lat], -1) @ w_out[i]
        h_lat = layer_norm(DEEPNORM_ALPHA * h_lat + m_lat[2] * o_lat, ln_mix_g[i], ln_mix_b[i])
        h_lat = ffn_sublayer(h_lat, m_lat[3], m_lat[4], m_lat[5], ffn_w1[i], ffn_w2[i], ln_ffn_g[i], ln_ffn_b[i])
        if need_ctx:
            o_ctx = jnp.concatenate([o1_ctx, o2_ctx], -1) @ w_out[i]
            h_ctx = layer_norm(DEEPNORM_ALPHA * h_ctx + m_ctx[2] * o_ctx, ln_mix_g[i], ln_mix_b[i])
            h_ctx = ffn_sublayer(h_ctx, m_ctx[3], m_ctx[4], m_ctx[5], ffn_w1[i], ffn_w2[i], ln_ffn_g[i], ln_ffn_b[i])
    return h_lat
```

```python
import contextlib
import math
import numpy as np
import concourse.bass as bass
import concourse.mybir as mybir
from concourse.bass_utils import run_bass_kernel_spmd

F32 = mybir.dt.float32
BF16 = mybir.dt.bfloat16
AF = mybir.ActivationFunctionType
ALU = mybir.AluOpType
AX = mybir.AxisListType

ENGS = ["pe", "act", "dve", "pool", "sp"]
N_DMA_SEMS = 24

D = 1024
NB = 2
TC = 256
TL = 2048
T = TC + TL
NCH = T // 128
ALPHA = 4.0 ** 0.25
LN_EPS = 1e-5
EPS_P = LN_EPS / (ALPHA * ALPHA)
CDEC = math.exp(-0.5)


class Res:
    __slots__ = ("w", "r", "excl")

    def __init__(self):
        self.w = None
        self.r = []
        self.excl = False


class Prog:
    def __init__(self, nc, self_wait=True):
        self.nc = nc
        self.ops = {e: [] for e in ENGS}
        self.count = {e: 0 for e in ENGS}
        self.seen = {e: {} for e in ENGS}
        self.dma_val = [0] * N_DMA_SEMS
        self.dma_rr = 0
        self.self_wait = self_wait
        self.sb_top = 16512
        self.SB_BYTES = 229376
        self.n_alloc = 0

    def sb(self, shape, dtype, name=None):
        esz = 2 if dtype == BF16 else 4
        free = int(np.prod(shape[1:])) * esz
        free = (free + 63) // 64 * 64
        off = self.sb_top
        self.sb_top += free
        assert self.sb_top <= self.SB_BYTES, f"SBUF overflow {self.sb_top} ({name})"
        self.n_alloc += 1
        return self.nc.alloc_sbuf_tensor_at(f"{name or 't'}_{self.n_alloc}", list(shape), dtype, offset=off)

    def mark(self):
        return self.sb_top

    def release(self, m):
        self.sb_top = m

    def _collect(self, eng, reads, writes):
        deps = []
        for r in reads:
            if r.w is not None:
                deps.append(r.w)
            if r.excl:
                deps.extend(r.r)
        for w in writes:
            if w.w is not None:
                deps.append(w.w)
            deps.extend(w.r)
        waits = {}
        for (key, val, src) in deps:
            if src == eng and (eng == "pe" or not self.self_wait):
                continue
            if self.seen[eng].get(key, 0) >= val:
                continue
            if waits.get(key, 0) < val:
                waits[key] = val
        for k, v in waits.items():
            self.seen[eng][k] = v
        return list(waits.items())

    def _commit(self, tok, reads, writes):
        for r in reads:
            r.r.append(tok)
        for w in writes:
            w.w = tok
            w.r = []

    def op(self, eng, fn, reads=(), writes=()):
        waits = self._collect(eng, reads, writes)
        self.count[eng] += 1
        tok = (eng, self.count[eng], eng)
        self.ops[eng].append((waits, fn, ("eng", eng)))
        self._commit(tok, reads, writes)

    def dma(self, q, out, in_, reads=(), writes=()):
        idx = self.dma_rr
        self.dma_rr = (self.dma_rr + 1) % N_DMA_SEMS
        key = ("dma", idx)
        waits = self._collect(q, reads, writes)
        prev = self.dma_val[idx]
        if prev > 0 and self.seen[q].get(key, 0) < prev:
            waits.append((key, prev))
            self.seen[q][key] = prev
        self.dma_val[idx] = prev + 16
        tok = (key, prev + 16, "dma")
        self.ops[q].append((waits, lambda e: e.dma_start(out=out, in_=in_), ("dma", idx)))
        self._commit(tok, reads, writes)

    def barrier(self):
        for e in ENGS:
            waits = []
            for o in ENGS:
                if o == e or o == "sp":
                    continue
                v = self.count[o]
                if v > 0 and self.seen[e].get(o, 0) < v:
                    waits.append((o, v))
                    self.seen[e][o] = v
            for i in range(N_DMA_SEMS):
                v = self.dma_val[i]
                key = ("dma", i)
                if v > 0 and self.seen[e].get(key, 0) < v:
                    waits.append((key, v))
                    self.seen[e][key] = v
            if waits:
                self.ops[e].append((waits, None, None))

    def emit(self):
        nc = self.nc
        with contextlib.ExitStack() as st:
            sems = {}
            for e in ENGS:
                sems[e] = st.enter_context(nc.semaphore(f"s_{e}"))
            for i in range(N_DMA_SEMS):
                sems[("dma", i)] = st.enter_context(nc.semaphore(f"s_dma{i}"))
            block = st.enter_context(nc.Block())

            def run(engname):
                def f(eng):
                    for waits, fn, kind in self.ops[engname]:
                        for k, v in waits:
                            eng.wait_ge(sems[k], v)
                        if fn is None:
                            continue
                        ins = fn(eng)
                        if kind[0] == "eng":
                            ins.then_inc(sems[kind[1]], 1)
                        else:
                            ins.then_inc(sems[("dma", kind[1])], 16)
                return f

            block.tensor(run("pe"))
            block.scalar(run("act"))
            block.vector(run("dve"))
            block.gpsimd(run("pool"))
            block.sync(run("sp"))


class Tl:
    def __init__(self, t, nres=1):
        self.t = t
        self.r = [Res() for _ in range(nres)]

    def __getitem__(self, idx):
        return self.t[idx]


class Builder:
    def __init__(self, debug=(), stop=None, skip=()):
        self.debug = set(debug)
        self.stop = stop
        self.skip = set(skip)
        self.stop_mix = None
        self.nc = bass.Bass("TRN2", target_bir_lowering=False)
        self.P = Prog(self.nc)
        self.dram = {}

    def din(self, name, shape, dtype=F32):
        self.dram[name] = self.nc.dram_tensor(name, list(shape), dtype, kind="ExternalInput").ap()
        return self.dram[name]

    def dscr(self, name, shape, dtype):
        kind = "ExternalOutput" if name in self.debug else "Internal"
        self.dram[name] = self.nc.dram_tensor(name, list(shape), dtype, kind=kind).ap()
        return self.dram[name]

    def tile(self, shape, dtype, name=None, nres=1):
        return Tl(self.P.sb(shape, dtype, name), nres)

    def MM(self, out, lhsT, rhs, start, stop, R, W):
        self.P.op("pe", lambda e: e.matmul(out, lhsT, rhs, start=start, stop=stop), R, W)

    def TR(self, out, in_, ident, R, W):
        self.P.op("pe", lambda e: e.transpose(out, in_, ident), R, W)

    def ACT(self, out, in_, func, R, W, bias=None, scale=1.0):
        if bias is None:
            self.P.op("act", lambda e: e.activation(out, in_, func, scale=scale), R, W)
        else:
            self.P.op("act", lambda e: e.activation(out, in_, func, bias=bias, scale=scale), R, W)

    def TT(self, eng, out, in0, in1, op, R, W):
        self.P.op(eng, lambda e: e.tensor_tensor(out, in0, in1, op), R, W)

    def TS(self, eng, out, in0, s1, s2, op0, op1, R, W):
        if s2 is None:
            self.P.op(eng, lambda e: e.tensor_scalar(out, in0, s1, None, op0), R, W)
        else:
            self.P.op(eng, lambda e: e.tensor_scalar(out, in0, s1, s2, op0, op1), R, W)

    def STT(self, eng, out, in0, scalar, in1, op0, op1, R, W):
        self.P.op(eng, lambda e: e.scalar_tensor_tensor(out, in0, scalar, in1, op0, op1), R, W)

    def CP(self, eng, out, in_, R, W):
        if eng == "act":
            self.P.op("act", lambda e: e.copy(out, in_), R, W)
        else:
            self.P.op(eng, lambda e: e.tensor_copy(out, in_), R, W)

    def RECIP(self, out, in_, R, W):
        self.P.op("dve", lambda e: e.reciprocal(out, in_), R, W)

    def RED(self, eng, out, in_, op, R, W):
        self.P.op(eng, lambda e: e.tensor_reduce(out, in_, AX.X, op), R, W)

    def MEMSET(self, eng, out, val, R, W):
        self.P.op(eng, lambda e: e.memset(out, val), R, W)

    def DMA(self, q, out, in_, R=(), W=()):
        self.P.dma(q, out, in_, R, W)

    def build(self):
        nc, P = self.nc, self.P
        x = self.din("x", [NB, TL, D])
        ctx = self.din("ctx", [NB, TC, D])
        cvec = self.din("cvec", [128, 8, 3])
        mod_w = self.din("mod_w", [2, D, 6 * D])
        mod_b = self.din("mod_b", [128, 2, 48])
        lnp = self.din("lnp", [128, 2, 4, 8])
        ffn_w1 = self.din("ffn_w1", [2, D, 4 * D])
        ffn_w2 = self.din("ffn_w2", [2, 4 * D, D])
        w_out = self.din("w_out", [2, D, D])
        cid = self.din("c_ident", [128, 128])
        self.ab_w = self.din("ab_w", [D, 3328])
        self.ab_perm = self.din("ab_perm", [D, 1024])
        self.cd_w = self.din("cd_w", [D, 2320])
        self.cd_perm = self.din("cd_perm", [D, 640])
        self.rope_d = self.din("c_rope", [2, 128, TL])
        self.swam_d = self.din("c_swamask", [6, 128, 512])
        self.diffl_d = self.din("diff_l", [1, 4, 64])
        self.subln_d = self.din("subln", [128, 1])
        self.sink_d = self.din("sink", [1, 8])
        self.masks_d = self.din("c_masks", [4, 128, 128])
        self.neg_d = self.din("c_neg", [2, 128, 128])
        self.convw_d = self.din("convw", [128, 8, 5])
        self.convb_d = self.din("convb", [128, 8])
        self.dtb_d = self.din("dtb", [1, 16])
        self.alog_d = self.din("alog", [1, 16])
        self.dskip_d = self.din("dskip", [128, 4])
        self.ssdg_d = self.din("ssdg", [128, 4])
        self.rvec_d = self.din("rwkv_vec", [9, 512])
        self.rmu_d = self.din("rwkv_mu", [1, 1792])
        self.rw2_d = self.din("rwkv_w2", [2, 64, 512])
        self.ra2_d = self.din("rwkv_a2", [2, 64, 512])
        self.rg2_d = self.din("rwkv_g2", [128, 512])
        out = self.nc.dram_tensor("out", [NB, TL, D], F32, kind="ExternalOutput").ap()
        self.out = out
        H = self.dscr("H", [NB, 8, 128, T], F32)
        HM = self.dscr("HM", [NB, 8, 128, T], BF16)
        O = self.dscr("O", [NB, 8, 128, T], BF16)
        self.H, self.HM, self.O = H, HM, O

        self.ident_f = self.tile([128, 128], F32, "identf")
        self.ident_b = self.tile([128, 128], BF16, "identb")
        self.ones_f = self.tile([128, 128], F32, "onesf")
        self.ones_b = self.tile([128, 128], BF16, "onesb")
        self.cst = self.tile([128, 8], F32, "cst")
        self.DMA("sp", self.ident_f[:], cid, W=self.ident_f.r)
        self.DMA("pool", self.ident_b[:], cid, W=self.ident_b.r)
        self.MEMSET("pool", self.ones_f[:], 1.0, [], self.ones_f.r)
        self.MEMSET("pool", self.ones_b[:], 1.0, [], self.ones_b.r)
        self.MEMSET("dve", self.cst[:, 0:1], EPS_P, [], self.cst.r)
        self.MEMSET("dve", self.cst[:, 1:2], 0.0, [], self.cst.r)
        self.MEMSET("dve", self.cst[:, 2:3], LN_EPS, [], self.cst.r)
        self.MEMSET("dve", self.cst[:, 3:4], 1.0, [], self.cst.r)
        self.MEMSET("dve", self.cst[:, 4:5], 64e-5, [], self.cst.r)
        self.masks = self.tile([128, 4, 128], F32, "masks")
        self.DMA("sp", self.masks[:], self.masks_d.rearrange("m p t -> p m t"), W=self.masks.r)
        self.negm = self.tile([128, 2, 128], F32, "negm")
        self.DMA("sp", self.negm[:], self.neg_d.rearrange("m p t -> p m t"), W=self.negm.r)
        self.lnp_t = self.tile([128, 2, 4, 8], F32, "lnp")
        self.DMA("sp", self.lnp_t[:], lnp, W=self.lnp_t.r)
        self.MOD = self.tile([128, 2, 48, 3], F32, "MOD")
        self.S1 = self.tile([128, 2, 8, 3], F32, "S1")
        self.S1F = self.tile([128, 2, 8, 3], F32, "S1F")
        self.GA = self.tile([128, 2, 8, 3], F32, "GA")
        self.GFA = self.tile([128, 2, 8, 3], F32, "GFA")
        self.ps = [Tl(nc.alloc_psum_tensor(f"ps{i}", [128, 512], F32)) for i in range(8)]
        for p_ in self.ps:
            p_.r[0].excl = True
        keep = P.mark()

        self.phase_mod(cvec, mod_w, mod_b)
        P.barrier()
        P.release(keep)
        self.phase_init(x, ctx)
        P.barrier()
        P.release(keep)
        if "MODD" in self.debug:
            md = self.dscr("MODD", [128, 2 * 48 * 3], F32)
            self.DMA("sp", md, self.MOD[:].rearrange("p l j w -> p (l j w)"), R=self.MOD.r)
        for layer in range(2 if self.stop is None else self.stop):
            self.layer = layer
            self.last = layer == 1
            if layer == 0 and "rwkv" not in self.skip:
                self.phase_rwkv()
                P.barrier()
                P.release(keep)
            for b in range(NB):
                self.phase_mixers(layer, b)
                P.barrier()
                P.release(keep)
            if getattr(self, "stop_mix", None) == layer:
                break
            self.phase_ffn(layer, w_out[layer], ffn_w1[layer], ffn_w2[layer])
            P.barrier()
            P.release(keep)
        P.barrier()
        P.emit()
        return nc

    def who(self, b, t0):
        return 2 if t0 < TC else b

    def SH(self, layer, c, w):
        return self.MOD[:, layer, 0 + c, w:w + 1]

    def SHF(self, layer, c, w):
        return self.MOD[:, layer, 24 + c, w:w + 1]

    def phase_mod(self, cvec, mod_w, mod_b):
        P = self.P
        sc = self.tile([128, 8, 3], F32, "silu_c")
        self.DMA("sp", sc[:], cvec, W=sc.r)
        self.ACT(sc[:], sc[:], AF.Silu, sc.r, sc.r)
        mb = self.tile([128, 2, 48], F32, "modb")
        self.DMA("sp", mb[:], mod_b, W=mb.r)
        GW = 768
        wbuf = [self.tile([128, 8, GW], F32, f"modw{i}") for i in range(2)]
        pst = self.ps[0]
        n = 0
        for layer in range(2):
            for g in range(6 * D // GW):
                wb = wbuf[n % 2]
                n += 1
                for c in range(8):
                    self.DMA("sp", wb[:, c, :], mod_w[layer, c * 128:(c + 1) * 128, g * GW:(g + 1) * GW], W=wb.r)
                for jj in range(GW // 128):
                    j = g * (GW // 128) + jj
                    for c in range(8):
                        self.MM(pst[:, j * 3:j * 3 + 3], wb[:, c, jj * 128:(jj + 1) * 128], sc[:, c, :],
                                c == 0, c == 7, wb.r + sc.r, pst.r)
            pv = pst[:, 0:144].rearrange("p (j w) -> p j w", w=3)
            self.TT("dve", self.MOD[:, layer, :, :], pv, mb[:, layer, :].unsqueeze(2).to_broadcast([128, 48, 3]),
                    ALU.add, pst.r + mb.r, self.MOD.r)
        for layer in range(2):
            self.TS("dve", self.S1[:, layer], self.MOD[:, layer, 8:16, :], 1.0, None, ALU.add, None, self.MOD.r, self.S1.r)
            self.TS("dve", self.S1F[:, layer], self.MOD[:, layer, 32:40, :], 1.0, None, ALU.add, None, self.MOD.r, self.S1F.r)
            self.TS("dve", self.GA[:, layer], self.MOD[:, layer, 16:24, :], 1.0 / ALPHA, None, ALU.mult, None, self.MOD.r, self.GA.r)
            self.TS("dve", self.GFA[:, layer], self.MOD[:, layer, 40:48, :], 1.0 / ALPHA, None, ALU.mult, None, self.MOD.r, self.GFA.r)

    def phase_init(self, x, ctx):
        xin = [self.tile([128, D], F32, f"xin{i}") for i in range(2)]
        hf = [self.tile([128, 8, 128], F32, f"hf{i}") for i in range(2)]
        hm = [self.tile([128, 8, 128], BF16, f"hm{i}") for i in range(2)]
        n = 0
        for b in range(NB):
            for ch in range(NCH):
                t0 = ch * 128
                xi, hfi, hmi = xin[n % 2], hf[n % 2], hm[n % 2]
                src = ctx[b, t0:t0 + 128, :] if t0 < TC else x[b, t0 - TC:t0 - TC + 128, :]
                self.DMA("sp", xi[:], src, W=xi.r)
                w = self.who(b, t0)
                for half in range(2):
                    pst = self.ps[(n * 2 + half) % 8]
                    for cc in range(4):
                        c = half * 4 + cc
                        self.TR(pst[:, cc * 128:(cc + 1) * 128], xi[:, c * 128:(c + 1) * 128], self.ident_f[:],
                                xi.r + self.ident_f.r, pst.r)
                    pv = pst[:, :].rearrange("p (c t) -> p c t", c=4)
                    self.CP("dve", hfi[:, half * 4:half * 4 + 4, :], pv, pst.r, hfi.r)
                for c in range(8):
                    self.ACT(hmi[:, c, :], hfi[:, c, :], AF.Identity, hfi.r + self.S1.r + self.MOD.r, hmi.r,
                             bias=self.SH(0, c, w), scale=self.S1[:, 0, c, w:w + 1])
                self.DMA("sp", self.H[b, :, :, t0:t0 + 128].rearrange("c p t -> p c t"), hfi[:], R=hfi.r)
                self.DMA("sp", self.HM[b, :, :, t0:t0 + 128].rearrange("c p t -> p c t"), hmi[:], R=hmi.r)
                n += 1

    def zero_o(self, b, c0, c1):
        z = self.tile([128, 512], BF16, "zero")
        self.MEMSET("pool", z[:], 0.0, [], z.r)
        for c in range(c0, c1):
            for t0 in range(0, T, 512):
                n = min(512, T - t0)
                self.DMA("sp", self.O[b, c, :, t0:t0 + n], z[:, 0:n], R=z.r)

    def phase_mixers(self, layer, b):
        P = self.P
        skip = getattr(self, "skip", set())
        self.hmod = self.tile([128, 8, T], BF16, "hmod_b")
        for c in range(8):
            self.DMA("sp", self.hmod[:, c, :], self.HM[b, c, :, :], W=self.hmod.r)
        keep = P.mark()
        if layer == 0:
            if "rwkv" in skip:
                self.zero_o(b, 0, 4)
            P.barrier(); P.release(keep)
            if "diff" in skip:
                self.zero_o(b, 4, 8)
            else:
                self.mix_diff(b)
        else:
            if "ssd" in skip:
                self.zero_o(b, 0, 4)
            else:
                self.mix_ssd(b)
            P.barrier(); P.release(keep)
            if "swa" in skip:
                self.zero_o(b, 4, 8)
            else:
                self.mix_swa(b)

    TILES = [(0, 256), (256, 512), (768, 512), (1280, 512), (1792, 512)]

    def load_w(self, wt, src_cols):
        n = src_cols.shape[1]
        self.DMA("pool", wt[:, :, 0:n], src_cols.rearrange("(c p) n -> p c n", p=128), W=wt.r)

    def proj_fm(self, pst, wt, col0, t0, n):
        for c in range(8):
            self.MM(pst[:, 0:n], wt[:, c, col0:col0 + 128], self.hmod[:, c, t0:t0 + n], c == 0, c == 7,
                    wt.r + self.hmod.r, pst.r)

    def proj_tok(self, pst, wt, col0, ncols, ch):
        for c in range(8):
            self.MM(pst[:, 0:ncols], self.hmod[:, c, ch * 128:(ch + 1) * 128], wt[:, c, col0:col0 + ncols], c == 0, c == 7,
                    wt.r + self.hmod.r, pst.r)

    def proj_rope(self, dst, wt, wpt, col0, pcol0, rope):
        cos, sin = rope
        for i, (t0, n) in enumerate(self.TILES):
            p1 = self.ps[(2 * i) % 4]
            self.proj_fm(p1, wt, col0, t0, n)
            if t0 < TC:
                self.CP("act", dst[:, t0:t0 + n], p1[:, 0:n], p1.r, dst.r)
                continue
            p2 = self.ps[(2 * i + 1) % 4]
            self.proj_fm(p2, wpt, pcol0, t0, n)
            l0 = t0 - TC
            ta, tb = self.rtmp
            self.TT("dve", ta[:, 0:n], p1[:, 0:n], cos[:, l0:l0 + n], ALU.mult, p1.r + cos.r, ta.r)
            self.TT("dve", tb[:, 0:n], p2[:, 0:n], sin[:, l0:l0 + n], ALU.mult, p2.r + sin.r, tb.r)
            self.TT("pool", dst[:, t0:t0 + n], ta[:, 0:n], tb[:, 0:n], ALU.add, ta.r + tb.r, dst.r)

    def load_rope(self):
        cos = self.tile([128, TL], F32, "cos")
        sin = self.tile([128, TL], F32, "sin")
        self.DMA("sp", cos[:], self.rope_d[0], W=cos.r)
        self.DMA("sp", sin[:], self.rope_d[1], W=sin.r)
        self.rtmp = (self.tile([128, 512], F32, "rta"), self.tile([128, 512], F32, "rtb"))
        return cos, sin


    def mix_swa(self, b):
        rope = self.load_rope()
        C0 = 1552
        sk = self.tile([1, 8], F32, "sk")
        self.DMA("sp", sk[:], self.sink_d, W=sk.r)
        self.ACT(sk[:], sk[:], AF.Exp, sk.r, sk.r)
        pb = self.ps[7]
        self.MM(pb[:, 0:8], self.ones_f[0:1, :], sk[0:1, 0:8], True, True, self.ones_f.r + sk.r, pb.r)
        esink = self.tile([128, 8], F32, "esink")
        self.CP("dve", esink[:], pb[:, 0:8], pb.r, esink.r)
        mk = self.tile([128, 6, 512], BF16, "swamask")
        self.DMA("pool", mk[:], self.swam_d.rearrange("r p q -> p r q"), W=mk.r)
        wk = [self.tile([128, 8, 128], BF16, f"wk{i}") for i in range(2)]
        kbs = [self.tile([128, T], BF16, f"kb{g}") for g in range(2)]
        for g in range(2):
            for half in range(2):
                self.DMA("pool", wk[0][:, :, half * 64:(half + 1) * 64],
                         self.cd_w[:, C0 + 512 + g * 64:C0 + 512 + (g + 1) * 64].rearrange("(c p) n -> p c n", p=128), W=wk[0].r)
                self.DMA("pool", wk[1][:, :, half * 64:(half + 1) * 64],
                         self.cd_perm[:, 512 + g * 64:512 + (g + 1) * 64].rearrange("(c p) n -> p c n", p=128), W=wk[1].r)
            self.proj_rope(kbs[g], wk[0], wk[1], 0, 0, rope)
        wv = self.tile([128, 8, 128], BF16, "wv")
        self.load_w(wv, self.cd_w[:, C0 + 640:C0 + 768])
        vtok = self.tile([128, NCH, 128], BF16, "vtok")
        for ch in range(NCH):
            pst = self.ps[ch % 4]
            self.proj_tok(pst, wv, 0, 128, ch)
            self.CP("act" if ch % 2 else "dve", vtok[:, ch, :], pst[:, 0:128], pst.r, vtok.r)
        wq = [self.tile([128, 8, 128], BF16, f"wq{i}") for i in range(2)]
        qb = self.tile([128, T], BF16, "qb")
        pt = [self.tile([128, 512], BF16, f"pt{i}") for i in range(3)]
        ft = self.tile([128, 512], F32, "ft")
        ob = [self.tile([128, 512], BF16, f"ob{i}") for i in range(2)]
        npt = 0
        cnt = 0
        for cq in range(4):
            self.load_w(wq[0], self.cd_w[:, C0 + cq * 128:C0 + (cq + 1) * 128])
            self.load_w(wq[1], self.cd_perm[:, cq * 128:(cq + 1) * 128])
            self.proj_rope(qb, wq[0], wq[1], 0, 0, rope)
            for hh in range(2):
                hq = cq * 2 + hh
                kv = hq // 4
                qs = hh * 64
                ks = qs
                kb = kbs[kv]
                for qt in range(4):
                    t0 = TC + qt * 512
                    kts = [(0, None), (1, None)] + [(2 + kk, kk - 4 * qt + 1)
                                                    for kk in range(max(0, 4 * qt - 1), min(15, 4 * qt + 4) + 1)]
                    Oa, Da = self.ps[4 + cnt % 2], self.ps[6 + cnt % 2]
                    cnt += 1
                    for ki, (ch, r) in enumerate(kts):
                        S = self.ps[npt % 4]
                        p = pt[npt % 3]
                        npt += 1
                        self.MM(S[:, :], kb[ks:ks + 64, ch * 128:(ch + 1) * 128], qb[qs:qs + 64, t0:t0 + 512],
                                True, True, kb.r + qb.r, S.r)
                        self.ACT(p[:, :], S[:, :], AF.Exp, S.r, p.r, scale=0.125)
                        if r is not None:
                            self.TT("pool", p[:, :], p[:, :], mk[:, r, :], ALU.mult, p.r + mk.r, p.r)
                        first, lastk = ki == 0, ki == len(kts) - 1
                        self.MM(Oa[0:64, :], vtok[:, ch, kv * 64:(kv + 1) * 64], p[:, :], first, lastk, vtok.r + p.r, Oa.r)
                        self.MM(Da[0:64, :], self.ones_b[:, 0:64], p[:, :], first, lastk, self.ones_b.r + p.r, Da.r)
                    self.TS("dve", ft[0:64, :], Da[0:64, :], esink[0:64, hq:hq + 1], None, ALU.add, None, Da.r + esink.r, ft.r)
                    self.RECIP(ft[0:64, :], ft[0:64, :], ft.r, ft.r)
                    o = ob[cnt % 2]
                    self.TT("dve", o[0:64, :], Oa[0:64, :], ft[0:64, :], ALU.mult, Oa.r + ft.r, o.r)
                    self.DMA("sp", self.O[b, 4 + cq, qs:qs + 64, t0:t0 + 512], o[0:64, :], R=o.r)

    def mix_ssd(self, b):
        P = self.P
        hmod = self.hmod
        cw = self.tile([128, 8, 5], F32, "cw")
        cb = self.tile([128, 8], F32, "cb")
        dsk = self.tile([128, 4], F32, "dsk")
        ng = self.tile([128, 4], F32, "ng")
        self.DMA("sp", cw[:], self.convw_d, W=cw.r)
        self.DMA("sp", cb[:], self.convb_d, W=cb.r)
        self.DMA("sp", dsk[:], self.dskip_d, W=dsk.r)
        self.DMA("sp", ng[:], self.ssdg_d, W=ng.r)
        dtb = self.tile([1, 16], F32, "dtb")
        al = self.tile([1, 16], F32, "al")
        self.DMA("sp", dtb[:], self.dtb_d, W=dtb.r)
        self.DMA("sp", al[:], self.alog_d, W=al.r)
        self.ACT(al[:], al[:], AF.Exp, al.r, al.r)
        pb = self.ps[7]
        self.MM(pb[:, 0:16], self.ones_f[0:1, :], al[0:1, 0:16], True, True, self.ones_f.r + al.r, pb.r)
        aneg = self.tile([128, 16], F32, "aneg")
        self.TS("dve", aneg[:], pb[:, 0:16], -1.0, None, ALU.mult, None, pb.r, aneg.r)
        zs = self.tile([128, 4, T], BF16, "zs")
        xact = self.tile([128, 8, T], BF16, "xact")
        xs_tok = self.tile([128, NCH, 512], BF16, "xs_tok")
        B_tok = self.tile([128, NCH, 256], BF16, "B_tok")
        Yacc = self.tile([128, NCH, 512], F32, "Yacc", nres=NCH)
        dt_all = self.tile([128, NCH, 16], F32, "dt_all")
        a_all = self.tile([128, NCH, 16], F32, "a_all")
        keep2 = P.mark()
        wt = [self.tile([128, 8, 128], BF16, f"wssd{i}") for i in range(2)]
        pre = self.tile([128, T], F32, "pre")
        acc = self.tile([128, T], F32, "acc")
        for c in range(4):
            w = wt[c % 2]
            self.load_w(w, self.cd_w[:, c * 128:(c + 1) * 128])
            for i, (t0, n) in enumerate(self.TILES):
                pst = self.ps[i % 4]
                self.proj_fm(pst, w, 0, t0, n)
                self.ACT(zs[:, c, t0:t0 + n], pst[:, 0:n], AF.Silu, pst.r, zs.r)
        for c in range(8):
            w = wt[c % 2]
            self.load_w(w, self.cd_w[:, 512 + c * 128:512 + (c + 1) * 128])
            for i, (t0, n) in enumerate(self.TILES):
                pst = self.ps[i % 4]
                self.proj_fm(pst, w, 0, t0, n)
                self.CP("act" if i % 2 else "dve", pre[:, t0:t0 + n], pst[:, 0:n], pst.r, pre.r)
            self.ACT(acc[:, :], pre[:, :], AF.Identity, pre.r + cw.r + cb.r, acc.r, bias=cb[:, c:c + 1], scale=cw[:, c, 2:3])
            for j in (0, 1, 3, 4):
                sft = j - 2
                for (lo, hi) in ((0, TC), (TC, T)):
                    a0, a1 = max(lo, lo - sft), min(hi, hi - sft)
                    self.STT("dve", acc[:, a0:a1], pre[:, a0 + sft:a1 + sft], cw[:, c, j:j + 1], acc[:, a0:a1],
                             ALU.mult, ALU.add, pre.r + cw.r + acc.r, acc.r)
            self.ACT(xact[:, c, :], acc[:, :], AF.Silu, acc.r, xact.r)
        wdt = self.tile([128, 8, 16], BF16, "wdt")
        self.load_w(wdt, self.cd_w[:, 1536:1552])
        for ch in range(NCH):
            pst = self.ps[ch % 4]
            self.MM(pst[:, 0:16], self.ones_f[0:1, :], dtb[0:1, 0:16], True, False, self.ones_f.r + dtb.r, pst.r)
            for c in range(8):
                self.MM(pst[:, 0:16], hmod[:, c, ch * 128:(ch + 1) * 128], wdt[:, c, :], False, c == 7, wdt.r + hmod.r, pst.r)
            self.ACT(dt_all[:, ch, :], pst[:, 0:16], AF.Exp, pst.r, dt_all.r)
        self.ACT(dt_all[:, :, :], dt_all[:, :, :], AF.Ln, dt_all.r + self.cst.r, dt_all.r, bias=self.cst[:, 3:4])
        self.TT("dve", a_all[:, :, :], dt_all[:, :, :], aneg[:, :].unsqueeze(1).to_broadcast([128, NCH, 16]), ALU.mult,
                dt_all.r + aneg.r, a_all.r)
        for ch in range(NCH):
            pst = self.ps[ch % 4]
            for c in range(4):
                self.MM(pst[:, c * 128:(c + 1) * 128], xact[:, c, ch * 128:(ch + 1) * 128], self.ident_b[:], True, True,
                        xact.r + self.ident_b.r, pst.r)
            self.CP("act", xs_tok[:, ch, :], pst[:, :], pst.r, xs_tok.r)
            pst2 = self.ps[4 + ch % 2]
            for g in range(2):
                self.MM(pst2[:, g * 128:(g + 1) * 128], xact[:, 4 + g, ch * 128:(ch + 1) * 128], self.ident_b[:], True, True,
                        xact.r + self.ident_b.r, pst2.r)
            self.CP("dve", B_tok[:, ch, :], pst2[:, 0:256], pst2.r, B_tok.r)
        P.barrier()
        P.release(keep2)
        keep3 = P.mark()
        hT = [self.tile([128, 2, 256], F32, f"hT{d}") for d in range(2)]
        hTb = [self.tile([128, 2, 256], BF16, f"hTb{d}") for d in range(2)]
        for d in range(2):
            self.MEMSET("pool", hT[d][:], 0.0, [], hT[d].r)
            self.MEMSET("pool", hTb[d][:], 0.0, [], hTb[d].r)
        ex = [self.tile([128, 24], F32, f"ex{d}") for d in range(2)]
        nacs = [self.tile([128, 8], F32, f"nacs{d}") for d in range(2)]
        xdt = [self.tile([128, 8, 64], BF16, f"xdt{d}") for d in range(2)]
        Xd = [self.tile([128, 8, 64], BF16, f"Xd{d}") for d in range(2)]
        Abc = [self.tile([128, 8, 128], F32, f"Abc{d}") for d in range(2)]
        Gs = [self.tile([128, 2, 128], BF16, f"Gs{d}") for d in range(2)]
        Lm = [self.tile([128, 128], BF16, f"Lm{i}") for i in range(4)]
        Wh = [self.tile([128, 128], BF16, f"Wh{i}") for i in range(4)]
        zt = [self.tile([128, 512], F32, f"zt{d}") for d in range(2)]
        order = [list(range(NCH)), [1, 0] + list(range(NCH - 1, 1, -1))]
        written = set()
        nl = 0
        for step in range(NCH):
            for d in range(2):
                ch = order[d][step]
                MI, MSo = self.masks[:, 2 * d, :], self.masks[:, 2 * (1 - d) + 1, :]
                NEG = self.negm[:, d, :]
                tk = slice(ch * 128, (ch + 1) * 128)
                a = a_all[:, ch, d * 8:(d + 1) * 8]
                pA = self.ps[d]
                self.MM(pA[:, 0:8], MI, a, True, True, self.masks.r + a_all.r, pA.r)
                self.MM(pA[:, 8:16], self.ones_f[:], a, True, True, self.ones_f.r + a_all.r, pA.r)
                self.MM(pA[:, 16:24], MSo, a, True, True, self.masks.r + a_all.r, pA.r)
                self.ACT(ex[d][:], pA[:, 0:24], AF.Exp, pA.r, ex[d].r)
                self.ACT(nacs[d][:], pA[:, 0:8], AF.Copy, pA.r, nacs[d].r, scale=-1.0)
                xsv = xs_tok[:, ch, :].rearrange("p (h e) -> p h e", h=8)
                self.TT("dve", xdt[d][:], xsv, dt_all[:, ch, d * 8:(d + 1) * 8].unsqueeze(2).to_broadcast([128, 8, 64]), ALU.mult,
                        xs_tok.r + dt_all.r, xdt[d].r)
                self.TT("pool", Xd[d][:], xdt[d][:], ex[d][:, 16:24].unsqueeze(2).to_broadcast([128, 8, 64]), ALU.mult,
                        xdt[d].r + ex[d].r, Xd[d].r)
                if ch >= 2:
                    self.CP("pool", Abc[d][:], a.unsqueeze(2).to_broadcast([128, 8, 128]), a_all.r, Abc[d].r)
                    pG = self.ps[2 + d]
                    for g in range(2):
                        self.MM(pG[:, g * 128:(g + 1) * 128], xact[:, 4 + g, tk], xact[:, 6 + g, tk], True, True, xact.r, pG.r)
                    self.CP("act", Gs[d][:], pG[:, 0:256].rearrange("p (g t) -> p g t", g=2), pG.r, Gs[d].r)
                    pY = self.ps[4 + d]
                    for h in range(8):
                        g = h // 4
                        pR = self.ps[6 + (nl % 2)]
                        lm, wh = Lm[nl % 4], Wh[nl % 4]
                        nl += 1
                        self.MM(pR[:, 0:128], Abc[d][:, h, :], MI, True, False, Abc[d].r + self.masks.r, pR.r)
                        self.MM(pR[:, 0:128], self.ident_f[:], NEG, False, True, self.ident_f.r + self.negm.r, pR.r)
                        self.ACT(lm[:], pR[:, 0:128], AF.Exp, pR.r + nacs[d].r, lm.r, bias=nacs[d][:, h:h + 1])
                        self.TT("pool", wh[:], lm[:], Gs[d][:, g, :], ALU.mult, lm.r + Gs[d].r, wh.r)
                        self.MM(pY[:, h * 64:(h + 1) * 64], wh[:], xdt[d][:, h, :], True, True, wh.r + xdt[d].r, pY.r)
                    pZ = self.ps[2 + d]
                    for g in range(2):
                        self.MM(pZ[:, g * 256:(g + 1) * 256], xact[:, 6 + g, tk], hTb[d][:, g, :], True, True,
                                xact.r + hTb[d].r, pZ.r)
                    z = zt[d]
                    self.TT("dve", z[:].rearrange("p (h e) -> p h e", h=8), pZ[:, :].rearrange("p (h e) -> p h e", h=8),
                            ex[d][:, 0:8].unsqueeze(2).to_broadcast([128, 8, 64]), ALU.mult, pZ.r + ex[d].r, z.r)
                    self.TT("dve", z[:], pY[:, :], z[:], ALU.add, pY.r + z.r, z.r)
                    if ch in written:
                        self.TT("pool", Yacc[:, ch, :], Yacc[:, ch, :], z[:], ALU.add, [Yacc.r[ch]] + z.r, [Yacc.r[ch]])
                    else:
                        self.CP("pool", Yacc[:, ch, :], z[:], z.r, [Yacc.r[ch]])
                        written.add(ch)
                pH = self.ps[d]
                for g in range(2):
                    self.MM(pH[:, g * 256:(g + 1) * 256], B_tok[:, ch, g * 128:(g + 1) * 128],
                            Xd[d][:, 4 * g:4 * g + 4, :].rearrange("p h e -> p (h e)"), True, True, B_tok.r + Xd[d].r, pH.r)
                hv = hT[d][:].rearrange("p g (h e) -> p (g h) e", h=4)
                self.TT("dve", hv, hv, ex[d][:, 8:16].unsqueeze(2).to_broadcast([128, 8, 64]), ALU.mult, hT[d].r + ex[d].r, hT[d].r)
                hf = hT[d][:].rearrange("p g x -> p (g x)")
                self.TT("dve", hf, hf, pH[:, :], ALU.add, hT[d].r + pH.r, hT[d].r)
                self.CP("act", hTb[d][:].rearrange("p g x -> p (g x)"), hf, hT[d].r, hTb[d].r)
        P.barrier()
        P.release(keep3)
        yg = [self.tile([128, 512], F32, f"yg{i}") for i in range(4)]
        sq = [self.tile([128, 512], F32, f"sq{i}") for i in range(2)]
        rs = self.tile([128, 512], F32, "rs")
        ob = [self.tile([128, 512], BF16, f"ob{i}") for i in range(2)]
        no = 0
        for qt in range(4):
            t0 = TC + qt * 512
            for c in range(4):
                pT = self.ps[c]
                for k4 in range(4):
                    ch = 2 + qt * 4 + k4
                    self.TR(pT[:, k4 * 128:(k4 + 1) * 128], Yacc[:, ch, c * 128:(c + 1) * 128], self.ident_f[:],
                            [Yacc.r[ch]] + self.ident_f.r, pT.r)
                self.STT("dve", yg[c][:], xact[:, c, t0:t0 + 512], dsk[:, c:c + 1], pT[:, :], ALU.mult, ALU.add,
                         xact.r + dsk.r + pT.r, yg[c].r)
                self.TT("pool", yg[c][:], yg[c][:], zs[:, c, t0:t0 + 512], ALU.mult, yg[c].r + zs.r, yg[c].r)
            for g in range(2):
                st = self.ps[4 + g]
                for k2 in range(2):
                    c = 2 * g + k2
                    self.ACT(sq[k2][:], yg[c][:], AF.Square, yg[c].r, sq[k2].r)
                    self.MM(st[:, :], self.ones_f[:], sq[k2][:], k2 == 0, k2 == 1, self.ones_f.r + sq[k2].r, st.r)
                self.ACT(rs[:], st[:, :], AF.Sqrt, st.r + self.cst.r, rs.r, bias=self.cst[:, 2:3], scale=1.0 / 256)
                self.RECIP(rs[:], rs[:], rs.r, rs.r)
                for k2 in range(2):
                    c = 2 * g + k2
                    self.TT("dve", yg[c][:], yg[c][:], rs[:], ALU.mult, yg[c].r + rs.r, yg[c].r)
                    o = ob[no % 2]
                    no += 1
                    self.ACT(o[:], yg[c][:], AF.Copy, yg[c].r + ng.r, o.r, scale=ng[:, c:c + 1])
                    self.DMA("sp", self.O[b, c, :, t0:t0 + 512], o[:], R=o.r)


    def bcast_row(self, src_row, n, name):
        t = self.tile([128, n], F32, name)
        self.DMA("sp", t[:], src_row.partition_broadcast(128), W=t.r)
        return t

    def phase_rwkv(self):
        P = self.P
        base = P.mark()
        self.PR = self.dscr("PR", [NB, 3, T, 512], F32)
        self.WD = self.dscr("WD", [NB, 2, T, 512], F32)
        self.PB = self.dscr("PB", [NB, 2, T, 4, 512], BF16)
        self.BG = self.dscr("BG", [NB, 2, T, 512], F32)
        Vp = self.tile([128, 8, T], BF16, "Vp")
        keep = P.mark()
        for b in range(NB):
            self.rwkv_prep(b, Vp)
            P.barrier()
            P.release(keep)
        import os
        stage = int(os.environ.get("RWKV_STAGE", 3))
        Y = self.tile([128, 8, T], BF16, "Y")
        keep2 = P.mark()
        if stage >= 2:
            self.rwkv_scan(Vp, Y)
            P.barrier()
            P.release(keep2)
        if stage >= 3:
            for b in range(NB):
                self.rwkv_finish(b, Y)
                P.barrier()
                P.release(keep2)
        P.release(base)

    def rwkv_prep(self, b, Vp):
        P = self.P
        tw = self.tile([128, T], BF16, "twxa")
        sg = self.tile([128, T], BF16, "sg")
        keepA = P.mark()
        hmod = self.tile([128, 8, T], BF16, "hmod_b")
        hs = self.tile([128, 8, T], BF16, "hs_b")
        for c in range(8):
            self.DMA("sp", hmod[:, c, :], self.HM[b, c, :, :], W=hmod.r)
        for (lo, hi) in ((0, TC), (TC, T)):
            self.TT("pool", hs[:, :, lo + 1:hi - 1], hmod[:, :, lo:hi - 2], hmod[:, :, lo + 2:hi], ALU.add, hmod.r, hs.r)
            self.CP("dve", hs[:, :, lo:lo + 1], hmod[:, :, lo + 1:lo + 2], hmod.r, hs.r)
            self.CP("dve", hs[:, :, hi - 1:hi], hmod[:, :, hi - 2:hi - 1], hmod.r, hs.r)
        omm = self.bcast_row(self.rmu_d[0, :], 1792, "omm")
        hmu = self.tile([128, 1792], F32, "hmu")
        self.TS("dve", hmu[:], omm[:], 0.5, None, ALU.mult, None, omm.r, hmu.r)
        self.TS("dve", omm[:], omm[:], -1.0, 1.0, ALU.mult, ALU.add, omm.r, omm.r)

        def shifted_weights(wt, w1t, w2t, col0, n):
            self.load_w(wt, self.ab_w[:, col0:col0 + n])
            self.TT("dve", w1t[:, :, 0:n], wt[:, :, 0:n], omm[:, col0:col0 + n].unsqueeze(1).to_broadcast([128, 8, n]), ALU.mult,
                    wt.r + omm.r, w1t.r)
            self.TT("pool", w2t[:, :, 0:n], wt[:, :, 0:n], hmu[:, col0:col0 + n].unsqueeze(1).to_broadcast([128, 8, n]), ALU.mult,
                    wt.r + hmu.r, w2t.r)

        import os
        sub = int(os.environ.get("RWKV_SUB", 9))
        if sub <= 0:
            return
        wt = self.tile([128, 8, 512], BF16, "rw")
        w1t = self.tile([128, 8, 512], BF16, "rw1")
        w2t = self.tile([128, 8, 512], BF16, "rw2")
        for gi, col0 in enumerate((1536, 1664)):
            shifted_weights(wt, w1t, w2t, col0, 128)
            for i, (t0, n) in enumerate(self.TILES):
                pst = self.ps[i % 4]
                for c in range(8):
                    self.MM(pst[:, 0:n], w1t[:, c, 0:128], hmod[:, c, t0:t0 + n], c == 0, False, w1t.r + hmod.r, pst.r)
                for c in range(8):
                    self.MM(pst[:, 0:n], w2t[:, c, 0:128], hs[:, c, t0:t0 + n], False, c == 7, w2t.r + hs.r, pst.r)
                if gi == 0:
                    self.ACT(tw[0:64, t0:t0 + n], pst[0:64, 0:n], AF.Tanh, pst.r, tw.r)
                    self.ACT(tw[64:128, t0:t0 + n], pst[64:128, 0:n], AF.Copy, pst.r, tw.r)
                else:
                    self.ACT(sg[:, t0:t0 + n], pst[:, 0:n], AF.Sigmoid, pst.r, sg.r)
        if sub <= 1:
            return
        stg = [self.tile([128, 512], F32, f"stg{i}") for i in range(2)]
        vb16 = self.tile([128, 8, 128], BF16, "vb16")
        self.MEMSET("pool", vb16[:], 0.0, [], vb16.r)
        ns = 0
        for grp in range(3):
            shifted_weights(wt, w1t, w2t, grp * 512, 512)
            for ch in range(NCH):
                tk = slice(ch * 128, (ch + 1) * 128)
                pst = self.ps[ch % 4]
                for c in range(8):
                    self.MM(pst[:, :], hmod[:, c, tk], w1t[:, c, :], c == 0, False, w1t.r + hmod.r, pst.r)
                for c in range(8):
                    self.MM(pst[:, :], hs[:, c, tk], w2t[:, c, :], False, c == 7, w2t.r + hs.r, pst.r)
                st = stg[ns % 2]
                ns += 1
                self.CP("act", st[:], pst[:, :], pst.r, st.r)
                self.DMA("sp", self.PR[b, grp, tk, :], st[:], R=st.r)
                if grp == 2 and not os.environ.get('NO_VP'):
                    off = 64 * b
                    self.CP("dve", vb16[:, :, off:off + 64], st[:].rearrange("p (h e) -> p h e", h=8), st.r, vb16.r)
                    for hh in range(2):
                        pV = self.ps[4 + hh]
                        for h4 in range(4):
                            h = hh * 4 + h4
                            self.MM(pV[0:64 + off, h4 * 128:(h4 + 1) * 128], vb16[:, h, 0:64 + off], self.ident_b[:], True, True,
                                    vb16.r + self.ident_b.r, pV.r)
                        self.CP("dve" if hh else "act", Vp[off:off + 64, hh * 4:hh * 4 + 4, tk],
                                pV[off:off + 64, :].rearrange("p (h t) -> p h t", h=4), pV.r, Vp.r)
        P.barrier()
        P.release(keepA)
        if sub <= 2:
            return
        rv = [self.bcast_row(self.rvec_d[i, :], 512, f"rv{i}") for i in range(9)]
        kk_bc, ka_bc, rk_bc, _, _, w0a, w0b, a0a, a0b = rv
        omka = self.tile([128, 512], F32, "omka")
        self.TS("dve", omka[:], ka_bc[:], -1.0, 1.0, ALU.mult, ALU.add, ka_bc.r, omka.r)
        w2b = self.tile([64, 2, 512], BF16, "w2b")
        a2b = self.tile([128, 2, 512], BF16, "a2b")
        g2b = self.tile([128, 512], BF16, "g2b")
        self.DMA("pool", w2b[:], self.rw2_d.rearrange("d k n -> k d n"), W=w2b.r)
        self.DMA("pool", a2b[64:128, :, :], self.ra2_d.rearrange("d k n -> k d n"), W=a2b.r)
        self.DMA("pool", g2b[:], self.rg2_d, W=g2b.r)
        rkv = [[self.tile([128, 512], F32, f"in{j}{i}") for i in range(3)] for j in range(2)]
        tmp = [self.tile([128, 512], F32, f"tm{i}") for i in range(4)]
        kk = self.tile([128, 512], F32, "kk")
        sm = [self.tile([128, 8], F32, f"sm{i}") for i in range(2)]
        pbst = [self.tile([128, 4, 512], BF16, f"pbst{i}") for i in range(2)]
        wdec = [self.tile([128, 512], F32, f"wdec{i}") for i in range(2)]
        bg = [self.tile([128, 512], F32, f"bg{i}") for i in range(2)]
        v3 = lambda t: t[:].rearrange("p (h e) -> p h e", h=8)
        bc8 = lambda t: t[:, 0:8].unsqueeze(2).to_broadcast([128, 8, 64])
        for ch in range(NCH):
            tk = slice(ch * 128, (ch + 1) * 128)
            r_t, k_t, v_t = rkv[ch % 2]
            for gi, tt in enumerate((r_t, k_t, v_t)):
                self.DMA("sp", tt[:], self.PR[b, gi, tk, :], W=tt.r)
            t0_, t1_, t2_, t3_ = tmp
            self.TT("dve", t0_[:], k_t[:], kk_bc[:], ALU.mult, k_t.r + kk_bc.r, t0_.r)
            self.TT("pool", t1_[:], t0_[:], t0_[:], ALU.mult, t0_.r, t1_.r)
            self.RED("dve", sm[0][:, 0:8], v3(t1_), ALU.add, t1_.r, sm[0].r)
            self.TS("dve", sm[0][:], sm[0][:], 1e-24, None, ALU.max, None, sm[0].r, sm[0].r)
            self.ACT(sm[0][:], sm[0][:], AF.Sqrt, sm[0].r, sm[0].r)
            self.RECIP(sm[0][:], sm[0][:], sm[0].r, sm[0].r)
            self.TT("dve", v3(kk), v3(t0_), bc8(sm[0]), ALU.mult, t0_.r + sm[0].r, kk.r)
            self.TT("pool", t1_[:], r_t[:], k_t[:], ALU.mult, r_t.r + k_t.r, t1_.r)
            self.TT("pool", t1_[:], t1_[:], rk_bc[:], ALU.mult, t1_.r + rk_bc.r, t1_.r)
            self.RED("dve", sm[1][:, 0:8], v3(t1_), ALU.add, t1_.r, sm[1].r)
            self.TT("dve", v3(bg[0]), v3(v_t), bc8(sm[1]), ALU.mult, v_t.r + sm[1].r, bg[0].r)
            self.DMA("sp", self.BG[b, 0, tk, :], bg[0][:], R=bg[0].r)
            pg = self.ps[4]
            self.MM(pg[:, :], sg[:, tk], g2b[:], True, True, sg.r + g2b.r, pg.r)
            self.CP("act", bg[1][:], pg[:, :], pg.r, bg[1].r)
            self.DMA("sp", self.BG[b, 1, tk, :], bg[1][:], R=bg[1].r)
            for d in range(2):
                pb_ = pbst[d]
                w0_bc, a0_bc = (w0a, a0a) if d == 0 else (w0b, a0b)
                pz = self.ps[d]
                self.MM(pz[:, :], tw[0:64, tk], w2b[0:64, d, :], True, True, tw.r + w2b.r, pz.r)
                self.TT("dve", t1_[:], pz[:, :], w0_bc[:], ALU.add, pz.r + w0_bc.r, t1_.r)
                self.ACT(t1_[:], t1_[:], AF.Sigmoid, t1_.r, t1_.r)
                self.ACT(wdec[d][:], t1_[:], AF.Exp, t1_.r, wdec[d].r, scale=-CDEC)
                self.DMA("sp", self.WD[b, d, tk, :], wdec[d][:], R=wdec[d].r)
                pa = self.ps[2 + d]
                self.MM(pa[:, :], tw[64:128, tk], a2b[64:128, d, :], True, True, tw.r + a2b.r, pa.r)
                self.TT("dve", t2_[:], pa[:, :], a0_bc[:], ALU.add, pa.r + a0_bc.r, t2_.r)
                self.ACT(t2_[:], t2_[:], AF.Sigmoid, t2_.r, t2_.r)
                self.TS("pool", pb_[:, 0, :], kk[:], -1.0, None, ALU.mult, None, kk.r, pb_.r)
                self.TT("pool", pb_[:, 1, :], kk[:], t2_[:], ALU.mult, kk.r + t2_.r, pb_.r)
                self.TT("dve", t3_[:], t2_[:], ka_bc[:], ALU.mult, t2_.r + ka_bc.r, t3_.r)
                self.TT("dve", t3_[:], t3_[:], omka[:], ALU.add, t3_.r + omka.r, t3_.r)
                self.TT("pool", pb_[:, 2, :], k_t[:], t3_[:], ALU.mult, k_t.r + t3_.r, pb_.r)
                self.CP("act", pb_[:, 3, :], r_t[:], r_t.r, pb_.r)
                self.DMA("sp", self.PB[b, d, tk, :, :], pb_[:], R=pb_.r)

    def rwkv_scan(self, Vp, Y):
        NS = 2
        NBUF = 3
        S = [self.tile([128, 512], F32, f"S{d}") for d in range(2)]
        for d in range(2):
            self.MEMSET("dve", S[d][:], 0.0, [], S[d].r)
        Wb = [[self.tile([128, NS, 512], F32, f"Wb{d}{i}") for i in range(NBUF)] for d in range(2)]
        Vb = [[self.tile([128, NS, 4, 512], BF16, f"Vb{d}{i}") for i in range(NBUF)] for d in range(2)]
        t1 = [self.tile([128, 512], F32, f"sc1{d}") for d in range(2)]
        t2 = [self.tile([128, 512], F32, f"sc2{d}") for d in range(2)]
        t3 = [self.tile([128, 512], F32, f"sc3{d}") for d in range(2)]
        sa = [self.tile([128, 8], F32, f"sa{d}") for d in range(2)]
        yts = [self.tile([128, 8], F32, f"yt{d}") for d in range(2)]
        order = [list(range(T)), list(range(TC - 1, -1, -1)) + list(range(T - 1, TC - 1, -1))]
        v3 = lambda ap: ap.rearrange("p (h e) -> p h e", h=8)
        nblk = T // NS
        ywritten = set()
        import os
        nblk = int(os.environ.get('RWKV_MAXBLK', nblk))

        def load(d, bi):
            toks = order[d][bi * NS:(bi + 1) * NS]
            lo = min(toks)
            wb, vb = Wb[d][bi % NBUF], Vb[d][bi % NBUF]
            for b in range(NB):
                self.DMA("sp", wb[b * 64:(b + 1) * 64, :, :], self.WD[b, d, lo:lo + NS, :].partition_broadcast(64), W=wb.r)
                self.DMA("sp", vb[b * 64:(b + 1) * 64, :, :, :], self.PB[b, d, lo:lo + NS, :, :].partition_broadcast(64), W=vb.r)

        for bi in range(min(NBUF - 1, nblk)):
            for d in range(2):
                load(d, bi)
        for bi in range(nblk):
            for d in range(2):
                if bi + NBUF - 1 < nblk:
                    load(d, bi + NBUF - 1)
            for j in range(NS):
                for d in range(2):
                    t = order[d][bi * NS + j]
                    lo = min(order[d][bi * NS:(bi + 1) * NS])
                    jj = t - lo
                    wb, vb = Wb[d][bi % NBUF], Vb[d][bi % NBUF]
                    Sd = S[d]
                    a_bc, b_bc, k_bc, r_bc = (vb[:, jj, q, :] for q in range(4))
                    self.TT("dve", t1[d][:], Sd[:], a_bc, ALU.mult, Sd.r + vb.r, t1[d].r)
                    self.RED("dve", sa[d][:, 0:8], v3(t1[d][:]), ALU.add, t1[d].r, sa[d].r)
                    self.TT("pool", Sd[:], Sd[:], wb[:, jj, :], ALU.mult, Sd.r + wb.r, Sd.r)
                    self.TT("dve", v3(t2[d][:]), v3(b_bc), sa[d][:, 0:8].unsqueeze(2).to_broadcast([128, 8, 64]), ALU.mult,
                            vb.r + sa[d].r, t2[d].r)
                    self.TT("dve", Sd[:], Sd[:], t2[d][:], ALU.add, Sd.r + t2[d].r, Sd.r)
                    self.TT("pool", v3(t3[d][:]), v3(k_bc), Vp[:, :, t:t + 1].to_broadcast([128, 8, 64]), ALU.mult,
                            vb.r + Vp.r, t3[d].r)
                    self.TT("dve", Sd[:], Sd[:], t3[d][:], ALU.add, Sd.r + t3[d].r, Sd.r)
                    self.TT("dve", t1[d][:], Sd[:], r_bc, ALU.mult, Sd.r + vb.r, t1[d].r)
                    ytd = yts[d]
                    self.RED("dve", ytd[:, 0:8], v3(t1[d][:]), ALU.add, t1[d].r, ytd.r)
                    if t not in ywritten:
                        ywritten.add(t)
                        self.CP("act", Y[:, :, t], ytd[:, 0:8], ytd.r, Y.r)
                    else:
                        self.TT("pool", Y[:, :, t], Y[:, :, t], ytd[:, 0:8], ALU.add, Y.r + ytd.r, Y.r)

    def rwkv_finish(self, b, Y):
        rv = [self.bcast_row(self.rvec_d[i, :], 512, f"fv{i}") for i in (3, 4)]
        gng, gnb = rv
        y = [self.tile([128, 512], F32, f"fy{i}") for i in range(2)]
        sq = self.tile([128, 512], F32, "fsq")
        bon = [self.tile([128, 512], F32, f"fb{i}") for i in range(2)]
        gg = [self.tile([128, 512], F32, f"fg{i}") for i in range(2)]
        sm = [self.tile([128, 8], F32, f"fsm{i}") for i in range(2)]
        ot = [self.tile([128, 512], BF16, f"fot{i}") for i in range(2)]
        ob = [self.tile([128, 4, 128], BF16, f"fob{i}") for i in range(2)]
        v3 = lambda t: t[:].rearrange("p (h e) -> p h e", h=8)
        bc8 = lambda t: t[:, 0:8].unsqueeze(2).to_broadcast([128, 8, 64])
        rs = slice(b * 64, (b + 1) * 64)
        for ch in range(NCH):
            tk = slice(ch * 128, (ch + 1) * 128)
            yy, bo, g_ = y[ch % 2], bon[ch % 2], gg[ch % 2]
            self.DMA("sp", bo[:], self.BG[b, 0, tk, :], W=bo.r)
            self.DMA("sp", g_[:], self.BG[b, 1, tk, :], W=g_.r)
            pT = self.ps[ch % 2]
            for h in range(8):
                self.MM(pT[:, h * 64:(h + 1) * 64], Y[rs, h, tk], self.ident_b[rs, rs], True, True, Y.r + self.ident_b.r, pT.r)
            self.RED("dve", sm[0][:, 0:8], pT[:, :].rearrange("p (h e) -> p h e", h=8), ALU.add, pT.r, sm[0].r)
            self.TS("dve", sm[0][:], sm[0][:], 1.0 / 64, None, ALU.mult, None, sm[0].r, sm[0].r)
            self.TT("dve", v3(yy), pT[:, :].rearrange("p (h e) -> p h e", h=8), bc8(sm[0]), ALU.subtract, pT.r + sm[0].r, yy.r)
            self.ACT(sq[:], yy[:], AF.Square, yy.r, sq.r)
            self.RED("dve", sm[1][:, 0:8], v3(sq), ALU.add, sq.r, sm[1].r)
            self.ACT(sm[1][:], sm[1][:], AF.Sqrt, sm[1].r + self.cst.r, sm[1].r, bias=self.cst[:, 4:5], scale=1.0 / 64)
            self.RECIP(sm[1][:], sm[1][:], sm[1].r, sm[1].r)
            self.TT("dve", v3(yy), v3(yy), bc8(sm[1]), ALU.mult, yy.r + sm[1].r, yy.r)
            self.TT("pool", yy[:], yy[:], gng[:], ALU.mult, yy.r + gng.r, yy.r)
            self.TT("pool", yy[:], yy[:], gnb[:], ALU.add, yy.r + gnb.r, yy.r)
            self.TT("pool", yy[:], yy[:], bo[:], ALU.add, yy.r + bo.r, yy.r)
            o = ot[ch % 2]
            self.TT("dve", o[:], yy[:], g_[:], ALU.mult, yy.r + g_.r, o.r)
            pO = self.ps[2 + ch % 2]
            for c in range(4):
                self.MM(pO[:, c * 128:(c + 1) * 128], o[:, c * 128:(c + 1) * 128], self.ident_b[:], True, True,
                        o.r + self.ident_b.r, pO.r)
            oo = ob[ch % 2]
            self.CP("act", oo[:], pO[:, :].rearrange("p (c t) -> p c t", c=4), pO.r, oo.r)
            self.DMA("sp", self.O[b, 0:4, :, tk].rearrange("c p t -> p c t"), oo[:], R=oo.r)

    def mix_diff(self, b):
        P = self.P
        rope = self.load_rope()
        LAM_INIT = 0.2
        dl = self.tile([1, 4, 64], F32, "dl")
        self.DMA("sp", dl[:], self.diffl_d, W=dl.r)
        pr = self.tile([1, 2, 64], F32, "dlp")
        sm = self.tile([1, 2], F32, "dls")
        self.TT("dve", pr[:, 0, :], dl[:, 0, :], dl[:, 1, :], ALU.mult, dl.r, pr.r)
        self.TT("dve", pr[:, 1, :], dl[:, 2, :], dl[:, 3, :], ALU.mult, dl.r, pr.r)
        self.RED("dve", sm[:, 0:2], pr[:, :, :], ALU.add, pr.r, sm.r)
        self.ACT(sm[:, 0:2], sm[:, 0:2], AF.Exp, sm.r, sm.r)
        nl = self.tile([1, 1], F32, "nl")
        self.TT("dve", nl[:, 0:1], sm[:, 1:2], sm[:, 0:1], ALU.subtract, sm.r, nl.r)
        self.TS("dve", nl[:, 0:1], nl[:, 0:1], -LAM_INIT, None, ALU.add, None, nl.r, nl.r)
        nlam = self.tile([128, 1], F32, "nlam")
        pb = self.ps[7]
        self.MM(pb[:, 0:1], self.ones_f[0:1, :], nl[0:1, 0:1], True, True, self.ones_f.r + nl.r, pb.r)
        self.CP("dve", nlam[:], pb[:, 0:1], pb.r, nlam.r)
        sg = self.tile([128, 1], F32, "subg")
        self.DMA("sp", sg[:], self.subln_d, W=sg.r)
        self.TS("dve", sg[:], sg[:], 1.0 - LAM_INIT, None, ALU.mult, None, sg.r, sg.r)
        wv = self.tile([128, 8, 512], BF16, "wv")
        self.load_w(wv, self.ab_w[:, 2816:3328])
        vtok = self.tile([128, NCH, 512], BF16, "vtok")
        for ch in range(NCH):
            pst = self.ps[ch % 4]
            self.proj_tok(pst, wv, 0, 512, ch)
            self.CP("act" if ch % 2 else "dve", vtok[:, ch, :], pst[:, :], pst.r, vtok.r)
        wq = [self.tile([128, 8, 128], BF16, f"wq{i}") for i in range(4)]
        qb = self.tile([128, T], BF16, "qb")
        kb = self.tile([128, T], BF16, "kb")
        pt = [self.tile([128, 512], BF16, f"pt{i}") for i in range(3)]
        ft = [self.tile([128, 512], F32, f"ft{i}") for i in range(4)]
        ob = [self.tile([128, 512], BF16, f"ob{i}") for i in range(2)]
        npt = 0
        nout = 0
        for h in range(4):
            self.load_w(wq[0], self.ab_w[:, 1792 + h * 128:1792 + (h + 1) * 128])
            self.load_w(wq[1], self.ab_perm[:, h * 128:(h + 1) * 128])
            self.load_w(wq[2], self.ab_w[:, 2304 + h * 128:2304 + (h + 1) * 128])
            self.load_w(wq[3], self.ab_perm[:, 512 + h * 128:512 + (h + 1) * 128])
            self.proj_rope(qb, wq[0], wq[1], 0, 0, rope)
            self.proj_rope(kb, wq[2], wq[3], 0, 0, rope)
            for (t0, n) in self.TILES:
                kts = [0, 1] if t0 < TC else list(range(NCH))
                for m in range(2):
                    Oa, Da = self.ps[4 + m], self.ps[6 + m]
                    for ki, kt in enumerate(kts):
                        S = self.ps[npt % 4]
                        p = pt[npt % 3]
                        npt += 1
                        self.MM(S[:, 0:n], kb[m * 64:(m + 1) * 64, kt * 128:(kt + 1) * 128], qb[m * 64:(m + 1) * 64, t0:t0 + n],
                                True, True, kb.r + qb.r, S.r)
                        self.ACT(p[:, 0:n], S[:, 0:n], AF.Exp, S.r, p.r, scale=0.125)
                        self.MM(Oa[:, 0:n], vtok[:, kt, h * 128:(h + 1) * 128], p[:, 0:n], ki == 0, ki == len(kts) - 1,
                                vtok.r + p.r, Oa.r)
                        self.MM(Da[:, 0:n], self.ones_b[:], p[:, 0:n], ki == 0, ki == len(kts) - 1,
                                self.ones_b.r + p.r, Da.r)
                for m in range(2):
                    self.RECIP(ft[2 + m][:, 0:n], self.ps[6 + m][:, 0:n], self.ps[6 + m].r, ft[2 + m].r)
                    self.TT("dve", ft[m][:, 0:n], self.ps[4 + m][:, 0:n], ft[2 + m][:, 0:n], ALU.mult,
                            self.ps[4 + m].r + ft[2 + m].r, ft[m].r)
                self.STT("dve", ft[0][:, 0:n], ft[1][:, 0:n], nlam[:, 0:1], ft[0][:, 0:n], ALU.mult, ALU.add,
                         ft[1].r + nlam.r + ft[0].r, ft[0].r)
                self.ACT(ft[1][:, 0:n], ft[0][:, 0:n], AF.Square, ft[0].r, ft[1].r)
                st = self.ps[6]
                self.MM(st[:, 0:n], self.ones_f[:], ft[1][:, 0:n], True, True, self.ones_f.r + ft[1].r, st.r)
                self.ACT(ft[2][:, 0:n], st[:, 0:n], AF.Sqrt, st.r + self.cst.r, ft[2].r, bias=self.cst[:, 2:3], scale=1.0 / 128)
                self.RECIP(ft[2][:, 0:n], ft[2][:, 0:n], ft[2].r, ft[2].r)
                self.TT("pool", ft[0][:, 0:n], ft[0][:, 0:n], ft[2][:, 0:n], ALU.mult, ft[0].r + ft[2].r, ft[0].r)
                o = ob[nout % 2]
                nout += 1
                self.ACT(o[:, 0:n], ft[0][:, 0:n], AF.Copy, ft[0].r + sg.r, o.r, scale=sg[:, 0:1])
                self.DMA("sp", self.O[b, 4 + h, :, t0:t0 + n], o[:, 0:n], R=o.r)

    def layer_norm(self, y, N, gcol, bcol, hout, extra=None):
        st = self.ps[7]
        st2 = self.ps[6]
        sq = self.ln_sq
        for c in range(8):
            s = sq[c % 2]
            self.ACT(s[:, 0:N], y[:, c, :], AF.Square, y.r, s.r)
            self.MM(st[:, 0:N], self.ones_f[:], y[:, c, :], c == 0, c == 7, self.ones_f.r + y.r, st.r)
            self.MM(st2[:, 0:N], self.ones_f[:], s[:, 0:N], c == 0, c == 7, self.ones_f.r + s.r, st2.r)
        mean, rstd = self.ln_mean, self.ln_rstd
        self.ACT(mean[:, 0:N], st[:, 0:N], AF.Copy, st.r, mean.r, scale=1.0 / D)
        self.TT("dve", rstd[:, 0:N], mean[:, 0:N], mean[:, 0:N], ALU.mult, mean.r, rstd.r)
        self.STT("dve", rstd[:, 0:N], st2[:, 0:N], 1.0 / D, rstd[:, 0:N], ALU.mult, ALU.subtract, st2.r + rstd.r, rstd.r)
        self.ACT(rstd[:, 0:N], rstd[:, 0:N], AF.Sqrt, rstd.r + self.cst.r, rstd.r, bias=self.cst[:, 0:1])
        self.RECIP(rstd[:, 0:N], rstd[:, 0:N], rstd.r, rstd.r)
        for c in range(8):
            tmp = self.ln_tmp[c % 2]
            self.TT("dve", tmp[:, 0:N], y[:, c, :], mean[:, 0:N], ALU.subtract, y.r + mean.r, tmp.r)
            self.TT("pool", tmp[:, 0:N], tmp[:, 0:N], rstd[:, 0:N], ALU.mult, tmp.r + rstd.r, tmp.r)
            self.ACT(hout[:, c, :], tmp[:, 0:N], AF.Identity, tmp.r + self.lnp_t.r, hout.r,
                     bias=bcol(c), scale=gcol(c))
            if extra is not None:
                et, sfn, bfn = extra
                self.ACT(et[:, c, :], hout[:, c, :], AF.Identity, hout.r + self.S1.r + self.S1F.r + self.MOD.r, et.r,
                         bias=bfn(c), scale=sfn(c))

    def phase_ffn(self, layer, w_out, w1, w2):
        P = self.P
        last = layer == 1
        N = 256
        wo = self.tile([128, 8, D], BF16, "wo")
        w1t = self.tile([128, 8, 4 * D], BF16, "w1", nres=8)
        w2t = self.tile([128, 32, D], BF16, "w2", nres=32)
        for c in range(8):
            self.DMA("pool", wo[:, c, :], w_out[c * 128:(c + 1) * 128, :], W=wo.r)
        for c in range(8):
            self.DMA("pool", w1t[:, c, :], w1[c * 128:(c + 1) * 128, :], W=[w1t.r[c]])
        for c in range(32):
            self.DMA("pool", w2t[:, c, :], w2[c * 128:(c + 1) * 128, :], W=[w2t.r[c]])
        self.ln_sq = [self.tile([128, N], F32, f"lnsq{i}") for i in range(2)]
        self.ln_tmp = [self.tile([128, N], F32, f"lntmp{i}") for i in range(2)]
        self.ln_mean = self.tile([128, N], F32, "lnmean")
        self.ln_rstd = self.tile([128, N], F32, "lnrstd")
        o_t = [self.tile([128, 8, N], BF16, f"o_t{i}") for i in range(1)]
        h_t = [self.tile([128, 8, N], F32, f"h_t{i}") for i in range(2)]
        hmod = self.tile([128, 8, N], BF16, "hmodf")
        f_t = self.tile([128, 32, N], BF16, "f_t", nres=32)
        rl = [self.tile([128, N], F32, f"rl{i}") for i in range(2)]
        tok = [self.tile([128, D], F32, f"tok{i}") for i in range(2)] if last else None
        hm2 = self.tile([128, 8, N], BF16, "hm2") if not last else None
        tiles = []
        for b in range(NB):
            for t0 in range(0, T, N):
                if last and t0 < TC:
                    continue
                tiles.append((b, t0))
        lg = lambda k, c: self.lnp_t[:, layer, k, c:c + 1]
        npsum = 0
        ntok = 0
        for n, (b, t0) in enumerate(tiles):
            w = self.who(b, t0)
            ot, ht = o_t[0], h_t[n % 2]
            self.DMA("sp", ot[:], self.O[b, :, :, t0:t0 + N].rearrange("c p t -> p c t"), W=ot.r)
            self.DMA("sp", ht[:], self.H[b, :, :, t0:t0 + N].rearrange("c p t -> p c t"), W=ht.r)
            for oc in range(8):
                pst = self.ps[npsum % 6]
                npsum += 1
                for c in range(8):
                    self.MM(pst[:, 0:N], wo[:, c, oc * 128:(oc + 1) * 128], ot[:, c, :], c == 0, c == 7, wo.r + ot.r, pst.r)
                self.STT("dve", ht[:, oc, :], pst[:, 0:N], self.GA[:, layer, oc, w:w + 1], ht[:, oc, :], ALU.mult, ALU.add,
                         pst.r + self.GA.r + ht.r, ht.r)
            self.layer_norm(ht, N, lambda c: lg(0, c), lambda c: lg(1, c), ht,
                            extra=(hmod, lambda c: self.S1F[:, layer, c, w:w + 1], lambda c: self.SHF(layer, c, w)))
            for j in range(32):
                pst = self.ps[npsum % 6]
                npsum += 1
                for c in range(8):
                    self.MM(pst[:, 0:N], w1t[:, c, j * 128:(j + 1) * 128], hmod[:, c, :], c == 0, c == 7,
                            [w1t.r[c]] + hmod.r, pst.r)
                r = rl[j % 2]
                self.ACT(r[:, 0:N], pst[:, 0:N], AF.Relu, pst.r, r.r)
                self.TT("pool", f_t[:, j, :], r[:, 0:N], r[:, 0:N], ALU.mult, r.r, [f_t.r[j]])
            for oc in range(8):
                pst = self.ps[npsum % 6]
                npsum += 1
                for j in range(32):
                    self.MM(pst[:, 0:N], w2t[:, j, oc * 128:(oc + 1) * 128], f_t[:, j, :], j == 0, j == 31,
                            [w2t.r[j], f_t.r[j]], pst.r)
                self.STT("dve", ht[:, oc, :], pst[:, 0:N], self.GFA[:, layer, oc, w:w + 1], ht[:, oc, :], ALU.mult, ALU.add,
                         pst.r + self.GFA.r + ht.r, ht.r)
            if not last:
                self.layer_norm(ht, N, lambda c: lg(2, c), lambda c: lg(3, c), ht,
                                extra=(hm2, lambda c: self.S1[:, layer + 1, c, w:w + 1], lambda c: self.SH(layer + 1, c, w)))
                self.DMA("sp", self.H[b, :, :, t0:t0 + N].rearrange("c p t -> p c t"), ht[:], R=ht.r)
                self.DMA("sp", self.HM[b, :, :, t0:t0 + N].rearrange("c p t -> p c t"), hm2[:], R=hm2.r)
            else:
                self.layer_norm(ht, N, lambda c: lg(2, c), lambda c: lg(3, c), ht)
                for sub in range(N // 128):
                    tk = tok[ntok % 2]
                    ntok += 1
                    for half in range(2):
                        pst = self.ps[npsum % 6]
                        npsum += 1
                        for cc in range(4):
                            c = half * 4 + cc
                            self.TR(pst[:, cc * 128:(cc + 1) * 128], ht[:, c, sub * 128:(sub + 1) * 128], self.ident_f[:],
                                    ht.r + self.ident_f.r, pst.r)
                        self.CP("act", tk[:, half * 512:(half + 1) * 512], pst[:, :], pst.r, tk.r)
                    tl0 = t0 - TC + sub * 128
                    self.DMA("sp", self.out[b, tl0:tl0 + 128, :], tk[:], R=tk.r)


def _per_core_inputs(inp, core):
    b0 = core * NB
    f = lambda a: np.ascontiguousarray(a, dtype=np.float32)
    cv = np.stack([inp["c"][b0], inp["c"][b0 + 1], inp["c_ctx"]], 0)
    m = {
        "x": f(inp["x"][b0:b0 + NB]),
        "ctx": f(inp["ctx"][b0:b0 + NB]),
        "cvec": f(cv.reshape(3, 8, 128).transpose(2, 1, 0)),
        "mod_w": f(inp["mod_w"]),
        "mod_b": f(inp["mod_b"].reshape(2, 48, 128).transpose(2, 0, 1)),
        "lnp": f(np.stack([inp["ln_mix_g"], inp["ln_mix_b"], inp["ln_ffn_g"], inp["ln_ffn_b"]], 1)
                 .reshape(2, 4, 8, 128).transpose(3, 0, 1, 2)),
        "ffn_w1": f(inp["ffn_w1"]),
        "ffn_w2": f(inp["ffn_w2"]),
        "w_out": f(inp["w_out"]),
        "c_ident": np.eye(128, dtype=np.float32),
        "ab_w": f(inp["ab_w_in"][0]),
        "ab_perm": f(inp["ab_w_in"][0][:, 1792:2816][:, _PERM1024]),
        "cd_w": f(inp["cd_w_in"][0]),
        "cd_perm": f(inp["cd_w_in"][0][:, 1552:2192][:, _PERM1024[:640]]),
        "c_rope": _ROPE,
        "c_swamask": _SWAMASK,
        "diff_l": f(np.stack([inp["diff_lq1"][0], inp["diff_lk1"][0], inp["diff_lq2"][0], inp["diff_lk2"][0]], 0)[None]),
        "subln": f(inp["diff_subln_g"][0].reshape(128, 1)),
        "sink": f(inp["swa_sink"]),
        "c_masks": _MASKS,
        "c_neg": _NEG,
        "convw": f(inp["ssd_conv_w"][0].reshape(5, 8, 128).transpose(2, 1, 0)),
        "convb": f(inp["ssd_conv_b"][0].reshape(8, 128).T),
        "dtb": f(inp["ssd_dt_bias"][0].reshape(1, 16)),
        "alog": f(inp["ssd_a_log"][0].reshape(1, 16)),
        "dskip": f(np.repeat(inp["ssd_d"][0], 64).reshape(4, 128).T),
        "ssdg": f(inp["ssd_norm_g"][0].reshape(4, 128).T),
        "rwkv_vec": f(np.stack([inp["rwkv_k_k"][0], inp["rwkv_k_a"][0], inp["rwkv_r_k"][0].reshape(512), inp["rwkv_gn_g"][0],
                                inp["rwkv_gn_b"][0], inp["rwkv_w0"][0, 0], inp["rwkv_w0"][0, 1], inp["rwkv_a0"][0, 0],
                                inp["rwkv_a0"][0, 1]], 0)),
        "rwkv_mu": f(inp["rwkv_mu"]),
        "rwkv_w2": f(inp["rwkv_w2"][0]),
        "rwkv_a2": f(inp["rwkv_a2"][0]),
        "rwkv_g2": f(inp["rwkv_g2"][0]),
    }
    return m


def _mk_consts():
    d = np.arange(64)
    partner = np.where((d % 32) < 16, d + 16, d - 16)
    perm = (np.arange(1024) // 64) * 64 + partner[np.arange(1024) % 64]
    t = np.arange(TL)
    rows, cols = t // 64, t % 64
    p = np.arange(128)
    dd = p % 64
    i = dd % 16
    inv = 10000.0 ** (-(i.astype(np.float64)) / 16.0)
    pos = np.where((dd // 32)[:, None] == 0, rows[None, :], cols[None, :]).astype(np.float64)
    ang = (pos.astype(np.float32) * inv.astype(np.float32)[:, None]).astype(np.float32)
    cos = np.cos(ang).astype(np.float32)
    sin = np.sin(ang).astype(np.float32)
    sgn = np.where((dd % 32) < 16, -1.0, 1.0).astype(np.float32)[:, None]
    rope = np.stack([cos, sin * sgn], 0).astype(np.float32)
    k = np.arange(128)[:, None]
    q = np.arange(512)[None, :]
    m = np.stack([(np.abs(q - k - 128 * (r - 1)) <= 128) for r in range(6)], 0).astype(np.float32)
    return perm, rope, m


_PERM1024, _ROPE, _SWAMASK = _mk_consts()
_i = np.arange(128)[:, None]
_t = np.arange(128)[None, :]
_MASKS = np.stack([_i <= _t, _i < _t, _i >= _t, _i > _t], 0).astype(np.float32)
_NEG = ((_MASKS[[0, 2]] - 1.0) * 1e30).astype(np.float32)


_CACHE = {}


def kernel(**inputs):
    inp = {k: np.asarray(v) for k, v in inputs.items()}
    if "nc" not in _CACHE:
        _CACHE["nc"] = Builder().build()
    nc = _CACHE["nc"]
    in_maps = [_per_core_inputs(inp, c) for c in range(8)]
    res = run_bass_kernel_spmd(nc, in_maps, core_ids=list(range(8)))
    return np.concatenate([r["out"] for r in res.results], axis=0).astype(np.float32)
```

```python
import contextlib
import math
import numpy as np
import concourse.bass as bass
import concourse.mybir as mybir
from concourse.bass_utils import run_bass_kernel_spmd

F32 = mybir.dt.float32
BF16 = mybir.dt.bfloat16
AF = mybir.ActivationFunctionType
ALU = mybir.AluOpType
AX = mybir.AxisListType

ENGS = ["pe", "act", "dve", "pool", "sp"]
N_DMA_SEMS = 24

D = 1024
NB = 2
TC = 256
TL = 2048
T = TC + TL
NCH = T // 128
ALPHA = 4.0 ** 0.25
LN_EPS = 1e-5
EPS_P = LN_EPS / (ALPHA * ALPHA)
CDEC = math.exp(-0.5)


class Res:
    __slots__ = ("w", "r", "excl")

    def __init__(self):
        self.w = None
        self.r = []
        self.excl = False


class Prog:
    def __init__(self, nc, self_wait=True):
        self.nc = nc
        self.ops = {e: [] for e in ENGS}
        self.count = {e: 0 for e in ENGS}
        self.seen = {e: {} for e in ENGS}
        self.dma_val = [0] * N_DMA_SEMS
        self.dma_rr = 0
        self.self_wait = self_wait
        self.sb_top = 16512
        self.SB_BYTES = 229376
        self.n_alloc = 0

    def sb(self, shape, dtype, name=None):
        esz = 2 if dtype == BF16 else 4
        free = int(np.prod(shape[1:])) * esz
        free = (free + 63) // 64 * 64
        off = self.sb_top
        self.sb_top += free
        assert self.sb_top <= self.SB_BYTES, f"SBUF overflow {self.sb_top} ({name})"
        self.n_alloc += 1
        return self.nc.alloc_sbuf_tensor_at(f"{name or 't'}_{self.n_alloc}", list(shape), dtype, offset=off)

    def mark(self):
        return self.sb_top

    def release(self, m):
        self.sb_top = m

    def _collect(self, eng, reads, writes):
        deps = []
        for r in reads:
            if r.w is not None:
                deps.append(r.w)
            if r.excl:
                deps.extend(r.r)
        for w in writes:
            if w.w is not None:
                deps.append(w.w)
            deps.extend(w.r)
        waits = {}
        for (key, val, src) in deps:
            if src == eng and (eng == "pe" or not self.self_wait):
                continue
            if self.seen[eng].get(key, 0) >= val:
                continue
            if waits.get(key, 0) < val:
                waits[key] = val
        for k, v in waits.items():
            self.seen[eng][k] = v
        return list(waits.items())

    def _commit(self, tok, reads, writes):
        for r in reads:
            r.r.append(tok)
        for w in writes:
            w.w = tok
            w.r = []

    def op(self, eng, fn, reads=(), writes=()):
        waits = self._collect(eng, reads, writes)
        self.count[eng] += 1
        tok = (eng, self.count[eng], eng)
        self.ops[eng].append((waits, fn, ("eng", eng)))
        self._commit(tok, reads, writes)

    def dma(self, q, out, in_, reads=(), writes=()):
        idx = self.dma_rr
        self.dma_rr = (self.dma_rr + 1) % N_DMA_SEMS
        key = ("dma", idx)
        waits = self._collect(q, reads, writes)
        prev = self.dma_val[idx]
        if prev > 0 and self.seen[q].get(key, 0) < prev:
            waits.append((key, prev))
            self.seen[q][key] = prev
        self.dma_val[idx] = prev + 16
        tok = (key, prev + 16, "dma")
        self.ops[q].append((waits, lambda e: e.dma_start(out=out, in_=in_), ("dma", idx)))
        self._commit(tok, reads, writes)

    def barrier(self):
        for e in ENGS:
            waits = []
            for o in ENGS:
                if o == e or o == "sp":
                    continue
                v = self.count[o]
                if v > 0 and self.seen[e].get(o, 0) < v:
                    waits.append((o, v))
                    self.seen[e][o] = v
            for i in range(N_DMA_SEMS):
                v = self.dma_val[i]
                key = ("dma", i)
                if v > 0 and self.seen[e].get(key, 0) < v:
                    waits.append((key, v))
                    self.seen[e][key] = v
            if waits:
                self.ops[e].append((waits, None, None))

    def emit(self):
        nc = self.nc
        with contextlib.ExitStack() as st:
            sems = {}
            for e in ENGS:
                sems[e] = st.enter_context(nc.semaphore(f"s_{e}"))
            for i in range(N_DMA_SEMS):
                sems[("dma", i)] = st.enter_context(nc.semaphore(f"s_dma{i}"))
            block = st.enter_context(nc.Block())

            def run(engname):
                def f(eng):
                    for waits, fn, kind in self.ops[engname]:
                        for k, v in waits:
                            eng.wait_ge(sems[k], v)
                        if fn is None:
                            continue
                        ins = fn(eng)
                        if kind[0] == "eng":
                            ins.then_inc(sems[kind[1]], 1)
                        else:
                            ins.then_inc(sems[("dma", kind[1])], 16)
                return f

            block.tensor(run("pe"))
            block.scalar(run("act"))
            block.vector(run("dve"))
            block.gpsimd(run("pool"))
            block.sync(run("sp"))


class Tl:
    def __init__(self, t, nres=1):
        self.t = t
        self.r = [Res() for _ in range(nres)]

    def __getitem__(self, idx):
        return self.t[idx]


class Builder:
    def __init__(self, debug=(), stop=None, skip=()):
        self.debug = set(debug)
        self.stop = stop
        self.skip = set(skip)
        self.stop_mix = None
        self.nc = bass.Bass("TRN2", target_bir_lowering=False)
        import os
        self.P = Prog(self.nc, self_wait=not os.environ.get('NO_SELF_WAIT'))
        self.dram = {}

    def din(self, name, shape, dtype=F32):
        self.dram[name] = self.nc.dram_tensor(name, list(shape), dtype, kind="ExternalInput").ap()
        return self.dram[name]

    def dscr(self, name, shape, dtype):
        kind = "ExternalOutput" if name in self.debug else "Internal"
        self.dram[name] = self.nc.dram_tensor(name, list(shape), dtype, kind=kind).ap()
        return self.dram[name]

    def tile(self, shape, dtype, name=None, nres=1):
        return Tl(self.P.sb(shape, dtype, name), nres)

    def MM(self, out, lhsT, rhs, start, stop, R, W):
        self.P.op("pe", lambda e: e.matmul(out, lhsT, rhs, start=start, stop=stop), R, W)

    def TR(self, out, in_, ident, R, W):
        self.P.op("pe", lambda e: e.transpose(out, in_, ident), R, W)

    def ACT(self, out, in_, func, R, W, bias=None, scale=1.0):
        if bias is None:
            self.P.op("act", lambda e: e.activation(out, in_, func, scale=scale), R, W)
        else:
            self.P.op("act", lambda e: e.activation(out, in_, func, bias=bias, scale=scale), R, W)

    def TT(self, eng, out, in0, in1, op, R, W):
        self.P.op(eng, lambda e: e.tensor_tensor(out, in0, in1, op), R, W)

    def TS(self, eng, out, in0, s1, s2, op0, op1, R, W):
        if s2 is None:
            self.P.op(eng, lambda e: e.tensor_scalar(out, in0, s1, None, op0), R, W)
        else:
            self.P.op(eng, lambda e: e.tensor_scalar(out, in0, s1, s2, op0, op1), R, W)

    def STT(self, eng, out, in0, scalar, in1, op0, op1, R, W):
        self.P.op(eng, lambda e: e.scalar_tensor_tensor(out, in0, scalar, in1, op0, op1), R, W)

    def CP(self, eng, out, in_, R, W):
        if eng == "act":
            self.P.op("act", lambda e: e.copy(out, in_), R, W)
        else:
            self.P.op(eng, lambda e: e.tensor_copy(out, in_), R, W)

    def RECIP(self, out, in_, R, W):
        self.P.op("dve", lambda e: e.reciprocal(out, in_), R, W)

    def RED(self, eng, out, in_, op, R, W):
        self.P.op(eng, lambda e: e.tensor_reduce(out, in_, AX.X, op), R, W)

    def MEMSET(self, eng, out, val, R, W):
        self.P.op(eng, lambda e: e.memset(out, val), R, W)

    def DMA(self, q, out, in_, R=(), W=()):
        self.P.dma(q, out, in_, R, W)

    def build(self):
        nc, P = self.nc, self.P
        x = self.din("x", [NB, TL, D])
        ctx = self.din("ctx", [NB, TC, D])
        cvec = self.din("cvec", [128, 8, 3])
        mod_w = self.din("mod_w", [2, D, 6 * D])
        mod_b = self.din("mod_b", [128, 2, 48])
        lnp = self.din("lnp", [128, 2, 4, 8])
        ffn_w1 = self.din("ffn_w1", [2, D, 4 * D])
        ffn_w2 = self.din("ffn_w2", [2, 4 * D, D])
        w_out = self.din("w_out", [2, D, D])
        cid = self.din("c_ident", [128, 128])
        self.ab_w = self.din("ab_w", [D, 3328])
        self.ab_perm = self.din("ab_perm", [D, 1024])
        self.cd_w = self.din("cd_w", [D, 2320])
        self.cd_perm = self.din("cd_perm", [D, 640])
        self.rope_d = self.din("c_rope", [2, 128, TL])
        self.swam_d = self.din("c_swamask", [6, 128, 512])
        self.diffl_d = self.din("diff_l", [1, 4, 64])
        self.subln_d = self.din("subln", [128, 1])
        self.sink_d = self.din("sink", [1, 8])
        self.masks_d = self.din("c_masks", [4, 128, 128])
        self.neg_d = self.din("c_neg", [2, 128, 128])
        self.convw_d = self.din("convw", [128, 8, 5])
        self.convb_d = self.din("convb", [128, 8])
        self.dtb_d = self.din("dtb", [1, 16])
        self.alog_d = self.din("alog", [1, 16])
        self.dskip_d = self.din("dskip", [128, 4])
        self.ssdg_d = self.din("ssdg", [128, 4])
        self.rvec_d = self.din("rwkv_vec", [9, 512])
        self.rmu_d = self.din("rwkv_mu", [1, 1792])
        self.rw2_d = self.din("rwkv_w2", [2, 64, 512])
        self.ra2_d = self.din("rwkv_a2", [2, 64, 512])
        self.rg2_d = self.din("rwkv_g2", [128, 512])
        out = self.nc.dram_tensor("out", [NB, TL, D], F32, kind="ExternalOutput").ap()
        self.out = out
        H = self.dscr("H", [NB, 8, 128, T], F32)
        HM = self.dscr("HM", [NB, 8, 128, T], BF16)
        O = self.dscr("O", [NB, 8, 128, T], BF16)
        self.H, self.HM, self.O = H, HM, O

        self.ident_f = self.tile([128, 128], F32, "identf")
        self.ident_b = self.tile([128, 128], BF16, "identb")
        self.ones_f = self.tile([128, 128], F32, "onesf")
        self.ones_b = self.tile([128, 128], BF16, "onesb")
        self.cst = self.tile([128, 8], F32, "cst")
        self.DMA("sp", self.ident_f[:], cid, W=self.ident_f.r)
        self.DMA("pool", self.ident_b[:], cid, W=self.ident_b.r)
        self.MEMSET("pool", self.ones_f[:], 1.0, [], self.ones_f.r)
        self.MEMSET("pool", self.ones_b[:], 1.0, [], self.ones_b.r)
        self.MEMSET("dve", self.cst[:, 0:1], EPS_P, [], self.cst.r)
        self.MEMSET("dve", self.cst[:, 1:2], 0.0, [], self.cst.r)
        self.MEMSET("dve", self.cst[:, 2:3], LN_EPS, [], self.cst.r)
        self.MEMSET("dve", self.cst[:, 3:4], 1.0, [], self.cst.r)
        self.MEMSET("dve", self.cst[:, 4:5], 64e-5, [], self.cst.r)
        self.masks = self.tile([128, 4, 128], F32, "masks")
        self.DMA("sp", self.masks[:], self.masks_d.rearrange("m p t -> p m t"), W=self.masks.r)
        self.negm = self.tile([128, 2, 128], F32, "negm")
        self.DMA("sp", self.negm[:], self.neg_d.rearrange("m p t -> p m t"), W=self.negm.r)
        self.lnp_t = self.tile([128, 2, 4, 8], F32, "lnp")
        self.DMA("sp", self.lnp_t[:], lnp, W=self.lnp_t.r)
        self.MOD = self.tile([128, 2, 48, 3], F32, "MOD")
        self.S1 = self.tile([128, 2, 8, 3], F32, "S1")
        self.S1F = self.tile([128, 2, 8, 3], F32, "S1F")
        self.GA = self.tile([128, 2, 8, 3], F32, "GA")
        self.GFA = self.tile([128, 2, 8, 3], F32, "GFA")
        self.ps = [Tl(nc.alloc_psum_tensor(f"ps{i}", [128, 512], F32)) for i in range(8)]
        for p_ in self.ps:
            p_.r[0].excl = True
        keep = P.mark()

        self.phase_mod(cvec, mod_w, mod_b)
        P.barrier()
        P.release(keep)
        self.phase_init(x, ctx)
        P.barrier()
        P.release(keep)
        if "MODD" in self.debug:
            md = self.dscr("MODD", [128, 2 * 48 * 3], F32)
            self.DMA("sp", md, self.MOD[:].rearrange("p l j w -> p (l j w)"), R=self.MOD.r)
        for layer in range(2 if self.stop is None else self.stop):
            self.layer = layer
            self.last = layer == 1
            if layer == 0 and "rwkv" not in self.skip:
                self.phase_rwkv()
                P.barrier()
                P.release(keep)
            for b in range(NB):
                self.phase_mixers(layer, b)
                P.barrier()
                P.release(keep)
            if getattr(self, "stop_mix", None) == layer:
                break
            self.phase_ffn(layer, w_out[layer], ffn_w1[layer], ffn_w2[layer])
            P.barrier()
            P.release(keep)
        P.barrier()
        P.emit()
        return nc

    def who(self, b, t0):
        return 2 if t0 < TC else b

    def SH(self, layer, c, w):
        return self.MOD[:, layer, 0 + c, w:w + 1]

    def SHF(self, layer, c, w):
        return self.MOD[:, layer, 24 + c, w:w + 1]

    def phase_mod(self, cvec, mod_w, mod_b):
        P = self.P
        sc = self.tile([128, 8, 3], F32, "silu_c")
        self.DMA("sp", sc[:], cvec, W=sc.r)
        self.ACT(sc[:], sc[:], AF.Silu, sc.r, sc.r)
        mb = self.tile([128, 2, 48], F32, "modb")
        self.DMA("sp", mb[:], mod_b, W=mb.r)
        GW = 768
        wbuf = [self.tile([128, 8, GW], F32, f"modw{i}") for i in range(2)]
        pst = self.ps[0]
        n = 0
        for layer in range(2):
            for g in range(6 * D // GW):
                wb = wbuf[n % 2]
                n += 1
                for c in range(8):
                    self.DMA("sp", wb[:, c, :], mod_w[layer, c * 128:(c + 1) * 128, g * GW:(g + 1) * GW], W=wb.r)
                for jj in range(GW // 128):
                    j = g * (GW // 128) + jj
                    for c in range(8):
                        self.MM(pst[:, j * 3:j * 3 + 3], wb[:, c, jj * 128:(jj + 1) * 128], sc[:, c, :],
                                c == 0, c == 7, wb.r + sc.r, pst.r)
            pv = pst[:, 0:144].rearrange("p (j w) -> p j w", w=3)
            self.TT("dve", self.MOD[:, layer, :, :], pv, mb[:, layer, :].unsqueeze(2).to_broadcast([128, 48, 3]),
                    ALU.add, pst.r + mb.r, self.MOD.r)
        for layer in range(2):
            self.TS("dve", self.S1[:, layer], self.MOD[:, layer, 8:16, :], 1.0, None, ALU.add, None, self.MOD.r, self.S1.r)
            self.TS("dve", self.S1F[:, layer], self.MOD[:, layer, 32:40, :], 1.0, None, ALU.add, None, self.MOD.r, self.S1F.r)
            self.TS("dve", self.GA[:, layer], self.MOD[:, layer, 16:24, :], 1.0 / ALPHA, None, ALU.mult, None, self.MOD.r, self.GA.r)
            self.TS("dve", self.GFA[:, layer], self.MOD[:, layer, 40:48, :], 1.0 / ALPHA, None, ALU.mult, None, self.MOD.r, self.GFA.r)

    def phase_init(self, x, ctx):
        xin = [self.tile([128, D], F32, f"xin{i}") for i in range(2)]
        hf = [self.tile([128, 8, 128], F32, f"hf{i}") for i in range(2)]
        hm = [self.tile([128, 8, 128], BF16, f"hm{i}") for i in range(2)]
        n = 0
        for b in range(NB):
            for ch in range(NCH):
                t0 = ch * 128
                xi, hfi, hmi = xin[n % 2], hf[n % 2], hm[n % 2]
                src = ctx[b, t0:t0 + 128, :] if t0 < TC else x[b, t0 - TC:t0 - TC + 128, :]
                self.DMA("sp", xi[:], src, W=xi.r)
                w = self.who(b, t0)
                for half in range(2):
                    pst = self.ps[(n * 2 + half) % 8]
                    for cc in range(4):
                        c = half * 4 + cc
                        self.TR(pst[:, cc * 128:(cc + 1) * 128], xi[:, c * 128:(c + 1) * 128], self.ident_f[:],
                                xi.r + self.ident_f.r, pst.r)
                    pv = pst[:, :].rearrange("p (c t) -> p c t", c=4)
                    self.CP("dve", hfi[:, half * 4:half * 4 + 4, :], pv, pst.r, hfi.r)
                for c in range(8):
                    self.ACT(hmi[:, c, :], hfi[:, c, :], AF.Identity, hfi.r + self.S1.r + self.MOD.r, hmi.r,
                             bias=self.SH(0, c, w), scale=self.S1[:, 0, c, w:w + 1])
                self.DMA("sp", self.H[b, :, :, t0:t0 + 128].rearrange("c p t -> p c t"), hfi[:], R=hfi.r)
                self.DMA("sp", self.HM[b, :, :, t0:t0 + 128].rearrange("c p t -> p c t"), hmi[:], R=hmi.r)
                n += 1

    def zero_o(self, b, c0, c1):
        z = self.tile([128, 512], BF16, "zero")
        self.MEMSET("pool", z[:], 0.0, [], z.r)
        for c in range(c0, c1):
            for t0 in range(0, T, 512):
                n = min(512, T - t0)
                self.DMA("sp", self.O[b, c, :, t0:t0 + n], z[:, 0:n], R=z.r)

    def phase_mixers(self, layer, b):
        P = self.P
        skip = getattr(self, "skip", set())
        self.hmod = self.tile([128, 8, T], BF16, "hmod_b")
        for c in range(8):
            self.DMA("sp", self.hmod[:, c, :], self.HM[b, c, :, :], W=self.hmod.r)
        keep = P.mark()
        if layer == 0:
            if "rwkv" in skip:
                self.zero_o(b, 0, 4)
            P.barrier(); P.release(keep)
            if "diff" in skip:
                self.zero_o(b, 4, 8)
            else:
                self.mix_diff(b)
        else:
            if "ssd" in skip:
                self.zero_o(b, 0, 4)
            else:
                self.mix_ssd(b)
            P.barrier(); P.release(keep)
            if "swa" in skip:
                self.zero_o(b, 4, 8)
            else:
                self.mix_swa(b)

    TILES = [(0, 256), (256, 512), (768, 512), (1280, 512), (1792, 512)]

    def load_w(self, wt, src_cols):
        n = src_cols.shape[1]
        self.DMA("pool", wt[:, :, 0:n], src_cols.rearrange("(c p) n -> p c n", p=128), W=wt.r)

    def proj_fm(self, pst, wt, col0, t0, n):
        for c in range(8):
            self.MM(pst[:, 0:n], wt[:, c, col0:col0 + 128], self.hmod[:, c, t0:t0 + n], c == 0, c == 7,
                    wt.r + self.hmod.r, pst.r)

    def proj_tok(self, pst, wt, col0, ncols, ch):
        for c in range(8):
            self.MM(pst[:, 0:ncols], self.hmod[:, c, ch * 128:(ch + 1) * 128], wt[:, c, col0:col0 + ncols], c == 0, c == 7,
                    wt.r + self.hmod.r, pst.r)

    def proj_rope(self, dst, wt, wpt, col0, pcol0, rope):
        cos, sin = rope
        for i, (t0, n) in enumerate(self.TILES):
            p1 = self.ps[(2 * i) % 4]
            self.proj_fm(p1, wt, col0, t0, n)
            if t0 < TC:
                self.CP("act", dst[:, t0:t0 + n], p1[:, 0:n], p1.r, dst.r)
                continue
            p2 = self.ps[(2 * i + 1) % 4]
            self.proj_fm(p2, wpt, pcol0, t0, n)
            l0 = t0 - TC
            ta, tb = self.rtmp
            self.TT("dve", ta[:, 0:n], p1[:, 0:n], cos[:, l0:l0 + n], ALU.mult, p1.r + cos.r, ta.r)
            self.TT("dve", tb[:, 0:n], p2[:, 0:n], sin[:, l0:l0 + n], ALU.mult, p2.r + sin.r, tb.r)
            self.TT("pool", dst[:, t0:t0 + n], ta[:, 0:n], tb[:, 0:n], ALU.add, ta.r + tb.r, dst.r)

    def load_rope(self):
        cos = self.tile([128, TL], F32, "cos")
        sin = self.tile([128, TL], F32, "sin")
        self.DMA("sp", cos[:], self.rope_d[0], W=cos.r)
        self.DMA("sp", sin[:], self.rope_d[1], W=sin.r)
        self.rtmp = (self.tile([128, 512], F32, "rta"), self.tile([128, 512], F32, "rtb"))
        return cos, sin


    def mix_swa(self, b):
        rope = self.load_rope()
        C0 = 1552
        sk = self.tile([1, 8], F32, "sk")
        self.DMA("sp", sk[:], self.sink_d, W=sk.r)
        self.ACT(sk[:], sk[:], AF.Exp, sk.r, sk.r)
        pb = self.ps[7]
        self.MM(pb[:, 0:8], self.ones_f[0:1, :], sk[0:1, 0:8], True, True, self.ones_f.r + sk.r, pb.r)
        esink = self.tile([128, 8], F32, "esink")
        self.CP("dve", esink[:], pb[:, 0:8], pb.r, esink.r)
        mk = self.tile([128, 6, 512], BF16, "swamask")
        self.DMA("pool", mk[:], self.swam_d.rearrange("r p q -> p r q"), W=mk.r)
        wk = [self.tile([128, 8, 128], BF16, f"wk{i}") for i in range(2)]
        kbs = [self.tile([128, T], BF16, f"kb{g}") for g in range(2)]
        for g in range(2):
            for half in range(2):
                self.DMA("pool", wk[0][:, :, half * 64:(half + 1) * 64],
                         self.cd_w[:, C0 + 512 + g * 64:C0 + 512 + (g + 1) * 64].rearrange("(c p) n -> p c n", p=128), W=wk[0].r)
                self.DMA("pool", wk[1][:, :, half * 64:(half + 1) * 64],
                         self.cd_perm[:, 512 + g * 64:512 + (g + 1) * 64].rearrange("(c p) n -> p c n", p=128), W=wk[1].r)
            self.proj_rope(kbs[g], wk[0], wk[1], 0, 0, rope)
        wv = self.tile([128, 8, 128], BF16, "wv")
        self.load_w(wv, self.cd_w[:, C0 + 640:C0 + 768])
        vtok = self.tile([128, NCH, 128], BF16, "vtok")
        for ch in range(NCH):
            pst = self.ps[ch % 4]
            self.proj_tok(pst, wv, 0, 128, ch)
            self.CP("act" if ch % 2 else "dve", vtok[:, ch, :], pst[:, 0:128], pst.r, vtok.r)
        wq = [self.tile([128, 8, 128], BF16, f"wq{i}") for i in range(2)]
        qb = self.tile([128, T], BF16, "qb")
        pt = [self.tile([128, 512], BF16, f"pt{i}") for i in range(3)]
        ft = self.tile([128, 512], F32, "ft")
        ob = [self.tile([128, 512], BF16, f"ob{i}") for i in range(2)]
        npt = 0
        cnt = 0
        for cq in range(4):
            self.load_w(wq[0], self.cd_w[:, C0 + cq * 128:C0 + (cq + 1) * 128])
            self.load_w(wq[1], self.cd_perm[:, cq * 128:(cq + 1) * 128])
            self.proj_rope(qb, wq[0], wq[1], 0, 0, rope)
            for hh in range(2):
                hq = cq * 2 + hh
                kv = hq // 4
                qs = hh * 64
                ks = qs
                kb = kbs[kv]
                for qt in range(4):
                    t0 = TC + qt * 512
                    kts = [(0, None), (1, None)] + [(2 + kk, kk - 4 * qt + 1)
                                                    for kk in range(max(0, 4 * qt - 1), min(15, 4 * qt + 4) + 1)]
                    Oa, Da = self.ps[4 + cnt % 2], self.ps[6 + cnt % 2]
                    cnt += 1
                    for ki, (ch, r) in enumerate(kts):
                        S = self.ps[npt % 4]
                        p = pt[npt % 3]
                        npt += 1
                        self.MM(S[:, :], kb[ks:ks + 64, ch * 128:(ch + 1) * 128], qb[qs:qs + 64, t0:t0 + 512],
                                True, True, kb.r + qb.r, S.r)
                        self.ACT(p[:, :], S[:, :], AF.Exp, S.r, p.r, scale=0.125)
                        if r is not None:
                            self.TT("pool", p[:, :], p[:, :], mk[:, r, :], ALU.mult, p.r + mk.r, p.r)
                        first, lastk = ki == 0, ki == len(kts) - 1
                        self.MM(Oa[0:64, :], vtok[:, ch, kv * 64:(kv + 1) * 64], p[:, :], first, lastk, vtok.r + p.r, Oa.r)
                        self.MM(Da[0:64, :], self.ones_b[:, 0:64], p[:, :], first, lastk, self.ones_b.r + p.r, Da.r)
                    self.TS("dve", ft[0:64, :], Da[0:64, :], esink[0:64, hq:hq + 1], None, ALU.add, None, Da.r + esink.r, ft.r)
                    self.RECIP(ft[0:64, :], ft[0:64, :], ft.r, ft.r)
                    o = ob[cnt % 2]
                    self.TT("dve", o[0:64, :], Oa[0:64, :], ft[0:64, :], ALU.mult, Oa.r + ft.r, o.r)
                    self.DMA("sp", self.O[b, 4 + cq, qs:qs + 64, t0:t0 + 512], o[0:64, :], R=o.r)

    def mix_ssd(self, b):
        P = self.P
        hmod = self.hmod
        cw = self.tile([128, 8, 5], F32, "cw")
        cb = self.tile([128, 8], F32, "cb")
        dsk = self.tile([128, 4], F32, "dsk")
        ng = self.tile([128, 4], F32, "ng")
        self.DMA("sp", cw[:], self.convw_d, W=cw.r)
        self.DMA("sp", cb[:], self.convb_d, W=cb.r)
        self.DMA("sp", dsk[:], self.dskip_d, W=dsk.r)
        self.DMA("sp", ng[:], self.ssdg_d, W=ng.r)
        dtb = self.tile([1, 16], F32, "dtb")
        al = self.tile([1, 16], F32, "al")
        self.DMA("sp", dtb[:], self.dtb_d, W=dtb.r)
        self.DMA("sp", al[:], self.alog_d, W=al.r)
        self.ACT(al[:], al[:], AF.Exp, al.r, al.r)
        pb = self.ps[7]
        self.MM(pb[:, 0:16], self.ones_f[0:1, :], al[0:1, 0:16], True, True, self.ones_f.r + al.r, pb.r)
        aneg = self.tile([128, 16], F32, "aneg")
        self.TS("dve", aneg[:], pb[:, 0:16], -1.0, None, ALU.mult, None, pb.r, aneg.r)
        zs = self.tile([128, 4, T], BF16, "zs")
        xact = self.tile([128, 8, T], BF16, "xact")
        xs_tok = self.tile([128, NCH, 512], BF16, "xs_tok")
        B_tok = self.tile([128, NCH, 256], BF16, "B_tok")
        Yacc = self.tile([128, NCH, 512], F32, "Yacc", nres=NCH)
        dt_all = self.tile([128, NCH, 16], F32, "dt_all")
        a_all = self.tile([128, NCH, 16], F32, "a_all")
        keep2 = P.mark()
        wt = [self.tile([128, 8, 128], BF16, f"wssd{i}") for i in range(2)]
        pre = self.tile([128, T], F32, "pre")
        acc = self.tile([128, T], F32, "acc")
        for c in range(4):
            w = wt[c % 2]
            self.load_w(w, self.cd_w[:, c * 128:(c + 1) * 128])
            for i, (t0, n) in enumerate(self.TILES):
                pst = self.ps[i % 4]
                self.proj_fm(pst, w, 0, t0, n)
                self.ACT(zs[:, c, t0:t0 + n], pst[:, 0:n], AF.Silu, pst.r, zs.r)
        for c in range(8):
            w = wt[c % 2]
            self.load_w(w, self.cd_w[:, 512 + c * 128:512 + (c + 1) * 128])
            for i, (t0, n) in enumerate(self.TILES):
                pst = self.ps[i % 4]
                self.proj_fm(pst, w, 0, t0, n)
                self.CP("act" if i % 2 else "dve", pre[:, t0:t0 + n], pst[:, 0:n], pst.r, pre.r)
            self.ACT(acc[:, :], pre[:, :], AF.Identity, pre.r + cw.r + cb.r, acc.r, bias=cb[:, c:c + 1], scale=cw[:, c, 2:3])
            for j in (0, 1, 3, 4):
                sft = j - 2
                for (lo, hi) in ((0, TC), (TC, T)):
                    a0, a1 = max(lo, lo - sft), min(hi, hi - sft)
                    self.STT("dve", acc[:, a0:a1], pre[:, a0 + sft:a1 + sft], cw[:, c, j:j + 1], acc[:, a0:a1],
                             ALU.mult, ALU.add, pre.r + cw.r + acc.r, acc.r)
            self.ACT(xact[:, c, :], acc[:, :], AF.Silu, acc.r, xact.r)
        wdt = self.tile([128, 8, 16], BF16, "wdt")
        self.load_w(wdt, self.cd_w[:, 1536:1552])
        for ch in range(NCH):
            pst = self.ps[ch % 4]
            self.MM(pst[:, 0:16], self.ones_f[0:1, :], dtb[0:1, 0:16], True, False, self.ones_f.r + dtb.r, pst.r)
            for c in range(8):
                self.MM(pst[:, 0:16], hmod[:, c, ch * 128:(ch + 1) * 128], wdt[:, c, :], False, c == 7, wdt.r + hmod.r, pst.r)
            self.ACT(dt_all[:, ch, :], pst[:, 0:16], AF.Exp, pst.r, dt_all.r)
        self.ACT(dt_all[:, :, :], dt_all[:, :, :], AF.Ln, dt_all.r + self.cst.r, dt_all.r, bias=self.cst[:, 3:4])
        self.TT("dve", a_all[:, :, :], dt_all[:, :, :], aneg[:, :].unsqueeze(1).to_broadcast([128, NCH, 16]), ALU.mult,
                dt_all.r + aneg.r, a_all.r)
        for ch in range(NCH):
            pst = self.ps[ch % 4]
            for c in range(4):
                self.MM(pst[:, c * 128:(c + 1) * 128], xact[:, c, ch * 128:(ch + 1) * 128], self.ident_b[:], True, True,
                        xact.r + self.ident_b.r, pst.r)
            self.CP("act", xs_tok[:, ch, :], pst[:, :], pst.r, xs_tok.r)
            pst2 = self.ps[4 + ch % 2]
            for g in range(2):
                self.MM(pst2[:, g * 128:(g + 1) * 128], xact[:, 4 + g, ch * 128:(ch + 1) * 128], self.ident_b[:], True, True,
                        xact.r + self.ident_b.r, pst2.r)
            self.CP("dve", B_tok[:, ch, :], pst2[:, 0:256], pst2.r, B_tok.r)
        P.barrier()
        P.release(keep2)
        keep3 = P.mark()
        hT = [self.tile([128, 2, 256], F32, f"hT{d}") for d in range(2)]
        hTb = [self.tile([128, 2, 256], BF16, f"hTb{d}") for d in range(2)]
        for d in range(2):
            self.MEMSET("pool", hT[d][:], 0.0, [], hT[d].r)
            self.MEMSET("pool", hTb[d][:], 0.0, [], hTb[d].r)
        ex = [self.tile([128, 24], F32, f"ex{d}") for d in range(2)]
        nacs = [self.tile([128, 8], F32, f"nacs{d}") for d in range(2)]
        xdt = [self.tile([128, 8, 64], BF16, f"xdt{d}") for d in range(2)]
        Xd = [self.tile([128, 8, 64], BF16, f"Xd{d}") for d in range(2)]
        Abc = [self.tile([128, 8, 128], F32, f"Abc{d}") for d in range(2)]
        Gs = [self.tile([128, 2, 128], BF16, f"Gs{d}") for d in range(2)]
        Lm = [self.tile([128, 128], BF16, f"Lm{i}") for i in range(4)]
        Wh = [self.tile([128, 128], BF16, f"Wh{i}") for i in range(4)]
        zt = [self.tile([128, 512], F32, f"zt{d}") for d in range(2)]
        order = [list(range(NCH)), [1, 0] + list(range(NCH - 1, 1, -1))]
        written = set()
        nl = 0
        for step in range(NCH):
            for d in range(2):
                ch = order[d][step]
                MI, MSo = self.masks[:, 2 * d, :], self.masks[:, 2 * (1 - d) + 1, :]
                NEG = self.negm[:, d, :]
                tk = slice(ch * 128, (ch + 1) * 128)
                a = a_all[:, ch, d * 8:(d + 1) * 8]
                pA = self.ps[d]
                self.MM(pA[:, 0:8], MI, a, True, True, self.masks.r + a_all.r, pA.r)
                self.MM(pA[:, 8:16], self.ones_f[:], a, True, True, self.ones_f.r + a_all.r, pA.r)
                self.MM(pA[:, 16:24], MSo, a, True, True, self.masks.r + a_all.r, pA.r)
                self.ACT(ex[d][:], pA[:, 0:24], AF.Exp, pA.r, ex[d].r)
                self.ACT(nacs[d][:], pA[:, 0:8], AF.Copy, pA.r, nacs[d].r, scale=-1.0)
                xsv = xs_tok[:, ch, :].rearrange("p (h e) -> p h e", h=8)
                self.TT("dve", xdt[d][:], xsv, dt_all[:, ch, d * 8:(d + 1) * 8].unsqueeze(2).to_broadcast([128, 8, 64]), ALU.mult,
                        xs_tok.r + dt_all.r, xdt[d].r)
                self.TT("pool", Xd[d][:], xdt[d][:], ex[d][:, 16:24].unsqueeze(2).to_broadcast([128, 8, 64]), ALU.mult,
                        xdt[d].r + ex[d].r, Xd[d].r)
                if ch >= 2:
                    self.CP("pool", Abc[d][:], a.unsqueeze(2).to_broadcast([128, 8, 128]), a_all.r, Abc[d].r)
                    pG = self.ps[2 + d]
                    for g in range(2):
                        self.MM(pG[:, g * 128:(g + 1) * 128], xact[:, 4 + g, tk], xact[:, 6 + g, tk], True, True, xact.r, pG.r)
                    self.CP("act", Gs[d][:], pG[:, 0:256].rearrange("p (g t) -> p g t", g=2), pG.r, Gs[d].r)
                    pY = self.ps[4 + d]
                    for h in range(8):
                        g = h // 4
                        pR = self.ps[6 + (nl % 2)]
                        lm, wh = Lm[nl % 4], Wh[nl % 4]
                        nl += 1
                        self.MM(pR[:, 0:128], Abc[d][:, h, :], MI, True, False, Abc[d].r + self.masks.r, pR.r)
                        self.MM(pR[:, 0:128], self.ident_f[:], NEG, False, True, self.ident_f.r + self.negm.r, pR.r)
                        self.ACT(lm[:], pR[:, 0:128], AF.Exp, pR.r + nacs[d].r, lm.r, bias=nacs[d][:, h:h + 1])
                        self.TT("pool", wh[:], lm[:], Gs[d][:, g, :], ALU.mult, lm.r + Gs[d].r, wh.r)
                        self.MM(pY[:, h * 64:(h + 1) * 64], wh[:], xdt[d][:, h, :], True, True, wh.r + xdt[d].r, pY.r)
                    pZ = self.ps[2 + d]
                    for g in range(2):
                        self.MM(pZ[:, g * 256:(g + 1) * 256], xact[:, 6 + g, tk], hTb[d][:, g, :], True, True,
                                xact.r + hTb[d].r, pZ.r)
                    z = zt[d]
                    self.TT("dve", z[:].rearrange("p (h e) -> p h e", h=8), pZ[:, :].rearrange("p (h e) -> p h e", h=8),
                            ex[d][:, 0:8].unsqueeze(2).to_broadcast([128, 8, 64]), ALU.mult, pZ.r + ex[d].r, z.r)
                    self.TT("dve", z[:], pY[:, :], z[:], ALU.add, pY.r + z.r, z.r)
                    if ch in written:
                        self.TT("pool", Yacc[:, ch, :], Yacc[:, ch, :], z[:], ALU.add, [Yacc.r[ch]] + z.r, [Yacc.r[ch]])
                    else:
                        self.CP("pool", Yacc[:, ch, :], z[:], z.r, [Yacc.r[ch]])
                        written.add(ch)
                pH = self.ps[d]
                for g in range(2):
                    self.MM(pH[:, g * 256:(g + 1) * 256], B_tok[:, ch, g * 128:(g + 1) * 128],
                            Xd[d][:, 4 * g:4 * g + 4, :].rearrange("p h e -> p (h e)"), True, True, B_tok.r + Xd[d].r, pH.r)
                hv = hT[d][:].rearrange("p g (h e) -> p (g h) e", h=4)
                self.TT("dve", hv, hv, ex[d][:, 8:16].unsqueeze(2).to_broadcast([128, 8, 64]), ALU.mult, hT[d].r + ex[d].r, hT[d].r)
                hf = hT[d][:].rearrange("p g x -> p (g x)")
                self.TT("dve", hf, hf, pH[:, :], ALU.add, hT[d].r + pH.r, hT[d].r)
                self.CP("act", hTb[d][:].rearrange("p g x -> p (g x)"), hf, hT[d].r, hTb[d].r)
        P.barrier()
        P.release(keep3)
        yg = [self.tile([128, 512], F32, f"yg{i}") for i in range(4)]
        sq = [self.tile([128, 512], F32, f"sq{i}") for i in range(2)]
        rs = self.tile([128, 512], F32, "rs")
        ob = [self.tile([128, 512], BF16, f"ob{i}") for i in range(2)]
        no = 0
        for qt in range(4):
            t0 = TC + qt * 512
            for c in range(4):
                pT = self.ps[c]
                for k4 in range(4):
                    ch = 2 + qt * 4 + k4
                    self.TR(pT[:, k4 * 128:(k4 + 1) * 128], Yacc[:, ch, c * 128:(c + 1) * 128], self.ident_f[:],
                            [Yacc.r[ch]] + self.ident_f.r, pT.r)
                self.STT("dve", yg[c][:], xact[:, c, t0:t0 + 512], dsk[:, c:c + 1], pT[:, :], ALU.mult, ALU.add,
                         xact.r + dsk.r + pT.r, yg[c].r)
                self.TT("pool", yg[c][:], yg[c][:], zs[:, c, t0:t0 + 512], ALU.mult, yg[c].r + zs.r, yg[c].r)
            for g in range(2):
                st = self.ps[4 + g]
                for k2 in range(2):
                    c = 2 * g + k2
                    self.ACT(sq[k2][:], yg[c][:], AF.Square, yg[c].r, sq[k2].r)
                    self.MM(st[:, :], self.ones_f[:], sq[k2][:], k2 == 0, k2 == 1, self.ones_f.r + sq[k2].r, st.r)
                self.ACT(rs[:], st[:, :], AF.Sqrt, st.r + self.cst.r, rs.r, bias=self.cst[:, 2:3], scale=1.0 / 256)
                self.RECIP(rs[:], rs[:], rs.r, rs.r)
                for k2 in range(2):
                    c = 2 * g + k2
                    self.TT("dve", yg[c][:], yg[c][:], rs[:], ALU.mult, yg[c].r + rs.r, yg[c].r)
                    o = ob[no % 2]
                    no += 1
                    self.ACT(o[:], yg[c][:], AF.Copy, yg[c].r + ng.r, o.r, scale=ng[:, c:c + 1])
                    self.DMA("sp", self.O[b, c, :, t0:t0 + 512], o[:], R=o.r)


    def bcast_row(self, src_row, n, name):
        t = self.tile([128, n], F32, name)
        self.DMA("sp", t[:], src_row.partition_broadcast(128), W=t.r)
        return t

    def phase_rwkv(self):
        P = self.P
        base = P.mark()
        self.PR = self.dscr("PR", [NB, 3, T, 512], F32)
        self.WD = self.dscr("WD", [NB, 2, T, 512], F32)
        self.PB = self.dscr("PB", [NB, 2, T, 5, 512], BF16)
        self.BG = self.dscr("BG", [NB, 2, T, 512], F32)
        keep = P.mark()
        for b in range(NB):
            self.rwkv_prep(b, None)
            P.barrier()
            P.release(keep)
        Yacc = [self.tile([128, NCH, 512], F32, f"Yacc{b}", nres=NCH) for b in range(NB)]
        keep2 = P.mark()
        self.rwkv_chunked(Yacc)
        P.barrier()
        if "YD" in self.debug:
            yd = self.dscr("YD", [NB, 128, NCH, 512], F32)
            for b in range(NB):
                self.DMA("sp", yd[b], Yacc[b][:], R=Yacc[b].r)
            P.barrier()
        P.release(keep2)
        for b in range(NB):
            self.rwkv_finish(b, Yacc[b])
            P.barrier()
            P.release(keep2)
        P.release(base)

    def rwkv_chunked(self, Yacc):
        P = self.P
        ps = self.ps
        c_ = CDEC
        mask4 = [self.tile([128, 4, 128], F32, f"mask4{d}") for d in range(2)]
        for d in range(2):
            MS, MI = self.masks[:, 2 * d + 1, :], self.masks[:, 2 * d, :]
            for q, m in enumerate((MS, MI, MS, MI)):
                self.CP("pool", mask4[d][:, q, :], m, self.masks.r, mask4[d].r)
        Sf = [[self.tile([128, 4, 64], F32, f"Sf{b}{d}") for d in range(2)] for b in range(NB)]
        Sb = [[self.tile([128, 4, 64], BF16, f"Sb{b}{d}") for d in range(2)] for b in range(NB)]
        for b in range(NB):
            for d in range(2):
                self.MEMSET("pool", Sf[b][d][:], 0.0, [], Sf[b][d].r)
                self.MEMSET("pool", Sb[b][d][:], 0.0, [], Sb[b][d].r)
        pbin = [self.tile([128, 5, 512], BF16, f"pbin{i}") for i in range(2)]
        lw = [self.tile([128, 512], F32, f"lw{i}") for i in range(2)]
        E = [self.tile([128, 512], F32, f"E{i}") for i in range(3)]
        Xt = self.tile([128, 4, 512], BF16, "Xt")
        FM = [self.tile([128, 4, 128], BF16, f"FM{c}") for c in range(4)]
        PC = self.tile([128, 4], F32, "PC")
        Mm = [self.tile([128, 4, 128], BF16, f"Mm{h}") for h in range(8)]
        AAT = [[self.tile([128, 2, 128], F32, f"AAT{h}{i}") for i in range(2)] for h in range(8)]
        Wb = [self.tile([128, 64], F32, f"Wb{h}") for h in range(8)]
        Up = self.tile([128, 8, 64], BF16, "Up")
        ysb = self.tile([128, 512], F32, "ysb")
        tS = self.tile([128, 4, 64], F32, "tS")
        order = [list(range(NCH)), [1, 0] + list(range(NCH - 1, 1, -1))]
        written = [set() for _ in range(NB)]
        nu = 0
        import os
        ndirs = int(os.environ.get("RWKV_DIRS", 2))
        for step in range(NCH):
            for d in range(ndirs):
                for b in range(NB):
                    ch = order[d][step]
                    tk = slice(ch * 128, (ch + 1) * 128)
                    MI, MS, MSo = self.masks[:, 2 * d, :], self.masks[:, 2 * d + 1, :], self.masks[:, 2 * (1 - d) + 1, :]
                    pin, lwt = pbin[nu % 2], lw[nu % 2]
                    nu += 1
                    self.DMA("sp", pin[:], self.PB[b, d, tk, :, :], W=pin.r)
                    self.DMA("sp", lwt[:], self.WD[b, d, tk, :], W=lwt.r)
                    S_f, S_b = Sf[b][d], Sb[b][d]
                    self.MM(ps[0][:, :], MI, lwt[:], True, True, self.masks.r + lwt.r, ps[0].r)
                    self.MM(ps[1][:, :], MS, lwt[:], True, True, self.masks.r + lwt.r, ps[1].r)
                    for c in range(4):
                        self.MM(ps[2][:, c:c + 1], lwt[:, c * 128:(c + 1) * 128], self.ones_f[:, 0:1], True, True,
                                lwt.r + self.ones_f.r, ps[2].r)
                    self.ACT(E[0][:], ps[0][:, :], AF.Exp, ps[0].r, E[0].r, scale=-c_)
                    self.ACT(E[1][:], ps[0][:, :], AF.Exp, ps[0].r, E[1].r, scale=c_)
                    self.ACT(E[2][:], ps[1][:, :], AF.Exp, ps[1].r, E[2].r, scale=-c_)
                    self.ACT(PC[:], ps[2][:, 0:4], AF.Exp, ps[2].r, PC.r, scale=-c_)
                    self.TT("dve", Xt[:, 0, :], pin[:, 0, :], E[2][:], ALU.mult, pin.r + E[2].r, Xt.r)
                    self.TT("pool", Xt[:, 1, :], pin[:, 3, :], E[0][:], ALU.mult, pin.r + E[0].r, Xt.r)
                    self.TT("dve", Xt[:, 2, :], pin[:, 1, :], E[1][:], ALU.mult, pin.r + E[1].r, Xt.r)
                    self.TT("pool", Xt[:, 3, :], pin[:, 2, :], E[1][:], ALU.mult, pin.r + E[1].r, Xt.r)
                    for c in range(4):
                        pt_ = ps[2 + c % 2]
                        for q in range(4):
                            self.MM(pt_[:, q * 128:(q + 1) * 128], Xt[:, q, c * 128:(c + 1) * 128], self.ident_b[:], True, True,
                                    Xt.r + self.ident_b.r, pt_.r)
                        self.CP("act" if c % 2 else "dve", FM[c][:], pt_[:, :].rearrange("p (q t) -> p q t", q=4), pt_.r, FM[c].r)
                    first_w = True
                    first_y = True
                    for grp in range(2):
                        heads = list(range(grp * 4, grp * 4 + 4))
                        cur = {}
                        for h in heads:
                            c, hb = h // 2, (h % 2) * 64
                            fm = FM[c]
                            hs_ = slice(hb, hb + 64)
                            pm = ps[3]
                            AR = fm[hs_, 0:2, :].rearrange("p q t -> p (q t)")
                            self.MM(pm[:, 0:256], fm[hs_, 2, :], AR, True, True, fm.r, pm.r)
                            self.MM(pm[:, 256:512], fm[hs_, 3, :], AR, True, True, fm.r, pm.r)
                            self.TT("dve", Mm[h][:], pm[:, :].rearrange("p (q t) -> p q t", q=4), mask4[d][:], ALU.mult,
                                    pm.r + mask4[d].r, Mm[h].r)
                            pn = ps[6]
                            self.MM(pn[:, 0:128], fm[hs_, 0, :], fm[hs_, 2, :], True, True, fm.r, pn.r)
                            a0 = AAT[h][0]
                            self.TT("dve", a0[:, 0, :], pm[:, 0:128], MS, ALU.mult, pm.r + self.masks.r, a0.r)
                            self.TT("dve", a0[:, 1, :], pn[:, 0:128], MSo, ALU.mult, pn.r + self.masks.r, a0.r)
                            cur[h] = 0
                            wp = ps[7][:, h * 64:(h + 1) * 64]
                            self.MM(wp, fm[hs_, 0, :], S_b[hs_, c, :], first_w, False, fm.r + S_b.r, ps[7].r)
                            first_w = False
                            self.MM(wp, Mm[h][:, 2, :], pin[:, 4, h * 64:(h + 1) * 64], False, False, Mm[h].r + pin.r, ps[7].r)
                        for lvl in range(7):
                            for h in heads:
                                wp = ps[7][:, h * 64:(h + 1) * 64]
                                self.CP("act" if h % 2 else "dve", Wb[h][:], wp, ps[7].r, Wb[h].r)
                                A = AAT[h][cur[h]]
                                self.MM(wp, A[:, 0, :], Wb[h][:], False, False, A.r + Wb[h].r, ps[7].r)
                            if lvl == 6:
                                break
                            for j, h in enumerate(heads):
                                A = AAT[h][cur[h]]
                                pq = ps[4 + j // 2]
                                o0 = (j % 2) * 256
                                self.MM(pq[:, o0:o0 + 128], A[:, 1, :], A[:, 0, :], True, True, A.r, pq.r)
                                if lvl < 5:
                                    self.MM(pq[:, o0 + 128:o0 + 256], A[:, 0, :], A[:, 1, :], True, True, A.r, pq.r)
                            for j, h in enumerate(heads):
                                An = AAT[h][1 - cur[h]]
                                pq = ps[4 + j // 2]
                                o0 = (j % 2) * 256
                                if lvl < 5:
                                    self.CP("act" if j // 2 else "dve", An[:, :, :],
                                            pq[:, o0:o0 + 256].rearrange("p (q t) -> p q t", q=2), pq.r, An.r)
                                else:
                                    self.CP("act" if j // 2 else "dve", An[:, 0, :], pq[:, o0:o0 + 128], pq.r, An.r)
                                cur[h] = 1 - cur[h]
                        for h in heads:
                            c, hb = h // 2, (h % 2) * 64
                            fm = FM[c]
                            hs_ = slice(hb, hb + 64)
                            wp = ps[7][:, h * 64:(h + 1) * 64]
                            self.CP("act" if h % 2 else "dve", Up[:, h, :], wp, ps[7].r, Up.r)
                            yp = ps[0][:, h * 64:(h + 1) * 64]
                            self.MM(yp, fm[hs_, 1, :], S_b[hs_, c, :], first_y, False, fm.r + S_b.r, ps[0].r)
                            first_y = False
                            self.MM(yp, Mm[h][:, 1, :], Up[:, h, :], False, False, Mm[h].r + Up.r, ps[0].r)
                            self.MM(yp, Mm[h][:, 3, :], pin[:, 4, h * 64:(h + 1) * 64], False, False, Mm[h].r + pin.r, ps[0].r)
                    if ch in written[b]:
                        self.TT("dve", Yacc[b][:, ch, :], Yacc[b][:, ch, :], ps[0][:, :], ALU.add, [Yacc[b].r[ch]] + ps[0].r,
                                [Yacc[b].r[ch]])
                    else:
                        written[b].add(ch)
                        self.CP("dve", Yacc[b][:, ch, :], ps[0][:, :], ps[0].r, [Yacc[b].r[ch]])
                    for c in range(4):
                        pd = ps[1][:, c * 128:(c + 1) * 128]
                        self.MM(pd, Xt[:, 2, c * 128:(c + 1) * 128], Up[:, 2 * c:2 * c + 2, :].rearrange("p h e -> p (h e)"),
                                c == 0, False, Xt.r + Up.r, ps[1].r)
                        self.MM(pd, Xt[:, 3, c * 128:(c + 1) * 128], pin[:, 4, c * 128:(c + 1) * 128], False, False,
                                Xt.r + pin.r, ps[1].r)
                    for c in range(4):
                        self.TS("dve", tS[:, c, :], S_f[:, c, :], PC[:, c:c + 1], None, ALU.mult, None, S_f.r + PC.r, tS.r)
                        for hh in range(2):
                            hs_ = slice(hh * 64, hh * 64 + 64)
                            self.STT("dve", S_f[hs_, c, :], ps[1][hs_, c * 128 + hh * 64:c * 128 + hh * 64 + 64], PC[hs_, c:c + 1],
                                     tS[hs_, c, :], ALU.mult, ALU.add, ps[1].r + PC.r + tS.r, S_f.r)
                    self.CP("act", S_b[:], S_f[:], S_f.r, S_b.r)

    def rwkv_prep(self, b, Vp):
        P = self.P
        tw = self.tile([128, T], BF16, "twxa")
        sg = self.tile([128, T], BF16, "sg")
        keepA = P.mark()
        hmod = self.tile([128, 8, T], BF16, "hmod_b")
        hs = self.tile([128, 8, T], BF16, "hs_b")
        for c in range(8):
            self.DMA("sp", hmod[:, c, :], self.HM[b, c, :, :], W=hmod.r)
        for (lo, hi) in ((0, TC), (TC, T)):
            self.TT("pool", hs[:, :, lo + 1:hi - 1], hmod[:, :, lo:hi - 2], hmod[:, :, lo + 2:hi], ALU.add, hmod.r, hs.r)
            self.CP("dve", hs[:, :, lo:lo + 1], hmod[:, :, lo + 1:lo + 2], hmod.r, hs.r)
            self.CP("dve", hs[:, :, hi - 1:hi], hmod[:, :, hi - 2:hi - 1], hmod.r, hs.r)
        omm = self.bcast_row(self.rmu_d[0, :], 1792, "omm")
        hmu = self.tile([128, 1792], F32, "hmu")
        self.TS("dve", hmu[:], omm[:], 0.5, None, ALU.mult, None, omm.r, hmu.r)
        self.TS("dve", omm[:], omm[:], -1.0, 1.0, ALU.mult, ALU.add, omm.r, omm.r)

        def shifted_weights(wt, w1t, w2t, col0, n):
            self.load_w(wt, self.ab_w[:, col0:col0 + n])
            self.TT("dve", w1t[:, :, 0:n], wt[:, :, 0:n], omm[:, col0:col0 + n].unsqueeze(1).to_broadcast([128, 8, n]), ALU.mult,
                    wt.r + omm.r, w1t.r)
            self.TT("pool", w2t[:, :, 0:n], wt[:, :, 0:n], hmu[:, col0:col0 + n].unsqueeze(1).to_broadcast([128, 8, n]), ALU.mult,
                    wt.r + hmu.r, w2t.r)

        import os
        sub = int(os.environ.get("RWKV_SUB", 9))
        if sub <= 0:
            return
        wt = self.tile([128, 8, 512], BF16, "rw")
        w1t = self.tile([128, 8, 512], BF16, "rw1")
        w2t = self.tile([128, 8, 512], BF16, "rw2")
        for gi, col0 in enumerate((1536, 1664)):
            shifted_weights(wt, w1t, w2t, col0, 128)
            for i, (t0, n) in enumerate(self.TILES):
                pst = self.ps[i % 4]
                for c in range(8):
                    self.MM(pst[:, 0:n], w1t[:, c, 0:128], hmod[:, c, t0:t0 + n], c == 0, False, w1t.r + hmod.r, pst.r)
                for c in range(8):
                    self.MM(pst[:, 0:n], w2t[:, c, 0:128], hs[:, c, t0:t0 + n], False, c == 7, w2t.r + hs.r, pst.r)
                if gi == 0:
                    self.ACT(tw[0:64, t0:t0 + n], pst[0:64, 0:n], AF.Tanh, pst.r, tw.r)
                    self.ACT(tw[64:128, t0:t0 + n], pst[64:128, 0:n], AF.Copy, pst.r, tw.r)
                else:
                    self.ACT(sg[:, t0:t0 + n], pst[:, 0:n], AF.Sigmoid, pst.r, sg.r)
        if sub <= 1:
            return
        stg = [self.tile([128, 512], F32, f"stg{i}") for i in range(2)]
        vb16 = self.tile([128, 8, 128], BF16, "vb16")
        self.MEMSET("pool", vb16[:], 0.0, [], vb16.r)
        ns = 0
        for grp in range(3):
            shifted_weights(wt, w1t, w2t, grp * 512, 512)
            for ch in range(NCH):
                tk = slice(ch * 128, (ch + 1) * 128)
                pst = self.ps[ch % 4]
                for c in range(8):
                    self.MM(pst[:, :], hmod[:, c, tk], w1t[:, c, :], c == 0, False, w1t.r + hmod.r, pst.r)
                for c in range(8):
                    self.MM(pst[:, :], hs[:, c, tk], w2t[:, c, :], False, c == 7, w2t.r + hs.r, pst.r)
                st = stg[ns % 2]
                ns += 1
                self.CP("act", st[:], pst[:, :], pst.r, st.r)
                self.DMA("sp", self.PR[b, grp, tk, :], st[:], R=st.r)
                if False:
                    off = 64 * b
                    self.CP("dve", vb16[:, :, off:off + 64], st[:].rearrange("p (h e) -> p h e", h=8), st.r, vb16.r)
                    for hh in range(2):
                        pV = self.ps[4 + hh]
                        for h4 in range(4):
                            h = hh * 4 + h4
                            self.MM(pV[0:64 + off, h4 * 128:(h4 + 1) * 128], vb16[:, h, 0:64 + off], self.ident_b[:], True, True,
                                    vb16.r + self.ident_b.r, pV.r)
                        self.CP("dve" if hh else "act", Vp[off:off + 64, hh * 4:hh * 4 + 4, tk],
                                pV[off:off + 64, :].rearrange("p (h t) -> p h t", h=4), pV.r, Vp.r)
        P.barrier()
        P.release(keepA)
        if sub <= 2:
            return
        rv = [self.bcast_row(self.rvec_d[i, :], 512, f"rv{i}") for i in range(9)]
        kk_bc, ka_bc, rk_bc, _, _, w0a, w0b, a0a, a0b = rv
        omka = self.tile([128, 512], F32, "omka")
        self.TS("dve", omka[:], ka_bc[:], -1.0, 1.0, ALU.mult, ALU.add, ka_bc.r, omka.r)
        w2b = self.tile([64, 2, 512], BF16, "w2b")
        a2b = self.tile([128, 2, 512], BF16, "a2b")
        g2b = self.tile([128, 512], BF16, "g2b")
        self.DMA("pool", w2b[:], self.rw2_d.rearrange("d k n -> k d n"), W=w2b.r)
        self.DMA("pool", a2b[64:128, :, :], self.ra2_d.rearrange("d k n -> k d n"), W=a2b.r)
        self.DMA("pool", g2b[:], self.rg2_d, W=g2b.r)
        rkv = [[self.tile([128, 512], F32, f"in{j}{i}") for i in range(3)] for j in range(2)]
        tmp = [self.tile([128, 512], F32, f"tm{i}") for i in range(4)]
        kk = self.tile([128, 512], F32, "kk")
        sm = [self.tile([128, 8], F32, f"sm{i}") for i in range(2)]
        pbst = [self.tile([128, 5, 512], BF16, f"pbst{i}") for i in range(2)]
        wdec = [self.tile([128, 512], F32, f"wdec{i}") for i in range(2)]
        bg = [self.tile([128, 512], F32, f"bg{i}") for i in range(2)]
        v3 = lambda t: t[:].rearrange("p (h e) -> p h e", h=8)
        bc8 = lambda t: t[:, 0:8].unsqueeze(2).to_broadcast([128, 8, 64])
        for ch in range(NCH):
            tk = slice(ch * 128, (ch + 1) * 128)
            r_t, k_t, v_t = rkv[ch % 2]
            for gi, tt in enumerate((r_t, k_t, v_t)):
                self.DMA("sp", tt[:], self.PR[b, gi, tk, :], W=tt.r)
            t0_, t1_, t2_, t3_ = tmp
            self.TT("dve", t0_[:], k_t[:], kk_bc[:], ALU.mult, k_t.r + kk_bc.r, t0_.r)
            self.TT("pool", t1_[:], t0_[:], t0_[:], ALU.mult, t0_.r, t1_.r)
            self.RED("dve", sm[0][:, 0:8], v3(t1_), ALU.add, t1_.r, sm[0].r)
            self.TS("dve", sm[0][:], sm[0][:], 1e-24, None, ALU.max, None, sm[0].r, sm[0].r)
            self.ACT(sm[0][:], sm[0][:], AF.Sqrt, sm[0].r, sm[0].r)
            self.RECIP(sm[0][:], sm[0][:], sm[0].r, sm[0].r)
            self.TT("dve", v3(kk), v3(t0_), bc8(sm[0]), ALU.mult, t0_.r + sm[0].r, kk.r)
            self.TT("pool", t1_[:], r_t[:], k_t[:], ALU.mult, r_t.r + k_t.r, t1_.r)
            self.TT("pool", t1_[:], t1_[:], rk_bc[:], ALU.mult, t1_.r + rk_bc.r, t1_.r)
            self.RED("dve", sm[1][:, 0:8], v3(t1_), ALU.add, t1_.r, sm[1].r)
            self.TT("dve", v3(bg[0]), v3(v_t), bc8(sm[1]), ALU.mult, v_t.r + sm[1].r, bg[0].r)
            self.DMA("sp", self.BG[b, 0, tk, :], bg[0][:], R=bg[0].r)
            pg = self.ps[4]
            self.MM(pg[:, :], sg[:, tk], g2b[:], True, True, sg.r + g2b.r, pg.r)
            self.CP("act", bg[1][:], pg[:, :], pg.r, bg[1].r)
            self.DMA("sp", self.BG[b, 1, tk, :], bg[1][:], R=bg[1].r)
            for d in range(2):
                pb_ = pbst[d]
                w0_bc, a0_bc = (w0a, a0a) if d == 0 else (w0b, a0b)
                pz = self.ps[d]
                self.MM(pz[:, :], tw[0:64, tk], w2b[0:64, d, :], True, True, tw.r + w2b.r, pz.r)
                self.TT("dve", t1_[:], pz[:, :], w0_bc[:], ALU.add, pz.r + w0_bc.r, t1_.r)
                self.ACT(wdec[d][:], t1_[:], AF.Sigmoid, t1_.r, wdec[d].r)
                self.DMA("sp", self.WD[b, d, tk, :], wdec[d][:], R=wdec[d].r)
                pa = self.ps[2 + d]
                self.MM(pa[:, :], tw[64:128, tk], a2b[64:128, d, :], True, True, tw.r + a2b.r, pa.r)
                self.TT("dve", t2_[:], pa[:, :], a0_bc[:], ALU.add, pa.r + a0_bc.r, t2_.r)
                self.ACT(t2_[:], t2_[:], AF.Sigmoid, t2_.r, t2_.r)
                self.TS("pool", pb_[:, 0, :], kk[:], -1.0, None, ALU.mult, None, kk.r, pb_.r)
                self.TT("pool", pb_[:, 1, :], kk[:], t2_[:], ALU.mult, kk.r + t2_.r, pb_.r)
                self.TT("dve", t3_[:], t2_[:], ka_bc[:], ALU.mult, t2_.r + ka_bc.r, t3_.r)
                self.TT("dve", t3_[:], t3_[:], omka[:], ALU.add, t3_.r + omka.r, t3_.r)
                self.TT("pool", pb_[:, 2, :], k_t[:], t3_[:], ALU.mult, k_t.r + t3_.r, pb_.r)
                self.CP("act", pb_[:, 3, :], r_t[:], r_t.r, pb_.r)
                self.CP("act", pb_[:, 4, :], v_t[:], v_t.r, pb_.r)
                self.DMA("sp", self.PB[b, d, tk, :, :], pb_[:], R=pb_.r)

    def rwkv_scan(self, Vp, Y):
        NS = 2
        NBUF = 3
        S = [self.tile([128, 512], F32, f"S{d}") for d in range(2)]
        for d in range(2):
            self.MEMSET("dve", S[d][:], 0.0, [], S[d].r)
        Wb = [[self.tile([128, NS, 512], F32, f"Wb{d}{i}") for i in range(NBUF)] for d in range(2)]
        Vb = [[self.tile([128, NS, 4, 512], BF16, f"Vb{d}{i}") for i in range(NBUF)] for d in range(2)]
        t1 = [self.tile([128, 512], F32, f"sc1{d}") for d in range(2)]
        t2 = [self.tile([128, 512], F32, f"sc2{d}") for d in range(2)]
        t3 = [self.tile([128, 512], F32, f"sc3{d}") for d in range(2)]
        sa = [self.tile([128, 8], F32, f"sa{d}") for d in range(2)]
        yts = [self.tile([128, 8], F32, f"yt{d}") for d in range(2)]
        order = [list(range(T)), list(range(TC - 1, -1, -1)) + list(range(T - 1, TC - 1, -1))]
        v3 = lambda ap: ap.rearrange("p (h e) -> p h e", h=8)
        nblk = T // NS
        ywritten = set()
        import os
        nblk = int(os.environ.get('RWKV_MAXBLK', nblk))

        def load(d, bi):
            toks = order[d][bi * NS:(bi + 1) * NS]
            lo = min(toks)
            wb, vb = Wb[d][bi % NBUF], Vb[d][bi % NBUF]
            for b in range(NB):
                self.DMA("sp", wb[b * 64:(b + 1) * 64, :, :], self.WD[b, d, lo:lo + NS, :].partition_broadcast(64), W=wb.r)
                self.DMA("sp", vb[b * 64:(b + 1) * 64, :, :, :], self.PB[b, d, lo:lo + NS, :, :].partition_broadcast(64), W=vb.r)

        for bi in range(min(NBUF - 1, nblk)):
            for d in range(2):
                load(d, bi)
        for bi in range(nblk):
            for d in range(2):
                if bi + NBUF - 1 < nblk:
                    load(d, bi + NBUF - 1)
            for j in range(NS):
                for d in range(2):
                    t = order[d][bi * NS + j]
                    lo = min(order[d][bi * NS:(bi + 1) * NS])
                    jj = t - lo
                    wb, vb = Wb[d][bi % NBUF], Vb[d][bi % NBUF]
                    Sd = S[d]
                    a_bc, b_bc, k_bc, r_bc = (vb[:, jj, q, :] for q in range(4))
                    self.TT("dve", t1[d][:], Sd[:], a_bc, ALU.mult, Sd.r + vb.r, t1[d].r)
                    self.RED("dve", sa[d][:, 0:8], v3(t1[d][:]), ALU.add, t1[d].r, sa[d].r)
                    self.TT("pool", Sd[:], Sd[:], wb[:, jj, :], ALU.mult, Sd.r + wb.r, Sd.r)
                    self.TT("dve", v3(t2[d][:]), v3(b_bc), sa[d][:, 0:8].unsqueeze(2).to_broadcast([128, 8, 64]), ALU.mult,
                            vb.r + sa[d].r, t2[d].r)
                    self.TT("dve", Sd[:], Sd[:], t2[d][:], ALU.add, Sd.r + t2[d].r, Sd.r)
                    self.TT("pool", v3(t3[d][:]), v3(k_bc), Vp[:, :, t:t + 1].to_broadcast([128, 8, 64]), ALU.mult,
                            vb.r + Vp.r, t3[d].r)
                    self.TT("dve", Sd[:], Sd[:], t3[d][:], ALU.add, Sd.r + t3[d].r, Sd.r)
                    self.TT("dve", t1[d][:], Sd[:], r_bc, ALU.mult, Sd.r + vb.r, t1[d].r)
                    ytd = yts[d]
                    self.RED("dve", ytd[:, 0:8], v3(t1[d][:]), ALU.add, t1[d].r, ytd.r)
                    if t not in ywritten:
                        ywritten.add(t)
                        self.CP("act", Y[:, :, t], ytd[:, 0:8], ytd.r, Y.r)
                    else:
                        self.TT("pool", Y[:, :, t], Y[:, :, t], ytd[:, 0:8], ALU.add, Y.r + ytd.r, Y.r)

    def rwkv_finish(self, b, Ya):
        rv = [self.bcast_row(self.rvec_d[i, :], 512, f"fv{i}") for i in (3, 4)]
        gng, gnb = rv
        y = [self.tile([128, 512], F32, f"fy{i}") for i in range(2)]
        sq = self.tile([128, 512], F32, "fsq")
        bon = [self.tile([128, 512], F32, f"fb{i}") for i in range(2)]
        gg = [self.tile([128, 512], F32, f"fg{i}") for i in range(2)]
        sm = [self.tile([128, 8], F32, f"fsm{i}") for i in range(2)]
        ot = [self.tile([128, 512], BF16, f"fot{i}") for i in range(2)]
        ob = [self.tile([128, 4, 128], BF16, f"fob{i}") for i in range(2)]
        v3 = lambda t: t[:].rearrange("p (h e) -> p h e", h=8)
        bc8 = lambda t: t[:, 0:8].unsqueeze(2).to_broadcast([128, 8, 64])
        for ch in range(NCH):
            tk = slice(ch * 128, (ch + 1) * 128)
            yy, bo, g_ = y[ch % 2], bon[ch % 2], gg[ch % 2]
            self.DMA("sp", bo[:], self.BG[b, 0, tk, :], W=bo.r)
            self.DMA("sp", g_[:], self.BG[b, 1, tk, :], W=g_.r)
            ysrc = Ya[:, ch, :].rearrange("p (h e) -> p h e", h=8)
            yr = [Ya.r[ch]]
            self.RED("dve", sm[0][:, 0:8], ysrc, ALU.add, yr, sm[0].r)
            self.TS("dve", sm[0][:], sm[0][:], 1.0 / 64, None, ALU.mult, None, sm[0].r, sm[0].r)
            self.TT("dve", v3(yy), ysrc, bc8(sm[0]), ALU.subtract, yr + sm[0].r, yy.r)
            self.ACT(sq[:], yy[:], AF.Square, yy.r, sq.r)
            self.RED("dve", sm[1][:, 0:8], v3(sq), ALU.add, sq.r, sm[1].r)
            self.ACT(sm[1][:], sm[1][:], AF.Sqrt, sm[1].r + self.cst.r, sm[1].r, bias=self.cst[:, 4:5], scale=1.0 / 64)
            self.RECIP(sm[1][:], sm[1][:], sm[1].r, sm[1].r)
            self.TT("dve", v3(yy), v3(yy), bc8(sm[1]), ALU.mult, yy.r + sm[1].r, yy.r)
            self.TT("pool", yy[:], yy[:], gng[:], ALU.mult, yy.r + gng.r, yy.r)
            self.TT("pool", yy[:], yy[:], gnb[:], ALU.add, yy.r + gnb.r, yy.r)
            self.TT("pool", yy[:], yy[:], bo[:], ALU.add, yy.r + bo.r, yy.r)
            o = ot[ch % 2]
            self.TT("dve", o[:], yy[:], g_[:], ALU.mult, yy.r + g_.r, o.r)
            pO = self.ps[2 + ch % 2]
            for c in range(4):
                self.MM(pO[:, c * 128:(c + 1) * 128], o[:, c * 128:(c + 1) * 128], self.ident_b[:], True, True,
                        o.r + self.ident_b.r, pO.r)
            oo = ob[ch % 2]
            self.CP("act", oo[:], pO[:, :].rearrange("p (c t) -> p c t", c=4), pO.r, oo.r)
            self.DMA("sp", self.O[b, 0:4, :, tk].rearrange("c p t -> p c t"), oo[:], R=oo.r)

    def mix_diff(self, b):
        P = self.P
        rope = self.load_rope()
        LAM_INIT = 0.2
        dl = self.tile([1, 4, 64], F32, "dl")
        self.DMA("sp", dl[:], self.diffl_d, W=dl.r)
        pr = self.tile([1, 2, 64], F32, "dlp")
        sm = self.tile([1, 2], F32, "dls")
        self.TT("dve", pr[:, 0, :], dl[:, 0, :], dl[:, 1, :], ALU.mult, dl.r, pr.r)
        self.TT("dve", pr[:, 1, :], dl[:, 2, :], dl[:, 3, :], ALU.mult, dl.r, pr.r)
        self.RED("dve", sm[:, 0:2], pr[:, :, :], ALU.add, pr.r, sm.r)
        self.ACT(sm[:, 0:2], sm[:, 0:2], AF.Exp, sm.r, sm.r)
        nl = self.tile([1, 1], F32, "nl")
        self.TT("dve", nl[:, 0:1], sm[:, 1:2], sm[:, 0:1], ALU.subtract, sm.r, nl.r)
        self.TS("dve", nl[:, 0:1], nl[:, 0:1], -LAM_INIT, None, ALU.add, None, nl.r, nl.r)
        nlam = self.tile([128, 1], F32, "nlam")
        pb = self.ps[7]
        self.MM(pb[:, 0:1], self.ones_f[0:1, :], nl[0:1, 0:1], True, True, self.ones_f.r + nl.r, pb.r)
        self.CP("dve", nlam[:], pb[:, 0:1], pb.r, nlam.r)
        sg = self.tile([128, 1], F32, "subg")
        self.DMA("sp", sg[:], self.subln_d, W=sg.r)
        self.TS("dve", sg[:], sg[:], 1.0 - LAM_INIT, None, ALU.mult, None, sg.r, sg.r)
        wv = self.tile([128, 8, 512], BF16, "wv")
        self.load_w(wv, self.ab_w[:, 2816:3328])
        vtok = self.tile([128, NCH, 512], BF16, "vtok")
        for ch in range(NCH):
            pst = self.ps[ch % 4]
            self.proj_tok(pst, wv, 0, 512, ch)
            self.CP("act" if ch % 2 else "dve", vtok[:, ch, :], pst[:, :], pst.r, vtok.r)
        wq = [self.tile([128, 8, 128], BF16, f"wq{i}") for i in range(4)]
        qb = self.tile([128, T], BF16, "qb")
        kb = self.tile([128, T], BF16, "kb")
        pt = [self.tile([128, 512], BF16, f"pt{i}") for i in range(3)]
        ft = [self.tile([128, 512], F32, f"ft{i}") for i in range(4)]
        ob = [self.tile([128, 512], BF16, f"ob{i}") for i in range(2)]
        npt = 0
        nout = 0
        for h in range(4):
            self.load_w(wq[0], self.ab_w[:, 1792 + h * 128:1792 + (h + 1) * 128])
            self.load_w(wq[1], self.ab_perm[:, h * 128:(h + 1) * 128])
            self.load_w(wq[2], self.ab_w[:, 2304 + h * 128:2304 + (h + 1) * 128])
            self.load_w(wq[3], self.ab_perm[:, 512 + h * 128:512 + (h + 1) * 128])
            self.proj_rope(qb, wq[0], wq[1], 0, 0, rope)
            self.proj_rope(kb, wq[2], wq[3], 0, 0, rope)
            for (t0, n) in self.TILES:
                kts = [0, 1] if t0 < TC else list(range(NCH))
                for m in range(2):
                    Oa, Da = self.ps[4 + m], self.ps[6 + m]
                    for ki, kt in enumerate(kts):
                        S = self.ps[npt % 4]
                        p = pt[npt % 3]
                        npt += 1
                        self.MM(S[:, 0:n], kb[m * 64:(m + 1) * 64, kt * 128:(kt + 1) * 128], qb[m * 64:(m + 1) * 64, t0:t0 + n],
                                True, True, kb.r + qb.r, S.r)
                        self.ACT(p[:, 0:n], S[:, 0:n], AF.Exp, S.r, p.r, scale=0.125)
                        self.MM(Oa[:, 0:n], vtok[:, kt, h * 128:(h + 1) * 128], p[:, 0:n], ki == 0, ki == len(kts) - 1,
                                vtok.r + p.r, Oa.r)
                        self.MM(Da[:, 0:n], self.ones_b[:], p[:, 0:n], ki == 0, ki == len(kts) - 1,
                                self.ones_b.r + p.r, Da.r)
                for m in range(2):
                    self.RECIP(ft[2 + m][:, 0:n], self.ps[6 + m][:, 0:n], self.ps[6 + m].r, ft[2 + m].r)
                    self.TT("dve", ft[m][:, 0:n], self.ps[4 + m][:, 0:n], ft[2 + m][:, 0:n], ALU.mult,
                            self.ps[4 + m].r + ft[2 + m].r, ft[m].r)
                self.STT("dve", ft[0][:, 0:n], ft[1][:, 0:n], nlam[:, 0:1], ft[0][:, 0:n], ALU.mult, ALU.add,
                         ft[1].r + nlam.r + ft[0].r, ft[0].r)
                self.ACT(ft[1][:, 0:n], ft[0][:, 0:n], AF.Square, ft[0].r, ft[1].r)
                st = self.ps[6]
                self.MM(st[:, 0:n], self.ones_f[:], ft[1][:, 0:n], True, True, self.ones_f.r + ft[1].r, st.r)
                self.ACT(ft[2][:, 0:n], st[:, 0:n], AF.Sqrt, st.r + self.cst.r, ft[2].r, bias=self.cst[:, 2:3], scale=1.0 / 128)
                self.RECIP(ft[2][:, 0:n], ft[2][:, 0:n], ft[2].r, ft[2].r)
                self.TT("pool", ft[0][:, 0:n], ft[0][:, 0:n], ft[2][:, 0:n], ALU.mult, ft[0].r + ft[2].r, ft[0].r)
                o = ob[nout % 2]
                nout += 1
                self.ACT(o[:, 0:n], ft[0][:, 0:n], AF.Copy, ft[0].r + sg.r, o.r, scale=sg[:, 0:1])
                self.DMA("sp", self.O[b, 4 + h, :, t0:t0 + n], o[:, 0:n], R=o.r)

    def layer_norm(self, y, N, gcol, bcol, hout, extra=None):
        st = self.ps[7]
        st2 = self.ps[6]
        sq = self.ln_sq
        for c in range(8):
            s = sq[c % 2]
            self.ACT(s[:, 0:N], y[:, c, :], AF.Square, y.r, s.r)
            self.MM(st[:, 0:N], self.ones_f[:], y[:, c, :], c == 0, c == 7, self.ones_f.r + y.r, st.r)
            self.MM(st2[:, 0:N], self.ones_f[:], s[:, 0:N], c == 0, c == 7, self.ones_f.r + s.r, st2.r)
        mean, rstd = self.ln_mean, self.ln_rstd
        self.ACT(mean[:, 0:N], st[:, 0:N], AF.Copy, st.r, mean.r, scale=1.0 / D)
        self.TT("dve", rstd[:, 0:N], mean[:, 0:N], mean[:, 0:N], ALU.mult, mean.r, rstd.r)
        self.STT("dve", rstd[:, 0:N], st2[:, 0:N], 1.0 / D, rstd[:, 0:N], ALU.mult, ALU.subtract, st2.r + rstd.r, rstd.r)
        self.ACT(rstd[:, 0:N], rstd[:, 0:N], AF.Sqrt, rstd.r + self.cst.r, rstd.r, bias=self.cst[:, 0:1])
        self.RECIP(rstd[:, 0:N], rstd[:, 0:N], rstd.r, rstd.r)
        for c in range(8):
            tmp = self.ln_tmp[c % 2]
            self.TT("dve", tmp[:, 0:N], y[:, c, :], mean[:, 0:N], ALU.subtract, y.r + mean.r, tmp.r)
            self.TT("pool", tmp[:, 0:N], tmp[:, 0:N], rstd[:, 0:N], ALU.mult, tmp.r + rstd.r, tmp.r)
            self.ACT(hout[:, c, :], tmp[:, 0:N], AF.Identity, tmp.r + self.lnp_t.r, hout.r,
                     bias=bcol(c), scale=gcol(c))
            if extra is not None:
                et, sfn, bfn = extra
                self.ACT(et[:, c, :], hout[:, c, :], AF.Identity, hout.r + self.S1.r + self.S1F.r + self.MOD.r, et.r,
                         bias=bfn(c), scale=sfn(c))

    def phase_ffn(self, layer, w_out, w1, w2):
        P = self.P
        last = layer == 1
        N = 256
        wo = self.tile([128, 8, D], BF16, "wo")
        w1t = self.tile([128, 8, 4 * D], BF16, "w1", nres=8)
        w2t = self.tile([128, 32, D], BF16, "w2", nres=32)
        for c in range(8):
            self.DMA("pool", wo[:, c, :], w_out[c * 128:(c + 1) * 128, :], W=wo.r)
        for c in range(8):
            self.DMA("pool", w1t[:, c, :], w1[c * 128:(c + 1) * 128, :], W=[w1t.r[c]])
        for c in range(32):
            self.DMA("pool", w2t[:, c, :], w2[c * 128:(c + 1) * 128, :], W=[w2t.r[c]])
        self.ln_sq = [self.tile([128, N], F32, f"lnsq{i}") for i in range(2)]
        self.ln_tmp = [self.tile([128, N], F32, f"lntmp{i}") for i in range(2)]
        self.ln_mean = self.tile([128, N], F32, "lnmean")
        self.ln_rstd = self.tile([128, N], F32, "lnrstd")
        o_t = [self.tile([128, 8, N], BF16, f"o_t{i}") for i in range(1)]
        h_t = [self.tile([128, 8, N], F32, f"h_t{i}") for i in range(2)]
        hmod = self.tile([128, 8, N], BF16, "hmodf")
        f_t = self.tile([128, 32, N], BF16, "f_t", nres=32)
        rl = [self.tile([128, N], F32, f"rl{i}") for i in range(2)]
        tok = [self.tile([128, D], F32, f"tok{i}") for i in range(2)] if last else None
        hm2 = self.tile([128, 8, N], BF16, "hm2") if not last else None
        tiles = []
        for b in range(NB):
            for t0 in range(0, T, N):
                if last and t0 < TC:
                    continue
                tiles.append((b, t0))
        lg = lambda k, c: self.lnp_t[:, layer, k, c:c + 1]
        npsum = 0
        ntok = 0
        for n, (b, t0) in enumerate(tiles):
            w = self.who(b, t0)
            ot, ht = o_t[0], h_t[n % 2]
            self.DMA("sp", ot[:], self.O[b, :, :, t0:t0 + N].rearrange("c p t -> p c t"), W=ot.r)
            self.DMA("sp", ht[:], self.H[b, :, :, t0:t0 + N].rearrange("c p t -> p c t"), W=ht.r)
            for oc in range(8):
                pst = self.ps[npsum % 6]
                npsum += 1
                for c in range(8):
                    self.MM(pst[:, 0:N], wo[:, c, oc * 128:(oc + 1) * 128], ot[:, c, :], c == 0, c == 7, wo.r + ot.r, pst.r)
                self.STT("dve", ht[:, oc, :], pst[:, 0:N], self.GA[:, layer, oc, w:w + 1], ht[:, oc, :], ALU.mult, ALU.add,
                         pst.r + self.GA.r + ht.r, ht.r)
            self.layer_norm(ht, N, lambda c: lg(0, c), lambda c: lg(1, c), ht,
                            extra=(hmod, lambda c: self.S1F[:, layer, c, w:w + 1], lambda c: self.SHF(layer, c, w)))
            for j in range(32):
                pst = self.ps[npsum % 6]
                npsum += 1
                for c in range(8):
                    self.MM(pst[:, 0:N], w1t[:, c, j * 128:(j + 1) * 128], hmod[:, c, :], c == 0, c == 7,
                            [w1t.r[c]] + hmod.r, pst.r)
                r = rl[j % 2]
                self.ACT(r[:, 0:N], pst[:, 0:N], AF.Relu, pst.r, r.r)
                self.TT("pool", f_t[:, j, :], r[:, 0:N], r[:, 0:N], ALU.mult, r.r, [f_t.r[j]])
            for oc in range(8):
                pst = self.ps[npsum % 6]
                npsum += 1
                for j in range(32):
                    self.MM(pst[:, 0:N], w2t[:, j, oc * 128:(oc + 1) * 128], f_t[:, j, :], j == 0, j == 31,
                            [w2t.r[j], f_t.r[j]], pst.r)
                self.STT("dve", ht[:, oc, :], pst[:, 0:N], self.GFA[:, layer, oc, w:w + 1], ht[:, oc, :], ALU.mult, ALU.add,
                         pst.r + self.GFA.r + ht.r, ht.r)
            if not last:
                self.layer_norm(ht, N, lambda c: lg(2, c), lambda c: lg(3, c), ht,
                                extra=(hm2, lambda c: self.S1[:, layer + 1, c, w:w + 1], lambda c: self.SH(layer + 1, c, w)))
                self.DMA("sp", self.H[b, :, :, t0:t0 + N].rearrange("c p t -> p c t"), ht[:], R=ht.r)
                self.DMA("sp", self.HM[b, :, :, t0:t0 + N].rearrange("c p t -> p c t"), hm2[:], R=hm2.r)
            else:
                self.layer_norm(ht, N, lambda c: lg(2, c), lambda c: lg(3, c), ht)
                for sub in range(N // 128):
                    tk = tok[ntok % 2]
                    ntok += 1
                    for half in range(2):
                        pst = self.ps[npsum % 6]
                        npsum += 1
                        for cc in range(4):
                            c = half * 4 + cc
                            self.TR(pst[:, cc * 128:(cc + 1) * 128], ht[:, c, sub * 128:(sub + 1) * 128], self.ident_f[:],
                                    ht.r + self.ident_f.r, pst.r)
                        self.CP("act", tk[:, half * 512:(half + 1) * 512], pst[:, :], pst.r, tk.r)
                    tl0 = t0 - TC + sub * 128
                    self.DMA("sp", self.out[b, tl0:tl0 + 128, :], tk[:], R=tk.r)


def _per_core_inputs(inp, core):
    b0 = core * NB
    f = lambda a: np.ascontiguousarray(a, dtype=np.float32)
    cv = np.stack([inp["c"][b0], inp["c"][b0 + 1], inp["c_ctx"]], 0)
    m = {
        "x": f(inp["x"][b0:b0 + NB]),
        "ctx": f(inp["ctx"][b0:b0 + NB]),
        "cvec": f(cv.reshape(3, 8, 128).transpose(2, 1, 0)),
        "mod_w": f(inp["mod_w"]),
        "mod_b": f(inp["mod_b"].reshape(2, 48, 128).transpose(2, 0, 1)),
        "lnp": f(np.stack([inp["ln_mix_g"], inp["ln_mix_b"], inp["ln_ffn_g"], inp["ln_ffn_b"]], 1)
                 .reshape(2, 4, 8, 128).transpose(3, 0, 1, 2)),
        "ffn_w1": f(inp["ffn_w1"]),
        "ffn_w2": f(inp["ffn_w2"]),
        "w_out": f(inp["w_out"]),
        "c_ident": np.eye(128, dtype=np.float32),
        "ab_w": f(inp["ab_w_in"][0]),
        "ab_perm": f(inp["ab_w_in"][0][:, 1792:2816][:, _PERM1024]),
        "cd_w": f(inp["cd_w_in"][0]),
        "cd_perm": f(inp["cd_w_in"][0][:, 1552:2192][:, _PERM1024[:640]]),
        "c_rope": _ROPE,
        "c_swamask": _SWAMASK,
        "diff_l": f(np.stack([inp["diff_lq1"][0], inp["diff_lk1"][0], inp["diff_lq2"][0], inp["diff_lk2"][0]], 0)[None]),
        "subln": f(inp["diff_subln_g"][0].reshape(128, 1)),
        "sink": f(inp["swa_sink"]),
        "c_masks": _MASKS,
        "c_neg": _NEG,
        "convw": f(inp["ssd_conv_w"][0].reshape(5, 8, 128).transpose(2, 1, 0)),
        "convb": f(inp["ssd_conv_b"][0].reshape(8, 128).T),
        "dtb": f(inp["ssd_dt_bias"][0].reshape(1, 16)),
        "alog": f(inp["ssd_a_log"][0].reshape(1, 16)),
        "dskip": f(np.repeat(inp["ssd_d"][0], 64).reshape(4, 128).T),
        "ssdg": f(inp["ssd_norm_g"][0].reshape(4, 128).T),
        "rwkv_vec": f(np.stack([inp["rwkv_k_k"][0], inp["rwkv_k_a"][0], inp["rwkv_r_k"][0].reshape(512), inp["rwkv_gn_g"][0],
                                inp["rwkv_gn_b"][0], inp["rwkv_w0"][0, 0], inp["rwkv_w0"][0, 1], inp["rwkv_a0"][0, 0],
                                inp["rwkv_a0"][0, 1]], 0)),
        "rwkv_mu": f(inp["rwkv_mu"]),
        "rwkv_w2": f(inp["rwkv_w2"][0]),
        "rwkv_a2": f(inp["rwkv_a2"][0]),
        "rwkv_g2": f(inp["rwkv_g2"][0]),
    }
    return m


def _mk_consts():
    d = np.arange(64)
    partner = np.where((d % 32) < 16, d + 16, d - 16)
    perm = (np.arange(1024) // 64) * 64 + partner[np.arange(1024) % 64]
    t = np.arange(TL)
    rows, cols = t // 64, t % 64
    p = np.arange(128)
    dd = p % 64
    i = dd % 16
    inv = 10000.0 ** (-(i.astype(np.float64)) / 16.0)
    pos = np.where((dd // 32)[:, None] == 0, rows[None, :], cols[None, :]).astype(np.float64)
    ang = (pos.astype(np.float32) * inv.astype(np.float32)[:, None]).astype(np.float32)
    cos = np.cos(ang).astype(np.float32)
    sin = np.sin(ang).astype(np.float32)
    sgn = np.where((dd % 32) < 16, -1.0, 1.0).astype(np.float32)[:, None]
    rope = np.stack([cos, sin * sgn], 0).astype(np.float32)
    k = np.arange(128)[:, None]
    q = np.arange(512)[None, :]
    m = np.stack([(np.abs(q - k - 128 * (r - 1)) <= 128) for r in range(6)], 0).astype(np.float32)
    return perm, rope, m


_PERM1024, _ROPE, _SWAMASK = _mk_consts()
_i = np.arange(128)[:, None]
_t = np.arange(128)[None, :]
_MASKS = np.stack([_i <= _t, _i < _t, _i >= _t, _i > _t], 0).astype(np.float32)
_NEG = ((_MASKS[[0, 2]] - 1.0) * 1e30).astype(np.float32)


_CACHE = {}


def kernel(**inputs):
    inp = {k: np.asarray(v) for k, v in inputs.items()}
    if "nc" not in _CACHE:
        _CACHE["nc"] = Builder().build()
    nc = _CACHE["nc"]
    in_maps = [_per_core_inputs(inp, c) for c in range(8)]
    res = run_bass_kernel_spmd(nc, in_maps, core_ids=list(range(8)))
    return np.concatenate([r["out"] for r in res.results], axis=0).astype(np.float32)
```

```python
import contextlib
import math
import numpy as np
import concourse.bass as bass
import concourse.mybir as mybir
from concourse.bass_utils import run_bass_kernel_spmd

F32 = mybir.dt.float32
BF16 = mybir.dt.bfloat16
AF = mybir.ActivationFunctionType
ALU = mybir.AluOpType
AX = mybir.AxisListType

ENGS = ["pe", "act", "dve", "pool", "sp"]
N_DMA_SEMS = 24

D = 1024
NB = 2
TC = 256
TL = 2048
T = TC + TL
NCH = T // 128
ALPHA = 4.0 ** 0.25
LN_EPS = 1e-5
EPS_P = LN_EPS / (ALPHA * ALPHA)
CDEC = math.exp(-0.5)


class Res:
    __slots__ = ("w", "r", "excl")

    def __init__(self):
        self.w = None
        self.r = []
        self.excl = False


class Prog:
    def __init__(self, nc, self_wait=True):
        self.nc = nc
        self.ops = {e: [] for e in ENGS}
        self.count = {e: 0 for e in ENGS}
        self.seen = {e: {} for e in ENGS}
        self.dma_val = [0] * N_DMA_SEMS
        self.dma_rr = 0
        self.self_wait = self_wait
        self.sb_top = 16512
        self.SB_BYTES = 229376
        self.n_alloc = 0

    def sb(self, shape, dtype, name=None):
        esz = 2 if dtype == BF16 else 4
        free = int(np.prod(shape[1:])) * esz
        free = (free + 63) // 64 * 64
        off = self.sb_top
        self.sb_top += free
        assert self.sb_top <= self.SB_BYTES, f"SBUF overflow {self.sb_top} ({name})"
        self.n_alloc += 1
        return self.nc.alloc_sbuf_tensor_at(f"{name or 't'}_{self.n_alloc}", list(shape), dtype, offset=off)

    def mark(self):
        return self.sb_top

    def release(self, m):
        self.sb_top = m

    def _collect(self, eng, reads, writes):
        deps = []
        for r in reads:
            if r.w is not None:
                deps.append(r.w)
            if r.excl:
                deps.extend(r.r)
        for w in writes:
            if w.w is not None:
                deps.append(w.w)
            deps.extend(w.r)
        waits = {}
        for (key, val, src) in deps:
            if src == eng and (eng == "pe" or not self.self_wait):
                continue
            if self.seen[eng].get(key, 0) >= val:
                continue
            if waits.get(key, 0) < val:
                waits[key] = val
        for k, v in waits.items():
            self.seen[eng][k] = v
        return list(waits.items())

    def _commit(self, tok, reads, writes):
        for r in reads:
            r.r.append(tok)
        for w in writes:
            w.w = tok
            w.r = []

    def op(self, eng, fn, reads=(), writes=()):
        waits = self._collect(eng, reads, writes)
        self.count[eng] += 1
        tok = (eng, self.count[eng], eng)
        self.ops[eng].append((waits, fn, ("eng", eng)))
        self._commit(tok, reads, writes)

    def dma(self, q, out, in_, reads=(), writes=()):
        idx = self.dma_rr
        self.dma_rr = (self.dma_rr + 1) % N_DMA_SEMS
        key = ("dma", idx)
        waits = self._collect(q, reads, writes)
        prev = self.dma_val[idx]
        if prev > 0 and self.seen[q].get(key, 0) < prev:
            waits.append((key, prev))
            self.seen[q][key] = prev
        self.dma_val[idx] = prev + 16
        tok = (key, prev + 16, "dma")
        self.ops[q].append((waits, lambda e: e.dma_start(out=out, in_=in_), ("dma", idx)))
        self._commit(tok, reads, writes)

    def barrier(self):
        for e in ENGS:
            waits = []
            for o in ENGS:
                if o == e or o == "sp":
                    continue
                v = self.count[o]
                if v > 0 and self.seen[e].get(o, 0) < v:
                    waits.append((o, v))
                    self.seen[e][o] = v
            for i in range(N_DMA_SEMS):
                v = self.dma_val[i]
                key = ("dma", i)
                if v > 0 and self.seen[e].get(key, 0) < v:
                    waits.append((key, v))
                    self.seen[e][key] = v
            if waits:
                self.ops[e].append((waits, None, None))

    def emit(self):
        nc = self.nc
        with contextlib.ExitStack() as st:
            sems = {}
            for e in ENGS:
                sems[e] = st.enter_context(nc.semaphore(f"s_{e}"))
            for i in range(N_DMA_SEMS):
                sems[("dma", i)] = st.enter_context(nc.semaphore(f"s_dma{i}"))
            block = st.enter_context(nc.Block())

            def run(engname):
                def f(eng):
                    for waits, fn, kind in self.ops[engname]:
                        for k, v in waits:
                            eng.wait_ge(sems[k], v)
                        if fn is None:
                            continue
                        ins = fn(eng)
                        if kind[0] == "eng":
                            ins.then_inc(sems[kind[1]], 1)
                        else:
                            ins.then_inc(sems[("dma", kind[1])], 16)
                return f

            block.tensor(run("pe"))
            block.scalar(run("act"))
            block.vector(run("dve"))
            block.gpsimd(run("pool"))
            block.sync(run("sp"))


class Tl:
    def __init__(self, t, nres=1):
        self.t = t
        self.r = [Res() for _ in range(nres)]

    def __getitem__(self, idx):
        return self.t[idx]


class Builder:
    def __init__(self, debug=(), stop=None, skip=()):
        self.debug = set(debug)
        self.stop = stop
        self.skip = set(skip)
        self.stop_mix = None
        self.nc = bass.Bass("TRN2", target_bir_lowering=False)
        import os
        self.P = Prog(self.nc, self_wait=not os.environ.get('NO_SELF_WAIT'))
        self.dram = {}

    def din(self, name, shape, dtype=F32):
        self.dram[name] = self.nc.dram_tensor(name, list(shape), dtype, kind="ExternalInput").ap()
        return self.dram[name]

    def dscr(self, name, shape, dtype):
        kind = "ExternalOutput" if name in self.debug else "Internal"
        self.dram[name] = self.nc.dram_tensor(name, list(shape), dtype, kind=kind).ap()
        return self.dram[name]

    def tile(self, shape, dtype, name=None, nres=1):
        return Tl(self.P.sb(shape, dtype, name), nres)

    def MM(self, out, lhsT, rhs, start, stop, R, W):
        self.P.op("pe", lambda e: e.matmul(out, lhsT, rhs, start=start, stop=stop), R, W)

    def TR(self, out, in_, ident, R, W):
        self.P.op("pe", lambda e: e.transpose(out, in_, ident), R, W)

    def ACT(self, out, in_, func, R, W, bias=None, scale=1.0):
        if bias is None:
            self.P.op("act", lambda e: e.activation(out, in_, func, scale=scale), R, W)
        else:
            self.P.op("act", lambda e: e.activation(out, in_, func, bias=bias, scale=scale), R, W)

    def TT(self, eng, out, in0, in1, op, R, W):
        self.P.op(eng, lambda e: e.tensor_tensor(out, in0, in1, op), R, W)

    def TS(self, eng, out, in0, s1, s2, op0, op1, R, W):
        if s2 is None:
            self.P.op(eng, lambda e: e.tensor_scalar(out, in0, s1, None, op0), R, W)
        else:
            self.P.op(eng, lambda e: e.tensor_scalar(out, in0, s1, s2, op0, op1), R, W)

    def STT(self, eng, out, in0, scalar, in1, op0, op1, R, W):
        self.P.op(eng, lambda e: e.scalar_tensor_tensor(out, in0, scalar, in1, op0, op1), R, W)

    def CP(self, eng, out, in_, R, W):
        if eng == "act":
            self.P.op("act", lambda e: e.copy(out, in_), R, W)
        else:
            self.P.op(eng, lambda e: e.tensor_copy(out, in_), R, W)

    def RECIP(self, out, in_, R, W):
        self.P.op("dve", lambda e: e.reciprocal(out, in_), R, W)

    def RED(self, eng, out, in_, op, R, W):
        self.P.op(eng, lambda e: e.tensor_reduce(out, in_, AX.X, op), R, W)

    def MEMSET(self, eng, out, val, R, W):
        self.P.op(eng, lambda e: e.memset(out, val), R, W)

    def DMA(self, q, out, in_, R=(), W=()):
        self.P.dma(q, out, in_, R, W)

    def build(self):
        nc, P = self.nc, self.P
        x = self.din("x", [NB, TL, D])
        ctx = self.din("ctx", [NB, TC, D])
        cvec = self.din("cvec", [128, 8, 3])
        mod_w = self.din("mod_w", [2, D, 6 * D])
        mod_b = self.din("mod_b", [128, 2, 48])
        lnp = self.din("lnp", [128, 2, 4, 8])
        ffn_w1 = self.din("ffn_w1", [2, D, 4 * D])
        ffn_w2 = self.din("ffn_w2", [2, 4 * D, D])
        w_out = self.din("w_out", [2, D, D])
        cid = self.din("c_ident", [128, 128])
        self.ab_w = self.din("ab_w", [D, 3328])
        self.ab_perm = self.din("ab_perm", [D, 1024])
        self.cd_w = self.din("cd_w", [D, 2320])
        self.cd_perm = self.din("cd_perm", [D, 640])
        self.rope_d = self.din("c_rope", [2, 128, TL])
        self.swam_d = self.din("c_swamask", [6, 128, 512])
        self.diffl_d = self.din("diff_l", [1, 4, 64])
        self.subln_d = self.din("subln", [128, 1])
        self.sink_d = self.din("sink", [1, 8])
        self.masks_d = self.din("c_masks", [4, 128, 128])
        self.neg_d = self.din("c_neg", [2, 128, 128])
        self.convw_d = self.din("convw", [128, 8, 5])
        self.convb_d = self.din("convb", [128, 8])
        self.dtb_d = self.din("dtb", [1, 16])
        self.alog_d = self.din("alog", [1, 16])
        self.dskip_d = self.din("dskip", [128, 4])
        self.ssdg_d = self.din("ssdg", [128, 4])
        self.rvec_d = self.din("rwkv_vec", [9, 512])
        self.rmu_d = self.din("rwkv_mu", [1, 1792])
        self.rw2_d = self.din("rwkv_w2", [2, 64, 512])
        self.ra2_d = self.din("rwkv_a2", [2, 64, 512])
        self.rg2_d = self.din("rwkv_g2", [128, 512])
        out = self.nc.dram_tensor("out", [NB, TL, D], F32, kind="ExternalOutput").ap()
        self.out = out
        H = self.dscr("H", [NB, 8, 128, T], F32)
        HM = self.dscr("HM", [NB, 8, 128, T], BF16)
        O = self.dscr("O", [NB, 8, 128, T], BF16)
        self.H, self.HM, self.O = H, HM, O

        self.ident_f = self.tile([128, 128], F32, "identf")
        self.ident_b = self.tile([128, 128], BF16, "identb")
        self.ones_f = self.tile([128, 128], F32, "onesf")
        self.ones_b = self.tile([128, 128], BF16, "onesb")
        self.cst = self.tile([128, 8], F32, "cst")
        self.DMA("sp", self.ident_f[:], cid, W=self.ident_f.r)
        self.DMA("pool", self.ident_b[:], cid, W=self.ident_b.r)
        self.MEMSET("pool", self.ones_f[:], 1.0, [], self.ones_f.r)
        self.MEMSET("pool", self.ones_b[:], 1.0, [], self.ones_b.r)
        self.MEMSET("dve", self.cst[:, 0:1], EPS_P, [], self.cst.r)
        self.MEMSET("dve", self.cst[:, 1:2], 0.0, [], self.cst.r)
        self.MEMSET("dve", self.cst[:, 2:3], LN_EPS, [], self.cst.r)
        self.MEMSET("dve", self.cst[:, 3:4], 1.0, [], self.cst.r)
        self.MEMSET("dve", self.cst[:, 4:5], 64e-5, [], self.cst.r)
        self.masks = self.tile([128, 4, 128], F32, "masks")
        self.DMA("sp", self.masks[:], self.masks_d.rearrange("m p t -> p m t"), W=self.masks.r)
        self.negm = self.tile([128, 2, 128], F32, "negm")
        self.DMA("sp", self.negm[:], self.neg_d.rearrange("m p t -> p m t"), W=self.negm.r)
        self.lnp_t = self.tile([128, 2, 4, 8], F32, "lnp")
        self.DMA("sp", self.lnp_t[:], lnp, W=self.lnp_t.r)
        self.MOD = self.tile([128, 2, 48, 3], F32, "MOD")
        self.S1 = self.tile([128, 2, 8, 3], F32, "S1")
        self.S1F = self.tile([128, 2, 8, 3], F32, "S1F")
        self.GA = self.tile([128, 2, 8, 3], F32, "GA")
        self.GFA = self.tile([128, 2, 8, 3], F32, "GFA")
        self.ps = [Tl(nc.alloc_psum_tensor(f"ps{i}", [128, 512], F32)) for i in range(8)]
        for p_ in self.ps:
            p_.r[0].excl = True
        keep = P.mark()

        self.phase_mod(cvec, mod_w, mod_b)
        P.barrier()
        P.release(keep)
        self.phase_init(x, ctx)
        P.barrier()
        P.release(keep)
        if "MODD" in self.debug:
            md = self.dscr("MODD", [128, 2 * 48 * 3], F32)
            self.DMA("sp", md, self.MOD[:].rearrange("p l j w -> p (l j w)"), R=self.MOD.r)
        for layer in range(2 if self.stop is None else self.stop):
            self.layer = layer
            self.last = layer == 1
            if layer == 0 and "rwkv" not in self.skip:
                self.phase_rwkv()
                P.barrier()
                P.release(keep)
            for b in range(NB):
                self.phase_mixers(layer, b)
                P.barrier()
                P.release(keep)
            if getattr(self, "stop_mix", None) == layer:
                break
            self.phase_ffn(layer, w_out[layer], ffn_w1[layer], ffn_w2[layer])
            P.barrier()
            P.release(keep)
        P.barrier()
        P.emit()
        return nc

    def who(self, b, t0):
        return 2 if t0 < TC else b

    def SH(self, layer, c, w):
        return self.MOD[:, layer, 0 + c, w:w + 1]

    def SHF(self, layer, c, w):
        return self.MOD[:, layer, 24 + c, w:w + 1]

    def phase_mod(self, cvec, mod_w, mod_b):
        P = self.P
        sc = self.tile([128, 8, 3], F32, "silu_c")
        self.DMA("sp", sc[:], cvec, W=sc.r)
        self.ACT(sc[:], sc[:], AF.Silu, sc.r, sc.r)
        mb = self.tile([128, 2, 48], F32, "modb")
        self.DMA("sp", mb[:], mod_b, W=mb.r)
        GW = 768
        wbuf = [self.tile([128, 8, GW], F32, f"modw{i}") for i in range(2)]
        pst = self.ps[0]
        n = 0
        for layer in range(2):
            for g in range(6 * D // GW):
                wb = wbuf[n % 2]
                n += 1
                for c in range(8):
                    self.DMA("sp", wb[:, c, :], mod_w[layer, c * 128:(c + 1) * 128, g * GW:(g + 1) * GW], W=wb.r)
                for jj in range(GW // 128):
                    j = g * (GW // 128) + jj
                    for c in range(8):
                        self.MM(pst[:, j * 3:j * 3 + 3], wb[:, c, jj * 128:(jj + 1) * 128], sc[:, c, :],
                                c == 0, c == 7, wb.r + sc.r, pst.r)
            pv = pst[:, 0:144].rearrange("p (j w) -> p j w", w=3)
            self.TT("dve", self.MOD[:, layer, :, :], pv, mb[:, layer, :].unsqueeze(2).to_broadcast([128, 48, 3]),
                    ALU.add, pst.r + mb.r, self.MOD.r)
        for layer in range(2):
            self.TS("dve", self.S1[:, layer], self.MOD[:, layer, 8:16, :], 1.0, None, ALU.add, None, self.MOD.r, self.S1.r)
            self.TS("dve", self.S1F[:, layer], self.MOD[:, layer, 32:40, :], 1.0, None, ALU.add, None, self.MOD.r, self.S1F.r)
            self.TS("dve", self.GA[:, layer], self.MOD[:, layer, 16:24, :], 1.0 / ALPHA, None, ALU.mult, None, self.MOD.r, self.GA.r)
            self.TS("dve", self.GFA[:, layer], self.MOD[:, layer, 40:48, :], 1.0 / ALPHA, None, ALU.mult, None, self.MOD.r, self.GFA.r)

    def phase_init(self, x, ctx):
        xin = [self.tile([128, D], F32, f"xin{i}") for i in range(2)]
        hf = [self.tile([128, 8, 128], F32, f"hf{i}") for i in range(2)]
        hm = [self.tile([128, 8, 128], BF16, f"hm{i}") for i in range(2)]
        n = 0
        for b in range(NB):
            for ch in range(NCH):
                t0 = ch * 128
                xi, hfi, hmi = xin[n % 2], hf[n % 2], hm[n % 2]
                src = ctx[b, t0:t0 + 128, :] if t0 < TC else x[b, t0 - TC:t0 - TC + 128, :]
                self.DMA("sp", xi[:], src, W=xi.r)
                w = self.who(b, t0)
                for half in range(2):
                    pst = self.ps[(n * 2 + half) % 8]
                    for cc in range(4):
                        c = half * 4 + cc
                        self.TR(pst[:, cc * 128:(cc + 1) * 128], xi[:, c * 128:(c + 1) * 128], self.ident_f[:],
                                xi.r + self.ident_f.r, pst.r)
                    pv = pst[:, :].rearrange("p (c t) -> p c t", c=4)
                    self.CP("dve", hfi[:, half * 4:half * 4 + 4, :], pv, pst.r, hfi.r)
                for c in range(8):
                    self.ACT(hmi[:, c, :], hfi[:, c, :], AF.Identity, hfi.r + self.S1.r + self.MOD.r, hmi.r,
                             bias=self.SH(0, c, w), scale=self.S1[:, 0, c, w:w + 1])
                self.DMA("sp", self.H[b, :, :, t0:t0 + 128].rearrange("c p t -> p c t"), hfi[:], R=hfi.r)
                self.DMA("sp", self.HM[b, :, :, t0:t0 + 128].rearrange("c p t -> p c t"), hmi[:], R=hmi.r)
                n += 1

    def zero_o(self, b, c0, c1):
        z = self.tile([128, 512], BF16, "zero")
        self.MEMSET("pool", z[:], 0.0, [], z.r)
        for c in range(c0, c1):
            for t0 in range(0, T, 512):
                n = min(512, T - t0)
                self.DMA("sp", self.O[b, c, :, t0:t0 + n], z[:, 0:n], R=z.r)

    def phase_mixers(self, layer, b):
        P = self.P
        skip = getattr(self, "skip", set())
        self.hmod = self.tile([128, 8, T], BF16, "hmod_b")
        for c in range(8):
            self.DMA("sp", self.hmod[:, c, :], self.HM[b, c, :, :], W=self.hmod.r)
        keep = P.mark()
        if layer == 0:
            if "rwkv" in skip:
                self.zero_o(b, 0, 4)
            P.barrier(); P.release(keep)
            if "diff" in skip:
                self.zero_o(b, 4, 8)
            else:
                self.mix_diff(b)
        else:
            if "ssd" in skip:
                self.zero_o(b, 0, 4)
            else:
                self.mix_ssd(b)
            P.barrier(); P.release(keep)
            if "swa" in skip:
                self.zero_o(b, 4, 8)
            else:
                self.mix_swa(b)

    TILES = [(0, 256), (256, 512), (768, 512), (1280, 512), (1792, 512)]

    def load_w(self, wt, src_cols):
        n = src_cols.shape[1]
        self.DMA("pool", wt[:, :, 0:n], src_cols.rearrange("(c p) n -> p c n", p=128), W=wt.r)

    def proj_fm(self, pst, wt, col0, t0, n):
        for c in range(8):
            self.MM(pst[:, 0:n], wt[:, c, col0:col0 + 128], self.hmod[:, c, t0:t0 + n], c == 0, c == 7,
                    wt.r + self.hmod.r, pst.r)

    def proj_tok(self, pst, wt, col0, ncols, ch):
        for c in range(8):
            self.MM(pst[:, 0:ncols], self.hmod[:, c, ch * 128:(ch + 1) * 128], wt[:, c, col0:col0 + ncols], c == 0, c == 7,
                    wt.r + self.hmod.r, pst.r)

    def proj_rope(self, dst, wt, wpt, col0, pcol0, rope):
        cos, sin = rope
        for i, (t0, n) in enumerate(self.TILES):
            p1 = self.ps[(2 * i) % 4]
            self.proj_fm(p1, wt, col0, t0, n)
            if t0 < TC:
                self.CP("act", dst[:, t0:t0 + n], p1[:, 0:n], p1.r, dst.r)
                continue
            p2 = self.ps[(2 * i + 1) % 4]
            self.proj_fm(p2, wpt, pcol0, t0, n)
            l0 = t0 - TC
            ta, tb = self.rtmp
            self.TT("dve", ta[:, 0:n], p1[:, 0:n], cos[:, l0:l0 + n], ALU.mult, p1.r + cos.r, ta.r)
            self.TT("dve", tb[:, 0:n], p2[:, 0:n], sin[:, l0:l0 + n], ALU.mult, p2.r + sin.r, tb.r)
            self.TT("pool", dst[:, t0:t0 + n], ta[:, 0:n], tb[:, 0:n], ALU.add, ta.r + tb.r, dst.r)

    def load_rope(self):
        cos = self.tile([128, TL], F32, "cos")
        sin = self.tile([128, TL], F32, "sin")
        self.DMA("sp", cos[:], self.rope_d[0], W=cos.r)
        self.DMA("sp", sin[:], self.rope_d[1], W=sin.r)
        self.rtmp = (self.tile([128, 512], F32, "rta"), self.tile([128, 512], F32, "rtb"))
        return cos, sin


    def mix_swa(self, b):
        rope = self.load_rope()
        C0 = 1552
        sk = self.tile([1, 8], F32, "sk")
        self.DMA("sp", sk[:], self.sink_d, W=sk.r)
        self.ACT(sk[:], sk[:], AF.Exp, sk.r, sk.r)
        pb = self.ps[7]
        self.MM(pb[:, 0:8], self.ones_f[0:1, :], sk[0:1, 0:8], True, True, self.ones_f.r + sk.r, pb.r)
        esink = self.tile([128, 8], F32, "esink")
        self.CP("dve", esink[:], pb[:, 0:8], pb.r, esink.r)
        mk = self.tile([128, 6, 512], BF16, "swamask")
        self.DMA("pool", mk[:], self.swam_d.rearrange("r p q -> p r q"), W=mk.r)
        wk = [self.tile([128, 8, 128], BF16, f"wk{i}") for i in range(2)]
        kbs = [self.tile([128, T], BF16, f"kb{g}") for g in range(2)]
        for g in range(2):
            for half in range(2):
                self.DMA("pool", wk[0][:, :, half * 64:(half + 1) * 64],
                         self.cd_w[:, C0 + 512 + g * 64:C0 + 512 + (g + 1) * 64].rearrange("(c p) n -> p c n", p=128), W=wk[0].r)
                self.DMA("pool", wk[1][:, :, half * 64:(half + 1) * 64],
                         self.cd_perm[:, 512 + g * 64:512 + (g + 1) * 64].rearrange("(c p) n -> p c n", p=128), W=wk[1].r)
            self.proj_rope(kbs[g], wk[0], wk[1], 0, 0, rope)
        wv = self.tile([128, 8, 128], BF16, "wv")
        self.load_w(wv, self.cd_w[:, C0 + 640:C0 + 768])
        vtok = self.tile([128, NCH, 128], BF16, "vtok")
        for ch in range(NCH):
            pst = self.ps[ch % 4]
            self.proj_tok(pst, wv, 0, 128, ch)
            self.CP("act" if ch % 2 else "dve", vtok[:, ch, :], pst[:, 0:128], pst.r, vtok.r)
        wq = [self.tile([128, 8, 128], BF16, f"wq{i}") for i in range(2)]
        qb = self.tile([128, T], BF16, "qb")
        pt = [self.tile([128, 512], BF16, f"pt{i}") for i in range(3)]
        ft = self.tile([128, 512], F32, "ft")
        ob = [self.tile([128, 512], BF16, f"ob{i}") for i in range(2)]
        npt = 0
        cnt = 0
        for cq in range(4):
            self.load_w(wq[0], self.cd_w[:, C0 + cq * 128:C0 + (cq + 1) * 128])
            self.load_w(wq[1], self.cd_perm[:, cq * 128:(cq + 1) * 128])
            self.proj_rope(qb, wq[0], wq[1], 0, 0, rope)
            for hh in range(2):
                hq = cq * 2 + hh
                kv = hq // 4
                qs = hh * 64
                ks = qs
                kb = kbs[kv]
                for qt in range(4):
                    t0 = TC + qt * 512
                    kts = [(0, None), (1, None)] + [(2 + kk, kk - 4 * qt + 1)
                                                    for kk in range(max(0, 4 * qt - 1), min(15, 4 * qt + 4) + 1)]
                    Oa, Da = self.ps[4 + cnt % 2], self.ps[6 + cnt % 2]
                    cnt += 1
                    for ki, (ch, r) in enumerate(kts):
                        S = self.ps[npt % 4]
                        p = pt[npt % 3]
                        npt += 1
                        self.MM(S[:, :], kb[ks:ks + 64, ch * 128:(ch + 1) * 128], qb[qs:qs + 64, t0:t0 + 512],
                                True, True, kb.r + qb.r, S.r)
                        self.ACT(p[:, :], S[:, :], AF.Exp, S.r, p.r, scale=0.125)
                        if r is not None:
                            self.TT("pool", p[:, :], p[:, :], mk[:, r, :], ALU.mult, p.r + mk.r, p.r)
                        first, lastk = ki == 0, ki == len(kts) - 1
                        self.MM(Oa[0:64, :], vtok[:, ch, kv * 64:(kv + 1) * 64], p[:, :], first, lastk, vtok.r + p.r, Oa.r)
                        self.MM(Da[0:64, :], self.ones_b[:, 0:64], p[:, :], first, lastk, self.ones_b.r + p.r, Da.r)
                    self.TS("dve", ft[0:64, :], Da[0:64, :], esink[0:64, hq:hq + 1], None, ALU.add, None, Da.r + esink.r, ft.r)
                    self.RECIP(ft[0:64, :], ft[0:64, :], ft.r, ft.r)
                    o = ob[cnt % 2]
                    self.TT("dve", o[0:64, :], Oa[0:64, :], ft[0:64, :], ALU.mult, Oa.r + ft.r, o.r)
                    self.DMA("sp", self.O[b, 4 + cq, qs:qs + 64, t0:t0 + 512], o[0:64, :], R=o.r)

    def mix_ssd(self, b):
        P = self.P
        hmod = self.hmod
        cw = self.tile([128, 8, 5], F32, "cw")
        cb = self.tile([128, 8], F32, "cb")
        dsk = self.tile([128, 4], F32, "dsk")
        ng = self.tile([128, 4], F32, "ng")
        self.DMA("sp", cw[:], self.convw_d, W=cw.r)
        self.DMA("sp", cb[:], self.convb_d, W=cb.r)
        self.DMA("sp", dsk[:], self.dskip_d, W=dsk.r)
        self.DMA("sp", ng[:], self.ssdg_d, W=ng.r)
        dtb = self.tile([1, 16], F32, "dtb")
        al = self.tile([1, 16], F32, "al")
        self.DMA("sp", dtb[:], self.dtb_d, W=dtb.r)
        self.DMA("sp", al[:], self.alog_d, W=al.r)
        self.ACT(al[:], al[:], AF.Exp, al.r, al.r)
        pb = self.ps[7]
        self.MM(pb[:, 0:16], self.ones_f[0:1, :], al[0:1, 0:16], True, True, self.ones_f.r + al.r, pb.r)
        aneg = self.tile([128, 16], F32, "aneg")
        self.TS("dve", aneg[:], pb[:, 0:16], -1.0, None, ALU.mult, None, pb.r, aneg.r)
        zs = self.tile([128, 4, T], BF16, "zs")
        xact = self.tile([128, 8, T], BF16, "xact")
        xs_tok = self.tile([128, NCH, 512], BF16, "xs_tok")
        B_tok = self.tile([128, NCH, 256], BF16, "B_tok")
        Yacc = self.tile([128, NCH, 512], F32, "Yacc", nres=NCH)
        dt_all = self.tile([128, NCH, 16], F32, "dt_all")
        a_all = self.tile([128, NCH, 16], F32, "a_all")
        keep2 = P.mark()
        wt = [self.tile([128, 8, 128], BF16, f"wssd{i}") for i in range(2)]
        pre = self.tile([128, T], F32, "pre")
        acc = self.tile([128, T], F32, "acc")
        for c in range(4):
            w = wt[c % 2]
            self.load_w(w, self.cd_w[:, c * 128:(c + 1) * 128])
            for i, (t0, n) in enumerate(self.TILES):
                pst = self.ps[i % 4]
                self.proj_fm(pst, w, 0, t0, n)
                self.ACT(zs[:, c, t0:t0 + n], pst[:, 0:n], AF.Silu, pst.r, zs.r)
        for c in range(8):
            w = wt[c % 2]
            self.load_w(w, self.cd_w[:, 512 + c * 128:512 + (c + 1) * 128])
            for i, (t0, n) in enumerate(self.TILES):
                pst = self.ps[i % 4]
                self.proj_fm(pst, w, 0, t0, n)
                self.CP("act" if i % 2 else "dve", pre[:, t0:t0 + n], pst[:, 0:n], pst.r, pre.r)
            self.ACT(acc[:, :], pre[:, :], AF.Identity, pre.r + cw.r + cb.r, acc.r, bias=cb[:, c:c + 1], scale=cw[:, c, 2:3])
            for j in (0, 1, 3, 4):
                sft = j - 2
                for (lo, hi) in ((0, TC), (TC, T)):
                    a0, a1 = max(lo, lo - sft), min(hi, hi - sft)
                    self.STT("dve", acc[:, a0:a1], pre[:, a0 + sft:a1 + sft], cw[:, c, j:j + 1], acc[:, a0:a1],
                             ALU.mult, ALU.add, pre.r + cw.r + acc.r, acc.r)
            self.ACT(xact[:, c, :], acc[:, :], AF.Silu, acc.r, xact.r)
        wdt = self.tile([128, 8, 16], BF16, "wdt")
        self.load_w(wdt, self.cd_w[:, 1536:1552])
        for ch in range(NCH):
            pst = self.ps[ch % 4]
            self.MM(pst[:, 0:16], self.ones_f[0:1, :], dtb[0:1, 0:16], True, False, self.ones_f.r + dtb.r, pst.r)
            for c in range(8):
                self.MM(pst[:, 0:16], hmod[:, c, ch * 128:(ch + 1) * 128], wdt[:, c, :], False, c == 7, wdt.r + hmod.r, pst.r)
            self.ACT(dt_all[:, ch, :], pst[:, 0:16], AF.Exp, pst.r, dt_all.r)
        self.ACT(dt_all[:, :, :], dt_all[:, :, :], AF.Ln, dt_all.r + self.cst.r, dt_all.r, bias=self.cst[:, 3:4])
        self.TT("dve", a_all[:, :, :], dt_all[:, :, :], aneg[:, :].unsqueeze(1).to_broadcast([128, NCH, 16]), ALU.mult,
                dt_all.r + aneg.r, a_all.r)
        for ch in range(NCH):
            pst = self.ps[ch % 4]
            for c in range(4):
                self.MM(pst[:, c * 128:(c + 1) * 128], xact[:, c, ch * 128:(ch + 1) * 128], self.ident_b[:], True, True,
                        xact.r + self.ident_b.r, pst.r)
            self.CP("act", xs_tok[:, ch, :], pst[:, :], pst.r, xs_tok.r)
            pst2 = self.ps[4 + ch % 2]
            for g in range(2):
                self.MM(pst2[:, g * 128:(g + 1) * 128], xact[:, 4 + g, ch * 128:(ch + 1) * 128], self.ident_b[:], True, True,
                        xact.r + self.ident_b.r, pst2.r)
            self.CP("dve", B_tok[:, ch, :], pst2[:, 0:256], pst2.r, B_tok.r)
        P.barrier()
        P.release(keep2)
        keep3 = P.mark()
        hT = [self.tile([128, 2, 256], F32, f"hT{d}") for d in range(2)]
        hTb = [self.tile([128, 2, 256], BF16, f"hTb{d}") for d in range(2)]
        for d in range(2):
            self.MEMSET("pool", hT[d][:], 0.0, [], hT[d].r)
            self.MEMSET("pool", hTb[d][:], 0.0, [], hTb[d].r)
        ex = [self.tile([128, 24], F32, f"ex{d}") for d in range(2)]
        nacs = [self.tile([128, 8], F32, f"nacs{d}") for d in range(2)]
        xdt = [self.tile([128, 8, 64], BF16, f"xdt{d}") for d in range(2)]
        Xd = [self.tile([128, 8, 64], BF16, f"Xd{d}") for d in range(2)]
        Abc = [self.tile([128, 8, 128], F32, f"Abc{d}") for d in range(2)]
        Gs = [self.tile([128, 2, 128], BF16, f"Gs{d}") for d in range(2)]
        Lm = [self.tile([128, 128], BF16, f"Lm{i}") for i in range(4)]
        Wh = [self.tile([128, 128], BF16, f"Wh{i}") for i in range(4)]
        zt = [self.tile([128, 512], F32, f"zt{d}") for d in range(2)]
        order = [list(range(NCH)), [1, 0] + list(range(NCH - 1, 1, -1))]
        written = set()
        nl = 0
        for step in range(NCH):
            for d in range(2):
                ch = order[d][step]
                MI, MSo = self.masks[:, 2 * d, :], self.masks[:, 2 * (1 - d) + 1, :]
                NEG = self.negm[:, d, :]
                tk = slice(ch * 128, (ch + 1) * 128)
                a = a_all[:, ch, d * 8:(d + 1) * 8]
                pA = self.ps[d]
                self.MM(pA[:, 0:8], MI, a, True, True, self.masks.r + a_all.r, pA.r)
                self.MM(pA[:, 8:16], self.ones_f[:], a, True, True, self.ones_f.r + a_all.r, pA.r)
                self.MM(pA[:, 16:24], MSo, a, True, True, self.masks.r + a_all.r, pA.r)
                self.ACT(ex[d][:], pA[:, 0:24], AF.Exp, pA.r, ex[d].r)
                self.ACT(nacs[d][:], pA[:, 0:8], AF.Copy, pA.r, nacs[d].r, scale=-1.0)
                xsv = xs_tok[:, ch, :].rearrange("p (h e) -> p h e", h=8)
                self.TT("dve", xdt[d][:], xsv, dt_all[:, ch, d * 8:(d + 1) * 8].unsqueeze(2).to_broadcast([128, 8, 64]), ALU.mult,
                        xs_tok.r + dt_all.r, xdt[d].r)
                self.TT("pool", Xd[d][:], xdt[d][:], ex[d][:, 16:24].unsqueeze(2).to_broadcast([128, 8, 64]), ALU.mult,
                        xdt[d].r + ex[d].r, Xd[d].r)
                if ch >= 2:
                    self.CP("pool", Abc[d][:], a.unsqueeze(2).to_broadcast([128, 8, 128]), a_all.r, Abc[d].r)
                    pG = self.ps[2 + d]
                    for g in range(2):
                        self.MM(pG[:, g * 128:(g + 1) * 128], xact[:, 4 + g, tk], xact[:, 6 + g, tk], True, True, xact.r, pG.r)
                    self.CP("act", Gs[d][:], pG[:, 0:256].rearrange("p (g t) -> p g t", g=2), pG.r, Gs[d].r)
                    pY = self.ps[4 + d]
                    for h in range(8):
                        g = h // 4
                        pR = self.ps[6 + (nl % 2)]
                        lm, wh = Lm[nl % 4], Wh[nl % 4]
                        nl += 1
                        self.MM(pR[:, 0:128], Abc[d][:, h, :], MI, True, False, Abc[d].r + self.masks.r, pR.r)
                        self.MM(pR[:, 0:128], self.ident_f[:], NEG, False, True, self.ident_f.r + self.negm.r, pR.r)
                        self.ACT(lm[:], pR[:, 0:128], AF.Exp, pR.r + nacs[d].r, lm.r, bias=nacs[d][:, h:h + 1])
                        self.TT("pool", wh[:], lm[:], Gs[d][:, g, :], ALU.mult, lm.r + Gs[d].r, wh.r)
                        self.MM(pY[:, h * 64:(h + 1) * 64], wh[:], xdt[d][:, h, :], True, True, wh.r + xdt[d].r, pY.r)
                    pZ = self.ps[2 + d]
                    for g in range(2):
                        self.MM(pZ[:, g * 256:(g + 1) * 256], xact[:, 6 + g, tk], hTb[d][:, g, :], True, True,
                                xact.r + hTb[d].r, pZ.r)
                    z = zt[d]
                    self.TT("dve", z[:].rearrange("p (h e) -> p h e", h=8), pZ[:, :].rearrange("p (h e) -> p h e", h=8),
                            ex[d][:, 0:8].unsqueeze(2).to_broadcast([128, 8, 64]), ALU.mult, pZ.r + ex[d].r, z.r)
                    self.TT("dve", z[:], pY[:, :], z[:], ALU.add, pY.r + z.r, z.r)
                    if ch in written:
                        self.TT("pool", Yacc[:, ch, :], Yacc[:, ch, :], z[:], ALU.add, [Yacc.r[ch]] + z.r, [Yacc.r[ch]])
                    else:
                        self.CP("pool", Yacc[:, ch, :], z[:], z.r, [Yacc.r[ch]])
                        written.add(ch)
                pH = self.ps[d]
                for g in range(2):
                    self.MM(pH[:, g * 256:(g + 1) * 256], B_tok[:, ch, g * 128:(g + 1) * 128],
                            Xd[d][:, 4 * g:4 * g + 4, :].rearrange("p h e -> p (h e)"), True, True, B_tok.r + Xd[d].r, pH.r)
                hv = hT[d][:].rearrange("p g (h e) -> p (g h) e", h=4)
                self.TT("dve", hv, hv, ex[d][:, 8:16].unsqueeze(2).to_broadcast([128, 8, 64]), ALU.mult, hT[d].r + ex[d].r, hT[d].r)
                hf = hT[d][:].rearrange("p g x -> p (g x)")
                self.TT("dve", hf, hf, pH[:, :], ALU.add, hT[d].r + pH.r, hT[d].r)
                self.CP("act", hTb[d][:].rearrange("p g x -> p (g x)"), hf, hT[d].r, hTb[d].r)
        P.barrier()
        P.release(keep3)
        yg = [self.tile([128, 512], F32, f"yg{i}") for i in range(4)]
        sq = [self.tile([128, 512], F32, f"sq{i}") for i in range(2)]
        rs = self.tile([128, 512], F32, "rs")
        ob = [self.tile([128, 512], BF16, f"ob{i}") for i in range(2)]
        no = 0
        for qt in range(4):
            t0 = TC + qt * 512
            for c in range(4):
                pT = self.ps[c]
                for k4 in range(4):
                    ch = 2 + qt * 4 + k4
                    self.TR(pT[:, k4 * 128:(k4 + 1) * 128], Yacc[:, ch, c * 128:(c + 1) * 128], self.ident_f[:],
                            [Yacc.r[ch]] + self.ident_f.r, pT.r)
                self.STT("dve", yg[c][:], xact[:, c, t0:t0 + 512], dsk[:, c:c + 1], pT[:, :], ALU.mult, ALU.add,
                         xact.r + dsk.r + pT.r, yg[c].r)
                self.TT("pool", yg[c][:], yg[c][:], zs[:, c, t0:t0 + 512], ALU.mult, yg[c].r + zs.r, yg[c].r)
            for g in range(2):
                st = self.ps[4 + g]
                for k2 in range(2):
                    c = 2 * g + k2
                    self.ACT(sq[k2][:], yg[c][:], AF.Square, yg[c].r, sq[k2].r)
                    self.MM(st[:, :], self.ones_f[:], sq[k2][:], k2 == 0, k2 == 1, self.ones_f.r + sq[k2].r, st.r)
                self.ACT(rs[:], st[:, :], AF.Sqrt, st.r + self.cst.r, rs.r, bias=self.cst[:, 2:3], scale=1.0 / 256)
                self.RECIP(rs[:], rs[:], rs.r, rs.r)
                for k2 in range(2):
                    c = 2 * g + k2
                    self.TT("dve", yg[c][:], yg[c][:], rs[:], ALU.mult, yg[c].r + rs.r, yg[c].r)
                    o = ob[no % 2]
                    no += 1
                    self.ACT(o[:], yg[c][:], AF.Copy, yg[c].r + ng.r, o.r, scale=ng[:, c:c + 1])
                    self.DMA("sp", self.O[b, c, :, t0:t0 + 512], o[:], R=o.r)


    def bcast_row(self, src_row, n, name):
        t = self.tile([128, n], F32, name)
        self.DMA("sp", t[:], src_row.partition_broadcast(128), W=t.r)
        return t

    def phase_rwkv(self):
        P = self.P
        base = P.mark()
        self.PR = self.dscr("PR", [NB, 3, T, 512], F32)
        self.WD = self.dscr("WD", [NB, 2, T, 512], F32)
        self.PB = self.dscr("PB", [NB, 2, T, 5, 512], BF16)
        self.BG = self.dscr("BG", [NB, 2, T, 512], F32)
        keep = P.mark()
        for b in range(NB):
            self.rwkv_prep(b, None)
            P.barrier()
            P.release(keep)
        Yacc = [self.tile([128, NCH, 512], F32, f"Yacc{b}", nres=NCH) for b in range(NB)]
        keep2 = P.mark()
        import os
        stage = int(os.environ.get("RWKV_STAGE", 3))
        if stage >= 2:
            self.rwkv_chunked(Yacc)
        P.barrier()
        if "YD" in self.debug:
            yd = self.dscr("YD", [NB, 128, NCH, 512], F32)
            for b in range(NB):
                self.DMA("sp", yd[b], Yacc[b][:], R=Yacc[b].r)
            P.barrier()
        P.release(keep2)
        for b in range(NB if stage >= 3 else 0):
            self.rwkv_finish(b, Yacc[b])
            P.barrier()
            P.release(keep2)
        P.release(base)

    def rwkv_chunked(self, Yacc):
        P = self.P
        ps = self.ps
        c_ = CDEC
        mask4 = [self.tile([128, 4, 128], F32, f"mask4{d}") for d in range(2)]
        for d in range(2):
            MS, MI = self.masks[:, 2 * d + 1, :], self.masks[:, 2 * d, :]
            for q, m in enumerate((MS, MI, MS, MI)):
                self.CP("pool", mask4[d][:, q, :], m, self.masks.r, mask4[d].r)
        Sf = [[self.tile([128, 4, 64], F32, f"Sf{b}{d}") for d in range(2)] for b in range(NB)]
        Sb = [[self.tile([128, 4, 64], BF16, f"Sb{b}{d}") for d in range(2)] for b in range(NB)]
        for b in range(NB):
            for d in range(2):
                self.MEMSET("pool", Sf[b][d][:], 0.0, [], Sf[b][d].r)
                self.MEMSET("pool", Sb[b][d][:], 0.0, [], Sb[b][d].r)
        pbin = [self.tile([128, 5, 512], BF16, f"pbin{i}") for i in range(2)]
        lw = [self.tile([128, 512], F32, f"lw{i}") for i in range(2)]
        E = [self.tile([128, 512], F32, f"E{i}") for i in range(3)]
        Xt = self.tile([128, 4, 512], BF16, "Xt")
        FM = [self.tile([128, 4, 128], BF16, f"FM{c}") for c in range(4)]
        PC = self.tile([128, 4], F32, "PC")
        Mm = [self.tile([128, 4, 128], BF16, f"Mm{h}") for h in range(8)]
        AATg = [[[self.tile([128, 2, 2, 128], F32, f"AAT{g}{i}{jb}") for jb in range(2)] for i in range(2)] for g in range(2)]
        Wball = [self.tile([128, 256], F32, f"Wball{g}") for g in range(2)]
        Up = self.tile([128, 8, 64], BF16, "Up")
        ysb = self.tile([128, 512], F32, "ysb")
        tS = self.tile([128, 4, 64], F32, "tS")
        order = [list(range(NCH)), [1, 0] + list(range(NCH - 1, 1, -1))]
        written = [set() for _ in range(NB)]
        nu = 0
        import os
        ndirs = int(os.environ.get("RWKV_DIRS", 2))
        for step in range(NCH):
            for d in range(ndirs):
                for b in range(NB):
                    ch = order[d][step]
                    tk = slice(ch * 128, (ch + 1) * 128)
                    MI, MS, MSo = self.masks[:, 2 * d, :], self.masks[:, 2 * d + 1, :], self.masks[:, 2 * (1 - d) + 1, :]
                    pin, lwt = pbin[nu % 2], lw[nu % 2]
                    nu += 1
                    self.DMA("sp", pin[:], self.PB[b, d, tk, :, :], W=pin.r)
                    self.DMA("sp", lwt[:], self.WD[b, d, tk, :], W=lwt.r)
                    S_f, S_b = Sf[b][d], Sb[b][d]
                    self.MM(ps[0][:, :], MI, lwt[:], True, True, self.masks.r + lwt.r, ps[0].r)
                    self.MM(ps[1][:, :], MS, lwt[:], True, True, self.masks.r + lwt.r, ps[1].r)
                    for c in range(4):
                        self.MM(ps[2][:, c:c + 1], lwt[:, c * 128:(c + 1) * 128], self.ones_f[:, 0:1], True, True,
                                lwt.r + self.ones_f.r, ps[2].r)
                    self.ACT(E[0][:], ps[0][:, :], AF.Exp, ps[0].r, E[0].r, scale=-c_)
                    self.ACT(E[1][:], ps[0][:, :], AF.Exp, ps[0].r, E[1].r, scale=c_)
                    self.ACT(E[2][:], ps[1][:, :], AF.Exp, ps[1].r, E[2].r, scale=-c_)
                    self.ACT(PC[:], ps[2][:, 0:4], AF.Exp, ps[2].r, PC.r, scale=-c_)
                    self.TT("dve", Xt[:, 0, :], pin[:, 0, :], E[2][:], ALU.mult, pin.r + E[2].r, Xt.r)
                    self.TT("pool", Xt[:, 1, :], pin[:, 3, :], E[0][:], ALU.mult, pin.r + E[0].r, Xt.r)
                    self.TT("dve", Xt[:, 2, :], pin[:, 1, :], E[1][:], ALU.mult, pin.r + E[1].r, Xt.r)
                    self.TT("pool", Xt[:, 3, :], pin[:, 2, :], E[1][:], ALU.mult, pin.r + E[1].r, Xt.r)
                    for c in range(4):
                        pt_ = ps[2 + c % 2]
                        for q in range(4):
                            self.MM(pt_[:, q * 128:(q + 1) * 128], Xt[:, q, c * 128:(c + 1) * 128], self.ident_b[:], True, True,
                                    Xt.r + self.ident_b.r, pt_.r)
                        self.CP("act" if c % 2 else "dve", FM[c][:], pt_[:, :].rearrange("p (q t) -> p q t", q=4), pt_.r, FM[c].r)
                    for h in range(8):
                        c, hb = h // 2, (h % 2) * 64
                        grp, j = h // 4, h % 4
                        fm = FM[c]
                        hs_ = slice(hb, hb + 64)
                        pm = ps[4]
                        AR = fm[hs_, 0:2, :].rearrange("p q t -> p (q t)")
                        self.MM(pm[:, 0:256], fm[hs_, 2, :], AR, True, True, fm.r, pm.r)
                        self.MM(pm[:, 256:512], fm[hs_, 3, :], AR, True, True, fm.r, pm.r)
                        self.TT("dve", Mm[h][:], pm[:, :].rearrange("p (q t) -> p q t", q=4), mask4[d][:], ALU.mult,
                                pm.r + mask4[d].r, Mm[h].r)
                        a0 = AATg[grp][0][j // 2]
                        self.TT("dve", a0[:, j % 2, 0, :], pm[:, 0:128], MS, ALU.mult, pm.r + self.masks.r, a0.r)
                        pn = ps[5]
                        self.MM(pn[:, 0:128], fm[hs_, 0, :], fm[hs_, 2, :], True, True, fm.r, pn.r)
                        self.TT("dve", a0[:, j % 2, 1, :], pn[:, 0:128], MSo, ALU.mult, pn.r + self.masks.r, a0.r)
                        wbank = ps[7 - grp]
                        wp = wbank[:, j * 64:(j + 1) * 64]
                        self.MM(wp, fm[hs_, 0, :], S_b[hs_, c, :], j == 0, False, fm.r + S_b.r, wbank.r)
                        self.MM(wp, Mm[h][:, 2, :], pin[:, 4, h * 64:(h + 1) * 64], False, False, Mm[h].r + pin.r, wbank.r)
                    cur = 0
                    for lvl in range(7):
                        for grp in range(2):
                            wbank = ps[7 - grp]
                            wb = Wball[grp]
                            self.CP("dve", wb[:], wbank[:, 0:256], wbank.r, wb.r)
                            for j in range(4):
                                A = AATg[grp][cur][j // 2]
                                self.MM(wbank[:, j * 64:(j + 1) * 64], A[:, j % 2, 0, :], wb[:, j * 64:(j + 1) * 64], False, False,
                                        A.r + wb.r, wbank.r)
                            if lvl == 6:
                                continue
                            for jb in range(2):
                                A = AATg[grp][cur][jb]
                                pq = ps[2 + 2 * grp + jb]
                                for jj in range(2):
                                    o0 = jj * 256
                                    self.MM(pq[:, o0:o0 + 128], A[:, jj, 1, :], A[:, jj, 0, :], True, True, A.r, pq.r)
                                    self.MM(pq[:, o0 + 128:o0 + 256], A[:, jj, 0, :], A[:, jj, 1, :], True, True, A.r, pq.r)
                            for jb in range(2):
                                An = AATg[grp][1 - cur][jb]
                                pq = ps[2 + 2 * grp + jb]
                                self.CP("act", An[:].rearrange("p a b t -> p (a b t)"), pq[:, :], pq.r, An.r)
                        cur = 1 - cur
                    first_y = True
                    for grp in range(2):
                        wbank = ps[7 - grp]
                        self.CP("dve", Up[:, grp * 4:grp * 4 + 4, :].rearrange("p h e -> p (h e)"), wbank[:, 0:256], wbank.r, Up.r)
                    for h in range(8):
                        c, hb = h // 2, (h % 2) * 64
                        fm = FM[c]
                        hs_ = slice(hb, hb + 64)
                        yp = ps[0][:, h * 64:(h + 1) * 64]
                        self.MM(yp, fm[hs_, 1, :], S_b[hs_, c, :], first_y, False, fm.r + S_b.r, ps[0].r)
                        first_y = False
                        self.MM(yp, Mm[h][:, 1, :], Up[:, h, :], False, False, Mm[h].r + Up.r, ps[0].r)
                        self.MM(yp, Mm[h][:, 3, :], pin[:, 4, h * 64:(h + 1) * 64], False, False, Mm[h].r + pin.r, ps[0].r)
                    if ch in written[b]:
                        self.TT("dve", Yacc[b][:, ch, :], Yacc[b][:, ch, :], ps[0][:, :], ALU.add, [Yacc[b].r[ch]] + ps[0].r,
                                [Yacc[b].r[ch]])
                    else:
                        written[b].add(ch)
                        self.CP("dve", Yacc[b][:, ch, :], ps[0][:, :], ps[0].r, [Yacc[b].r[ch]])
                    for c in range(4):
                        pd = ps[1][:, c * 128:(c + 1) * 128]
                        self.MM(pd, Xt[:, 2, c * 128:(c + 1) * 128], Up[:, 2 * c:2 * c + 2, :].rearrange("p h e -> p (h e)"),
                                c == 0, False, Xt.r + Up.r, ps[1].r)
                        self.MM(pd, Xt[:, 3, c * 128:(c + 1) * 128], pin[:, 4, c * 128:(c + 1) * 128], False, False,
                                Xt.r + pin.r, ps[1].r)
                    for c in range(4):
                        self.TS("dve", tS[:, c, :], S_f[:, c, :], PC[:, c:c + 1], None, ALU.mult, None, S_f.r + PC.r, tS.r)
                        for hh in range(2):
                            hs_ = slice(hh * 64, hh * 64 + 64)
                            self.STT("dve", S_f[hs_, c, :], ps[1][hs_, c * 128 + hh * 64:c * 128 + hh * 64 + 64], PC[hs_, c:c + 1],
                                     tS[hs_, c, :], ALU.mult, ALU.add, ps[1].r + PC.r + tS.r, S_f.r)
                    self.CP("act", S_b[:], S_f[:], S_f.r, S_b.r)

    def rwkv_prep(self, b, Vp):
        P = self.P
        tw = self.tile([128, T], BF16, "twxa")
        sg = self.tile([128, T], BF16, "sg")
        keepA = P.mark()
        hmod = self.tile([128, 8, T], BF16, "hmod_b")
        hs = self.tile([128, 8, T], BF16, "hs_b")
        for c in range(8):
            self.DMA("sp", hmod[:, c, :], self.HM[b, c, :, :], W=hmod.r)
        for (lo, hi) in ((0, TC), (TC, T)):
            self.TT("pool", hs[:, :, lo + 1:hi - 1], hmod[:, :, lo:hi - 2], hmod[:, :, lo + 2:hi], ALU.add, hmod.r, hs.r)
            self.CP("dve", hs[:, :, lo:lo + 1], hmod[:, :, lo + 1:lo + 2], hmod.r, hs.r)
            self.CP("dve", hs[:, :, hi - 1:hi], hmod[:, :, hi - 2:hi - 1], hmod.r, hs.r)
        omm = self.bcast_row(self.rmu_d[0, :], 1792, "omm")
        hmu = self.tile([128, 1792], F32, "hmu")
        self.TS("dve", hmu[:], omm[:], 0.5, None, ALU.mult, None, omm.r, hmu.r)
        self.TS("dve", omm[:], omm[:], -1.0, 1.0, ALU.mult, ALU.add, omm.r, omm.r)

        def shifted_weights(wt, w1t, w2t, col0, n):
            self.load_w(wt, self.ab_w[:, col0:col0 + n])
            self.TT("dve", w1t[:, :, 0:n], wt[:, :, 0:n], omm[:, col0:col0 + n].unsqueeze(1).to_broadcast([128, 8, n]), ALU.mult,
                    wt.r + omm.r, w1t.r)
            self.TT("pool", w2t[:, :, 0:n], wt[:, :, 0:n], hmu[:, col0:col0 + n].unsqueeze(1).to_broadcast([128, 8, n]), ALU.mult,
                    wt.r + hmu.r, w2t.r)

        import os
        sub = int(os.environ.get("RWKV_SUB", 9))
        if sub <= 0:
            return
        wt = self.tile([128, 8, 512], BF16, "rw")
        w1t = self.tile([128, 8, 512], BF16, "rw1")
        w2t = self.tile([128, 8, 512], BF16, "rw2")
        for gi, col0 in enumerate((1536, 1664)):
            shifted_weights(wt, w1t, w2t, col0, 128)
            for i, (t0, n) in enumerate(self.TILES):
                pst = self.ps[i % 4]
                for c in range(8):
                    self.MM(pst[:, 0:n], w1t[:, c, 0:128], hmod[:, c, t0:t0 + n], c == 0, False, w1t.r + hmod.r, pst.r)
                for c in range(8):
                    self.MM(pst[:, 0:n], w2t[:, c, 0:128], hs[:, c, t0:t0 + n], False, c == 7, w2t.r + hs.r, pst.r)
                if gi == 0:
                    self.ACT(tw[0:64, t0:t0 + n], pst[0:64, 0:n], AF.Tanh, pst.r, tw.r)
                    self.ACT(tw[64:128, t0:t0 + n], pst[64:128, 0:n], AF.Copy, pst.r, tw.r)
                else:
                    self.ACT(sg[:, t0:t0 + n], pst[:, 0:n], AF.Sigmoid, pst.r, sg.r)
        if sub <= 1:
            return
        stg = [self.tile([128, 512], F32, f"stg{i}") for i in range(2)]
        vb16 = self.tile([128, 8, 128], BF16, "vb16")
        self.MEMSET("pool", vb16[:], 0.0, [], vb16.r)
        ns = 0
        for grp in range(3):
            shifted_weights(wt, w1t, w2t, grp * 512, 512)
            for ch in range(NCH):
                tk = slice(ch * 128, (ch + 1) * 128)
                pst = self.ps[ch % 4]
                for c in range(8):
                    self.MM(pst[:, :], hmod[:, c, tk], w1t[:, c, :], c == 0, False, w1t.r + hmod.r, pst.r)
                for c in range(8):
                    self.MM(pst[:, :], hs[:, c, tk], w2t[:, c, :], False, c == 7, w2t.r + hs.r, pst.r)
                st = stg[ns % 2]
                ns += 1
                self.CP("act", st[:], pst[:, :], pst.r, st.r)
                self.DMA("sp", self.PR[b, grp, tk, :], st[:], R=st.r)
                if False:
                    off = 64 * b
                    self.CP("dve", vb16[:, :, off:off + 64], st[:].rearrange("p (h e) -> p h e", h=8), st.r, vb16.r)
                    for hh in range(2):
                        pV = self.ps[4 + hh]
                        for h4 in range(4):
                            h = hh * 4 + h4
                            self.MM(pV[0:64 + off, h4 * 128:(h4 + 1) * 128], vb16[:, h, 0:64 + off], self.ident_b[:], True, True,
                                    vb16.r + self.ident_b.r, pV.r)
                        self.CP("dve" if hh else "act", Vp[off:off + 64, hh * 4:hh * 4 + 4, tk],
                                pV[off:off + 64, :].rearrange("p (h t) -> p h t", h=4), pV.r, Vp.r)
        P.barrier()
        P.release(keepA)
        if sub <= 2:
            return
        rv = [self.bcast_row(self.rvec_d[i, :], 512, f"rv{i}") for i in range(9)]
        kk_bc, ka_bc, rk_bc, _, _, w0a, w0b, a0a, a0b = rv
        omka = self.tile([128, 512], F32, "omka")
        self.TS("dve", omka[:], ka_bc[:], -1.0, 1.0, ALU.mult, ALU.add, ka_bc.r, omka.r)
        w2b = self.tile([64, 2, 512], BF16, "w2b")
        a2b = self.tile([128, 2, 512], BF16, "a2b")
        g2b = self.tile([128, 512], BF16, "g2b")
        self.DMA("pool", w2b[:], self.rw2_d.rearrange("d k n -> k d n"), W=w2b.r)
        self.DMA("pool", a2b[64:128, :, :], self.ra2_d.rearrange("d k n -> k d n"), W=a2b.r)
        self.DMA("pool", g2b[:], self.rg2_d, W=g2b.r)
        rkv = [[self.tile([128, 512], F32, f"in{j}{i}") for i in range(3)] for j in range(2)]
        tmp = [self.tile([128, 512], F32, f"tm{i}") for i in range(4)]
        kk = self.tile([128, 512], F32, "kk")
        sm = [self.tile([128, 8], F32, f"sm{i}") for i in range(2)]
        pbst = [self.tile([128, 5, 512], BF16, f"pbst{i}") for i in range(2)]
        wdec = [self.tile([128, 512], F32, f"wdec{i}") for i in range(2)]
        bg = [self.tile([128, 512], F32, f"bg{i}") for i in range(2)]
        v3 = lambda t: t[:].rearrange("p (h e) -> p h e", h=8)
        bc8 = lambda t: t[:, 0:8].unsqueeze(2).to_broadcast([128, 8, 64])
        for ch in range(NCH):
            tk = slice(ch * 128, (ch + 1) * 128)
            r_t, k_t, v_t = rkv[ch % 2]
            for gi, tt in enumerate((r_t, k_t, v_t)):
                self.DMA("sp", tt[:], self.PR[b, gi, tk, :], W=tt.r)
            t0_, t1_, t2_, t3_ = tmp
            self.TT("dve", t0_[:], k_t[:], kk_bc[:], ALU.mult, k_t.r + kk_bc.r, t0_.r)
            self.TT("pool", t1_[:], t0_[:], t0_[:], ALU.mult, t0_.r, t1_.r)
            self.RED("dve", sm[0][:, 0:8], v3(t1_), ALU.add, t1_.r, sm[0].r)
            self.TS("dve", sm[0][:], sm[0][:], 1e-24, None, ALU.max, None, sm[0].r, sm[0].r)
            self.ACT(sm[0][:], sm[0][:], AF.Sqrt, sm[0].r, sm[0].r)
            self.RECIP(sm[0][:], sm[0][:], sm[0].r, sm[0].r)
            self.TT("dve", v3(kk), v3(t0_), bc8(sm[0]), ALU.mult, t0_.r + sm[0].r, kk.r)
            self.TT("pool", t1_[:], r_t[:], k_t[:], ALU.mult, r_t.r + k_t.r, t1_.r)
            self.TT("pool", t1_[:], t1_[:], rk_bc[:], ALU.mult, t1_.r + rk_bc.r, t1_.r)
            self.RED("dve", sm[1][:, 0:8], v3(t1_), ALU.add, t1_.r, sm[1].r)
            self.TT("dve", v3(bg[0]), v3(v_t), bc8(sm[1]), ALU.mult, v_t.r + sm[1].r, bg[0].r)
            self.DMA("sp", self.BG[b, 0, tk, :], bg[0][:], R=bg[0].r)
            pg = self.ps[4]
            self.MM(pg[:, :], sg[:, tk], g2b[:], True, True, sg.r + g2b.r, pg.r)
            self.CP("act", bg[1][:], pg[:, :], pg.r, bg[1].r)
            self.DMA("sp", self.BG[b, 1, tk, :], bg[1][:], R=bg[1].r)
            for d in range(2):
                pb_ = pbst[d]
                w0_bc, a0_bc = (w0a, a0a) if d == 0 else (w0b, a0b)
                pz = self.ps[d]
                self.MM(pz[:, :], tw[0:64, tk], w2b[0:64, d, :], True, True, tw.r + w2b.r, pz.r)
                self.TT("dve", t1_[:], pz[:, :], w0_bc[:], ALU.add, pz.r + w0_bc.r, t1_.r)
                self.ACT(wdec[d][:], t1_[:], AF.Sigmoid, t1_.r, wdec[d].r)
                self.DMA("sp", self.WD[b, d, tk, :], wdec[d][:], R=wdec[d].r)
                pa = self.ps[2 + d]
                self.MM(pa[:, :], tw[64:128, tk], a2b[64:128, d, :], True, True, tw.r + a2b.r, pa.r)
                self.TT("dve", t2_[:], pa[:, :], a0_bc[:], ALU.add, pa.r + a0_bc.r, t2_.r)
                self.ACT(t2_[:], t2_[:], AF.Sigmoid, t2_.r, t2_.r)
                self.TS("pool", pb_[:, 0, :], kk[:], -1.0, None, ALU.mult, None, kk.r, pb_.r)
                self.TT("pool", pb_[:, 1, :], kk[:], t2_[:], ALU.mult, kk.r + t2_.r, pb_.r)
                self.TT("dve", t3_[:], t2_[:], ka_bc[:], ALU.mult, t2_.r + ka_bc.r, t3_.r)
                self.TT("dve", t3_[:], t3_[:], omka[:], ALU.add, t3_.r + omka.r, t3_.r)
                self.TT("pool", pb_[:, 2, :], k_t[:], t3_[:], ALU.mult, k_t.r + t3_.r, pb_.r)
                self.CP("act", pb_[:, 3, :], r_t[:], r_t.r, pb_.r)
                self.CP("act", pb_[:, 4, :], v_t[:], v_t.r, pb_.r)
                self.DMA("sp", self.PB[b, d, tk, :, :], pb_[:], R=pb_.r)

    def rwkv_scan(self, Vp, Y):
        NS = 2
        NBUF = 3
        S = [self.tile([128, 512], F32, f"S{d}") for d in range(2)]
        for d in range(2):
            self.MEMSET("dve", S[d][:], 0.0, [], S[d].r)
        Wb = [[self.tile([128, NS, 512], F32, f"Wb{d}{i}") for i in range(NBUF)] for d in range(2)]
        Vb = [[self.tile([128, NS, 4, 512], BF16, f"Vb{d}{i}") for i in range(NBUF)] for d in range(2)]
        t1 = [self.tile([128, 512], F32, f"sc1{d}") for d in range(2)]
        t2 = [self.tile([128, 512], F32, f"sc2{d}") for d in range(2)]
        t3 = [self.tile([128, 512], F32, f"sc3{d}") for d in range(2)]
        sa = [self.tile([128, 8], F32, f"sa{d}") for d in range(2)]
        yts = [self.tile([128, 8], F32, f"yt{d}") for d in range(2)]
        order = [list(range(T)), list(range(TC - 1, -1, -1)) + list(range(T - 1, TC - 1, -1))]
        v3 = lambda ap: ap.rearrange("p (h e) -> p h e", h=8)
        nblk = T // NS
        ywritten = set()
        import os
        nblk = int(os.environ.get('RWKV_MAXBLK', nblk))

        def load(d, bi):
            toks = order[d][bi * NS:(bi + 1) * NS]
            lo = min(toks)
            wb, vb = Wb[d][bi % NBUF], Vb[d][bi % NBUF]
            for b in range(NB):
                self.DMA("sp", wb[b * 64:(b + 1) * 64, :, :], self.WD[b, d, lo:lo + NS, :].partition_broadcast(64), W=wb.r)
                self.DMA("sp", vb[b * 64:(b + 1) * 64, :, :, :], self.PB[b, d, lo:lo + NS, :, :].partition_broadcast(64), W=vb.r)

        for bi in range(min(NBUF - 1, nblk)):
            for d in range(2):
                load(d, bi)
        for bi in range(nblk):
            for d in range(2):
                if bi + NBUF - 1 < nblk:
                    load(d, bi + NBUF - 1)
            for j in range(NS):
                for d in range(2):
                    t = order[d][bi * NS + j]
                    lo = min(order[d][bi * NS:(bi + 1) * NS])
                    jj = t - lo
                    wb, vb = Wb[d][bi % NBUF], Vb[d][bi % NBUF]
                    Sd = S[d]
                    a_bc, b_bc, k_bc, r_bc = (vb[:, jj, q, :] for q in range(4))
                    self.TT("dve", t1[d][:], Sd[:], a_bc, ALU.mult, Sd.r + vb.r, t1[d].r)
                    self.RED("dve", sa[d][:, 0:8], v3(t1[d][:]), ALU.add, t1[d].r, sa[d].r)
                    self.TT("pool", Sd[:], Sd[:], wb[:, jj, :], ALU.mult, Sd.r + wb.r, Sd.r)
                    self.TT("dve", v3(t2[d][:]), v3(b_bc), sa[d][:, 0:8].unsqueeze(2).to_broadcast([128, 8, 64]), ALU.mult,
                            vb.r + sa[d].r, t2[d].r)
                    self.TT("dve", Sd[:], Sd[:], t2[d][:], ALU.add, Sd.r + t2[d].r, Sd.r)
                    self.TT("pool", v3(t3[d][:]), v3(k_bc), Vp[:, :, t:t + 1].to_broadcast([128, 8, 64]), ALU.mult,
                            vb.r + Vp.r, t3[d].r)
                    self.TT("dve", Sd[:], Sd[:], t3[d][:], ALU.add, Sd.r + t3[d].r, Sd.r)
                    self.TT("dve", t1[d][:], Sd[:], r_bc, ALU.mult, Sd.r + vb.r, t1[d].r)
                    ytd = yts[d]
                    self.RED("dve", ytd[:, 0:8], v3(t1[d][:]), ALU.add, t1[d].r, ytd.r)
                    if t not in ywritten:
                        ywritten.add(t)
                        self.CP("act", Y[:, :, t], ytd[:, 0:8], ytd.r, Y.r)
                    else:
                        self.TT("pool", Y[:, :, t], Y[:, :, t], ytd[:, 0:8], ALU.add, Y.r + ytd.r, Y.r)

    def rwkv_finish(self, b, Ya):
        rv = [self.bcast_row(self.rvec_d[i, :], 512, f"fv{i}") for i in (3, 4)]
        gng, gnb = rv
        y = [self.tile([128, 512], F32, f"fy{i}") for i in range(2)]
        sq = self.tile([128, 512], F32, "fsq")
        bon = [self.tile([128, 512], F32, f"fb{i}") for i in range(2)]
        gg = [self.tile([128, 512], F32, f"fg{i}") for i in range(2)]
        sm = [self.tile([128, 8], F32, f"fsm{i}") for i in range(2)]
        ot = [self.tile([128, 512], BF16, f"fot{i}") for i in range(2)]
        ob = [self.tile([128, 4, 128], BF16, f"fob{i}") for i in range(2)]
        v3 = lambda t: t[:].rearrange("p (h e) -> p h e", h=8)
        bc8 = lambda t: t[:, 0:8].unsqueeze(2).to_broadcast([128, 8, 64])
        for ch in range(NCH):
            tk = slice(ch * 128, (ch + 1) * 128)
            yy, bo, g_ = y[ch % 2], bon[ch % 2], gg[ch % 2]
            self.DMA("sp", bo[:], self.BG[b, 0, tk, :], W=bo.r)
            self.DMA("sp", g_[:], self.BG[b, 1, tk, :], W=g_.r)
            ysrc = Ya[:, ch, :].rearrange("p (h e) -> p h e", h=8)
            yr = [Ya.r[ch]]
            self.RED("dve", sm[0][:, 0:8], ysrc, ALU.add, yr, sm[0].r)
            self.TS("dve", sm[0][:], sm[0][:], 1.0 / 64, None, ALU.mult, None, sm[0].r, sm[0].r)
            self.TT("dve", v3(yy), ysrc, bc8(sm[0]), ALU.subtract, yr + sm[0].r, yy.r)
            self.ACT(sq[:], yy[:], AF.Square, yy.r, sq.r)
            self.RED("dve", sm[1][:, 0:8], v3(sq), ALU.add, sq.r, sm[1].r)
            self.ACT(sm[1][:], sm[1][:], AF.Sqrt, sm[1].r + self.cst.r, sm[1].r, bias=self.cst[:, 4:5], scale=1.0 / 64)
            self.RECIP(sm[1][:], sm[1][:], sm[1].r, sm[1].r)
            self.TT("dve", v3(yy), v3(yy), bc8(sm[1]), ALU.mult, yy.r + sm[1].r, yy.r)
            self.TT("pool", yy[:], yy[:], gng[:], ALU.mult, yy.r + gng.r, yy.r)
            self.TT("pool", yy[:], yy[:], gnb[:], ALU.add, yy.r + gnb.r, yy.r)
            self.TT("pool", yy[:], yy[:], bo[:], ALU.add, yy.r + bo.r, yy.r)
            o = ot[ch % 2]
            self.TT("dve", o[:], yy[:], g_[:], ALU.mult, yy.r + g_.r, o.r)
            pO = self.ps[2 + ch % 2]
            for c in range(4):
                self.MM(pO[:, c * 128:(c + 1) * 128], o[:, c * 128:(c + 1) * 128], self.ident_b[:], True, True,
                        o.r + self.ident_b.r, pO.r)
            oo = ob[ch % 2]
            self.CP("act", oo[:], pO[:, :].rearrange("p (c t) -> p c t", c=4), pO.r, oo.r)
            self.DMA("sp", self.O[b, 0:4, :, tk].rearrange("c p t -> p c t"), oo[:], R=oo.r)

    def mix_diff(self, b):
        P = self.P
        rope = self.load_rope()
        LAM_INIT = 0.2
        dl = self.tile([1, 4, 64], F32, "dl")
        self.DMA("sp", dl[:], self.diffl_d, W=dl.r)
        pr = self.tile([1, 2, 64], F32, "dlp")
        sm = self.tile([1, 2], F32, "dls")
        self.TT("dve", pr[:, 0, :], dl[:, 0, :], dl[:, 1, :], ALU.mult, dl.r, pr.r)
        self.TT("dve", pr[:, 1, :], dl[:, 2, :], dl[:, 3, :], ALU.mult, dl.r, pr.r)
        self.RED("dve", sm[:, 0:2], pr[:, :, :], ALU.add, pr.r, sm.r)
        self.ACT(sm[:, 0:2], sm[:, 0:2], AF.Exp, sm.r, sm.r)
        nl = self.tile([1, 1], F32, "nl")
        self.TT("dve", nl[:, 0:1], sm[:, 1:2], sm[:, 0:1], ALU.subtract, sm.r, nl.r)
        self.TS("dve", nl[:, 0:1], nl[:, 0:1], -LAM_INIT, None, ALU.add, None, nl.r, nl.r)
        nlam = self.tile([128, 1], F32, "nlam")
        pb = self.ps[7]
        self.MM(pb[:, 0:1], self.ones_f[0:1, :], nl[0:1, 0:1], True, True, self.ones_f.r + nl.r, pb.r)
        self.CP("dve", nlam[:], pb[:, 0:1], pb.r, nlam.r)
        sg = self.tile([128, 1], F32, "subg")
        self.DMA("sp", sg[:], self.subln_d, W=sg.r)
        self.TS("dve", sg[:], sg[:], 1.0 - LAM_INIT, None, ALU.mult, None, sg.r, sg.r)
        wv = self.tile([128, 8, 512], BF16, "wv")
        self.load_w(wv, self.ab_w[:, 2816:3328])
        vtok = self.tile([128, NCH, 512], BF16, "vtok")
        for ch in range(NCH):
            pst = self.ps[ch % 4]
            self.proj_tok(pst, wv, 0, 512, ch)
            self.CP("act" if ch % 2 else "dve", vtok[:, ch, :], pst[:, :], pst.r, vtok.r)
        wq = [self.tile([128, 8, 128], BF16, f"wq{i}") for i in range(4)]
        qb = self.tile([128, T], BF16, "qb")
        kb = self.tile([128, T], BF16, "kb")
        pt = [self.tile([128, 512], BF16, f"pt{i}") for i in range(3)]
        ft = [self.tile([128, 512], F32, f"ft{i}") for i in range(4)]
        ob = [self.tile([128, 512], BF16, f"ob{i}") for i in range(2)]
        npt = 0
        nout = 0
        for h in range(4):
            self.load_w(wq[0], self.ab_w[:, 1792 + h * 128:1792 + (h + 1) * 128])
            self.load_w(wq[1], self.ab_perm[:, h * 128:(h + 1) * 128])
            self.load_w(wq[2], self.ab_w[:, 2304 + h * 128:2304 + (h + 1) * 128])
            self.load_w(wq[3], self.ab_perm[:, 512 + h * 128:512 + (h + 1) * 128])
            self.proj_rope(qb, wq[0], wq[1], 0, 0, rope)
            self.proj_rope(kb, wq[2], wq[3], 0, 0, rope)
            for (t0, n) in self.TILES:
                kts = [0, 1] if t0 < TC else list(range(NCH))
                for m in range(2):
                    Oa, Da = self.ps[4 + m], self.ps[6 + m]
                    for ki, kt in enumerate(kts):
                        S = self.ps[npt % 4]
                        p = pt[npt % 3]
                        npt += 1
                        self.MM(S[:, 0:n], kb[m * 64:(m + 1) * 64, kt * 128:(kt + 1) * 128], qb[m * 64:(m + 1) * 64, t0:t0 + n],
                                True, True, kb.r + qb.r, S.r)
                        self.ACT(p[:, 0:n], S[:, 0:n], AF.Exp, S.r, p.r, scale=0.125)
                        self.MM(Oa[:, 0:n], vtok[:, kt, h * 128:(h + 1) * 128], p[:, 0:n], ki == 0, ki == len(kts) - 1,
                                vtok.r + p.r, Oa.r)
                        self.MM(Da[:, 0:n], self.ones_b[:], p[:, 0:n], ki == 0, ki == len(kts) - 1,
                                self.ones_b.r + p.r, Da.r)
                for m in range(2):
                    self.RECIP(ft[2 + m][:, 0:n], self.ps[6 + m][:, 0:n], self.ps[6 + m].r, ft[2 + m].r)
                    self.TT("dve", ft[m][:, 0:n], self.ps[4 + m][:, 0:n], ft[2 + m][:, 0:n], ALU.mult,
                            self.ps[4 + m].r + ft[2 + m].r, ft[m].r)
                self.STT("dve", ft[0][:, 0:n], ft[1][:, 0:n], nlam[:, 0:1], ft[0][:, 0:n], ALU.mult, ALU.add,
                         ft[1].r + nlam.r + ft[0].r, ft[0].r)
                self.ACT(ft[1][:, 0:n], ft[0][:, 0:n], AF.Square, ft[0].r, ft[1].r)
                st = self.ps[6]
                self.MM(st[:, 0:n], self.ones_f[:], ft[1][:, 0:n], True, True, self.ones_f.r + ft[1].r, st.r)
                self.ACT(ft[2][:, 0:n], st[:, 0:n], AF.Sqrt, st.r + self.cst.r, ft[2].r, bias=self.cst[:, 2:3], scale=1.0 / 128)
                self.RECIP(ft[2][:, 0:n], ft[2][:, 0:n], ft[2].r, ft[2].r)
                self.TT("pool", ft[0][:, 0:n], ft[0][:, 0:n], ft[2][:, 0:n], ALU.mult, ft[0].r + ft[2].r, ft[0].r)
                o = ob[nout % 2]
                nout += 1
                self.ACT(o[:, 0:n], ft[0][:, 0:n], AF.Copy, ft[0].r + sg.r, o.r, scale=sg[:, 0:1])
                self.DMA("sp", self.O[b, 4 + h, :, t0:t0 + n], o[:, 0:n], R=o.r)

    def layer_norm(self, y, N, gcol, bcol, hout, extra=None):
        st = self.ps[7]
        st2 = self.ps[6]
        sq = self.ln_sq
        for c in range(8):
            s = sq[c % 2]
            self.ACT(s[:, 0:N], y[:, c, :], AF.Square, y.r, s.r)
            self.MM(st[:, 0:N], self.ones_f[:], y[:, c, :], c == 0, c == 7, self.ones_f.r + y.r, st.r)
            self.MM(st2[:, 0:N], self.ones_f[:], s[:, 0:N], c == 0, c == 7, self.ones_f.r + s.r, st2.r)
        mean, rstd = self.ln_mean, self.ln_rstd
        self.ACT(mean[:, 0:N], st[:, 0:N], AF.Copy, st.r, mean.r, scale=1.0 / D)
        self.TT("dve", rstd[:, 0:N], mean[:, 0:N], mean[:, 0:N], ALU.mult, mean.r, rstd.r)
        self.STT("dve", rstd[:, 0:N], st2[:, 0:N], 1.0 / D, rstd[:, 0:N], ALU.mult, ALU.subtract, st2.r + rstd.r, rstd.r)
        self.ACT(rstd[:, 0:N], rstd[:, 0:N], AF.Sqrt, rstd.r + self.cst.r, rstd.r, bias=self.cst[:, 0:1])
        self.RECIP(rstd[:, 0:N], rstd[:, 0:N], rstd.r, rstd.r)
        for c in range(8):
            tmp = self.ln_tmp[c % 2]
            self.TT("dve", tmp[:, 0:N], y[:, c, :], mean[:, 0:N], ALU.subtract, y.r + mean.r, tmp.r)
            self.TT("pool", tmp[:, 0:N], tmp[:, 0:N], rstd[:, 0:N], ALU.mult, tmp.r + rstd.r, tmp.r)
            self.ACT(hout[:, c, :], tmp[:, 0:N], AF.Identity, tmp.r + self.lnp_t.r, hout.r,
                     bias=bcol(c), scale=gcol(c))
            if extra is not None:
                et, sfn, bfn = extra
                self.ACT(et[:, c, :], hout[:, c, :], AF.Identity, hout.r + self.S1.r + self.S1F.r + self.MOD.r, et.r,
                         bias=bfn(c), scale=sfn(c))

    def phase_ffn(self, layer, w_out, w1, w2):
        P = self.P
        last = layer == 1
        N = 256
        wo = self.tile([128, 8, D], BF16, "wo")
        w1t = self.tile([128, 8, 4 * D], BF16, "w1", nres=8)
        w2t = self.tile([128, 32, D], BF16, "w2", nres=32)
        for c in range(8):
            self.DMA("pool", wo[:, c, :], w_out[c * 128:(c + 1) * 128, :], W=wo.r)
        for c in range(8):
            self.DMA("pool", w1t[:, c, :], w1[c * 128:(c + 1) * 128, :], W=[w1t.r[c]])
        for c in range(32):
            self.DMA("pool", w2t[:, c, :], w2[c * 128:(c + 1) * 128, :], W=[w2t.r[c]])
        self.ln_sq = [self.tile([128, N], F32, f"lnsq{i}") for i in range(2)]
        self.ln_tmp = [self.tile([128, N], F32, f"lntmp{i}") for i in range(2)]
        self.ln_mean = self.tile([128, N], F32, "lnmean")
        self.ln_rstd = self.tile([128, N], F32, "lnrstd")
        o_t = [self.tile([128, 8, N], BF16, f"o_t{i}") for i in range(1)]
        h_t = [self.tile([128, 8, N], F32, f"h_t{i}") for i in range(2)]
        hmod = self.tile([128, 8, N], BF16, "hmodf")
        f_t = self.tile([128, 32, N], BF16, "f_t", nres=32)
        rl = [self.tile([128, N], F32, f"rl{i}") for i in range(2)]
        tok = [self.tile([128, D], F32, f"tok{i}") for i in range(2)] if last else None
        hm2 = self.tile([128, 8, N], BF16, "hm2") if not last else None
        tiles = []
        for b in range(NB):
            for t0 in range(0, T, N):
                if last and t0 < TC:
                    continue
                tiles.append((b, t0))
        lg = lambda k, c: self.lnp_t[:, layer, k, c:c + 1]
        npsum = 0
        ntok = 0
        for n, (b, t0) in enumerate(tiles):
            w = self.who(b, t0)
            ot, ht = o_t[0], h_t[n % 2]
            self.DMA("sp", ot[:], self.O[b, :, :, t0:t0 + N].rearrange("c p t -> p c t"), W=ot.r)
            self.DMA("sp", ht[:], self.H[b, :, :, t0:t0 + N].rearrange("c p t -> p c t"), W=ht.r)
            for oc in range(8):
                pst = self.ps[npsum % 6]
                npsum += 1
                for c in range(8):
                    self.MM(pst[:, 0:N], wo[:, c, oc * 128:(oc + 1) * 128], ot[:, c, :], c == 0, c == 7, wo.r + ot.r, pst.r)
                self.STT("dve", ht[:, oc, :], pst[:, 0:N], self.GA[:, layer, oc, w:w + 1], ht[:, oc, :], ALU.mult, ALU.add,
                         pst.r + self.GA.r + ht.r, ht.r)
            self.layer_norm(ht, N, lambda c: lg(0, c), lambda c: lg(1, c), ht,
                            extra=(hmod, lambda c: self.S1F[:, layer, c, w:w + 1], lambda c: self.SHF(layer, c, w)))
            for j in range(32):
                pst = self.ps[npsum % 6]
                npsum += 1
                for c in range(8):
                    self.MM(pst[:, 0:N], w1t[:, c, j * 128:(j + 1) * 128], hmod[:, c, :], c == 0, c == 7,
                            [w1t.r[c]] + hmod.r, pst.r)
                r = rl[j % 2]
                self.ACT(r[:, 0:N], pst[:, 0:N], AF.Relu, pst.r, r.r)
                self.TT("pool", f_t[:, j, :], r[:, 0:N], r[:, 0:N], ALU.mult, r.r, [f_t.r[j]])
            for oc in range(8):
                pst = self.ps[npsum % 6]
                npsum += 1
                for j in range(32):
                    self.MM(pst[:, 0:N], w2t[:, j, oc * 128:(oc + 1) * 128], f_t[:, j, :], j == 0, j == 31,
                            [w2t.r[j], f_t.r[j]], pst.r)
                self.STT("dve", ht[:, oc, :], pst[:, 0:N], self.GFA[:, layer, oc, w:w + 1], ht[:, oc, :], ALU.mult, ALU.add,
                         pst.r + self.GFA.r + ht.r, ht.r)
            if not last:
                self.layer_norm(ht, N, lambda c: lg(2, c), lambda c: lg(3, c), ht,
                                extra=(hm2, lambda c: self.S1[:, layer + 1, c, w:w + 1], lambda c: self.SH(layer + 1, c, w)))
                self.DMA("sp", self.H[b, :, :, t0:t0 + N].rearrange("c p t -> p c t"), ht[:], R=ht.r)
                self.DMA("sp", self.HM[b, :, :, t0:t0 + N].rearrange("c p t -> p c t"), hm2[:], R=hm2.r)
            else:
                self.layer_norm(ht, N, lambda c: lg(2, c), lambda c: lg(3, c), ht)
                for sub in range(N // 128):
                    tk = tok[ntok % 2]
                    ntok += 1
                    for half in range(2):
                        pst = self.ps[npsum % 6]
                        npsum += 1
                        for cc in range(4):
                            c = half * 4 + cc
                            self.TR(pst[:, cc * 128:(cc + 1) * 128], ht[:, c, sub * 128:(sub + 1) * 128], self.ident_f[:],
                                    ht.r + self.ident_f.r, pst.r)
                        self.CP("act", tk[:, half * 512:(half + 1) * 512], pst[:, :], pst.r, tk.r)
                    tl0 = t0 - TC + sub * 128
                    self.DMA("sp", self.out[b, tl0:tl0 + 128, :], tk[:], R=tk.r)


def _per_core_inputs(inp, core):
    b0 = core * NB
    f = lambda a: np.ascontiguousarray(a, dtype=np.float32)
    cv = np.stack([inp["c"][b0], inp["c"][b0 + 1], inp["c_ctx"]], 0)
    m = {
        "x": f(inp["x"][b0:b0 + NB]),
        "ctx": f(inp["ctx"][b0:b0 + NB]),
        "cvec": f(cv.reshape(3, 8, 128).transpose(2, 1, 0)),
        "mod_w": f(inp["mod_w"]),
        "mod_b": f(inp["mod_b"].reshape(2, 48, 128).transpose(2, 0, 1)),
        "lnp": f(np.stack([inp["ln_mix_g"], inp["ln_mix_b"], inp["ln_ffn_g"], inp["ln_ffn_b"]], 1)
                 .reshape(2, 4, 8, 128).transpose(3, 0, 1, 2)),
        "ffn_w1": f(inp["ffn_w1"]),
        "ffn_w2": f(inp["ffn_w2"]),
        "w_out": f(inp["w_out"]),
        "c_ident": np.eye(128, dtype=np.float32),
        "ab_w": f(inp["ab_w_in"][0]),
        "ab_perm": f(inp["ab_w_in"][0][:, 1792:2816][:, _PERM1024]),
        "cd_w": f(inp["cd_w_in"][0]),
        "cd_perm": f(inp["cd_w_in"][0][:, 1552:2192][:, _PERM1024[:640]]),
        "c_rope": _ROPE,
        "c_swamask": _SWAMASK,
        "diff_l": f(np.stack([inp["diff_lq1"][0], inp["diff_lk1"][0], inp["diff_lq2"][0], inp["diff_lk2"][0]], 0)[None]),
        "subln": f(inp["diff_subln_g"][0].reshape(128, 1)),
        "sink": f(inp["swa_sink"]),
        "c_masks": _MASKS,
        "c_neg": _NEG,
        "convw": f(inp["ssd_conv_w"][0].reshape(5, 8, 128).transpose(2, 1, 0)),
        "convb": f(inp["ssd_conv_b"][0].reshape(8, 128).T),
        "dtb": f(inp["ssd_dt_bias"][0].reshape(1, 16)),
        "alog": f(inp["ssd_a_log"][0].reshape(1, 16)),
        "dskip": f(np.repeat(inp["ssd_d"][0], 64).reshape(4, 128).T),
        "ssdg": f(inp["ssd_norm_g"][0].reshape(4, 128).T),
        "rwkv_vec": f(np.stack([inp["rwkv_k_k"][0], inp["rwkv_k_a"][0], inp["rwkv_r_k"][0].reshape(512), inp["rwkv_gn_g"][0],
                                inp["rwkv_gn_b"][0], inp["rwkv_w0"][0, 0], inp["rwkv_w0"][0, 1], inp["rwkv_a0"][0, 0],
                                inp["rwkv_a0"][0, 1]], 0)),
        "rwkv_mu": f(inp["rwkv_mu"]),
        "rwkv_w2": f(inp["rwkv_w2"][0]),
        "rwkv_a2": f(inp["rwkv_a2"][0]),
        "rwkv_g2": f(inp["rwkv_g2"][0]),
    }
    return m


def _mk_consts():
    d = np.arange(64)
    partner = np.where((d % 32) < 16, d + 16, d - 16)
    perm = (np.arange(1024) // 64) * 64 + partner[np.arange(1024) % 64]
    t = np.arange(TL)
    rows, cols = t // 64, t % 64
    p = np.arange(128)
    dd = p % 64
    i = dd % 16
    inv = 10000.0 ** (-(i.astype(np.float64)) / 16.0)
    pos = np.where((dd // 32)[:, None] == 0, rows[None, :], cols[None, :]).astype(np.float64)
    ang = (pos.astype(np.float32) * inv.astype(np.float32)[:, None]).astype(np.float32)
    cos = np.cos(ang).astype(np.float32)
    sin = np.sin(ang).astype(np.float32)
    sgn = np.where((dd % 32) < 16, -1.0, 1.0).astype(np.float32)[:, None]
    rope = np.stack([cos, sin * sgn], 0).astype(np.float32)
    k = np.arange(128)[:, None]
    q = np.arange(512)[None, :]
    m = np.stack([(np.abs(q - k - 128 * (r - 1)) <= 128) for r in range(6)], 0).astype(np.float32)
    return perm, rope, m


_PERM1024, _ROPE, _SWAMASK = _mk_consts()
_i = np.arange(128)[:, None]
_t = np.arange(128)[None, :]
_MASKS = np.stack([_i <= _t, _i < _t, _i >= _t, _i > _t], 0).astype(np.float32)
_NEG = ((_MASKS[[0, 2]] - 1.0) * 1e30).astype(np.float32)


_CACHE = {}


def kernel(**inputs):
    inp = {k: np.asarray(v) for k, v in inputs.items()}
    if "nc" not in _CACHE:
        _CACHE["nc"] = Builder().build()
    nc = _CACHE["nc"]
    in_maps = [_per_core_inputs(inp, c) for c in range(8)]
    res = run_bass_kernel_spmd(nc, in_maps, core_ids=list(range(8)))
    return np.concatenate([r["out"] for r in res.results], axis=0).astype(np.float32)
```

```python
import contextlib
import math
import numpy as np
import concourse.bass as bass
import concourse.mybir as mybir
from concourse.bass_utils import run_bass_kernel_spmd

F32 = mybir.dt.float32
BF16 = mybir.dt.bfloat16
AF = mybir.ActivationFunctionType
ALU = mybir.AluOpType
AX = mybir.AxisListType

ENGS = ["pe", "act", "dve", "pool", "sp"]
N_DMA_SEMS = 24

D = 1024
NB = 2
TC = 256
TL = 2048
T = TC + TL
NCH = T // 128
ALPHA = 4.0 ** 0.25
LN_EPS = 1e-5
EPS_P = LN_EPS / (ALPHA * ALPHA)
CDEC = math.exp(-0.5)


class Res:
    __slots__ = ("w", "r", "excl")

    def __init__(self):
        self.w = None
        self.r = []
        self.excl = False


class Prog:
    def __init__(self, nc, self_wait=True):
        self.nc = nc
        self.ops = {e: [] for e in ENGS}
        self.count = {e: 0 for e in ENGS}
        self.seen = {e: {} for e in ENGS}
        self.dma_val = [0] * N_DMA_SEMS
        self.dma_rr = 0
        self.self_wait = self_wait
        self.sb_top = 16512
        self.SB_BYTES = 229376
        self.n_alloc = 0

    def sb(self, shape, dtype, name=None):
        esz = 2 if dtype == BF16 else 4
        free = int(np.prod(shape[1:])) * esz
        free = (free + 63) // 64 * 64
        off = self.sb_top
        self.sb_top += free
        assert self.sb_top <= self.SB_BYTES, f"SBUF overflow {self.sb_top} ({name})"
        self.n_alloc += 1
        return self.nc.alloc_sbuf_tensor_at(f"{name or 't'}_{self.n_alloc}", list(shape), dtype, offset=off)

    def mark(self):
        return self.sb_top

    def release(self, m):
        self.sb_top = m

    def _collect(self, eng, reads, writes):
        deps = []
        for r in reads:
            if r.w is not None:
                deps.append(r.w)
            if r.excl:
                deps.extend(r.r)
        for w in writes:
            if w.w is not None:
                deps.append(w.w)
            deps.extend(w.r)
        waits = {}
        for (key, val, src) in deps:
            if src == eng and (eng == "pe" or not self.self_wait):
                continue
            if self.seen[eng].get(key, 0) >= val:
                continue
            if waits.get(key, 0) < val:
                waits[key] = val
        for k, v in waits.items():
            self.seen[eng][k] = v
        return list(waits.items())

    def _commit(self, tok, reads, writes):
        for r in reads:
            r.r.append(tok)
        for w in writes:
            w.w = tok
            w.r = []

    def op(self, eng, fn, reads=(), writes=()):
        waits = self._collect(eng, reads, writes)
        self.count[eng] += 1
        tok = (eng, self.count[eng], eng)
        self.ops[eng].append((waits, fn, ("eng", eng)))
        self._commit(tok, reads, writes)

    def dma(self, q, out, in_, reads=(), writes=()):
        idx = self.dma_rr
        self.dma_rr = (self.dma_rr + 1) % N_DMA_SEMS
        key = ("dma", idx)
        waits = self._collect(q, reads, writes)
        prev = self.dma_val[idx]
        if prev > 0 and self.seen[q].get(key, 0) < prev:
            waits.append((key, prev))
            self.seen[q][key] = prev
        self.dma_val[idx] = prev + 16
        tok = (key, prev + 16, "dma")
        self.ops[q].append((waits, lambda e: e.dma_start(out=out, in_=in_), ("dma", idx)))
        self._commit(tok, reads, writes)

    def barrier(self):
        for e in ENGS:
            waits = []
            for o in ENGS:
                if o == e or o == "sp":
                    continue
                v = self.count[o]
                if v > 0 and self.seen[e].get(o, 0) < v:
                    waits.append((o, v))
                    self.seen[e][o] = v
            for i in range(N_DMA_SEMS):
                v = self.dma_val[i]
                key = ("dma", i)
                if v > 0 and self.seen[e].get(key, 0) < v:
                    waits.append((key, v))
                    self.seen[e][key] = v
            if waits:
                self.ops[e].append((waits, None, None))

    def emit(self):
        nc = self.nc
        with contextlib.ExitStack() as st:
            sems = {}
            for e in ENGS:
                sems[e] = st.enter_context(nc.semaphore(f"s_{e}"))
            for i in range(N_DMA_SEMS):
                sems[("dma", i)] = st.enter_context(nc.semaphore(f"s_dma{i}"))
            block = st.enter_context(nc.Block())

            def run(engname):
                def f(eng):
                    for waits, fn, kind in self.ops[engname]:
                        for k, v in waits:
                            eng.wait_ge(sems[k], v)
                        if fn is None:
                            continue
                        ins = fn(eng)
                        if kind[0] == "eng":
                            ins.then_inc(sems[kind[1]], 1)
                        else:
                            ins.then_inc(sems[("dma", kind[1])], 16)
                return f

            block.tensor(run("pe"))
            block.scalar(run("act"))
            block.vector(run("dve"))
            block.gpsimd(run("pool"))
            block.sync(run("sp"))


class Tl:
    def __init__(self, t, nres=1):
        self.t = t
        self.r = [Res() for _ in range(nres)]

    def __getitem__(self, idx):
        return self.t[idx]


class Builder:
    def __init__(self, debug=(), stop=None, skip=()):
        self.debug = set(debug)
        self.stop = stop
        self.skip = set(skip)
        self.stop_mix = None
        self.nc = bass.Bass("TRN2", target_bir_lowering=False)
        import os
        self.P = Prog(self.nc, self_wait=not os.environ.get('NO_SELF_WAIT'))
        self.dram = {}

    def din(self, name, shape, dtype=F32):
        self.dram[name] = self.nc.dram_tensor(name, list(shape), dtype, kind="ExternalInput").ap()
        return self.dram[name]

    def dscr(self, name, shape, dtype):
        kind = "ExternalOutput" if name in self.debug else "Internal"
        self.dram[name] = self.nc.dram_tensor(name, list(shape), dtype, kind=kind).ap()
        return self.dram[name]

    def tile(self, shape, dtype, name=None, nres=1):
        return Tl(self.P.sb(shape, dtype, name), nres)

    def MM(self, out, lhsT, rhs, start, stop, R, W):
        self.P.op("pe", lambda e: e.matmul(out, lhsT, rhs, start=start, stop=stop), R, W)

    def TR(self, out, in_, ident, R, W):
        self.P.op("pe", lambda e: e.transpose(out, in_, ident), R, W)

    def ACT(self, out, in_, func, R, W, bias=None, scale=1.0):
        if bias is None:
            self.P.op("act", lambda e: e.activation(out, in_, func, scale=scale), R, W)
        else:
            self.P.op("act", lambda e: e.activation(out, in_, func, bias=bias, scale=scale), R, W)

    def TT(self, eng, out, in0, in1, op, R, W):
        self.P.op(eng, lambda e: e.tensor_tensor(out, in0, in1, op), R, W)

    def TS(self, eng, out, in0, s1, s2, op0, op1, R, W):
        if s2 is None:
            self.P.op(eng, lambda e: e.tensor_scalar(out, in0, s1, None, op0), R, W)
        else:
            self.P.op(eng, lambda e: e.tensor_scalar(out, in0, s1, s2, op0, op1), R, W)

    def STT(self, eng, out, in0, scalar, in1, op0, op1, R, W):
        self.P.op(eng, lambda e: e.scalar_tensor_tensor(out, in0, scalar, in1, op0, op1), R, W)

    def CP(self, eng, out, in_, R, W):
        if eng == "act":
            self.P.op("act", lambda e: e.copy(out, in_), R, W)
        else:
            self.P.op(eng, lambda e: e.tensor_copy(out, in_), R, W)

    def RECIP(self, out, in_, R, W):
        self.P.op("dve", lambda e: e.reciprocal(out, in_), R, W)

    def RED(self, eng, out, in_, op, R, W):
        self.P.op(eng, lambda e: e.tensor_reduce(out, in_, AX.X, op), R, W)

    def MEMSET(self, eng, out, val, R, W):
        self.P.op(eng, lambda e: e.memset(out, val), R, W)

    def DMA(self, q, out, in_, R=(), W=()):
        self.P.dma(q, out, in_, R, W)

    def build(self):
        nc, P = self.nc, self.P
        x = self.din("x", [NB, TL, D])
        ctx = self.din("ctx", [NB, TC, D])
        cvec = self.din("cvec", [128, 8, 3])
        mod_w = self.din("mod_w", [2, D, 6 * D])
        mod_b = self.din("mod_b", [128, 2, 48])
        lnp = self.din("lnp", [128, 2, 4, 8])
        ffn_w1 = self.din("ffn_w1", [2, D, 4 * D])
        ffn_w2 = self.din("ffn_w2", [2, 4 * D, D])
        w_out = self.din("w_out", [2, D, D])
        cid = self.din("c_ident", [128, 128])
        self.ab_w = self.din("ab_w", [D, 3328])
        self.ab_perm = self.din("ab_perm", [D, 1024])
        self.cd_w = self.din("cd_w", [D, 2320])
        self.cd_perm = self.din("cd_perm", [D, 640])
        self.rope_d = self.din("c_rope", [2, 128, TL])
        self.swam_d = self.din("c_swamask", [6, 128, 512])
        self.diffl_d = self.din("diff_l", [1, 4, 64])
        self.subln_d = self.din("subln", [128, 1])
        self.sink_d = self.din("sink", [1, 8])
        self.masks_d = self.din("c_masks", [4, 128, 128])
        self.neg_d = self.din("c_neg", [2, 128, 128])
        self.convw_d = self.din("convw", [128, 8, 5])
        self.convb_d = self.din("convb", [128, 8])
        self.dtb_d = self.din("dtb", [1, 16])
        self.alog_d = self.din("alog", [1, 16])
        self.dskip_d = self.din("dskip", [128, 4])
        self.ssdg_d = self.din("ssdg", [128, 4])
        self.rvec_d = self.din("rwkv_vec", [9, 512])
        self.rmu_d = self.din("rwkv_mu", [1, 1792])
        self.rw2_d = self.din("rwkv_w2", [2, 64, 512])
        self.ra2_d = self.din("rwkv_a2", [2, 64, 512])
        self.rg2_d = self.din("rwkv_g2", [128, 512])
        out = self.nc.dram_tensor("out", [NB, TL, D], F32, kind="ExternalOutput").ap()
        self.out = out
        H = self.dscr("H", [NB, 8, 128, T], F32)
        HM = self.dscr("HM", [NB, 8, 128, T], BF16)
        O = self.dscr("O", [NB, 8, 128, T], BF16)
        self.H, self.HM, self.O = H, HM, O

        self.ident_f = self.tile([128, 128], F32, "identf")
        self.ident_b = self.tile([128, 128], BF16, "identb")
        self.ones_f = self.tile([128, 128], F32, "onesf")
        self.ones_b = self.tile([128, 128], BF16, "onesb")
        self.cst = self.tile([128, 8], F32, "cst")
        self.DMA("sp", self.ident_f[:], cid, W=self.ident_f.r)
        self.DMA("pool", self.ident_b[:], cid, W=self.ident_b.r)
        self.MEMSET("pool", self.ones_f[:], 1.0, [], self.ones_f.r)
        self.MEMSET("pool", self.ones_b[:], 1.0, [], self.ones_b.r)
        self.MEMSET("dve", self.cst[:, 0:1], EPS_P, [], self.cst.r)
        self.MEMSET("dve", self.cst[:, 1:2], 0.0, [], self.cst.r)
        self.MEMSET("dve", self.cst[:, 2:3], LN_EPS, [], self.cst.r)
        self.MEMSET("dve", self.cst[:, 3:4], 1.0, [], self.cst.r)
        self.MEMSET("dve", self.cst[:, 4:5], 64e-5, [], self.cst.r)
        self.masks = self.tile([128, 4, 128], F32, "masks")
        self.DMA("sp", self.masks[:], self.masks_d.rearrange("m p t -> p m t"), W=self.masks.r)
        self.negm = self.tile([128, 2, 128], F32, "negm")
        self.DMA("sp", self.negm[:], self.neg_d.rearrange("m p t -> p m t"), W=self.negm.r)
        self.lnp_t = self.tile([128, 2, 4, 8], F32, "lnp")
        self.DMA("sp", self.lnp_t[:], lnp, W=self.lnp_t.r)
        self.MOD = self.tile([128, 2, 48, 3], F32, "MOD")
        self.S1 = self.tile([128, 2, 8, 3], F32, "S1")
        self.S1F = self.tile([128, 2, 8, 3], F32, "S1F")
        self.GA = self.tile([128, 2, 8, 3], F32, "GA")
        self.GFA = self.tile([128, 2, 8, 3], F32, "GFA")
        self.ps = [Tl(nc.alloc_psum_tensor(f"ps{i}", [128, 512], F32)) for i in range(8)]
        for p_ in self.ps:
            p_.r[0].excl = True
        keep = P.mark()

        self.phase_mod(cvec, mod_w, mod_b)
        P.barrier()
        P.release(keep)
        self.phase_init(x, ctx)
        P.barrier()
        P.release(keep)
        if "MODD" in self.debug:
            md = self.dscr("MODD", [128, 2 * 48 * 3], F32)
            self.DMA("sp", md, self.MOD[:].rearrange("p l j w -> p (l j w)"), R=self.MOD.r)
        for layer in range(2 if self.stop is None else self.stop):
            self.layer = layer
            self.last = layer == 1
            if layer == 0 and "rwkv" not in self.skip:
                self.phase_rwkv()
                P.barrier()
                P.release(keep)
            for b in range(NB):
                self.phase_mixers(layer, b)
                P.barrier()
                P.release(keep)
            if getattr(self, "stop_mix", None) == layer:
                break
            self.phase_ffn(layer, w_out[layer], ffn_w1[layer], ffn_w2[layer])
            P.barrier()
            P.release(keep)
        P.barrier()
        P.emit()
        return nc

    def who(self, b, t0):
        return 2 if t0 < TC else b

    def SH(self, layer, c, w):
        return self.MOD[:, layer, 0 + c, w:w + 1]

    def SHF(self, layer, c, w):
        return self.MOD[:, layer, 24 + c, w:w + 1]

    def phase_mod(self, cvec, mod_w, mod_b):
        P = self.P
        sc = self.tile([128, 8, 3], F32, "silu_c")
        self.DMA("sp", sc[:], cvec, W=sc.r)
        self.ACT(sc[:], sc[:], AF.Silu, sc.r, sc.r)
        mb = self.tile([128, 2, 48], F32, "modb")
        self.DMA("sp", mb[:], mod_b, W=mb.r)
        GW = 768
        wbuf = [self.tile([128, 8, GW], F32, f"modw{i}") for i in range(2)]
        pst = self.ps[0]
        n = 0
        for layer in range(2):
            for g in range(6 * D // GW):
                wb = wbuf[n % 2]
                n += 1
                for c in range(8):
                    self.DMA("sp", wb[:, c, :], mod_w[layer, c * 128:(c + 1) * 128, g * GW:(g + 1) * GW], W=wb.r)
                for jj in range(GW // 128):
                    j = g * (GW // 128) + jj
                    for c in range(8):
                        self.MM(pst[:, j * 3:j * 3 + 3], wb[:, c, jj * 128:(jj + 1) * 128], sc[:, c, :],
                                c == 0, c == 7, wb.r + sc.r, pst.r)
            pv = pst[:, 0:144].rearrange("p (j w) -> p j w", w=3)
            self.TT("dve", self.MOD[:, layer, :, :], pv, mb[:, layer, :].unsqueeze(2).to_broadcast([128, 48, 3]),
                    ALU.add, pst.r + mb.r, self.MOD.r)
        for layer in range(2):
            self.TS("dve", self.S1[:, layer], self.MOD[:, layer, 8:16, :], 1.0, None, ALU.add, None, self.MOD.r, self.S1.r)
            self.TS("dve", self.S1F[:, layer], self.MOD[:, layer, 32:40, :], 1.0, None, ALU.add, None, self.MOD.r, self.S1F.r)
            self.TS("dve", self.GA[:, layer], self.MOD[:, layer, 16:24, :], 1.0 / ALPHA, None, ALU.mult, None, self.MOD.r, self.GA.r)
            self.TS("dve", self.GFA[:, layer], self.MOD[:, layer, 40:48, :], 1.0 / ALPHA, None, ALU.mult, None, self.MOD.r, self.GFA.r)

    def phase_init(self, x, ctx):
        xin = [self.tile([128, D], F32, f"xin{i}") for i in range(2)]
        hf = [self.tile([128, 8, 128], F32, f"hf{i}") for i in range(2)]
        hm = [self.tile([128, 8, 128], BF16, f"hm{i}") for i in range(2)]
        n = 0
        for b in range(NB):
            for ch in range(NCH):
                t0 = ch * 128
                xi, hfi, hmi = xin[n % 2], hf[n % 2], hm[n % 2]
                src = ctx[b, t0:t0 + 128, :] if t0 < TC else x[b, t0 - TC:t0 - TC + 128, :]
                self.DMA("sp", xi[:], src, W=xi.r)
                w = self.who(b, t0)
                for half in range(2):
                    pst = self.ps[(n * 2 + half) % 8]
                    for cc in range(4):
                        c = half * 4 + cc
                        self.TR(pst[:, cc * 128:(cc + 1) * 128], xi[:, c * 128:(c + 1) * 128], self.ident_f[:],
                                xi.r + self.ident_f.r, pst.r)
                    pv = pst[:, :].rearrange("p (c t) -> p c t", c=4)
                    self.CP("dve", hfi[:, half * 4:half * 4 + 4, :], pv, pst.r, hfi.r)
                for c in range(8):
                    self.ACT(hmi[:, c, :], hfi[:, c, :], AF.Identity, hfi.r + self.S1.r + self.MOD.r, hmi.r,
                             bias=self.SH(0, c, w), scale=self.S1[:, 0, c, w:w + 1])
                self.DMA("sp", self.H[b, :, :, t0:t0 + 128].rearrange("c p t -> p c t"), hfi[:], R=hfi.r)
                self.DMA("sp", self.HM[b, :, :, t0:t0 + 128].rearrange("c p t -> p c t"), hmi[:], R=hmi.r)
                n += 1

    def zero_o(self, b, c0, c1):
        z = self.tile([128, 512], BF16, "zero")
        self.MEMSET("pool", z[:], 0.0, [], z.r)
        for c in range(c0, c1):
            for t0 in range(0, T, 512):
                n = min(512, T - t0)
                self.DMA("sp", self.O[b, c, :, t0:t0 + n], z[:, 0:n], R=z.r)

    def phase_mixers(self, layer, b):
        P = self.P
        skip = getattr(self, "skip", set())
        self.hmod = self.tile([128, 8, T], BF16, "hmod_b")
        for c in range(8):
            self.DMA("sp", self.hmod[:, c, :], self.HM[b, c, :, :], W=self.hmod.r)
        keep = P.mark()
        if layer == 0:
            if "rwkv" in skip:
                self.zero_o(b, 0, 4)
            P.barrier(); P.release(keep)
            if "diff" in skip:
                self.zero_o(b, 4, 8)
            else:
                self.mix_diff(b)
        else:
            if "ssd" in skip:
                self.zero_o(b, 0, 4)
            else:
                self.mix_ssd(b)
            P.barrier(); P.release(keep)
            if "swa" in skip:
                self.zero_o(b, 4, 8)
            else:
                self.mix_swa(b)

    TILES = [(0, 256), (256, 512), (768, 512), (1280, 512), (1792, 512)]

    def load_w(self, wt, src_cols):
        n = src_cols.shape[1]
        self.DMA("pool", wt[:, :, 0:n], src_cols.rearrange("(c p) n -> p c n", p=128), W=wt.r)

    def proj_fm(self, pst, wt, col0, t0, n):
        for c in range(8):
            self.MM(pst[:, 0:n], wt[:, c, col0:col0 + 128], self.hmod[:, c, t0:t0 + n], c == 0, c == 7,
                    wt.r + self.hmod.r, pst.r)

    def proj_tok(self, pst, wt, col0, ncols, ch):
        for c in range(8):
            self.MM(pst[:, 0:ncols], self.hmod[:, c, ch * 128:(ch + 1) * 128], wt[:, c, col0:col0 + ncols], c == 0, c == 7,
                    wt.r + self.hmod.r, pst.r)

    def proj_rope(self, dst, wt, wpt, col0, pcol0, rope):
        cos, sin = rope
        for i, (t0, n) in enumerate(self.TILES):
            p1 = self.ps[(2 * i) % 4]
            self.proj_fm(p1, wt, col0, t0, n)
            if t0 < TC:
                self.CP("act", dst[:, t0:t0 + n], p1[:, 0:n], p1.r, dst.r)
                continue
            p2 = self.ps[(2 * i + 1) % 4]
            self.proj_fm(p2, wpt, pcol0, t0, n)
            l0 = t0 - TC
            ta, tb = self.rtmp
            self.TT("dve", ta[:, 0:n], p1[:, 0:n], cos[:, l0:l0 + n], ALU.mult, p1.r + cos.r, ta.r)
            self.TT("dve", tb[:, 0:n], p2[:, 0:n], sin[:, l0:l0 + n], ALU.mult, p2.r + sin.r, tb.r)
            self.TT("pool", dst[:, t0:t0 + n], ta[:, 0:n], tb[:, 0:n], ALU.add, ta.r + tb.r, dst.r)

    def load_rope(self):
        cos = self.tile([128, TL], F32, "cos")
        sin = self.tile([128, TL], F32, "sin")
        self.DMA("sp", cos[:], self.rope_d[0], W=cos.r)
        self.DMA("sp", sin[:], self.rope_d[1], W=sin.r)
        self.rtmp = (self.tile([128, 512], F32, "rta"), self.tile([128, 512], F32, "rtb"))
        return cos, sin


    def mix_swa(self, b):
        rope = self.load_rope()
        C0 = 1552
        sk = self.tile([1, 8], F32, "sk")
        self.DMA("sp", sk[:], self.sink_d, W=sk.r)
        self.ACT(sk[:], sk[:], AF.Exp, sk.r, sk.r)
        pb = self.ps[7]
        self.MM(pb[:, 0:8], self.ones_f[0:1, :], sk[0:1, 0:8], True, True, self.ones_f.r + sk.r, pb.r)
        esink = self.tile([128, 8], F32, "esink")
        self.CP("dve", esink[:], pb[:, 0:8], pb.r, esink.r)
        mk = self.tile([128, 6, 512], BF16, "swamask")
        self.DMA("pool", mk[:], self.swam_d.rearrange("r p q -> p r q"), W=mk.r)
        wk = [self.tile([128, 8, 128], BF16, f"wk{i}") for i in range(2)]
        kbs = [self.tile([128, T], BF16, f"kb{g}") for g in range(2)]
        for g in range(2):
            for half in range(2):
                self.DMA("pool", wk[0][:, :, half * 64:(half + 1) * 64],
                         self.cd_w[:, C0 + 512 + g * 64:C0 + 512 + (g + 1) * 64].rearrange("(c p) n -> p c n", p=128), W=wk[0].r)
                self.DMA("pool", wk[1][:, :, half * 64:(half + 1) * 64],
                         self.cd_perm[:, 512 + g * 64:512 + (g + 1) * 64].rearrange("(c p) n -> p c n", p=128), W=wk[1].r)
            self.proj_rope(kbs[g], wk[0], wk[1], 0, 0, rope)
        wv = self.tile([128, 8, 128], BF16, "wv")
        self.load_w(wv, self.cd_w[:, C0 + 640:C0 + 768])
        vtok = self.tile([128, NCH, 128], BF16, "vtok")
        for ch in range(NCH):
            pst = self.ps[ch % 4]
            self.proj_tok(pst, wv, 0, 128, ch)
            self.CP("act" if ch % 2 else "dve", vtok[:, ch, :], pst[:, 0:128], pst.r, vtok.r)
        wq = [self.tile([128, 8, 128], BF16, f"wq{i}") for i in range(2)]
        qb = self.tile([128, T], BF16, "qb")
        pt = [self.tile([128, 512], BF16, f"pt{i}") for i in range(4)]
        ft = self.tile([128, 512], F32, "ft")
        ob = [self.tile([128, 512], BF16, f"ob{i}") for i in range(2)]
        npt = 0
        cnt = 0
        for cq in range(4):
            self.load_w(wq[0], self.cd_w[:, C0 + cq * 128:C0 + (cq + 1) * 128])
            self.load_w(wq[1], self.cd_perm[:, cq * 128:(cq + 1) * 128])
            self.proj_rope(qb, wq[0], wq[1], 0, 0, rope)
            for hh in range(2):
                hq = cq * 2 + hh
                kv = hq // 4
                qs = hh * 64
                ks = qs
                kb = kbs[kv]
                for qt in range(4):
                    t0 = TC + qt * 512
                    kts = [(0, None), (1, None)] + [(2 + kk, kk - 4 * qt + 1)
                                                    for kk in range(max(0, 4 * qt - 1), min(15, 4 * qt + 4) + 1)]
                    Oa, Da = self.ps[4 + cnt % 2], self.ps[6 + cnt % 2]
                    cnt += 1
                    LA = 2
                    slots = {}
                    for i in range(len(kts) + LA):
                        if i < len(kts):
                            ch, r = kts[i]
                            S = self.ps[npt % 4]
                            p = pt[npt % 4]
                            npt += 1
                            slots[i] = p
                            self.MM(S[:, :], kb[ks:ks + 64, ch * 128:(ch + 1) * 128], qb[qs:qs + 64, t0:t0 + 512],
                                    True, True, kb.r + qb.r, S.r)
                            self.ACT(p[:, :], S[:, :], AF.Exp, S.r, p.r, scale=0.125)
                            if r is not None:
                                self.TT("pool", p[:, :], p[:, :], mk[:, r, :], ALU.mult, p.r + mk.r, p.r)
                        if i - LA >= 0:
                            ki = i - LA
                            ch, r = kts[ki]
                            p = slots.pop(ki)
                            first, lastk = ki == 0, ki == len(kts) - 1
                            self.MM(Oa[0:64, :], vtok[:, ch, kv * 64:(kv + 1) * 64], p[:, :], first, lastk, vtok.r + p.r, Oa.r)
                            self.MM(Da[0:64, :], self.ones_b[:, 0:64], p[:, :], first, lastk, self.ones_b.r + p.r, Da.r)
                    self.TS("dve", ft[0:64, :], Da[0:64, :], esink[0:64, hq:hq + 1], None, ALU.add, None, Da.r + esink.r, ft.r)
                    self.RECIP(ft[0:64, :], ft[0:64, :], ft.r, ft.r)
                    o = ob[cnt % 2]
                    self.TT("dve", o[0:64, :], Oa[0:64, :], ft[0:64, :], ALU.mult, Oa.r + ft.r, o.r)
                    self.DMA("sp", self.O[b, 4 + cq, qs:qs + 64, t0:t0 + 512], o[0:64, :], R=o.r)

    def mix_ssd(self, b):
        P = self.P
        hmod = self.hmod
        cw = self.tile([128, 8, 5], F32, "cw")
        cb = self.tile([128, 8], F32, "cb")
        dsk = self.tile([128, 4], F32, "dsk")
        ng = self.tile([128, 4], F32, "ng")
        self.DMA("sp", cw[:], self.convw_d, W=cw.r)
        self.DMA("sp", cb[:], self.convb_d, W=cb.r)
        self.DMA("sp", dsk[:], self.dskip_d, W=dsk.r)
        self.DMA("sp", ng[:], self.ssdg_d, W=ng.r)
        dtb = self.tile([1, 16], F32, "dtb")
        al = self.tile([1, 16], F32, "al")
        self.DMA("sp", dtb[:], self.dtb_d, W=dtb.r)
        self.DMA("sp", al[:], self.alog_d, W=al.r)
        self.ACT(al[:], al[:], AF.Exp, al.r, al.r)
        pb = self.ps[7]
        self.MM(pb[:, 0:16], self.ones_f[0:1, :], al[0:1, 0:16], True, True, self.ones_f.r + al.r, pb.r)
        aneg = self.tile([128, 16], F32, "aneg")
        self.TS("dve", aneg[:], pb[:, 0:16], -1.0, None, ALU.mult, None, pb.r, aneg.r)
        zs = self.tile([128, 4, T], BF16, "zs")
        xact = self.tile([128, 8, T], BF16, "xact")
        xs_tok = self.tile([128, NCH, 512], BF16, "xs_tok")
        B_tok = self.tile([128, NCH, 256], BF16, "B_tok")
        Yacc = self.tile([128, NCH, 512], F32, "Yacc", nres=NCH)
        dt_all = self.tile([128, NCH, 16], F32, "dt_all")
        a_all = self.tile([128, NCH, 16], F32, "a_all")
        keep2 = P.mark()
        wt = [self.tile([128, 8, 128], BF16, f"wssd{i}") for i in range(2)]
        pre = self.tile([128, T], F32, "pre")
        acc = self.tile([128, T], F32, "acc")
        for c in range(4):
            w = wt[c % 2]
            self.load_w(w, self.cd_w[:, c * 128:(c + 1) * 128])
            for i, (t0, n) in enumerate(self.TILES):
                pst = self.ps[i % 4]
                self.proj_fm(pst, w, 0, t0, n)
                self.ACT(zs[:, c, t0:t0 + n], pst[:, 0:n], AF.Silu, pst.r, zs.r)
        for c in range(8):
            w = wt[c % 2]
            self.load_w(w, self.cd_w[:, 512 + c * 128:512 + (c + 1) * 128])
            for i, (t0, n) in enumerate(self.TILES):
                pst = self.ps[i % 4]
                self.proj_fm(pst, w, 0, t0, n)
                self.CP("act" if i % 2 else "dve", pre[:, t0:t0 + n], pst[:, 0:n], pst.r, pre.r)
            self.ACT(acc[:, :], pre[:, :], AF.Identity, pre.r + cw.r + cb.r, acc.r, bias=cb[:, c:c + 1], scale=cw[:, c, 2:3])
            for j in (0, 1, 3, 4):
                sft = j - 2
                for (lo, hi) in ((0, TC), (TC, T)):
                    a0, a1 = max(lo, lo - sft), min(hi, hi - sft)
                    self.STT("dve", acc[:, a0:a1], pre[:, a0 + sft:a1 + sft], cw[:, c, j:j + 1], acc[:, a0:a1],
                             ALU.mult, ALU.add, pre.r + cw.r + acc.r, acc.r)
            self.ACT(xact[:, c, :], acc[:, :], AF.Silu, acc.r, xact.r)
        wdt = self.tile([128, 8, 16], BF16, "wdt")
        self.load_w(wdt, self.cd_w[:, 1536:1552])
        for ch in range(NCH):
            pst = self.ps[ch % 4]
            self.MM(pst[:, 0:16], self.ones_f[0:1, :], dtb[0:1, 0:16], True, False, self.ones_f.r + dtb.r, pst.r)
            for c in range(8):
                self.MM(pst[:, 0:16], hmod[:, c, ch * 128:(ch + 1) * 128], wdt[:, c, :], False, c == 7, wdt.r + hmod.r, pst.r)
            self.ACT(dt_all[:, ch, :], pst[:, 0:16], AF.Exp, pst.r, dt_all.r)
        self.ACT(dt_all[:, :, :], dt_all[:, :, :], AF.Ln, dt_all.r + self.cst.r, dt_all.r, bias=self.cst[:, 3:4])
        self.TT("dve", a_all[:, :, :], dt_all[:, :, :], aneg[:, :].unsqueeze(1).to_broadcast([128, NCH, 16]), ALU.mult,
                dt_all.r + aneg.r, a_all.r)
        for ch in range(NCH):
            pst = self.ps[ch % 4]
            for c in range(4):
                self.MM(pst[:, c * 128:(c + 1) * 128], xact[:, c, ch * 128:(ch + 1) * 128], self.ident_b[:], True, True,
                        xact.r + self.ident_b.r, pst.r)
            self.CP("act", xs_tok[:, ch, :], pst[:, :], pst.r, xs_tok.r)
            pst2 = self.ps[4 + ch % 2]
            for g in range(2):
                self.MM(pst2[:, g * 128:(g + 1) * 128], xact[:, 4 + g, ch * 128:(ch + 1) * 128], self.ident_b[:], True, True,
                        xact.r + self.ident_b.r, pst2.r)
            self.CP("dve", B_tok[:, ch, :], pst2[:, 0:256], pst2.r, B_tok.r)
        P.barrier()
        P.release(keep2)
        keep3 = P.mark()
        hT = [self.tile([128, 2, 256], F32, f"hT{d}") for d in range(2)]
        hTb = [self.tile([128, 2, 256], BF16, f"hTb{d}") for d in range(2)]
        for d in range(2):
            self.MEMSET("pool", hT[d][:], 0.0, [], hT[d].r)
            self.MEMSET("pool", hTb[d][:], 0.0, [], hTb[d].r)
        ex = [self.tile([128, 24], F32, f"ex{d}") for d in range(2)]
        nacs = [self.tile([128, 8], F32, f"nacs{d}") for d in range(2)]
        xdt = [self.tile([128, 8, 64], BF16, f"xdt{d}") for d in range(2)]
        Xd = [self.tile([128, 8, 64], BF16, f"Xd{d}") for d in range(2)]
        Abc = [self.tile([128, 8, 128], F32, f"Abc{d}") for d in range(2)]
        Gs = [self.tile([128, 2, 128], BF16, f"Gs{d}") for d in range(2)]
        Lm = [self.tile([128, 128], BF16, f"Lm{i}") for i in range(4)]
        Wh = [self.tile([128, 128], BF16, f"Wh{i}") for i in range(4)]
        zt = [self.tile([128, 512], F32, f"zt{d}") for d in range(2)]
        order = [list(range(NCH)), [1, 0] + list(range(NCH - 1, 1, -1))]
        written = set()
        nl = 0
        for step in range(NCH):
            for d in range(2):
                ch = order[d][step]
                MI, MSo = self.masks[:, 2 * d, :], self.masks[:, 2 * (1 - d) + 1, :]
                NEG = self.negm[:, d, :]
                tk = slice(ch * 128, (ch + 1) * 128)
                a = a_all[:, ch, d * 8:(d + 1) * 8]
                pA = self.ps[d]
                self.MM(pA[:, 0:8], MI, a, True, True, self.masks.r + a_all.r, pA.r)
                self.MM(pA[:, 8:16], self.ones_f[:], a, True, True, self.ones_f.r + a_all.r, pA.r)
                self.MM(pA[:, 16:24], MSo, a, True, True, self.masks.r + a_all.r, pA.r)
                self.ACT(ex[d][:], pA[:, 0:24], AF.Exp, pA.r, ex[d].r)
                self.ACT(nacs[d][:], pA[:, 0:8], AF.Copy, pA.r, nacs[d].r, scale=-1.0)
                xsv = xs_tok[:, ch, :].rearrange("p (h e) -> p h e", h=8)
                self.TT("dve", xdt[d][:], xsv, dt_all[:, ch, d * 8:(d + 1) * 8].unsqueeze(2).to_broadcast([128, 8, 64]), ALU.mult,
                        xs_tok.r + dt_all.r, xdt[d].r)
                self.TT("pool", Xd[d][:], xdt[d][:], ex[d][:, 16:24].unsqueeze(2).to_broadcast([128, 8, 64]), ALU.mult,
                        xdt[d].r + ex[d].r, Xd[d].r)
                if ch >= 2:
                    self.CP("pool", Abc[d][:], a.unsqueeze(2).to_broadcast([128, 8, 128]), a_all.r, Abc[d].r)
                    pG = self.ps[2 + d]
                    for g in range(2):
                        self.MM(pG[:, g * 128:(g + 1) * 128], xact[:, 4 + g, tk], xact[:, 6 + g, tk], True, True, xact.r, pG.r)
                    self.CP("act", Gs[d][:], pG[:, 0:256].rearrange("p (g t) -> p g t", g=2), pG.r, Gs[d].r)
                    pY = self.ps[4 + d]
                    LA = 2
                    whs = {}
                    for i in range(8 + LA):
                        if i < 8:
                            h = i
                            g = h // 4
                            pR = self.ps[6 + (nl % 2)]
                            lm, wh = Lm[nl % 4], Wh[nl % 4]
                            nl += 1
                            whs[h] = wh
                            self.MM(pR[:, 0:128], Abc[d][:, h, :], MI, True, False, Abc[d].r + self.masks.r, pR.r)
                            self.MM(pR[:, 0:128], self.ident_f[:], NEG, False, True, self.ident_f.r + self.negm.r, pR.r)
                            self.ACT(lm[:], pR[:, 0:128], AF.Exp, pR.r + nacs[d].r, lm.r, bias=nacs[d][:, h:h + 1])
                            self.TT("pool", wh[:], lm[:], Gs[d][:, g, :], ALU.mult, lm.r + Gs[d].r, wh.r)
                        if i - LA >= 0:
                            h = i - LA
                            wh = whs.pop(h)
                            self.MM(pY[:, h * 64:(h + 1) * 64], wh[:], xdt[d][:, h, :], True, True, wh.r + xdt[d].r, pY.r)
                    pZ = self.ps[2 + d]
                    for g in range(2):
                        self.MM(pZ[:, g * 256:(g + 1) * 256], xact[:, 6 + g, tk], hTb[d][:, g, :], True, True,
                                xact.r + hTb[d].r, pZ.r)
                    z = zt[d]
                    self.TT("dve", z[:].rearrange("p (h e) -> p h e", h=8), pZ[:, :].rearrange("p (h e) -> p h e", h=8),
                            ex[d][:, 0:8].unsqueeze(2).to_broadcast([128, 8, 64]), ALU.mult, pZ.r + ex[d].r, z.r)
                    self.TT("dve", z[:], pY[:, :], z[:], ALU.add, pY.r + z.r, z.r)
                    if ch in written:
                        self.TT("pool", Yacc[:, ch, :], Yacc[:, ch, :], z[:], ALU.add, [Yacc.r[ch]] + z.r, [Yacc.r[ch]])
                    else:
                        self.CP("pool", Yacc[:, ch, :], z[:], z.r, [Yacc.r[ch]])
                        written.add(ch)
                pH = self.ps[d]
                for g in range(2):
                    self.MM(pH[:, g * 256:(g + 1) * 256], B_tok[:, ch, g * 128:(g + 1) * 128],
                            Xd[d][:, 4 * g:4 * g + 4, :].rearrange("p h e -> p (h e)"), True, True, B_tok.r + Xd[d].r, pH.r)
                hv = hT[d][:].rearrange("p g (h e) -> p (g h) e", h=4)
                self.TT("dve", hv, hv, ex[d][:, 8:16].unsqueeze(2).to_broadcast([128, 8, 64]), ALU.mult, hT[d].r + ex[d].r, hT[d].r)
                hf = hT[d][:].rearrange("p g x -> p (g x)")
                self.TT("dve", hf, hf, pH[:, :], ALU.add, hT[d].r + pH.r, hT[d].r)
                self.CP("act", hTb[d][:].rearrange("p g x -> p (g x)"), hf, hT[d].r, hTb[d].r)
        P.barrier()
        P.release(keep3)
        yg = [self.tile([128, 512], F32, f"yg{i}") for i in range(4)]
        sq = [self.tile([128, 512], F32, f"sq{i}") for i in range(2)]
        rs = self.tile([128, 512], F32, "rs")
        ob = [self.tile([128, 512], BF16, f"ob{i}") for i in range(2)]
        no = 0
        for qt in range(4):
            t0 = TC + qt * 512
            for c in range(4):
                pT = self.ps[c]
                for k4 in range(4):
                    ch = 2 + qt * 4 + k4
                    self.TR(pT[:, k4 * 128:(k4 + 1) * 128], Yacc[:, ch, c * 128:(c + 1) * 128], self.ident_f[:],
                            [Yacc.r[ch]] + self.ident_f.r, pT.r)
                self.STT("dve", yg[c][:], xact[:, c, t0:t0 + 512], dsk[:, c:c + 1], pT[:, :], ALU.mult, ALU.add,
                         xact.r + dsk.r + pT.r, yg[c].r)
                self.TT("pool", yg[c][:], yg[c][:], zs[:, c, t0:t0 + 512], ALU.mult, yg[c].r + zs.r, yg[c].r)
            for g in range(2):
                st = self.ps[4 + g]
                for k2 in range(2):
                    c = 2 * g + k2
                    self.ACT(sq[k2][:], yg[c][:], AF.Square, yg[c].r, sq[k2].r)
                    self.MM(st[:, :], self.ones_f[:], sq[k2][:], k2 == 0, k2 == 1, self.ones_f.r + sq[k2].r, st.r)
                self.ACT(rs[:], st[:, :], AF.Sqrt, st.r + self.cst.r, rs.r, bias=self.cst[:, 2:3], scale=1.0 / 256)
                self.RECIP(rs[:], rs[:], rs.r, rs.r)
                for k2 in range(2):
                    c = 2 * g + k2
                    self.TT("dve", yg[c][:], yg[c][:], rs[:], ALU.mult, yg[c].r + rs.r, yg[c].r)
                    o = ob[no % 2]
                    no += 1
                    self.ACT(o[:], yg[c][:], AF.Copy, yg[c].r + ng.r, o.r, scale=ng[:, c:c + 1])
                    self.DMA("sp", self.O[b, c, :, t0:t0 + 512], o[:], R=o.r)


    def bcast_row(self, src_row, n, name):
        t = self.tile([128, n], F32, name)
        self.DMA("sp", t[:], src_row.partition_broadcast(128), W=t.r)
        return t

    def phase_rwkv(self):
        P = self.P
        base = P.mark()
        self.PR = self.dscr("PR", [NB, 3, T, 512], F32)
        self.WD = self.dscr("WD", [NB, 2, T, 512], F32)
        self.PB = self.dscr("PB", [NB, 2, T, 5, 512], BF16)
        self.BG = self.dscr("BG", [NB, 2, T, 512], F32)
        keep = P.mark()
        for b in range(NB):
            self.rwkv_prep(b, None)
            P.barrier()
            P.release(keep)
        Yacc = [self.tile([128, NCH, 512], F32, f"Yacc{b}", nres=NCH) for b in range(NB)]
        keep2 = P.mark()
        import os
        stage = int(os.environ.get("RWKV_STAGE", 3))
        if stage >= 2:
            self.rwkv_chunked(Yacc)
        P.barrier()
        if "YD" in self.debug:
            yd = self.dscr("YD", [NB, 128, NCH, 512], F32)
            for b in range(NB):
                self.DMA("sp", yd[b], Yacc[b][:], R=Yacc[b].r)
            P.barrier()
        P.release(keep2)
        for b in range(NB if stage >= 3 else 0):
            self.rwkv_finish(b, Yacc[b])
            P.barrier()
            P.release(keep2)
        P.release(base)

    def rwkv_chunked(self, Yacc):
        P = self.P
        ps = self.ps
        c_ = CDEC
        mask4 = [self.tile([128, 4, 128], F32, f"mask4{d}") for d in range(2)]
        for d in range(2):
            MS, MI = self.masks[:, 2 * d + 1, :], self.masks[:, 2 * d, :]
            for q, m in enumerate((MS, MI, MS, MI)):
                self.CP("pool", mask4[d][:, q, :], m, self.masks.r, mask4[d].r)
        Sf = [[self.tile([128, 4, 64], F32, f"Sf{b}{d}") for d in range(2)] for b in range(NB)]
        Sb = [[self.tile([128, 4, 64], BF16, f"Sb{b}{d}") for d in range(2)] for b in range(NB)]
        for b in range(NB):
            for d in range(2):
                self.MEMSET("pool", Sf[b][d][:], 0.0, [], Sf[b][d].r)
                self.MEMSET("pool", Sb[b][d][:], 0.0, [], Sb[b][d].r)
        pbin = [self.tile([128, 5, 512], BF16, f"pbin{i}") for i in range(2)]
        lw = [self.tile([128, 512], F32, f"lw{i}") for i in range(2)]
        E = [self.tile([128, 512], F32, f"E{i}") for i in range(3)]
        Xt = self.tile([128, 4, 512], BF16, "Xt")
        FM = [self.tile([128, 4, 128], BF16, f"FM{c}") for c in range(4)]
        PC = self.tile([128, 4], F32, "PC")
        Mm = [self.tile([128, 4, 128], BF16, f"Mm{h}") for h in range(8)]
        AATg = [[[self.tile([128, 2, 2, 128], F32, f"AAT{g}{i}{jb}") for jb in range(2)] for i in range(2)] for g in range(2)]
        Wball = [self.tile([128, 256], F32, f"Wball{g}") for g in range(2)]
        Up = self.tile([128, 8, 64], BF16, "Up")
        ysb = self.tile([128, 512], F32, "ysb")
        tS = self.tile([128, 4, 64], F32, "tS")
        order = [list(range(NCH)), [1, 0] + list(range(NCH - 1, 1, -1))]
        written = [set() for _ in range(NB)]
        nu = 0
        import os
        ndirs = int(os.environ.get("RWKV_DIRS", 2))
        for step in range(NCH):
            for d in range(ndirs):
                for b in range(NB):
                    ch = order[d][step]
                    tk = slice(ch * 128, (ch + 1) * 128)
                    MI, MS, MSo = self.masks[:, 2 * d, :], self.masks[:, 2 * d + 1, :], self.masks[:, 2 * (1 - d) + 1, :]
                    pin, lwt = pbin[nu % 2], lw[nu % 2]
                    nu += 1
                    self.DMA("sp", pin[:], self.PB[b, d, tk, :, :], W=pin.r)
                    self.DMA("sp", lwt[:], self.WD[b, d, tk, :], W=lwt.r)
                    S_f, S_b = Sf[b][d], Sb[b][d]
                    self.MM(ps[0][:, :], MI, lwt[:], True, True, self.masks.r + lwt.r, ps[0].r)
                    self.MM(ps[1][:, :], MS, lwt[:], True, True, self.masks.r + lwt.r, ps[1].r)
                    for c in range(4):
                        self.MM(ps[2][:, c:c + 1], lwt[:, c * 128:(c + 1) * 128], self.ones_f[:, 0:1], True, True,
                                lwt.r + self.ones_f.r, ps[2].r)
                    self.ACT(E[0][:], ps[0][:, :], AF.Exp, ps[0].r, E[0].r, scale=-c_)
                    self.ACT(E[1][:], ps[0][:, :], AF.Exp, ps[0].r, E[1].r, scale=c_)
                    self.ACT(E[2][:], ps[1][:, :], AF.Exp, ps[1].r, E[2].r, scale=-c_)
                    self.ACT(PC[:], ps[2][:, 0:4], AF.Exp, ps[2].r, PC.r, scale=-c_)
                    self.TT("dve", Xt[:, 0, :], pin[:, 0, :], E[2][:], ALU.mult, pin.r + E[2].r, Xt.r)
                    self.TT("pool", Xt[:, 1, :], pin[:, 3, :], E[0][:], ALU.mult, pin.r + E[0].r, Xt.r)
                    self.TT("dve", Xt[:, 2, :], pin[:, 1, :], E[1][:], ALU.mult, pin.r + E[1].r, Xt.r)
                    self.TT("pool", Xt[:, 3, :], pin[:, 2, :], E[1][:], ALU.mult, pin.r + E[1].r, Xt.r)
                    for c in range(4):
                        pt_ = ps[2 + c % 2]
                        for q in range(4):
                            self.MM(pt_[:, q * 128:(q + 1) * 128], Xt[:, q, c * 128:(c + 1) * 128], self.ident_b[:], True, True,
                                    Xt.r + self.ident_b.r, pt_.r)
                        self.CP("act" if c % 2 else "dve", FM[c][:], pt_[:, :].rearrange("p (q t) -> p q t", q=4), pt_.r, FM[c].r)
                    for h in range(8):
                        c, hb = h // 2, (h % 2) * 64
                        grp, j = h // 4, h % 4
                        fm = FM[c]
                        hs_ = slice(hb, hb + 64)
                        pm = ps[4]
                        AR = fm[hs_, 0:2, :].rearrange("p q t -> p (q t)")
                        self.MM(pm[:, 0:256], fm[hs_, 2, :], AR, True, True, fm.r, pm.r)
                        self.MM(pm[:, 256:512], fm[hs_, 3, :], AR, True, True, fm.r, pm.r)
                        self.TT("dve", Mm[h][:], pm[:, :].rearrange("p (q t) -> p q t", q=4), mask4[d][:], ALU.mult,
                                pm.r + mask4[d].r, Mm[h].r)
                        a0 = AATg[grp][0][j // 2]
                        self.TT("dve", a0[:, j % 2, 0, :], pm[:, 0:128], MS, ALU.mult, pm.r + self.masks.r, a0.r)
                        pn = ps[5]
                        self.MM(pn[:, 0:128], fm[hs_, 0, :], fm[hs_, 2, :], True, True, fm.r, pn.r)
                        self.TT("dve", a0[:, j % 2, 1, :], pn[:, 0:128], MSo, ALU.mult, pn.r + self.masks.r, a0.r)
                        wbank = ps[7 - grp]
                        wp = wbank[:, j * 64:(j + 1) * 64]
                        self.MM(wp, fm[hs_, 0, :], S_b[hs_, c, :], j == 0, False, fm.r + S_b.r, wbank.r)
                        self.MM(wp, Mm[h][:, 2, :], pin[:, 4, h * 64:(h + 1) * 64], False, False, Mm[h].r + pin.r, wbank.r)
                    cur = 0
                    for lvl in range(7):
                        for grp in range(2):
                            wbank = ps[7 - grp]
                            wb = Wball[grp]
                            self.CP("dve", wb[:], wbank[:, 0:256], wbank.r, wb.r)
                            for j in range(4):
                                A = AATg[grp][cur][j // 2]
                                self.MM(wbank[:, j * 64:(j + 1) * 64], A[:, j % 2, 0, :], wb[:, j * 64:(j + 1) * 64], False, False,
                                        A.r + wb.r, wbank.r)
                            if lvl == 6:
                                continue
                            for jb in range(2):
                                A = AATg[grp][cur][jb]
                                pq = ps[2 + 2 * grp + jb]
                                for jj in range(2):
                                    o0 = jj * 256
                                    self.MM(pq[:, o0:o0 + 128], A[:, jj, 1, :], A[:, jj, 0, :], True, True, A.r, pq.r)
                                    self.MM(pq[:, o0 + 128:o0 + 256], A[:, jj, 0, :], A[:, jj, 1, :], True, True, A.r, pq.r)
                            for jb in range(2):
                                An = AATg[grp][1 - cur][jb]
                                pq = ps[2 + 2 * grp + jb]
                                self.CP("act", An[:].rearrange("p a b t -> p (a b t)"), pq[:, :], pq.r, An.r)
                        cur = 1 - cur
                    first_y = True
                    for grp in range(2):
                        wbank = ps[7 - grp]
                        self.CP("dve", Up[:, grp * 4:grp * 4 + 4, :].rearrange("p h e -> p (h e)"), wbank[:, 0:256], wbank.r, Up.r)
                    for h in range(8):
                        c, hb = h // 2, (h % 2) * 64
                        fm = FM[c]
                        hs_ = slice(hb, hb + 64)
                        yp = ps[0][:, h * 64:(h + 1) * 64]
                        self.MM(yp, fm[hs_, 1, :], S_b[hs_, c, :], first_y, False, fm.r + S_b.r, ps[0].r)
                        first_y = False
                        self.MM(yp, Mm[h][:, 1, :], Up[:, h, :], False, False, Mm[h].r + Up.r, ps[0].r)
                        self.MM(yp, Mm[h][:, 3, :], pin[:, 4, h * 64:(h + 1) * 64], False, False, Mm[h].r + pin.r, ps[0].r)
                    if ch in written[b]:
                        self.TT("dve", Yacc[b][:, ch, :], Yacc[b][:, ch, :], ps[0][:, :], ALU.add, [Yacc[b].r[ch]] + ps[0].r,
                                [Yacc[b].r[ch]])
                    else:
                        written[b].add(ch)
                        self.CP("dve", Yacc[b][:, ch, :], ps[0][:, :], ps[0].r, [Yacc[b].r[ch]])
                    for c in range(4):
                        pd = ps[1][:, c * 128:(c + 1) * 128]
                        self.MM(pd, Xt[:, 2, c * 128:(c + 1) * 128], Up[:, 2 * c:2 * c + 2, :].rearrange("p h e -> p (h e)"),
                                c == 0, False, Xt.r + Up.r, ps[1].r)
                        self.MM(pd, Xt[:, 3, c * 128:(c + 1) * 128], pin[:, 4, c * 128:(c + 1) * 128], False, False,
                                Xt.r + pin.r, ps[1].r)
                    for c in range(4):
                        self.TS("dve", tS[:, c, :], S_f[:, c, :], PC[:, c:c + 1], None, ALU.mult, None, S_f.r + PC.r, tS.r)
                        for hh in range(2):
                            hs_ = slice(hh * 64, hh * 64 + 64)
                            self.STT("dve", S_f[hs_, c, :], ps[1][hs_, c * 128 + hh * 64:c * 128 + hh * 64 + 64], PC[hs_, c:c + 1],
                                     tS[hs_, c, :], ALU.mult, ALU.add, ps[1].r + PC.r + tS.r, S_f.r)
                    self.CP("act", S_b[:], S_f[:], S_f.r, S_b.r)

    def rwkv_prep(self, b, Vp):
        P = self.P
        tw = self.tile([128, T], BF16, "twxa")
        sg = self.tile([128, T], BF16, "sg")
        keepA = P.mark()
        hmod = self.tile([128, 8, T], BF16, "hmod_b")
        hs = self.tile([128, 8, T], BF16, "hs_b")
        for c in range(8):
            self.DMA("sp", hmod[:, c, :], self.HM[b, c, :, :], W=hmod.r)
        for (lo, hi) in ((0, TC), (TC, T)):
            self.TT("pool", hs[:, :, lo + 1:hi - 1], hmod[:, :, lo:hi - 2], hmod[:, :, lo + 2:hi], ALU.add, hmod.r, hs.r)
            self.CP("dve", hs[:, :, lo:lo + 1], hmod[:, :, lo + 1:lo + 2], hmod.r, hs.r)
            self.CP("dve", hs[:, :, hi - 1:hi], hmod[:, :, hi - 2:hi - 1], hmod.r, hs.r)
        omm = self.bcast_row(self.rmu_d[0, :], 1792, "omm")
        hmu = self.tile([128, 1792], F32, "hmu")
        self.TS("dve", hmu[:], omm[:], 0.5, None, ALU.mult, None, omm.r, hmu.r)
        self.TS("dve", omm[:], omm[:], -1.0, 1.0, ALU.mult, ALU.add, omm.r, omm.r)

        def shifted_weights(wt, w1t, w2t, col0, n):
            self.load_w(wt, self.ab_w[:, col0:col0 + n])
            self.TT("dve", w1t[:, :, 0:n], wt[:, :, 0:n], omm[:, col0:col0 + n].unsqueeze(1).to_broadcast([128, 8, n]), ALU.mult,
                    wt.r + omm.r, w1t.r)
            self.TT("pool", w2t[:, :, 0:n], wt[:, :, 0:n], hmu[:, col0:col0 + n].unsqueeze(1).to_broadcast([128, 8, n]), ALU.mult,
                    wt.r + hmu.r, w2t.r)

        import os
        sub = int(os.environ.get("RWKV_SUB", 9))
        if sub <= 0:
            return
        wt = self.tile([128, 8, 512], BF16, "rw")
        w1t = self.tile([128, 8, 512], BF16, "rw1")
        w2t = self.tile([128, 8, 512], BF16, "rw2")
        for gi, col0 in enumerate((1536, 1664)):
            shifted_weights(wt, w1t, w2t, col0, 128)
            for i, (t0, n) in enumerate(self.TILES):
                pst = self.ps[i % 4]
                for c in range(8):
                    self.MM(pst[:, 0:n], w1t[:, c, 0:128], hmod[:, c, t0:t0 + n], c == 0, False, w1t.r + hmod.r, pst.r)
                for c in range(8):
                    self.MM(pst[:, 0:n], w2t[:, c, 0:128], hs[:, c, t0:t0 + n], False, c == 7, w2t.r + hs.r, pst.r)
                if gi == 0:
                    self.ACT(tw[0:64, t0:t0 + n], pst[0:64, 0:n], AF.Tanh, pst.r, tw.r)
                    self.ACT(tw[64:128, t0:t0 + n], pst[64:128, 0:n], AF.Copy, pst.r, tw.r)
                else:
                    self.ACT(sg[:, t0:t0 + n], pst[:, 0:n], AF.Sigmoid, pst.r, sg.r)
        if sub <= 1:
            return
        stg = [self.tile([128, 512], F32, f"stg{i}") for i in range(2)]
        vb16 = self.tile([128, 8, 128], BF16, "vb16")
        self.MEMSET("pool", vb16[:], 0.0, [], vb16.r)
        ns = 0
        for grp in range(3):
            shifted_weights(wt, w1t, w2t, grp * 512, 512)
            for ch in range(NCH):
                tk = slice(ch * 128, (ch + 1) * 128)
                pst = self.ps[ch % 4]
                for c in range(8):
                    self.MM(pst[:, :], hmod[:, c, tk], w1t[:, c, :], c == 0, False, w1t.r + hmod.r, pst.r)
                for c in range(8):
                    self.MM(pst[:, :], hs[:, c, tk], w2t[:, c, :], False, c == 7, w2t.r + hs.r, pst.r)
                st = stg[ns % 2]
                ns += 1
                self.CP("act", st[:], pst[:, :], pst.r, st.r)
                self.DMA("sp", self.PR[b, grp, tk, :], st[:], R=st.r)
                if False:
                    off = 64 * b
                    self.CP("dve", vb16[:, :, off:off + 64], st[:].rearrange("p (h e) -> p h e", h=8), st.r, vb16.r)
                    for hh in range(2):
                        pV = self.ps[4 + hh]
                        for h4 in range(4):
                            h = hh * 4 + h4
                            self.MM(pV[0:64 + off, h4 * 128:(h4 + 1) * 128], vb16[:, h, 0:64 + off], self.ident_b[:], True, True,
                                    vb16.r + self.ident_b.r, pV.r)
                        self.CP("dve" if hh else "act", Vp[off:off + 64, hh * 4:hh * 4 + 4, tk],
                                pV[off:off + 64, :].rearrange("p (h t) -> p h t", h=4), pV.r, Vp.r)
        P.barrier()
        P.release(keepA)
        if sub <= 2:
            return
        rv = [self.bcast_row(self.rvec_d[i, :], 512, f"rv{i}") for i in range(9)]
        kk_bc, ka_bc, rk_bc, _, _, w0a, w0b, a0a, a0b = rv
        omka = self.tile([128, 512], F32, "omka")
        self.TS("dve", omka[:], ka_bc[:], -1.0, 1.0, ALU.mult, ALU.add, ka_bc.r, omka.r)
        w2b = self.tile([64, 2, 512], BF16, "w2b")
        a2b = self.tile([128, 2, 512], BF16, "a2b")
        g2b = self.tile([128, 512], BF16, "g2b")
        self.DMA("pool", w2b[:], self.rw2_d.rearrange("d k n -> k d n"), W=w2b.r)
        self.DMA("pool", a2b[64:128, :, :], self.ra2_d.rearrange("d k n -> k d n"), W=a2b.r)
        self.DMA("pool", g2b[:], self.rg2_d, W=g2b.r)
        rkv = [[self.tile([128, 512], F32, f"in{j}{i}") for i in range(3)] for j in range(2)]
        tmp = [self.tile([128, 512], F32, f"tm{i}") for i in range(4)]
        kk = self.tile([128, 512], F32, "kk")
        sm = [self.tile([128, 8], F32, f"sm{i}") for i in range(2)]
        pbst = [self.tile([128, 5, 512], BF16, f"pbst{i}") for i in range(2)]
        wdec = [self.tile([128, 512], F32, f"wdec{i}") for i in range(2)]
        bg = [self.tile([128, 512], F32, f"bg{i}") for i in range(2)]
        v3 = lambda t: t[:].rearrange("p (h e) -> p h e", h=8)
        bc8 = lambda t: t[:, 0:8].unsqueeze(2).to_broadcast([128, 8, 64])
        for ch in range(NCH):
            tk = slice(ch * 128, (ch + 1) * 128)
            r_t, k_t, v_t = rkv[ch % 2]
            for gi, tt in enumerate((r_t, k_t, v_t)):
                self.DMA("sp", tt[:], self.PR[b, gi, tk, :], W=tt.r)
            t0_, t1_, t2_, t3_ = tmp
            self.TT("dve", t0_[:], k_t[:], kk_bc[:], ALU.mult, k_t.r + kk_bc.r, t0_.r)
            self.TT("pool", t1_[:], t0_[:], t0_[:], ALU.mult, t0_.r, t1_.r)
            self.RED("dve", sm[0][:, 0:8], v3(t1_), ALU.add, t1_.r, sm[0].r)
            self.TS("dve", sm[0][:], sm[0][:], 1e-24, None, ALU.max, None, sm[0].r, sm[0].r)
            self.ACT(sm[0][:], sm[0][:], AF.Sqrt, sm[0].r, sm[0].r)
            self.RECIP(sm[0][:], sm[0][:], sm[0].r, sm[0].r)
            self.TT("dve", v3(kk), v3(t0_), bc8(sm[0]), ALU.mult, t0_.r + sm[0].r, kk.r)
            self.TT("pool", t1_[:], r_t[:], k_t[:], ALU.mult, r_t.r + k_t.r, t1_.r)
            self.TT("pool", t1_[:], t1_[:], rk_bc[:], ALU.mult, t1_.r + rk_bc.r, t1_.r)
            self.RED("dve", sm[1][:, 0:8], v3(t1_), ALU.add, t1_.r, sm[1].r)
            self.TT("dve", v3(bg[0]), v3(v_t), bc8(sm[1]), ALU.mult, v_t.r + sm[1].r, bg[0].r)
            self.DMA("sp", self.BG[b, 0, tk, :], bg[0][:], R=bg[0].r)
            pg = self.ps[4]
            self.MM(pg[:, :], sg[:, tk], g2b[:], True, True, sg.r + g2b.r, pg.r)
            self.CP("act", bg[1][:], pg[:, :], pg.r, bg[1].r)
            self.DMA("sp", self.BG[b, 1, tk, :], bg[1][:], R=bg[1].r)
            for d in range(2):
                pb_ = pbst[d]
                w0_bc, a0_bc = (w0a, a0a) if d == 0 else (w0b, a0b)
                pz = self.ps[d]
                self.MM(pz[:, :], tw[0:64, tk], w2b[0:64, d, :], True, True, tw.r + w2b.r, pz.r)
                self.TT("dve", t1_[:], pz[:, :], w0_bc[:], ALU.add, pz.r + w0_bc.r, t1_.r)
                self.ACT(wdec[d][:], t1_[:], AF.Sigmoid, t1_.r, wdec[d].r)
                self.DMA("sp", self.WD[b, d, tk, :], wdec[d][:], R=wdec[d].r)
                pa = self.ps[2 + d]
                self.MM(pa[:, :], tw[64:128, tk], a2b[64:128, d, :], True, True, tw.r + a2b.r, pa.r)
                self.TT("dve", t2_[:], pa[:, :], a0_bc[:], ALU.add, pa.r + a0_bc.r, t2_.r)
                self.ACT(t2_[:], t2_[:], AF.Sigmoid, t2_.r, t2_.r)
                self.TS("pool", pb_[:, 0, :], kk[:], -1.0, None, ALU.mult, None, kk.r, pb_.r)
                self.TT("pool", pb_[:, 1, :], kk[:], t2_[:], ALU.mult, kk.r + t2_.r, pb_.r)
                self.TT("dve", t3_[:], t2_[:], ka_bc[:], ALU.mult, t2_.r + ka_bc.r, t3_.r)
                self.TT("dve", t3_[:], t3_[:], omka[:], ALU.add, t3_.r + omka.r, t3_.r)
                self.TT("pool", pb_[:, 2, :], k_t[:], t3_[:], ALU.mult, k_t.r + t3_.r, pb_.r)
                self.CP("act", pb_[:, 3, :], r_t[:], r_t.r, pb_.r)
                self.CP("act", pb_[:, 4, :], v_t[:], v_t.r, pb_.r)
                self.DMA("sp", self.PB[b, d, tk, :, :], pb_[:], R=pb_.r)

    def rwkv_scan(self, Vp, Y):
        NS = 2
        NBUF = 3
        S = [self.tile([128, 512], F32, f"S{d}") for d in range(2)]
        for d in range(2):
            self.MEMSET("dve", S[d][:], 0.0, [], S[d].r)
        Wb = [[self.tile([128, NS, 512], F32, f"Wb{d}{i}") for i in range(NBUF)] for d in range(2)]
        Vb = [[self.tile([128, NS, 4, 512], BF16, f"Vb{d}{i}") for i in range(NBUF)] for d in range(2)]
        t1 = [self.tile([128, 512], F32, f"sc1{d}") for d in range(2)]
        t2 = [self.tile([128, 512], F32, f"sc2{d}") for d in range(2)]
        t3 = [self.tile([128, 512], F32, f"sc3{d}") for d in range(2)]
        sa = [self.tile([128, 8], F32, f"sa{d}") for d in range(2)]
        yts = [self.tile([128, 8], F32, f"yt{d}") for d in range(2)]
        order = [list(range(T)), list(range(TC - 1, -1, -1)) + list(range(T - 1, TC - 1, -1))]
        v3 = lambda ap: ap.rearrange("p (h e) -> p h e", h=8)
        nblk = T // NS
        ywritten = set()
        import os
        nblk = int(os.environ.get('RWKV_MAXBLK', nblk))

        def load(d, bi):
            toks = order[d][bi * NS:(bi + 1) * NS]
            lo = min(toks)
            wb, vb = Wb[d][bi % NBUF], Vb[d][bi % NBUF]
            for b in range(NB):
                self.DMA("sp", wb[b * 64:(b + 1) * 64, :, :], self.WD[b, d, lo:lo + NS, :].partition_broadcast(64), W=wb.r)
                self.DMA("sp", vb[b * 64:(b + 1) * 64, :, :, :], self.PB[b, d, lo:lo + NS, :, :].partition_broadcast(64), W=vb.r)

        for bi in range(min(NBUF - 1, nblk)):
            for d in range(2):
                load(d, bi)
        for bi in range(nblk):
            for d in range(2):
                if bi + NBUF - 1 < nblk:
                    load(d, bi + NBUF - 1)
            for j in range(NS):
                for d in range(2):
                    t = order[d][bi * NS + j]
                    lo = min(order[d][bi * NS:(bi + 1) * NS])
                    jj = t - lo
                    wb, vb = Wb[d][bi % NBUF], Vb[d][bi % NBUF]
                    Sd = S[d]
                    a_bc, b_bc, k_bc, r_bc = (vb[:, jj, q, :] for q in range(4))
                    self.TT("dve", t1[d][:], Sd[:], a_bc, ALU.mult, Sd.r + vb.r, t1[d].r)
                    self.RED("dve", sa[d][:, 0:8], v3(t1[d][:]), ALU.add, t1[d].r, sa[d].r)
                    self.TT("pool", Sd[:], Sd[:], wb[:, jj, :], ALU.mult, Sd.r + wb.r, Sd.r)
                    self.TT("dve", v3(t2[d][:]), v3(b_bc), sa[d][:, 0:8].unsqueeze(2).to_broadcast([128, 8, 64]), ALU.mult,
                            vb.r + sa[d].r, t2[d].r)
                    self.TT("dve", Sd[:], Sd[:], t2[d][:], ALU.add, Sd.r + t2[d].r, Sd.r)
                    self.TT("pool", v3(t3[d][:]), v3(k_bc), Vp[:, :, t:t + 1].to_broadcast([128, 8, 64]), ALU.mult,
                            vb.r + Vp.r, t3[d].r)
                    self.TT("dve", Sd[:], Sd[:], t3[d][:], ALU.add, Sd.r + t3[d].r, Sd.r)
                    self.TT("dve", t1[d][:], Sd[:], r_bc, ALU.mult, Sd.r + vb.r, t1[d].r)
                    ytd = yts[d]
                    self.RED("dve", ytd[:, 0:8], v3(t1[d][:]), ALU.add, t1[d].r, ytd.r)
                    if t not in ywritten:
                        ywritten.add(t)
                        self.CP("act", Y[:, :, t], ytd[:, 0:8], ytd.r, Y.r)
                    else:
                        self.TT("pool", Y[:, :, t], Y[:, :, t], ytd[:, 0:8], ALU.add, Y.r + ytd.r, Y.r)

    def rwkv_finish(self, b, Ya):
        rv = [self.bcast_row(self.rvec_d[i, :], 512, f"fv{i}") for i in (3, 4)]
        gng, gnb = rv
        y = [self.tile([128, 512], F32, f"fy{i}") for i in range(2)]
        sq = self.tile([128, 512], F32, "fsq")
        bon = [self.tile([128, 512], F32, f"fb{i}") for i in range(2)]
        gg = [self.tile([128, 512], F32, f"fg{i}") for i in range(2)]
        sm = [self.tile([128, 8], F32, f"fsm{i}") for i in range(2)]
        ot = [self.tile([128, 512], BF16, f"fot{i}") for i in range(2)]
        ob = [self.tile([128, 4, 128], BF16, f"fob{i}") for i in range(2)]
        v3 = lambda t: t[:].rearrange("p (h e) -> p h e", h=8)
        bc8 = lambda t: t[:, 0:8].unsqueeze(2).to_broadcast([128, 8, 64])
        for ch in range(NCH):
            tk = slice(ch * 128, (ch + 1) * 128)
            yy, bo, g_ = y[ch % 2], bon[ch % 2], gg[ch % 2]
            self.DMA("sp", bo[:], self.BG[b, 0, tk, :], W=bo.r)
            self.DMA("sp", g_[:], self.BG[b, 1, tk, :], W=g_.r)
            ysrc = Ya[:, ch, :].rearrange("p (h e) -> p h e", h=8)
            yr = [Ya.r[ch]]
            self.RED("dve", sm[0][:, 0:8], ysrc, ALU.add, yr, sm[0].r)
            self.TS("dve", sm[0][:], sm[0][:], 1.0 / 64, None, ALU.mult, None, sm[0].r, sm[0].r)
            self.TT("dve", v3(yy), ysrc, bc8(sm[0]), ALU.subtract, yr + sm[0].r, yy.r)
            self.ACT(sq[:], yy[:], AF.Square, yy.r, sq.r)
            self.RED("dve", sm[1][:, 0:8], v3(sq), ALU.add, sq.r, sm[1].r)
            self.ACT(sm[1][:], sm[1][:], AF.Sqrt, sm[1].r + self.cst.r, sm[1].r, bias=self.cst[:, 4:5], scale=1.0 / 64)
            self.RECIP(sm[1][:], sm[1][:], sm[1].r, sm[1].r)
            self.TT("dve", v3(yy), v3(yy), bc8(sm[1]), ALU.mult, yy.r + sm[1].r, yy.r)
            self.TT("pool", yy[:], yy[:], gng[:], ALU.mult, yy.r + gng.r, yy.r)
            self.TT("pool", yy[:], yy[:], gnb[:], ALU.add, yy.r + gnb.r, yy.r)
            self.TT("pool", yy[:], yy[:], bo[:], ALU.add, yy.r + bo.r, yy.r)
            o = ot[ch % 2]
            self.TT("dve", o[:], yy[:], g_[:], ALU.mult, yy.r + g_.r, o.r)
            pO = self.ps[2 + ch % 2]
            for c in range(4):
                self.MM(pO[:, c * 128:(c + 1) * 128], o[:, c * 128:(c + 1) * 128], self.ident_b[:], True, True,
                        o.r + self.ident_b.r, pO.r)
            oo = ob[ch % 2]
            self.CP("act", oo[:], pO[:, :].rearrange("p (c t) -> p c t", c=4), pO.r, oo.r)
            self.DMA("sp", self.O[b, 0:4, :, tk].rearrange("c p t -> p c t"), oo[:], R=oo.r)

    def mix_diff(self, b):
        P = self.P
        rope = self.load_rope()
        LAM_INIT = 0.2
        dl = self.tile([1, 4, 64], F32, "dl")
        self.DMA("sp", dl[:], self.diffl_d, W=dl.r)
        pr = self.tile([1, 2, 64], F32, "dlp")
        sm = self.tile([1, 2], F32, "dls")
        self.TT("dve", pr[:, 0, :], dl[:, 0, :], dl[:, 1, :], ALU.mult, dl.r, pr.r)
        self.TT("dve", pr[:, 1, :], dl[:, 2, :], dl[:, 3, :], ALU.mult, dl.r, pr.r)
        self.RED("dve", sm[:, 0:2], pr[:, :, :], ALU.add, pr.r, sm.r)
        self.ACT(sm[:, 0:2], sm[:, 0:2], AF.Exp, sm.r, sm.r)
        nl = self.tile([1, 1], F32, "nl")
        self.TT("dve", nl[:, 0:1], sm[:, 1:2], sm[:, 0:1], ALU.subtract, sm.r, nl.r)
        self.TS("dve", nl[:, 0:1], nl[:, 0:1], -LAM_INIT, None, ALU.add, None, nl.r, nl.r)
        nlam = self.tile([128, 1], F32, "nlam")
        pb = self.ps[7]
        self.MM(pb[:, 0:1], self.ones_f[0:1, :], nl[0:1, 0:1], True, True, self.ones_f.r + nl.r, pb.r)
        self.CP("dve", nlam[:], pb[:, 0:1], pb.r, nlam.r)
        sg = self.tile([128, 1], F32, "subg")
        self.DMA("sp", sg[:], self.subln_d, W=sg.r)
        self.TS("dve", sg[:], sg[:], 1.0 - LAM_INIT, None, ALU.mult, None, sg.r, sg.r)
        wv = self.tile([128, 8, 512], BF16, "wv")
        self.load_w(wv, self.ab_w[:, 2816:3328])
        vtok = self.tile([128, NCH, 512], BF16, "vtok")
        for ch in range(NCH):
            pst = self.ps[ch % 4]
            self.proj_tok(pst, wv, 0, 512, ch)
            self.CP("act" if ch % 2 else "dve", vtok[:, ch, :], pst[:, :], pst.r, vtok.r)
        wq = [self.tile([128, 8, 128], BF16, f"wq{i}") for i in range(4)]
        qb = self.tile([128, T], BF16, "qb")
        kb = self.tile([128, T], BF16, "kb")
        pt = [self.tile([128, 512], BF16, f"pt{i}") for i in range(4)]
        ft = [self.tile([128, 512], F32, f"ft{i}") for i in range(4)]
        ob = [self.tile([128, 512], BF16, f"ob{i}") for i in range(2)]
        npt = 0
        nout = 0
        for h in range(4):
            self.load_w(wq[0], self.ab_w[:, 1792 + h * 128:1792 + (h + 1) * 128])
            self.load_w(wq[1], self.ab_perm[:, h * 128:(h + 1) * 128])
            self.load_w(wq[2], self.ab_w[:, 2304 + h * 128:2304 + (h + 1) * 128])
            self.load_w(wq[3], self.ab_perm[:, 512 + h * 128:512 + (h + 1) * 128])
            self.proj_rope(qb, wq[0], wq[1], 0, 0, rope)
            self.proj_rope(kb, wq[2], wq[3], 0, 0, rope)
            for (t0, n) in self.TILES:
                kts = [0, 1] if t0 < TC else list(range(NCH))
                items = [(m, ki, kt) for m in range(2) for ki, kt in enumerate(kts)]
                LA = 2
                slots = {}
                for i in range(len(items) + LA):
                    if i < len(items):
                        m, ki, kt = items[i]
                        S = self.ps[npt % 4]
                        p = pt[npt % 4]
                        npt += 1
                        slots[i] = p
                        self.MM(S[:, 0:n], kb[m * 64:(m + 1) * 64, kt * 128:(kt + 1) * 128], qb[m * 64:(m + 1) * 64, t0:t0 + n],
                                True, True, kb.r + qb.r, S.r)
                        self.ACT(p[:, 0:n], S[:, 0:n], AF.Exp, S.r, p.r, scale=0.125)
                    if i - LA >= 0:
                        m, ki, kt = items[i - LA]
                        p = slots.pop(i - LA)
                        Oa, Da = self.ps[4 + m], self.ps[6 + m]
                        self.MM(Oa[:, 0:n], vtok[:, kt, h * 128:(h + 1) * 128], p[:, 0:n], ki == 0, ki == len(kts) - 1,
                                vtok.r + p.r, Oa.r)
                        self.MM(Da[:, 0:n], self.ones_b[:], p[:, 0:n], ki == 0, ki == len(kts) - 1,
                                self.ones_b.r + p.r, Da.r)
                for m in range(2):
                    self.RECIP(ft[2 + m][:, 0:n], self.ps[6 + m][:, 0:n], self.ps[6 + m].r, ft[2 + m].r)
                    self.TT("dve", ft[m][:, 0:n], self.ps[4 + m][:, 0:n], ft[2 + m][:, 0:n], ALU.mult,
                            self.ps[4 + m].r + ft[2 + m].r, ft[m].r)
                self.STT("dve", ft[0][:, 0:n], ft[1][:, 0:n], nlam[:, 0:1], ft[0][:, 0:n], ALU.mult, ALU.add,
                         ft[1].r + nlam.r + ft[0].r, ft[0].r)
                self.ACT(ft[1][:, 0:n], ft[0][:, 0:n], AF.Square, ft[0].r, ft[1].r)
                st = self.ps[6]
                self.MM(st[:, 0:n], self.ones_f[:], ft[1][:, 0:n], True, True, self.ones_f.r + ft[1].r, st.r)
                self.ACT(ft[2][:, 0:n], st[:, 0:n], AF.Sqrt, st.r + self.cst.r, ft[2].r, bias=self.cst[:, 2:3], scale=1.0 / 128)
                self.RECIP(ft[2][:, 0:n], ft[2][:, 0:n], ft[2].r, ft[2].r)
                self.TT("pool", ft[0][:, 0:n], ft[0][:, 0:n], ft[2][:, 0:n], ALU.mult, ft[0].r + ft[2].r, ft[0].r)
                o = ob[nout % 2]
                nout += 1
                self.ACT(o[:, 0:n], ft[0][:, 0:n], AF.Copy, ft[0].r + sg.r, o.r, scale=sg[:, 0:1])
                self.DMA("sp", self.O[b, 4 + h, :, t0:t0 + n], o[:, 0:n], R=o.r)

    def layer_norm(self, y, N, gcol, bcol, hout, extra=None):
        st = self.ps[7]
        st2 = self.ps[6]
        sq = self.ln_sq
        for c in range(8):
            s = sq[c % 2]
            self.ACT(s[:, 0:N], y[:, c, :], AF.Square, y.r, s.r)
            self.MM(st[:, 0:N], self.ones_f[:], y[:, c, :], c == 0, c == 7, self.ones_f.r + y.r, st.r)
            self.MM(st2[:, 0:N], self.ones_f[:], s[:, 0:N], c == 0, c == 7, self.ones_f.r + s.r, st2.r)
        mean, rstd = self.ln_mean, self.ln_rstd
        self.ACT(mean[:, 0:N], st[:, 0:N], AF.Copy, st.r, mean.r, scale=1.0 / D)
        self.TT("dve", rstd[:, 0:N], mean[:, 0:N], mean[:, 0:N], ALU.mult, mean.r, rstd.r)
        self.STT("dve", rstd[:, 0:N], st2[:, 0:N], 1.0 / D, rstd[:, 0:N], ALU.mult, ALU.subtract, st2.r + rstd.r, rstd.r)
        self.ACT(rstd[:, 0:N], rstd[:, 0:N], AF.Sqrt, rstd.r + self.cst.r, rstd.r, bias=self.cst[:, 0:1])
        self.RECIP(rstd[:, 0:N], rstd[:, 0:N], rstd.r, rstd.r)
        for c in range(8):
            tmp = self.ln_tmp[c % 2]
            self.TT("dve", tmp[:, 0:N], y[:, c, :], mean[:, 0:N], ALU.subtract, y.r + mean.r, tmp.r)
            self.TT("pool", tmp[:, 0:N], tmp[:, 0:N], rstd[:, 0:N], ALU.mult, tmp.r + rstd.r, tmp.r)
            self.ACT(hout[:, c, :], tmp[:, 0:N], AF.Identity, tmp.r + self.lnp_t.r, hout.r,
                     bias=bcol(c), scale=gcol(c))
            if extra is not None:
                et, sfn, bfn = extra
                self.ACT(et[:, c, :], hout[:, c, :], AF.Identity, hout.r + self.S1.r + self.S1F.r + self.MOD.r, et.r,
                         bias=bfn(c), scale=sfn(c))

    def phase_ffn(self, layer, w_out, w1, w2):
        P = self.P
        last = layer == 1
        N = 256
        wo = self.tile([128, 8, D], BF16, "wo")
        w1t = self.tile([128, 8, 4 * D], BF16, "w1", nres=8)
        w2t = self.tile([128, 32, D], BF16, "w2", nres=32)
        for c in range(8):
            self.DMA("pool", wo[:, c, :], w_out[c * 128:(c + 1) * 128, :], W=wo.r)
        for c in range(8):
            self.DMA("pool", w1t[:, c, :], w1[c * 128:(c + 1) * 128, :], W=[w1t.r[c]])
        for c in range(32):
            self.DMA("pool", w2t[:, c, :], w2[c * 128:(c + 1) * 128, :], W=[w2t.r[c]])
        self.ln_sq = [self.tile([128, N], F32, f"lnsq{i}") for i in range(2)]
        self.ln_tmp = [self.tile([128, N], F32, f"lntmp{i}") for i in range(2)]
        self.ln_mean = self.tile([128, N], F32, "lnmean")
        self.ln_rstd = self.tile([128, N], F32, "lnrstd")
        o_t = [self.tile([128, 8, N], BF16, f"o_t{i}") for i in range(1)]
        h_t = [self.tile([128, 8, N], F32, f"h_t{i}") for i in range(2)]
        hmods = [self.tile([128, 8, N], BF16, f"hmodf{i}") for i in range(2)]
        f_t = self.tile([128, 32, N], BF16, "f_t", nres=32)
        rl = [self.tile([128, N], F32, f"rl{i}") for i in range(2)]
        tok = [self.tile([128, D], F32, f"tok{i}") for i in range(1)] if last else None
        hm2 = self.tile([128, 8, N], BF16, "hm2") if not last else None
        tiles = []
        for b in range(NB):
            for t0 in range(0, T, N):
                if last and t0 < TC:
                    continue
                tiles.append((b, t0))
        lg = lambda k, c: self.lnp_t[:, layer, k, c:c + 1]
        st = {"npsum": 0, "ntok": 0}

        def nextps():
            p = self.ps[st["npsum"] % 6]
            st["npsum"] += 1
            return p

        def S1(n):
            b, t0 = tiles[n]
            w = self.who(b, t0)
            ot, ht, hmod = o_t[0], h_t[n % 2], hmods[n % 2]
            self.DMA("sp", ot[:], self.O[b, :, :, t0:t0 + N].rearrange("c p t -> p c t"), W=ot.r)
            self.DMA("sp", ht[:], self.H[b, :, :, t0:t0 + N].rearrange("c p t -> p c t"), W=ht.r)
            for oc in range(8):
                pst = nextps()
                for c in range(8):
                    self.MM(pst[:, 0:N], wo[:, c, oc * 128:(oc + 1) * 128], ot[:, c, :], c == 0, c == 7, wo.r + ot.r, pst.r)
                self.STT("dve", ht[:, oc, :], pst[:, 0:N], self.GA[:, layer, oc, w:w + 1], ht[:, oc, :], ALU.mult, ALU.add,
                         pst.r + self.GA.r + ht.r, ht.r)
            self.layer_norm(ht, N, lambda c: lg(0, c), lambda c: lg(1, c), ht,
                            extra=(hmod, lambda c: self.S1F[:, layer, c, w:w + 1], lambda c: self.SHF(layer, c, w)))

        def S23(n):
            b, t0 = tiles[n]
            w = self.who(b, t0)
            ht, hmod = h_t[n % 2], hmods[n % 2]
            for j in range(32):
                pst = nextps()
                for c in range(8):
                    self.MM(pst[:, 0:N], w1t[:, c, j * 128:(j + 1) * 128], hmod[:, c, :], c == 0, c == 7,
                            [w1t.r[c]] + hmod.r, pst.r)
                r = rl[j % 2]
                self.ACT(r[:, 0:N], pst[:, 0:N], AF.Relu, pst.r, r.r)
                self.TT("pool", f_t[:, j, :], r[:, 0:N], r[:, 0:N], ALU.mult, r.r, [f_t.r[j]])
            for oc in range(8):
                pst = nextps()
                for j in range(32):
                    self.MM(pst[:, 0:N], w2t[:, j, oc * 128:(oc + 1) * 128], f_t[:, j, :], j == 0, j == 31,
                            [w2t.r[j], f_t.r[j]], pst.r)
                self.STT("dve", ht[:, oc, :], pst[:, 0:N], self.GFA[:, layer, oc, w:w + 1], ht[:, oc, :], ALU.mult, ALU.add,
                         pst.r + self.GFA.r + ht.r, ht.r)
            if not last:
                self.layer_norm(ht, N, lambda c: lg(2, c), lambda c: lg(3, c), ht,
                                extra=(hm2, lambda c: self.S1[:, layer + 1, c, w:w + 1], lambda c: self.SH(layer + 1, c, w)))
                self.DMA("sp", self.H[b, :, :, t0:t0 + N].rearrange("c p t -> p c t"), ht[:], R=ht.r)
                self.DMA("sp", self.HM[b, :, :, t0:t0 + N].rearrange("c p t -> p c t"), hm2[:], R=hm2.r)
            else:
                self.layer_norm(ht, N, lambda c: lg(2, c), lambda c: lg(3, c), ht)
                for sub in range(N // 128):
                    tk = tok[0]
                    st["ntok"] += 1
                    for half in range(2):
                        pst = nextps()
                        for cc in range(4):
                            c = half * 4 + cc
                            self.TR(pst[:, cc * 128:(cc + 1) * 128], ht[:, c, sub * 128:(sub + 1) * 128], self.ident_f[:],
                                    ht.r + self.ident_f.r, pst.r)
                        self.CP("act", tk[:, half * 512:(half + 1) * 512], pst[:, :], pst.r, tk.r)
                    tl0 = t0 - TC + sub * 128
                    self.DMA("sp", self.out[b, tl0:tl0 + 128, :], tk[:], R=tk.r)

        S1(0)
        for n in range(len(tiles)):
            if n + 1 < len(tiles):
                S1(n + 1)
            S23(n)


def _per_core_inputs(inp, core):
    b0 = core * NB
    f = lambda a: np.ascontiguousarray(a, dtype=np.float32)
    cv = np.stack([inp["c"][b0], inp["c"][b0 + 1], inp["c_ctx"]], 0)
    m = {
        "x": f(inp["x"][b0:b0 + NB]),
        "ctx": f(inp["ctx"][b0:b0 + NB]),
        "cvec": f(cv.reshape(3, 8, 128).transpose(2, 1, 0)),
        "mod_w": f(inp["mod_w"]),
        "mod_b": f(inp["mod_b"].reshape(2, 48, 128).transpose(2, 0, 1)),
        "lnp": f(np.stack([inp["ln_mix_g"], inp["ln_mix_b"], inp["ln_ffn_g"], inp["ln_ffn_b"]], 1)
                 .reshape(2, 4, 8, 128).transpose(3, 0, 1, 2)),
        "ffn_w1": f(inp["ffn_w1"]),
        "ffn_w2": f(inp["ffn_w2"]),
        "w_out": f(inp["w_out"]),
        "c_ident": np.eye(128, dtype=np.float32),
        "ab_w": f(inp["ab_w_in"][0]),
        "ab_perm": f(inp["ab_w_in"][0][:, 1792:2816][:, _PERM1024]),
        "cd_w": f(inp["cd_w_in"][0]),
        "cd_perm": f(inp["cd_w_in"][0][:, 1552:2192][:, _PERM1024[:640]]),
        "c_rope": _ROPE,
        "c_swamask": _SWAMASK,
        "diff_l": f(np.stack([inp["diff_lq1"][0], inp["diff_lk1"][0], inp["diff_lq2"][0], inp["diff_lk2"][0]], 0)[None]),
        "subln": f(inp["diff_subln_g"][0].reshape(128, 1)),
        "sink": f(inp["swa_sink"]),
        "c_masks": _MASKS,
        "c_neg": _NEG,
        "convw": f(inp["ssd_conv_w"][0].reshape(5, 8, 128).transpose(2, 1, 0)),
        "convb": f(inp["ssd_conv_b"][0].reshape(8, 128).T),
        "dtb": f(inp["ssd_dt_bias"][0].reshape(1, 16)),
        "alog": f(inp["ssd_a_log"][0].reshape(1, 16)),
        "dskip": f(np.repeat(inp["ssd_d"][0], 64).reshape(4, 128).T),
        "ssdg": f(inp["ssd_norm_g"][0].reshape(4, 128).T),
        "rwkv_vec": f(np.stack([inp["rwkv_k_k"][0], inp["rwkv_k_a"][0], inp["rwkv_r_k"][0].reshape(512), inp["rwkv_gn_g"][0],
                                inp["rwkv_gn_b"][0], inp["rwkv_w0"][0, 0], inp["rwkv_w0"][0, 1], inp["rwkv_a0"][0, 0],
                                inp["rwkv_a0"][0, 1]], 0)),
        "rwkv_mu": f(inp["rwkv_mu"]),
        "rwkv_w2": f(inp["rwkv_w2"][0]),
        "rwkv_a2": f(inp["rwkv_a2"][0]),
        "rwkv_g2": f(inp["rwkv_g2"][0]),
    }
    return m


def _mk_consts():
    d = np.arange(64)
    partner = np.where((d % 32) < 16, d + 16, d - 16)
    perm = (np.arange(1024) // 64) * 64 + partner[np.arange(1024) % 64]
    t = np.arange(TL)
    rows, cols = t // 64, t % 64
    p = np.arange(128)
    dd = p % 64
    i = dd % 16
    inv = 10000.0 ** (-(i.astype(np.float64)) / 16.0)
    pos = np.where((dd // 32)[:, None] == 0, rows[None, :], cols[None, :]).astype(np.float64)
    ang = (pos.astype(np.float32) * inv.astype(np.float32)[:, None]).astype(np.float32)
    cos = np.cos(ang).astype(np.float32)
    sin = np.sin(ang).astype(np.float32)
    sgn = np.where((dd % 32) < 16, -1.0, 1.0).astype(np.float32)[:, None]
    rope = np.stack([cos, sin * sgn], 0).astype(np.float32)
    k = np.arange(128)[:, None]
    q = np.arange(512)[None, :]
    m = np.stack([(np.abs(q - k - 128 * (r - 1)) <= 128) for r in range(6)], 0).astype(np.float32)
    return perm, rope, m


_PERM1024, _ROPE, _SWAMASK = _mk_consts()
_i = np.arange(128)[:, None]
_t = np.arange(128)[None, :]
_MASKS = np.stack([_i <= _t, _i < _t, _i >= _t, _i > _t], 0).astype(np.float32)
_NEG = ((_MASKS[[0, 2]] - 1.0) * 1e30).astype(np.float32)


_CACHE = {}


def kernel(**inputs):
    inp = {k: np.asarray(v) for k, v in inputs.items()}
    if "nc" not in _CACHE:
        _CACHE["nc"] = Builder().build()
    nc = _CACHE["nc"]
    in_maps = [_per_core_inputs(inp, c) for c in range(8)]
    res = run_bass_kernel_spmd(nc, in_maps, core_ids=list(range(8)))
    return np.concatenate([r["out"] for r in res.results], axis=0).astype(np.float32)
```

```python
import contextlib
import math
import numpy as np
import concourse.bass as bass
import concourse.mybir as mybir
from concourse.bass_utils import run_bass_kernel_spmd

F32 = mybir.dt.float32
BF16 = mybir.dt.bfloat16
AF = mybir.ActivationFunctionType
ALU = mybir.AluOpType
AX = mybir.AxisListType

ENGS = ["pe", "act", "dve", "pool", "sp"]
N_DMA_SEMS = 24

D = 1024
NB = 2
TC = 256
TL = 2048
T = TC + TL
NCH = T // 128
ALPHA = 4.0 ** 0.25
LN_EPS = 1e-5
EPS_P = LN_EPS / (ALPHA * ALPHA)
CDEC = math.exp(-0.5)


class Res:
    __slots__ = ("w", "r", "excl")

    def __init__(self):
        self.w = None
        self.r = []
        self.excl = False


class Prog:
    def __init__(self, nc, self_wait=True):
        self.nc = nc
        self.ops = {e: [] for e in ENGS}
        self.count = {e: 0 for e in ENGS}
        self.seen = {e: {} for e in ENGS}
        self.dma_val = [0] * N_DMA_SEMS
        self.dma_rr = 0
        self.self_wait = self_wait
        self.sb_top = 16512
        self.SB_BYTES = 229376
        self.n_alloc = 0

    def sb(self, shape, dtype, name=None):
        esz = 2 if dtype == BF16 else 4
        free = int(np.prod(shape[1:])) * esz
        free = (free + 63) // 64 * 64
        off = self.sb_top
        self.sb_top += free
        assert self.sb_top <= self.SB_BYTES, f"SBUF overflow {self.sb_top} ({name})"
        self.n_alloc += 1
        return self.nc.alloc_sbuf_tensor_at(f"{name or 't'}_{self.n_alloc}", list(shape), dtype, offset=off)

    def mark(self):
        return self.sb_top

    def release(self, m):
        self.sb_top = m

    def _collect(self, eng, reads, writes):
        deps = []
        for r in reads:
            if r.w is not None:
                deps.append(r.w)
            if r.excl:
                deps.extend(r.r)
        for w in writes:
            if w.w is not None:
                deps.append(w.w)
            deps.extend(w.r)
        waits = {}
        for (key, val, src) in deps:
            if src == eng and (eng == "pe" or not self.self_wait):
                continue
            if self.seen[eng].get(key, 0) >= val:
                continue
            if waits.get(key, 0) < val:
                waits[key] = val
        for k, v in waits.items():
            self.seen[eng][k] = v
        return list(waits.items())

    def _commit(self, tok, reads, writes):
        for r in reads:
            r.r.append(tok)
        for w in writes:
            w.w = tok
            w.r = []

    def op(self, eng, fn, reads=(), writes=()):
        waits = self._collect(eng, reads, writes)
        self.count[eng] += 1
        tok = (eng, self.count[eng], eng)
        self.ops[eng].append((waits, fn, ("eng", eng)))
        self._commit(tok, reads, writes)

    def dma(self, q, out, in_, reads=(), writes=()):
        idx = self.dma_rr
        self.dma_rr = (self.dma_rr + 1) % N_DMA_SEMS
        key = ("dma", idx)
        waits = self._collect(q, reads, writes)
        prev = self.dma_val[idx]
        if prev > 0 and self.seen[q].get(key, 0) < prev:
            waits.append((key, prev))
            self.seen[q][key] = prev
        self.dma_val[idx] = prev + 16
        tok = (key, prev + 16, "dma")
        self.ops[q].append((waits, lambda e: e.dma_start(out=out, in_=in_), ("dma", idx)))
        self._commit(tok, reads, writes)

    def barrier(self):
        for e in ENGS:
            waits = []
            for o in ENGS:
                if o == e or o == "sp":
                    continue
                v = self.count[o]
                if v > 0 and self.seen[e].get(o, 0) < v:
                    waits.append((o, v))
                    self.seen[e][o] = v
            for i in range(N_DMA_SEMS):
                v = self.dma_val[i]
                key = ("dma", i)
                if v > 0 and self.seen[e].get(key, 0) < v:
                    waits.append((key, v))
                    self.seen[e][key] = v
            if waits:
                self.ops[e].append((waits, None, None))

    def emit(self):
        nc = self.nc
        with contextlib.ExitStack() as st:
            sems = {}
            for e in ENGS:
                sems[e] = st.enter_context(nc.semaphore(f"s_{e}"))
            for i in range(N_DMA_SEMS):
                sems[("dma", i)] = st.enter_context(nc.semaphore(f"s_dma{i}"))
            block = st.enter_context(nc.Block())

            def run(engname):
                def f(eng):
                    for waits, fn, kind in self.ops[engname]:
                        for k, v in waits:
                            eng.wait_ge(sems[k], v)
                        if fn is None:
                            continue
                        ins = fn(eng)
                        if kind[0] == "eng":
                            ins.then_inc(sems[kind[1]], 1)
                        else:
                            ins.then_inc(sems[("dma", kind[1])], 16)
                return f

            block.tensor(run("pe"))
            block.scalar(run("act"))
            block.vector(run("dve"))
            block.gpsimd(run("pool"))
            block.sync(run("sp"))


class Tl:
    def __init__(self, t, nres=1):
        self.t = t
        self.r = [Res() for _ in range(nres)]

    def __getitem__(self, idx):
        return self.t[idx]


class Builder:
    def __init__(self, debug=(), stop=None, skip=()):
        self.debug = set(debug)
        self.stop = stop
        self.skip = set(skip)
        self.stop_mix = None
        self.nc = bass.Bass("TRN2", target_bir_lowering=False)
        import os
        self.P = Prog(self.nc, self_wait=not os.environ.get('NO_SELF_WAIT'))
        self.dram = {}

    def din(self, name, shape, dtype=F32):
        self.dram[name] = self.nc.dram_tensor(name, list(shape), dtype, kind="ExternalInput").ap()
        return self.dram[name]

    def dscr(self, name, shape, dtype):
        kind = "ExternalOutput" if name in self.debug else "Internal"
        self.dram[name] = self.nc.dram_tensor(name, list(shape), dtype, kind=kind).ap()
        return self.dram[name]

    def tile(self, shape, dtype, name=None, nres=1):
        return Tl(self.P.sb(shape, dtype, name), nres)

    def MM(self, out, lhsT, rhs, start, stop, R, W):
        self.P.op("pe", lambda e: e.matmul(out, lhsT, rhs, start=start, stop=stop), R, W)

    def TR(self, out, in_, ident, R, W):
        self.P.op("pe", lambda e: e.transpose(out, in_, ident), R, W)

    def ACT(self, out, in_, func, R, W, bias=None, scale=1.0):
        if bias is None:
            self.P.op("act", lambda e: e.activation(out, in_, func, scale=scale), R, W)
        else:
            self.P.op("act", lambda e: e.activation(out, in_, func, bias=bias, scale=scale), R, W)

    def TT(self, eng, out, in0, in1, op, R, W):
        self.P.op(eng, lambda e: e.tensor_tensor(out, in0, in1, op), R, W)

    def TS(self, eng, out, in0, s1, s2, op0, op1, R, W):
        if s2 is None:
            self.P.op(eng, lambda e: e.tensor_scalar(out, in0, s1, None, op0), R, W)
        else:
            self.P.op(eng, lambda e: e.tensor_scalar(out, in0, s1, s2, op0, op1), R, W)

    def STT(self, eng, out, in0, scalar, in1, op0, op1, R, W):
        self.P.op(eng, lambda e: e.scalar_tensor_tensor(out, in0, scalar, in1, op0, op1), R, W)

    def CP(self, eng, out, in_, R, W):
        if eng == "act":
            self.P.op("act", lambda e: e.copy(out, in_), R, W)
        else:
            self.P.op(eng, lambda e: e.tensor_copy(out, in_), R, W)

    def RECIP(self, out, in_, R, W):
        self.P.op("dve", lambda e: e.reciprocal(out, in_), R, W)

    def RED(self, eng, out, in_, op, R, W):
        self.P.op(eng, lambda e: e.tensor_reduce(out, in_, AX.X, op), R, W)

    def MEMSET(self, eng, out, val, R, W):
        self.P.op(eng, lambda e: e.memset(out, val), R, W)

    def DMA(self, q, out, in_, R=(), W=()):
        self.P.dma(q, out, in_, R, W)

    def build(self):
        nc, P = self.nc, self.P
        x = self.din("x", [NB, TL, D])
        ctx = self.din("ctx", [NB, TC, D])
        cvec = self.din("cvec", [128, 8, 3])
        mod_w = self.din("mod_w", [2, D, 6 * D])
        mod_b = self.din("mod_b", [128, 2, 48])
        lnp = self.din("lnp", [128, 2, 4, 8])
        ffn_w1 = self.din("ffn_w1", [2, D, 4 * D])
        ffn_w2 = self.din("ffn_w2", [2, 4 * D, D])
        w_out = self.din("w_out", [2, D, D])
        cid = self.din("c_ident", [128, 128])
        self.ab_w = self.din("ab_w", [D, 3328])
        self.ab_perm = self.din("ab_perm", [D, 1024])
        self.cd_w = self.din("cd_w", [D, 2320])
        self.cd_perm = self.din("cd_perm", [D, 640])
        self.rope_d = self.din("c_rope", [2, 128, TL])
        self.swam_d = self.din("c_swamask", [6, 128, 512])
        self.diffl_d = self.din("diff_l", [1, 4, 64])
        self.subln_d = self.din("subln", [128, 1])
        self.sink_d = self.din("sink", [1, 8])
        self.masks_d = self.din("c_masks", [4, 128, 128])
        self.neg_d = self.din("c_neg", [2, 128, 128])
        self.convw_d = self.din("convw", [128, 8, 5])
        self.convb_d = self.din("convb", [128, 8])
        self.dtb_d = self.din("dtb", [1, 16])
        self.alog_d = self.din("alog", [1, 16])
        self.dskip_d = self.din("dskip", [128, 4])
        self.ssdg_d = self.din("ssdg", [128, 4])
        self.rvec_d = self.din("rwkv_vec", [9, 512])
        self.rmu_d = self.din("rwkv_mu", [1, 1792])
        self.rw2_d = self.din("rwkv_w2", [2, 64, 512])
        self.ra2_d = self.din("rwkv_a2", [2, 64, 512])
        self.rg2_d = self.din("rwkv_g2", [128, 512])
        out = self.nc.dram_tensor("out", [NB, TL, D], F32, kind="ExternalOutput").ap()
        self.out = out
        H = self.dscr("H", [NB, 8, 128, T], F32)
        HM = self.dscr("HM", [NB, 8, 128, T], BF16)
        O = self.dscr("O", [NB, 8, 128, T], BF16)
        self.H, self.HM, self.O = H, HM, O

        self.ident_f = self.tile([128, 128], F32, "identf")
        self.ident_b = self.tile([128, 128], BF16, "identb")
        self.ones_f = self.tile([128, 128], F32, "onesf")
        self.ones_b = self.tile([128, 128], BF16, "onesb")
        self.cst = self.tile([128, 8], F32, "cst")
        self.DMA("sp", self.ident_f[:], cid, W=self.ident_f.r)
        self.DMA("pool", self.ident_b[:], cid, W=self.ident_b.r)
        self.MEMSET("pool", self.ones_f[:], 1.0, [], self.ones_f.r)
        self.MEMSET("pool", self.ones_b[:], 1.0, [], self.ones_b.r)
        self.MEMSET("dve", self.cst[:, 0:1], EPS_P, [], self.cst.r)
        self.MEMSET("dve", self.cst[:, 1:2], 0.0, [], self.cst.r)
        self.MEMSET("dve", self.cst[:, 2:3], LN_EPS, [], self.cst.r)
        self.MEMSET("dve", self.cst[:, 3:4], 1.0, [], self.cst.r)
        self.MEMSET("dve", self.cst[:, 4:5], 64e-5, [], self.cst.r)
        self.masks = self.tile([128, 4, 128], F32, "masks")
        self.DMA("sp", self.masks[:], self.masks_d.rearrange("m p t -> p m t"), W=self.masks.r)
        self.negm = self.tile([128, 2, 128], F32, "negm")
        self.DMA("sp", self.negm[:], self.neg_d.rearrange("m p t -> p m t"), W=self.negm.r)
        self.lnp_t = self.tile([128, 2, 4, 8], F32, "lnp")
        self.DMA("sp", self.lnp_t[:], lnp, W=self.lnp_t.r)
        self.MOD = self.tile([128, 2, 48, 3], F32, "MOD")
        self.S1 = self.tile([128, 2, 8, 3], F32, "S1")
        self.S1F = self.tile([128, 2, 8, 3], F32, "S1F")
        self.GA = self.tile([128, 2, 8, 3], F32, "GA")
        self.GFA = self.tile([128, 2, 8, 3], F32, "GFA")
        self.ps = [Tl(nc.alloc_psum_tensor(f"ps{i}", [128, 512], F32)) for i in range(8)]
        for p_ in self.ps:
            p_.r[0].excl = True
        keep = P.mark()

        self.phase_mod(cvec, mod_w, mod_b)
        P.barrier()
        P.release(keep)
        self.phase_init(x, ctx)
        P.barrier()
        P.release(keep)
        if "MODD" in self.debug:
            md = self.dscr("MODD", [128, 2 * 48 * 3], F32)
            self.DMA("sp", md, self.MOD[:].rearrange("p l j w -> p (l j w)"), R=self.MOD.r)
        for layer in range(2 if self.stop is None else self.stop):
            self.layer = layer
            self.last = layer == 1
            if layer == 0 and "rwkv" not in self.skip:
                self.phase_rwkv()
                P.barrier()
                P.release(keep)
            for b in range(NB):
                self.phase_mixers(layer, b)
                P.barrier()
                P.release(keep)
            if getattr(self, "stop_mix", None) == layer:
                break
            self.phase_ffn(layer, w_out[layer], ffn_w1[layer], ffn_w2[layer])
            P.barrier()
            P.release(keep)
        P.barrier()
        P.emit()
        return nc

    def who(self, b, t0):
        return 2 if t0 < TC else b

    def SH(self, layer, c, w):
        return self.MOD[:, layer, 0 + c, w:w + 1]

    def SHF(self, layer, c, w):
        return self.MOD[:, layer, 24 + c, w:w + 1]

    def phase_mod(self, cvec, mod_w, mod_b):
        P = self.P
        sc = self.tile([128, 8, 3], F32, "silu_c")
        self.DMA("sp", sc[:], cvec, W=sc.r)
        self.ACT(sc[:], sc[:], AF.Silu, sc.r, sc.r)
        mb = self.tile([128, 2, 48], F32, "modb")
        self.DMA("sp", mb[:], mod_b, W=mb.r)
        GW = 768
        wbuf = [self.tile([128, 8, GW], F32, f"modw{i}") for i in range(2)]
        pst = self.ps[0]
        n = 0
        for layer in range(2):
            for g in range(6 * D // GW):
                wb = wbuf[n % 2]
                n += 1
                for c in range(8):
                    self.DMA("sp", wb[:, c, :], mod_w[layer, c * 128:(c + 1) * 128, g * GW:(g + 1) * GW], W=wb.r)
                for jj in range(GW // 128):
                    j = g * (GW // 128) + jj
                    for c in range(8):
                        self.MM(pst[:, j * 3:j * 3 + 3], wb[:, c, jj * 128:(jj + 1) * 128], sc[:, c, :],
                                c == 0, c == 7, wb.r + sc.r, pst.r)
            pv = pst[:, 0:144].rearrange("p (j w) -> p j w", w=3)
            self.TT("dve", self.MOD[:, layer, :, :], pv, mb[:, layer, :].unsqueeze(2).to_broadcast([128, 48, 3]),
                    ALU.add, pst.r + mb.r, self.MOD.r)
        for layer in range(2):
            self.TS("dve", self.S1[:, layer], self.MOD[:, layer, 8:16, :], 1.0, None, ALU.add, None, self.MOD.r, self.S1.r)
            self.TS("dve", self.S1F[:, layer], self.MOD[:, layer, 32:40, :], 1.0, None, ALU.add, None, self.MOD.r, self.S1F.r)
            self.TS("dve", self.GA[:, layer], self.MOD[:, layer, 16:24, :], 1.0 / ALPHA, None, ALU.mult, None, self.MOD.r, self.GA.r)
            self.TS("dve", self.GFA[:, layer], self.MOD[:, layer, 40:48, :], 1.0 / ALPHA, None, ALU.mult, None, self.MOD.r, self.GFA.r)

    def phase_init(self, x, ctx):
        xin = [self.tile([128, D], F32, f"xin{i}") for i in range(2)]
        hf = [self.tile([128, 8, 128], F32, f"hf{i}") for i in range(2)]
        hm = [self.tile([128, 8, 128], BF16, f"hm{i}") for i in range(2)]
        n = 0
        for b in range(NB):
            for ch in range(NCH):
                t0 = ch * 128
                xi, hfi, hmi = xin[n % 2], hf[n % 2], hm[n % 2]
                src = ctx[b, t0:t0 + 128, :] if t0 < TC else x[b, t0 - TC:t0 - TC + 128, :]
                self.DMA("sp", xi[:], src, W=xi.r)
                w = self.who(b, t0)
                for half in range(2):
                    pst = self.ps[(n * 2 + half) % 8]
                    for cc in range(4):
                        c = half * 4 + cc
                        self.TR(pst[:, cc * 128:(cc + 1) * 128], xi[:, c * 128:(c + 1) * 128], self.ident_f[:],
                                xi.r + self.ident_f.r, pst.r)
                    pv = pst[:, :].rearrange("p (c t) -> p c t", c=4)
                    self.CP("dve", hfi[:, half * 4:half * 4 + 4, :], pv, pst.r, hfi.r)
                for c in range(8):
                    self.ACT(hmi[:, c, :], hfi[:, c, :], AF.Identity, hfi.r + self.S1.r + self.MOD.r, hmi.r,
                             bias=self.SH(0, c, w), scale=self.S1[:, 0, c, w:w + 1])
                self.DMA("sp", self.H[b, :, :, t0:t0 + 128].rearrange("c p t -> p c t"), hfi[:], R=hfi.r)
                self.DMA("sp", self.HM[b, :, :, t0:t0 + 128].rearrange("c p t -> p c t"), hmi[:], R=hmi.r)
                n += 1

    def zero_o(self, b, c0, c1):
        z = self.tile([128, 512], BF16, "zero")
        self.MEMSET("pool", z[:], 0.0, [], z.r)
        for c in range(c0, c1):
            for t0 in range(0, T, 512):
                n = min(512, T - t0)
                self.DMA("sp", self.O[b, c, :, t0:t0 + n], z[:, 0:n], R=z.r)

    def phase_mixers(self, layer, b):
        P = self.P
        skip = getattr(self, "skip", set())
        self.hmod = self.tile([128, 8, T], BF16, "hmod_b")
        for c in range(8):
            self.DMA("sp", self.hmod[:, c, :], self.HM[b, c, :, :], W=self.hmod.r)
        keep = P.mark()
        if layer == 0:
            if "rwkv" in skip:
                self.zero_o(b, 0, 4)
            P.barrier(); P.release(keep)
            if "diff" in skip:
                self.zero_o(b, 4, 8)
            else:
                self.mix_diff(b)
        else:
            if "ssd" in skip:
                self.zero_o(b, 0, 4)
            else:
                self.mix_ssd(b)
            P.barrier(); P.release(keep)
            if "swa" in skip:
                self.zero_o(b, 4, 8)
            else:
                self.mix_swa(b)

    TILES = [(0, 256), (256, 512), (768, 512), (1280, 512), (1792, 512)]

    def load_w(self, wt, src_cols):
        n = src_cols.shape[1]
        self.DMA("pool", wt[:, :, 0:n], src_cols.rearrange("(c p) n -> p c n", p=128), W=wt.r)

    def proj_fm(self, pst, wt, col0, t0, n):
        for c in range(8):
            self.MM(pst[:, 0:n], wt[:, c, col0:col0 + 128], self.hmod[:, c, t0:t0 + n], c == 0, c == 7,
                    wt.r + self.hmod.r, pst.r)

    def proj_tok(self, pst, wt, col0, ncols, ch):
        for c in range(8):
            self.MM(pst[:, 0:ncols], self.hmod[:, c, ch * 128:(ch + 1) * 128], wt[:, c, col0:col0 + ncols], c == 0, c == 7,
                    wt.r + self.hmod.r, pst.r)

    def proj_rope(self, dst, wt, wpt, col0, pcol0, rope):
        cos, sin = rope
        for i, (t0, n) in enumerate(self.TILES):
            p1 = self.ps[(2 * i) % 4]
            self.proj_fm(p1, wt, col0, t0, n)
            if t0 < TC:
                self.CP("act", dst[:, t0:t0 + n], p1[:, 0:n], p1.r, dst.r)
                continue
            p2 = self.ps[(2 * i + 1) % 4]
            self.proj_fm(p2, wpt, pcol0, t0, n)
            l0 = t0 - TC
            ta, tb = self.rtmp
            self.TT("dve", ta[:, 0:n], p1[:, 0:n], cos[:, l0:l0 + n], ALU.mult, p1.r + cos.r, ta.r)
            self.TT("dve", tb[:, 0:n], p2[:, 0:n], sin[:, l0:l0 + n], ALU.mult, p2.r + sin.r, tb.r)
            self.TT("pool", dst[:, t0:t0 + n], ta[:, 0:n], tb[:, 0:n], ALU.add, ta.r + tb.r, dst.r)

    def load_rope(self):
        cos = self.tile([128, TL], F32, "cos")
        sin = self.tile([128, TL], F32, "sin")
        self.DMA("sp", cos[:], self.rope_d[0], W=cos.r)
        self.DMA("sp", sin[:], self.rope_d[1], W=sin.r)
        self.rtmp = (self.tile([128, 512], F32, "rta"), self.tile([128, 512], F32, "rtb"))
        return cos, sin


    def mix_swa(self, b):
        rope = self.load_rope()
        C0 = 1552
        sk = self.tile([1, 8], F32, "sk")
        self.DMA("sp", sk[:], self.sink_d, W=sk.r)
        self.ACT(sk[:], sk[:], AF.Exp, sk.r, sk.r)
        pb = self.ps[7]
        self.MM(pb[:, 0:8], self.ones_f[0:1, :], sk[0:1, 0:8], True, True, self.ones_f.r + sk.r, pb.r)
        esink = self.tile([128, 8], F32, "esink")
        self.CP("dve", esink[:], pb[:, 0:8], pb.r, esink.r)
        mk = self.tile([128, 6, 512], BF16, "swamask")
        self.DMA("pool", mk[:], self.swam_d.rearrange("r p q -> p r q"), W=mk.r)
        wk = [self.tile([128, 8, 128], BF16, f"wk{i}") for i in range(2)]
        kbs = [self.tile([128, T], BF16, f"kb{g}") for g in range(2)]
        for g in range(2):
            for half in range(2):
                self.DMA("pool", wk[0][:, :, half * 64:(half + 1) * 64],
                         self.cd_w[:, C0 + 512 + g * 64:C0 + 512 + (g + 1) * 64].rearrange("(c p) n -> p c n", p=128), W=wk[0].r)
                self.DMA("pool", wk[1][:, :, half * 64:(half + 1) * 64],
                         self.cd_perm[:, 512 + g * 64:512 + (g + 1) * 64].rearrange("(c p) n -> p c n", p=128), W=wk[1].r)
            self.proj_rope(kbs[g], wk[0], wk[1], 0, 0, rope)
        wv = self.tile([128, 8, 128], BF16, "wv")
        self.load_w(wv, self.cd_w[:, C0 + 640:C0 + 768])
        vtok = self.tile([128, NCH, 128], BF16, "vtok")
        for ch in range(NCH):
            pst = self.ps[ch % 4]
            self.proj_tok(pst, wv, 0, 128, ch)
            self.CP("act" if ch % 2 else "dve", vtok[:, ch, :], pst[:, 0:128], pst.r, vtok.r)
        wq = [self.tile([128, 8, 128], BF16, f"wq{i}") for i in range(2)]
        qb = self.tile([128, T], BF16, "qb")
        pt = [self.tile([128, 512], BF16, f"pt{i}") for i in range(4)]
        ft = self.tile([128, 512], F32, "ft")
        ob = [self.tile([128, 512], BF16, f"ob{i}") for i in range(2)]
        npt = 0
        cnt = 0
        for cq in range(4):
            self.load_w(wq[0], self.cd_w[:, C0 + cq * 128:C0 + (cq + 1) * 128])
            self.load_w(wq[1], self.cd_perm[:, cq * 128:(cq + 1) * 128])
            self.proj_rope(qb, wq[0], wq[1], 0, 0, rope)
            for hh in range(2):
                hq = cq * 2 + hh
                kv = hq // 4
                qs = hh * 64
                ks = qs
                kb = kbs[kv]
                for qt in range(4):
                    t0 = TC + qt * 512
                    kts = [(0, None), (1, None)] + [(2 + kk, kk - 4 * qt + 1)
                                                    for kk in range(max(0, 4 * qt - 1), min(15, 4 * qt + 4) + 1)]
                    Oa, Da = self.ps[4 + cnt % 2], self.ps[6 + cnt % 2]
                    cnt += 1
                    LA = 2
                    slots = {}
                    for i in range(len(kts) + LA):
                        if i < len(kts):
                            ch, r = kts[i]
                            S = self.ps[npt % 4]
                            p = pt[npt % 4]
                            npt += 1
                            slots[i] = p
                            self.MM(S[:, :], kb[ks:ks + 64, ch * 128:(ch + 1) * 128], qb[qs:qs + 64, t0:t0 + 512],
                                    True, True, kb.r + qb.r, S.r)
                            self.ACT(p[:, :], S[:, :], AF.Exp, S.r, p.r, scale=0.125)
                            if r is not None:
                                self.TT("pool", p[:, :], p[:, :], mk[:, r, :], ALU.mult, p.r + mk.r, p.r)
                        if i - LA >= 0:
                            ki = i - LA
                            ch, r = kts[ki]
                            p = slots.pop(ki)
                            first, lastk = ki == 0, ki == len(kts) - 1
                            self.MM(Oa[0:64, :], vtok[:, ch, kv * 64:(kv + 1) * 64], p[:, :], first, lastk, vtok.r + p.r, Oa.r)
                            self.MM(Da[0:64, :], self.ones_b[:, 0:64], p[:, :], first, lastk, self.ones_b.r + p.r, Da.r)
                    self.TS("dve", ft[0:64, :], Da[0:64, :], esink[0:64, hq:hq + 1], None, ALU.add, None, Da.r + esink.r, ft.r)
                    self.RECIP(ft[0:64, :], ft[0:64, :], ft.r, ft.r)
                    o = ob[cnt % 2]
                    self.TT("dve", o[0:64, :], Oa[0:64, :], ft[0:64, :], ALU.mult, Oa.r + ft.r, o.r)
                    self.DMA("sp", self.O[b, 4 + cq, qs:qs + 64, t0:t0 + 512], o[0:64, :], R=o.r)

    def mix_ssd(self, b):
        P = self.P
        hmod = self.hmod
        cw = self.tile([128, 8, 5], F32, "cw")
        cb = self.tile([128, 8], F32, "cb")
        dsk = self.tile([128, 4], F32, "dsk")
        ng = self.tile([128, 4], F32, "ng")
        self.DMA("sp", cw[:], self.convw_d, W=cw.r)
        self.DMA("sp", cb[:], self.convb_d, W=cb.r)
        self.DMA("sp", dsk[:], self.dskip_d, W=dsk.r)
        self.DMA("sp", ng[:], self.ssdg_d, W=ng.r)
        dtb = self.tile([1, 16], F32, "dtb")
        al = self.tile([1, 16], F32, "al")
        self.DMA("sp", dtb[:], self.dtb_d, W=dtb.r)
        self.DMA("sp", al[:], self.alog_d, W=al.r)
        self.ACT(al[:], al[:], AF.Exp, al.r, al.r)
        pb = self.ps[7]
        self.MM(pb[:, 0:16], self.ones_f[0:1, :], al[0:1, 0:16], True, True, self.ones_f.r + al.r, pb.r)
        aneg = self.tile([128, 16], F32, "aneg")
        self.TS("dve", aneg[:], pb[:, 0:16], -1.0, None, ALU.mult, None, pb.r, aneg.r)
        zs = self.tile([128, 4, T], BF16, "zs")
        xact = self.tile([128, 8, T], BF16, "xact")
        xs_tok = self.tile([128, NCH, 512], BF16, "xs_tok")
        B_tok = self.tile([128, NCH, 256], BF16, "B_tok")
        Yacc = self.tile([128, NCH, 512], F32, "Yacc", nres=NCH)
        dt_all = self.tile([128, NCH, 16], F32, "dt_all")
        a_all = self.tile([128, NCH, 16], F32, "a_all")
        keep2 = P.mark()
        wt = [self.tile([128, 8, 128], BF16, f"wssd{i}") for i in range(2)]
        pre = self.tile([128, T], F32, "pre")
        acc = self.tile([128, T], F32, "acc")
        for c in range(4):
            w = wt[c % 2]
            self.load_w(w, self.cd_w[:, c * 128:(c + 1) * 128])
            for i, (t0, n) in enumerate(self.TILES):
                pst = self.ps[i % 4]
                self.proj_fm(pst, w, 0, t0, n)
                self.ACT(zs[:, c, t0:t0 + n], pst[:, 0:n], AF.Silu, pst.r, zs.r)
        for c in range(8):
            w = wt[c % 2]
            self.load_w(w, self.cd_w[:, 512 + c * 128:512 + (c + 1) * 128])
            for i, (t0, n) in enumerate(self.TILES):
                pst = self.ps[i % 4]
                self.proj_fm(pst, w, 0, t0, n)
                self.CP("act" if i % 2 else "dve", pre[:, t0:t0 + n], pst[:, 0:n], pst.r, pre.r)
            self.ACT(acc[:, :], pre[:, :], AF.Identity, pre.r + cw.r + cb.r, acc.r, bias=cb[:, c:c + 1], scale=cw[:, c, 2:3])
            for j in (0, 1, 3, 4):
                sft = j - 2
                for (lo, hi) in ((0, TC), (TC, T)):
                    a0, a1 = max(lo, lo - sft), min(hi, hi - sft)
                    self.STT("dve", acc[:, a0:a1], pre[:, a0 + sft:a1 + sft], cw[:, c, j:j + 1], acc[:, a0:a1],
                             ALU.mult, ALU.add, pre.r + cw.r + acc.r, acc.r)
            self.ACT(xact[:, c, :], acc[:, :], AF.Silu, acc.r, xact.r)
        wdt = self.tile([128, 8, 16], BF16, "wdt")
        self.load_w(wdt, self.cd_w[:, 1536:1552])
        for ch in range(NCH):
            pst = self.ps[ch % 4]
            self.MM(pst[:, 0:16], self.ones_f[0:1, :], dtb[0:1, 0:16], True, False, self.ones_f.r + dtb.r, pst.r)
            for c in range(8):
                self.MM(pst[:, 0:16], hmod[:, c, ch * 128:(ch + 1) * 128], wdt[:, c, :], False, c == 7, wdt.r + hmod.r, pst.r)
            self.ACT(dt_all[:, ch, :], pst[:, 0:16], AF.Exp, pst.r, dt_all.r)
        self.ACT(dt_all[:, :, :], dt_all[:, :, :], AF.Ln, dt_all.r + self.cst.r, dt_all.r, bias=self.cst[:, 3:4])
        self.TT("dve", a_all[:, :, :], dt_all[:, :, :], aneg[:, :].unsqueeze(1).to_broadcast([128, NCH, 16]), ALU.mult,
                dt_all.r + aneg.r, a_all.r)
        for ch in range(NCH):
            pst = self.ps[ch % 4]
            for c in range(4):
                self.MM(pst[:, c * 128:(c + 1) * 128], xact[:, c, ch * 128:(ch + 1) * 128], self.ident_b[:], True, True,
                        xact.r + self.ident_b.r, pst.r)
            self.CP("act", xs_tok[:, ch, :], pst[:, :], pst.r, xs_tok.r)
            pst2 = self.ps[4 + ch % 2]
            for g in range(2):
                self.MM(pst2[:, g * 128:(g + 1) * 128], xact[:, 4 + g, ch * 128:(ch + 1) * 128], self.ident_b[:], True, True,
                        xact.r + self.ident_b.r, pst2.r)
            self.CP("dve", B_tok[:, ch, :], pst2[:, 0:256], pst2.r, B_tok.r)
        P.barrier()
        P.release(keep2)
        keep3 = P.mark()
        hT = [self.tile([128, 2, 256], F32, f"hT{d}") for d in range(2)]
        hTb = [self.tile([128, 2, 256], BF16, f"hTb{d}") for d in range(2)]
        for d in range(2):
            self.MEMSET("pool", hT[d][:], 0.0, [], hT[d].r)
            self.MEMSET("pool", hTb[d][:], 0.0, [], hTb[d].r)
        ex = [self.tile([128, 24], F32, f"ex{d}") for d in range(2)]
        nacs = [self.tile([128, 8], F32, f"nacs{d}") for d in range(2)]
        xdt = [self.tile([128, 8, 64], BF16, f"xdt{d}") for d in range(2)]
        Xd = [self.tile([128, 8, 64], BF16, f"Xd{d}") for d in range(2)]
        Abc = [self.tile([128, 8, 128], F32, f"Abc{d}") for d in range(2)]
        Gs = [self.tile([128, 2, 128], BF16, f"Gs{d}") for d in range(2)]
        Lm = [self.tile([128, 128], BF16, f"Lm{i}") for i in range(4)]
        Wh = [self.tile([128, 128], BF16, f"Wh{i}") for i in range(4)]
        zt = [self.tile([128, 512], F32, f"zt{d}") for d in range(2)]
        order = [list(range(NCH)), [1, 0] + list(range(NCH - 1, 1, -1))]
        written = set()
        nl = 0
        for step in range(NCH):
            for d in range(2):
                ch = order[d][step]
                MI, MSo = self.masks[:, 2 * d, :], self.masks[:, 2 * (1 - d) + 1, :]
                NEG = self.negm[:, d, :]
                tk = slice(ch * 128, (ch + 1) * 128)
                a = a_all[:, ch, d * 8:(d + 1) * 8]
                pA = self.ps[d]
                self.MM(pA[:, 0:8], MI, a, True, True, self.masks.r + a_all.r, pA.r)
                self.MM(pA[:, 8:16], self.ones_f[:], a, True, True, self.ones_f.r + a_all.r, pA.r)
                self.MM(pA[:, 16:24], MSo, a, True, True, self.masks.r + a_all.r, pA.r)
                self.ACT(ex[d][:], pA[:, 0:24], AF.Exp, pA.r, ex[d].r)
                self.ACT(nacs[d][:], pA[:, 0:8], AF.Copy, pA.r, nacs[d].r, scale=-1.0)
                xsv = xs_tok[:, ch, :].rearrange("p (h e) -> p h e", h=8)
                self.TT("dve", xdt[d][:], xsv, dt_all[:, ch, d * 8:(d + 1) * 8].unsqueeze(2).to_broadcast([128, 8, 64]), ALU.mult,
                        xs_tok.r + dt_all.r, xdt[d].r)
                self.TT("pool", Xd[d][:], xdt[d][:], ex[d][:, 16:24].unsqueeze(2).to_broadcast([128, 8, 64]), ALU.mult,
                        xdt[d].r + ex[d].r, Xd[d].r)
                if ch >= 2:
                    self.CP("pool", Abc[d][:], a.unsqueeze(2).to_broadcast([128, 8, 128]), a_all.r, Abc[d].r)
                    pG = self.ps[2 + d]
                    for g in range(2):
                        self.MM(pG[:, g * 128:(g + 1) * 128], xact[:, 4 + g, tk], xact[:, 6 + g, tk], True, True, xact.r, pG.r)
                    self.CP("act", Gs[d][:], pG[:, 0:256].rearrange("p (g t) -> p g t", g=2), pG.r, Gs[d].r)
                    pY = self.ps[4 + d]
                    LA = 2
                    whs = {}
                    for i in range(8 + LA):
                        if i < 8:
                            h = i
                            g = h // 4
                            pR = self.ps[6 + (nl % 2)]
                            lm, wh = Lm[nl % 4], Wh[nl % 4]
                            nl += 1
                            whs[h] = wh
                            self.MM(pR[:, 0:128], Abc[d][:, h, :], MI, True, False, Abc[d].r + self.masks.r, pR.r)
                            self.MM(pR[:, 0:128], self.ident_f[:], NEG, False, True, self.ident_f.r + self.negm.r, pR.r)
                            self.ACT(lm[:], pR[:, 0:128], AF.Exp, pR.r + nacs[d].r, lm.r, bias=nacs[d][:, h:h + 1])
                            self.TT("pool", wh[:], lm[:], Gs[d][:, g, :], ALU.mult, lm.r + Gs[d].r, wh.r)
                        if i - LA >= 0:
                            h = i - LA
                            wh = whs.pop(h)
                            self.MM(pY[:, h * 64:(h + 1) * 64], wh[:], xdt[d][:, h, :], True, True, wh.r + xdt[d].r, pY.r)
                    pZ = self.ps[2 + d]
                    for g in range(2):
                        self.MM(pZ[:, g * 256:(g + 1) * 256], xact[:, 6 + g, tk], hTb[d][:, g, :], True, True,
                                xact.r + hTb[d].r, pZ.r)
                    z = zt[d]
                    self.TT("dve", z[:].rearrange("p (h e) -> p h e", h=8), pZ[:, :].rearrange("p (h e) -> p h e", h=8),
                            ex[d][:, 0:8].unsqueeze(2).to_broadcast([128, 8, 64]), ALU.mult, pZ.r + ex[d].r, z.r)
                    self.TT("dve", z[:], pY[:, :], z[:], ALU.add, pY.r + z.r, z.r)
                    if ch in written:
                        self.TT("pool", Yacc[:, ch, :], Yacc[:, ch, :], z[:], ALU.add, [Yacc.r[ch]] + z.r, [Yacc.r[ch]])
                    else:
                        self.CP("pool", Yacc[:, ch, :], z[:], z.r, [Yacc.r[ch]])
                        written.add(ch)
                pH = self.ps[d]
                for g in range(2):
                    self.MM(pH[:, g * 256:(g + 1) * 256], B_tok[:, ch, g * 128:(g + 1) * 128],
                            Xd[d][:, 4 * g:4 * g + 4, :].rearrange("p h e -> p (h e)"), True, True, B_tok.r + Xd[d].r, pH.r)
                hv = hT[d][:].rearrange("p g (h e) -> p (g h) e", h=4)
                self.TT("dve", hv, hv, ex[d][:, 8:16].unsqueeze(2).to_broadcast([128, 8, 64]), ALU.mult, hT[d].r + ex[d].r, hT[d].r)
                hf = hT[d][:].rearrange("p g x -> p (g x)")
                self.TT("dve", hf, hf, pH[:, :], ALU.add, hT[d].r + pH.r, hT[d].r)
                self.CP("act", hTb[d][:].rearrange("p g x -> p (g x)"), hf, hT[d].r, hTb[d].r)
        P.barrier()
        P.release(keep3)
        yg = [self.tile([128, 512], F32, f"yg{i}") for i in range(4)]
        sq = [self.tile([128, 512], F32, f"sq{i}") for i in range(2)]
        rs = self.tile([128, 512], F32, "rs")
        ob = [self.tile([128, 512], BF16, f"ob{i}") for i in range(2)]
        no = 0
        for qt in range(4):
            t0 = TC + qt * 512
            for c in range(4):
                pT = self.ps[c]
                for k4 in range(4):
                    ch = 2 + qt * 4 + k4
                    self.TR(pT[:, k4 * 128:(k4 + 1) * 128], Yacc[:, ch, c * 128:(c + 1) * 128], self.ident_f[:],
                            [Yacc.r[ch]] + self.ident_f.r, pT.r)
                self.STT("dve", yg[c][:], xact[:, c, t0:t0 + 512], dsk[:, c:c + 1], pT[:, :], ALU.mult, ALU.add,
                         xact.r + dsk.r + pT.r, yg[c].r)
                self.TT("pool", yg[c][:], yg[c][:], zs[:, c, t0:t0 + 512], ALU.mult, yg[c].r + zs.r, yg[c].r)
            for g in range(2):
                st = self.ps[4 + g]
                for k2 in range(2):
                    c = 2 * g + k2
                    self.ACT(sq[k2][:], yg[c][:], AF.Square, yg[c].r, sq[k2].r)
                    self.MM(st[:, :], self.ones_f[:], sq[k2][:], k2 == 0, k2 == 1, self.ones_f.r + sq[k2].r, st.r)
                self.ACT(rs[:], st[:, :], AF.Sqrt, st.r + self.cst.r, rs.r, bias=self.cst[:, 2:3], scale=1.0 / 256)
                self.RECIP(rs[:], rs[:], rs.r, rs.r)
                for k2 in range(2):
                    c = 2 * g + k2
                    self.TT("dve", yg[c][:], yg[c][:], rs[:], ALU.mult, yg[c].r + rs.r, yg[c].r)
                    o = ob[no % 2]
                    no += 1
                    self.ACT(o[:], yg[c][:], AF.Copy, yg[c].r + ng.r, o.r, scale=ng[:, c:c + 1])
                    self.DMA("sp", self.O[b, c, :, t0:t0 + 512], o[:], R=o.r)


    def bcast_row(self, src_row, n, name):
        t = self.tile([128, n], F32, name)
        self.DMA("sp", t[:], src_row.partition_broadcast(128), W=t.r)
        return t

    def phase_rwkv(self):
        P = self.P
        base = P.mark()
        self.PR = self.dscr("PR", [NB, 3, T, 512], F32)
        self.WD = self.dscr("WD", [NB, 2, T, 512], F32)
        self.PB = self.dscr("PB", [NB, 2, T, 5, 512], BF16)
        self.BG = self.dscr("BG", [NB, 2, T, 512], F32)
        keep = P.mark()
        for b in range(NB):
            self.rwkv_prep(b, None)
            P.barrier()
            P.release(keep)
        Yacc = [self.tile([128, NCH, 512], F32, f"Yacc{b}", nres=NCH) for b in range(NB)]
        keep2 = P.mark()
        import os
        stage = int(os.environ.get("RWKV_STAGE", 3))
        if stage >= 2:
            self.rwkv_chunked(Yacc)
        P.barrier()
        if "YD" in self.debug:
            yd = self.dscr("YD", [NB, 128, NCH, 512], F32)
            for b in range(NB):
                self.DMA("sp", yd[b], Yacc[b][:], R=Yacc[b].r)
            P.barrier()
        P.release(keep2)
        for b in range(NB if stage >= 3 else 0):
            self.rwkv_finish(b, Yacc[b])
            P.barrier()
            P.release(keep2)
        P.release(base)

    def rwkv_chunked(self, Yacc):
        P = self.P
        ps = self.ps
        c_ = CDEC
        mask4 = [self.tile([128, 4, 128], F32, f"mask4{d}") for d in range(2)]
        for d in range(2):
            MS, MI = self.masks[:, 2 * d + 1, :], self.masks[:, 2 * d, :]
            for q, m in enumerate((MS, MI, MS, MI)):
                self.CP("pool", mask4[d][:, q, :], m, self.masks.r, mask4[d].r)
        Sf = [[self.tile([128, 4, 64], F32, f"Sf{b}{d}") for d in range(2)] for b in range(NB)]
        Sb = [[self.tile([128, 4, 64], BF16, f"Sb{b}{d}") for d in range(2)] for b in range(NB)]
        for b in range(NB):
            for d in range(2):
                self.MEMSET("pool", Sf[b][d][:], 0.0, [], Sf[b][d].r)
                self.MEMSET("pool", Sb[b][d][:], 0.0, [], Sb[b][d].r)
        pbin = [self.tile([128, 5, 512], BF16, f"pbin{i}") for i in range(2)]
        lw = [self.tile([128, 512], F32, f"lw{i}") for i in range(2)]
        E = [self.tile([128, 512], F32, f"E{i}") for i in range(3)]
        Xt = self.tile([128, 4, 512], BF16, "Xt")
        FM = [self.tile([128, 4, 128], BF16, f"FM{c}") for c in range(4)]
        PC = self.tile([128, 4], F32, "PC")
        Mm = [self.tile([128, 4, 128], BF16, f"Mm{h}") for h in range(8)]
        AATg = [[[self.tile([128, 2, 2, 128], F32, f"AAT{g}{i}{jb}") for jb in range(2)] for i in range(2)] for g in range(2)]
        Wball = [self.tile([128, 256], F32, f"Wball{g}") for g in range(2)]
        Up = self.tile([128, 8, 64], BF16, "Up")
        ysb = self.tile([128, 512], F32, "ysb")
        tS = self.tile([128, 4, 64], F32, "tS")
        order = [list(range(NCH)), [1, 0] + list(range(NCH - 1, 1, -1))]
        written = [set() for _ in range(NB)]
        nu = 0
        import os
        ndirs = int(os.environ.get("RWKV_DIRS", 2))
        for step in range(NCH):
            for d in range(ndirs):
                for b in range(NB):
                    ch = order[d][step]
                    tk = slice(ch * 128, (ch + 1) * 128)
                    MI, MS, MSo = self.masks[:, 2 * d, :], self.masks[:, 2 * d + 1, :], self.masks[:, 2 * (1 - d) + 1, :]
                    pin, lwt = pbin[nu % 2], lw[nu % 2]
                    nu += 1
                    self.DMA("sp", pin[:], self.PB[b, d, tk, :, :], W=pin.r)
                    self.DMA("sp", lwt[:], self.WD[b, d, tk, :], W=lwt.r)
                    S_f, S_b = Sf[b][d], Sb[b][d]
                    self.MM(ps[0][:, :], MI, lwt[:], True, True, self.masks.r + lwt.r, ps[0].r)
                    self.MM(ps[1][:, :], MS, lwt[:], True, True, self.masks.r + lwt.r, ps[1].r)
                    for c in range(4):
                        self.MM(ps[2][:, c:c + 1], lwt[:, c * 128:(c + 1) * 128], self.ones_f[:, 0:1], True, True,
                                lwt.r + self.ones_f.r, ps[2].r)
                    self.ACT(E[0][:], ps[0][:, :], AF.Exp, ps[0].r, E[0].r, scale=-c_)
                    self.ACT(E[1][:], ps[0][:, :], AF.Exp, ps[0].r, E[1].r, scale=c_)
                    self.ACT(E[2][:], ps[1][:, :], AF.Exp, ps[1].r, E[2].r, scale=-c_)
                    self.ACT(PC[:], ps[2][:, 0:4], AF.Exp, ps[2].r, PC.r, scale=-c_)
                    self.TT("dve", Xt[:, 0, :], pin[:, 0, :], E[2][:], ALU.mult, pin.r + E[2].r, Xt.r)
                    self.TT("pool", Xt[:, 1, :], pin[:, 3, :], E[0][:], ALU.mult, pin.r + E[0].r, Xt.r)
                    self.TT("dve", Xt[:, 2, :], pin[:, 1, :], E[1][:], ALU.mult, pin.r + E[1].r, Xt.r)
                    self.TT("pool", Xt[:, 3, :], pin[:, 2, :], E[1][:], ALU.mult, pin.r + E[1].r, Xt.r)
                    for c in range(4):
                        pt_ = ps[2 + c % 2]
                        for q in range(4):
                            self.MM(pt_[:, q * 128:(q + 1) * 128], Xt[:, q, c * 128:(c + 1) * 128], self.ident_b[:], True, True,
                                    Xt.r + self.ident_b.r, pt_.r)
                        self.CP("act" if c % 2 else "dve", FM[c][:], pt_[:, :].rearrange("p (q t) -> p q t", q=4), pt_.r, FM[c].r)
                    for h in range(8):
                        c, hb = h // 2, (h % 2) * 64
                        grp, j = h // 4, h % 4
                        fm = FM[c]
                        hs_ = slice(hb, hb + 64)
                        pm = ps[4]
                        AR = fm[hs_, 0:2, :].rearrange("p q t -> p (q t)")
                        self.MM(pm[:, 0:256], fm[hs_, 2, :], AR, True, True, fm.r, pm.r)
                        self.MM(pm[:, 256:512], fm[hs_, 3, :], AR, True, True, fm.r, pm.r)
                        self.TT("dve", Mm[h][:], pm[:, :].rearrange("p (q t) -> p q t", q=4), mask4[d][:], ALU.mult,
                                pm.r + mask4[d].r, Mm[h].r)
                        a0 = AATg[grp][0][j // 2]
                        self.TT("dve", a0[:, j % 2, 0, :], pm[:, 0:128], MS, ALU.mult, pm.r + self.masks.r, a0.r)
                        pn = ps[5]
                        self.MM(pn[:, 0:128], fm[hs_, 0, :], fm[hs_, 2, :], True, True, fm.r, pn.r)
                        self.TT("dve", a0[:, j % 2, 1, :], pn[:, 0:128], MSo, ALU.mult, pn.r + self.masks.r, a0.r)
                        wbank = ps[7 - grp]
                        wp = wbank[:, j * 64:(j + 1) * 64]
                        self.MM(wp, fm[hs_, 0, :], S_b[hs_, c, :], j == 0, False, fm.r + S_b.r, wbank.r)
                        self.MM(wp, Mm[h][:, 2, :], pin[:, 4, h * 64:(h + 1) * 64], False, False, Mm[h].r + pin.r, wbank.r)
                    cur = 0
                    for lvl in range(7):
                        for grp in range(2):
                            wbank = ps[7 - grp]
                            wb = Wball[grp]
                            self.CP("dve", wb[:], wbank[:, 0:256], wbank.r, wb.r)
                            for j in range(4):
                                A = AATg[grp][cur][j // 2]
                                self.MM(wbank[:, j * 64:(j + 1) * 64], A[:, j % 2, 0, :], wb[:, j * 64:(j + 1) * 64], False, False,
                                        A.r + wb.r, wbank.r)
                            if lvl == 6:
                                continue
                            for jb in range(2):
                                A = AATg[grp][cur][jb]
                                pq = ps[2 + 2 * grp + jb]
                                for jj in range(2):
                                    o0 = jj * 256
                                    self.MM(pq[:, o0:o0 + 128], A[:, jj, 1, :], A[:, jj, 0, :], True, True, A.r, pq.r)
                                    if lvl < 5:
                                        self.MM(pq[:, o0 + 128:o0 + 256], A[:, jj, 0, :], A[:, jj, 1, :], True, True, A.r, pq.r)
                            for jb in range(2):
                                An = AATg[grp][1 - cur][jb]
                                pq = ps[2 + 2 * grp + jb]
                                self.CP("act", An[:].rearrange("p a b t -> p (a b t)"), pq[:, :], pq.r, An.r)
                        cur = 1 - cur
                    first_y = True
                    for grp in range(2):
                        wbank = ps[7 - grp]
                        self.CP("dve", Up[:, grp * 4:grp * 4 + 4, :].rearrange("p h e -> p (h e)"), wbank[:, 0:256], wbank.r, Up.r)
                    for h in range(8):
                        c, hb = h // 2, (h % 2) * 64
                        fm = FM[c]
                        hs_ = slice(hb, hb + 64)
                        yp = ps[0][:, h * 64:(h + 1) * 64]
                        self.MM(yp, fm[hs_, 1, :], S_b[hs_, c, :], first_y, False, fm.r + S_b.r, ps[0].r)
                        first_y = False
                        self.MM(yp, Mm[h][:, 1, :], Up[:, h, :], False, False, Mm[h].r + Up.r, ps[0].r)
                        self.MM(yp, Mm[h][:, 3, :], pin[:, 4, h * 64:(h + 1) * 64], False, False, Mm[h].r + pin.r, ps[0].r)
                    if ch in written[b]:
                        self.TT("dve", Yacc[b][:, ch, :], Yacc[b][:, ch, :], ps[0][:, :], ALU.add, [Yacc[b].r[ch]] + ps[0].r,
                                [Yacc[b].r[ch]])
                    else:
                        written[b].add(ch)
                        self.CP("dve", Yacc[b][:, ch, :], ps[0][:, :], ps[0].r, [Yacc[b].r[ch]])
                    for c in range(4):
                        pd = ps[1][:, c * 128:(c + 1) * 128]
                        self.MM(pd, Xt[:, 2, c * 128:(c + 1) * 128], Up[:, 2 * c:2 * c + 2, :].rearrange("p h e -> p (h e)"),
                                c == 0, False, Xt.r + Up.r, ps[1].r)
                        self.MM(pd, Xt[:, 3, c * 128:(c + 1) * 128], pin[:, 4, c * 128:(c + 1) * 128], False, False,
                                Xt.r + pin.r, ps[1].r)
                    for c in range(4):
                        self.TS("dve", tS[:, c, :], S_f[:, c, :], PC[:, c:c + 1], None, ALU.mult, None, S_f.r + PC.r, tS.r)
                        for hh in range(2):
                            hs_ = slice(hh * 64, hh * 64 + 64)
                            self.STT("dve", S_f[hs_, c, :], ps[1][hs_, c * 128 + hh * 64:c * 128 + hh * 64 + 64], PC[hs_, c:c + 1],
                                     tS[hs_, c, :], ALU.mult, ALU.add, ps[1].r + PC.r + tS.r, S_f.r)
                    self.CP("act", S_b[:], S_f[:], S_f.r, S_b.r)

    def rwkv_prep(self, b, Vp):
        P = self.P
        tw = self.tile([128, T], BF16, "twxa")
        sg = self.tile([128, T], BF16, "sg")
        keepA = P.mark()
        hmod = self.tile([128, 8, T], BF16, "hmod_b")
        hs = self.tile([128, 8, T], BF16, "hs_b")
        for c in range(8):
            self.DMA("sp", hmod[:, c, :], self.HM[b, c, :, :], W=hmod.r)
        for (lo, hi) in ((0, TC), (TC, T)):
            self.TT("pool", hs[:, :, lo + 1:hi - 1], hmod[:, :, lo:hi - 2], hmod[:, :, lo + 2:hi], ALU.add, hmod.r, hs.r)
            self.CP("dve", hs[:, :, lo:lo + 1], hmod[:, :, lo + 1:lo + 2], hmod.r, hs.r)
            self.CP("dve", hs[:, :, hi - 1:hi], hmod[:, :, hi - 2:hi - 1], hmod.r, hs.r)
        omm = self.bcast_row(self.rmu_d[0, :], 1792, "omm")
        hmu = self.tile([128, 1792], F32, "hmu")
        self.TS("dve", hmu[:], omm[:], 0.5, None, ALU.mult, None, omm.r, hmu.r)
        self.TS("dve", omm[:], omm[:], -1.0, 1.0, ALU.mult, ALU.add, omm.r, omm.r)

        def shifted_weights(wt, w1t, w2t, col0, n):
            self.load_w(wt, self.ab_w[:, col0:col0 + n])
            self.TT("dve", w1t[:, :, 0:n], wt[:, :, 0:n], omm[:, col0:col0 + n].unsqueeze(1).to_broadcast([128, 8, n]), ALU.mult,
                    wt.r + omm.r, w1t.r)
            self.TT("pool", w2t[:, :, 0:n], wt[:, :, 0:n], hmu[:, col0:col0 + n].unsqueeze(1).to_broadcast([128, 8, n]), ALU.mult,
                    wt.r + hmu.r, w2t.r)

        import os
        sub = int(os.environ.get("RWKV_SUB", 9))
        if sub <= 0:
            return
        wt = self.tile([128, 8, 512], BF16, "rw")
        w1t = self.tile([128, 8, 512], BF16, "rw1")
        w2t = self.tile([128, 8, 512], BF16, "rw2")
        for gi, col0 in enumerate((1536, 1664)):
            shifted_weights(wt, w1t, w2t, col0, 128)
            for i, (t0, n) in enumerate(self.TILES):
                pst = self.ps[i % 4]
                for c in range(8):
                    self.MM(pst[:, 0:n], w1t[:, c, 0:128], hmod[:, c, t0:t0 + n], c == 0, False, w1t.r + hmod.r, pst.r)
                for c in range(8):
                    self.MM(pst[:, 0:n], w2t[:, c, 0:128], hs[:, c, t0:t0 + n], False, c == 7, w2t.r + hs.r, pst.r)
                if gi == 0:
                    self.ACT(tw[0:64, t0:t0 + n], pst[0:64, 0:n], AF.Tanh, pst.r, tw.r)
                    self.ACT(tw[64:128, t0:t0 + n], pst[64:128, 0:n], AF.Copy, pst.r, tw.r)
                else:
                    self.ACT(sg[:, t0:t0 + n], pst[:, 0:n], AF.Sigmoid, pst.r, sg.r)
        if sub <= 1:
            return
        stg = [self.tile([128, 512], F32, f"stg{i}") for i in range(2)]
        vb16 = self.tile([128, 8, 128], BF16, "vb16")
        self.MEMSET("pool", vb16[:], 0.0, [], vb16.r)
        ns = 0
        for grp in range(3):
            shifted_weights(wt, w1t, w2t, grp * 512, 512)
            for ch in range(NCH):
                tk = slice(ch * 128, (ch + 1) * 128)
                pst = self.ps[ch % 4]
                for c in range(8):
                    self.MM(pst[:, :], hmod[:, c, tk], w1t[:, c, :], c == 0, False, w1t.r + hmod.r, pst.r)
                for c in range(8):
                    self.MM(pst[:, :], hs[:, c, tk], w2t[:, c, :], False, c == 7, w2t.r + hs.r, pst.r)
                st = stg[ns % 2]
                ns += 1
                self.CP("act", st[:], pst[:, :], pst.r, st.r)
                self.DMA("sp", self.PR[b, grp, tk, :], st[:], R=st.r)
                if False:
                    off = 64 * b
                    self.CP("dve", vb16[:, :, off:off + 64], st[:].rearrange("p (h e) -> p h e", h=8), st.r, vb16.r)
                    for hh in range(2):
                        pV = self.ps[4 + hh]
                        for h4 in range(4):
                            h = hh * 4 + h4
                            self.MM(pV[0:64 + off, h4 * 128:(h4 + 1) * 128], vb16[:, h, 0:64 + off], self.ident_b[:], True, True,
                                    vb16.r + self.ident_b.r, pV.r)
                        self.CP("dve" if hh else "act", Vp[off:off + 64, hh * 4:hh * 4 + 4, tk],
                                pV[off:off + 64, :].rearrange("p (h t) -> p h t", h=4), pV.r, Vp.r)
        P.barrier()
        P.release(keepA)
        if sub <= 2:
            return
        rv = [self.bcast_row(self.rvec_d[i, :], 512, f"rv{i}") for i in range(9)]
        kk_bc, ka_bc, rk_bc, _, _, w0a, w0b, a0a, a0b = rv
        omka = self.tile([128, 512], F32, "omka")
        self.TS("dve", omka[:], ka_bc[:], -1.0, 1.0, ALU.mult, ALU.add, ka_bc.r, omka.r)
        w2b = self.tile([64, 2, 512], BF16, "w2b")
        a2b = self.tile([128, 2, 512], BF16, "a2b")
        g2b = self.tile([128, 512], BF16, "g2b")
        self.DMA("pool", w2b[:], self.rw2_d.rearrange("d k n -> k d n"), W=w2b.r)
        self.DMA("pool", a2b[64:128, :, :], self.ra2_d.rearrange("d k n -> k d n"), W=a2b.r)
        self.DMA("pool", g2b[:], self.rg2_d, W=g2b.r)
        rkv = [[self.tile([128, 512], F32, f"in{j}{i}") for i in range(3)] for j in range(2)]
        tmp = [self.tile([128, 512], F32, f"tm{i}") for i in range(4)]
        kk = self.tile([128, 512], F32, "kk")
        sm = [self.tile([128, 8], F32, f"sm{i}") for i in range(2)]
        pbst = [self.tile([128, 5, 512], BF16, f"pbst{i}") for i in range(2)]
        wdec = [self.tile([128, 512], F32, f"wdec{i}") for i in range(2)]
        bg = [self.tile([128, 512], F32, f"bg{i}") for i in range(2)]
        v3 = lambda t: t[:].rearrange("p (h e) -> p h e", h=8)
        bc8 = lambda t: t[:, 0:8].unsqueeze(2).to_broadcast([128, 8, 64])
        for ch in range(NCH):
            tk = slice(ch * 128, (ch + 1) * 128)
            r_t, k_t, v_t = rkv[ch % 2]
            for gi, tt in enumerate((r_t, k_t, v_t)):
                self.DMA("sp", tt[:], self.PR[b, gi, tk, :], W=tt.r)
            t0_, t1_, t2_, t3_ = tmp
            self.TT("dve", t0_[:], k_t[:], kk_bc[:], ALU.mult, k_t.r + kk_bc.r, t0_.r)
            self.TT("pool", t1_[:], t0_[:], t0_[:], ALU.mult, t0_.r, t1_.r)
            self.RED("dve", sm[0][:, 0:8], v3(t1_), ALU.add, t1_.r, sm[0].r)
            self.TS("dve", sm[0][:], sm[0][:], 1e-24, None, ALU.max, None, sm[0].r, sm[0].r)
            self.ACT(sm[0][:], sm[0][:], AF.Sqrt, sm[0].r, sm[0].r)
            self.RECIP(sm[0][:], sm[0][:], sm[0].r, sm[0].r)
            self.TT("dve", v3(kk), v3(t0_), bc8(sm[0]), ALU.mult, t0_.r + sm[0].r, kk.r)
            self.TT("pool", t1_[:], r_t[:], k_t[:], ALU.mult, r_t.r + k_t.r, t1_.r)
            self.TT("pool", t1_[:], t1_[:], rk_bc[:], ALU.mult, t1_.r + rk_bc.r, t1_.r)
            self.RED("dve", sm[1][:, 0:8], v3(t1_), ALU.add, t1_.r, sm[1].r)
            self.TT("dve", v3(bg[0]), v3(v_t), bc8(sm[1]), ALU.mult, v_t.r + sm[1].r, bg[0].r)
            self.DMA("sp", self.BG[b, 0, tk, :], bg[0][:], R=bg[0].r)
            pg = self.ps[4]
            self.MM(pg[:, :], sg[:, tk], g2b[:], True, True, sg.r + g2b.r, pg.r)
            self.CP("act", bg[1][:], pg[:, :], pg.r, bg[1].r)
            self.DMA("sp", self.BG[b, 1, tk, :], bg[1][:], R=bg[1].r)
            for d in range(2):
                pb_ = pbst[d]
                w0_bc, a0_bc = (w0a, a0a) if d == 0 else (w0b, a0b)
                pz = self.ps[d]
                self.MM(pz[:, :], tw[0:64, tk], w2b[0:64, d, :], True, True, tw.r + w2b.r, pz.r)
                self.TT("dve", t1_[:], pz[:, :], w0_bc[:], ALU.add, pz.r + w0_bc.r, t1_.r)
                self.ACT(wdec[d][:], t1_[:], AF.Sigmoid, t1_.r, wdec[d].r)
                self.DMA("sp", self.WD[b, d, tk, :], wdec[d][:], R=wdec[d].r)
                pa = self.ps[2 + d]
                self.MM(pa[:, :], tw[64:128, tk], a2b[64:128, d, :], True, True, tw.r + a2b.r, pa.r)
                self.TT("dve", t2_[:], pa[:, :], a0_bc[:], ALU.add, pa.r + a0_bc.r, t2_.r)
                self.ACT(t2_[:], t2_[:], AF.Sigmoid, t2_.r, t2_.r)
                self.TS("pool", pb_[:, 0, :], kk[:], -1.0, None, ALU.mult, None, kk.r, pb_.r)
                self.TT("pool", pb_[:, 1, :], kk[:], t2_[:], ALU.mult, kk.r + t2_.r, pb_.r)
                self.TT("dve", t3_[:], t2_[:], ka_bc[:], ALU.mult, t2_.r + ka_bc.r, t3_.r)
                self.TT("dve", t3_[:], t3_[:], omka[:], ALU.add, t3_.r + omka.r, t3_.r)
                self.TT("pool", pb_[:, 2, :], k_t[:], t3_[:], ALU.mult, k_t.r + t3_.r, pb_.r)
                self.CP("act", pb_[:, 3, :], r_t[:], r_t.r, pb_.r)
                self.CP("act", pb_[:, 4, :], v_t[:], v_t.r, pb_.r)
                self.DMA("sp", self.PB[b, d, tk, :, :], pb_[:], R=pb_.r)

    def rwkv_scan(self, Vp, Y):
        NS = 2
        NBUF = 3
        S = [self.tile([128, 512], F32, f"S{d}") for d in range(2)]
        for d in range(2):
            self.MEMSET("dve", S[d][:], 0.0, [], S[d].r)
        Wb = [[self.tile([128, NS, 512], F32, f"Wb{d}{i}") for i in range(NBUF)] for d in range(2)]
        Vb = [[self.tile([128, NS, 4, 512], BF16, f"Vb{d}{i}") for i in range(NBUF)] for d in range(2)]
        t1 = [self.tile([128, 512], F32, f"sc1{d}") for d in range(2)]
        t2 = [self.tile([128, 512], F32, f"sc2{d}") for d in range(2)]
        t3 = [self.tile([128, 512], F32, f"sc3{d}") for d in range(2)]
        sa = [self.tile([128, 8], F32, f"sa{d}") for d in range(2)]
        yts = [self.tile([128, 8], F32, f"yt{d}") for d in range(2)]
        order = [list(range(T)), list(range(TC - 1, -1, -1)) + list(range(T - 1, TC - 1, -1))]
        v3 = lambda ap: ap.rearrange("p (h e) -> p h e", h=8)
        nblk = T // NS
        ywritten = set()
        import os
        nblk = int(os.environ.get('RWKV_MAXBLK', nblk))

        def load(d, bi):
            toks = order[d][bi * NS:(bi + 1) * NS]
            lo = min(toks)
            wb, vb = Wb[d][bi % NBUF], Vb[d][bi % NBUF]
            for b in range(NB):
                self.DMA("sp", wb[b * 64:(b + 1) * 64, :, :], self.WD[b, d, lo:lo + NS, :].partition_broadcast(64), W=wb.r)
                self.DMA("sp", vb[b * 64:(b + 1) * 64, :, :, :], self.PB[b, d, lo:lo + NS, :, :].partition_broadcast(64), W=vb.r)

        for bi in range(min(NBUF - 1, nblk)):
            for d in range(2):
                load(d, bi)
        for bi in range(nblk):
            for d in range(2):
                if bi + NBUF - 1 < nblk:
                    load(d, bi + NBUF - 1)
            for j in range(NS):
                for d in range(2):
                    t = order[d][bi * NS + j]
                    lo = min(order[d][bi * NS:(bi + 1) * NS])
                    jj = t - lo
                    wb, vb = Wb[d][bi % NBUF], Vb[d][bi % NBUF]
                    Sd = S[d]
                    a_bc, b_bc, k_bc, r_bc = (vb[:, jj, q, :] for q in range(4))
                    self.TT("dve", t1[d][:], Sd[:], a_bc, ALU.mult, Sd.r + vb.r, t1[d].r)
                    self.RED("dve", sa[d][:, 0:8], v3(t1[d][:]), ALU.add, t1[d].r, sa[d].r)
                    self.TT("pool", Sd[:], Sd[:], wb[:, jj, :], ALU.mult, Sd.r + wb.r, Sd.r)
                    self.TT("dve", v3(t2[d][:]), v3(b_bc), sa[d][:, 0:8].unsqueeze(2).to_broadcast([128, 8, 64]), ALU.mult,
                            vb.r + sa[d].r, t2[d].r)
                    self.TT("dve", Sd[:], Sd[:], t2[d][:], ALU.add, Sd.r + t2[d].r, Sd.r)
                    self.TT("pool", v3(t3[d][:]), v3(k_bc), Vp[:, :, t:t + 1].to_broadcast([128, 8, 64]), ALU.mult,
                            vb.r + Vp.r, t3[d].r)
                    self.TT("dve", Sd[:], Sd[:], t3[d][:], ALU.add, Sd.r + t3[d].r, Sd.r)
                    self.TT("dve", t1[d][:], Sd[:], r_bc, ALU.mult, Sd.r + vb.r, t1[d].r)
                    ytd = yts[d]
                    self.RED("dve", ytd[:, 0:8], v3(t1[d][:]), ALU.add, t1[d].r, ytd.r)
                    if t not in ywritten:
                        ywritten.add(t)
                        self.CP("act", Y[:, :, t], ytd[:, 0:8], ytd.r, Y.r)
                    else:
                        self.TT("pool", Y[:, :, t], Y[:, :, t], ytd[:, 0:8], ALU.add, Y.r + ytd.r, Y.r)

    def rwkv_finish(self, b, Ya):
        rv = [self.bcast_row(self.rvec_d[i, :], 512, f"fv{i}") for i in (3, 4)]
        gng, gnb = rv
        y = [self.tile([128, 512], F32, f"fy{i}") for i in range(2)]
        sq = self.tile([128, 512], F32, "fsq")
        bon = [self.tile([128, 512], F32, f"fb{i}") for i in range(2)]
        gg = [self.tile([128, 512], F32, f"fg{i}") for i in range(2)]
        sm = [self.tile([128, 8], F32, f"fsm{i}") for i in range(2)]
        ot = [self.tile([128, 512], BF16, f"fot{i}") for i in range(2)]
        ob = [self.tile([128, 4, 128], BF16, f"fob{i}") for i in range(2)]
        v3 = lambda t: t[:].rearrange("p (h e) -> p h e", h=8)
        bc8 = lambda t: t[:, 0:8].unsqueeze(2).to_broadcast([128, 8, 64])
        for ch in range(NCH):
            tk = slice(ch * 128, (ch + 1) * 128)
            yy, bo, g_ = y[ch % 2], bon[ch % 2], gg[ch % 2]
            self.DMA("sp", bo[:], self.BG[b, 0, tk, :], W=bo.r)
            self.DMA("sp", g_[:], self.BG[b, 1, tk, :], W=g_.r)
            ysrc = Ya[:, ch, :].rearrange("p (h e) -> p h e", h=8)
            yr = [Ya.r[ch]]
            self.RED("dve", sm[0][:, 0:8], ysrc, ALU.add, yr, sm[0].r)
            self.TS("dve", sm[0][:], sm[0][:], 1.0 / 64, None, ALU.mult, None, sm[0].r, sm[0].r)
            self.TT("dve", v3(yy), ysrc, bc8(sm[0]), ALU.subtract, yr + sm[0].r, yy.r)
            self.ACT(sq[:], yy[:], AF.Square, yy.r, sq.r)
            self.RED("dve", sm[1][:, 0:8], v3(sq), ALU.add, sq.r, sm[1].r)
            self.ACT(sm[1][:], sm[1][:], AF.Sqrt, sm[1].r + self.cst.r, sm[1].r, bias=self.cst[:, 4:5], scale=1.0 / 64)
            self.RECIP(sm[1][:], sm[1][:], sm[1].r, sm[1].r)
            self.TT("dve", v3(yy), v3(yy), bc8(sm[1]), ALU.mult, yy.r + sm[1].r, yy.r)
            self.TT("pool", yy[:], yy[:], gng[:], ALU.mult, yy.r + gng.r, yy.r)
            self.TT("pool", yy[:], yy[:], gnb[:], ALU.add, yy.r + gnb.r, yy.r)
            self.TT("pool", yy[:], yy[:], bo[:], ALU.add, yy.r + bo.r, yy.r)
            o = ot[ch % 2]
            self.TT("dve", o[:], yy[:], g_[:], ALU.mult, yy.r + g_.r, o.r)
            pO = self.ps[2 + ch % 2]
            for c in range(4):
                self.MM(pO[:, c * 128:(c + 1) * 128], o[:, c * 128:(c + 1) * 128], self.ident_b[:], True, True,
                        o.r + self.ident_b.r, pO.r)
            oo = ob[ch % 2]
            self.CP("act", oo[:], pO[:, :].rearrange("p (c t) -> p c t", c=4), pO.r, oo.r)
            self.DMA("sp", self.O[b, 0:4, :, tk].rearrange("c p t -> p c t"), oo[:], R=oo.r)

    def mix_diff(self, b):
        P = self.P
        rope = self.load_rope()
        LAM_INIT = 0.2
        dl = self.tile([1, 4, 64], F32, "dl")
        self.DMA("sp", dl[:], self.diffl_d, W=dl.r)
        pr = self.tile([1, 2, 64], F32, "dlp")
        sm = self.tile([1, 2], F32, "dls")
        self.TT("dve", pr[:, 0, :], dl[:, 0, :], dl[:, 1, :], ALU.mult, dl.r, pr.r)
        self.TT("dve", pr[:, 1, :], dl[:, 2, :], dl[:, 3, :], ALU.mult, dl.r, pr.r)
        self.RED("dve", sm[:, 0:2], pr[:, :, :], ALU.add, pr.r, sm.r)
        self.ACT(sm[:, 0:2], sm[:, 0:2], AF.Exp, sm.r, sm.r)
        nl = self.tile([1, 1], F32, "nl")
        self.TT("dve", nl[:, 0:1], sm[:, 1:2], sm[:, 0:1], ALU.subtract, sm.r, nl.r)
        self.TS("dve", nl[:, 0:1], nl[:, 0:1], -LAM_INIT, None, ALU.add, None, nl.r, nl.r)
        nlam = self.tile([128, 1], F32, "nlam")
        pb = self.ps[7]
        self.MM(pb[:, 0:1], self.ones_f[0:1, :], nl[0:1, 0:1], True, True, self.ones_f.r + nl.r, pb.r)
        self.CP("dve", nlam[:], pb[:, 0:1], pb.r, nlam.r)
        sg = self.tile([128, 1], F32, "subg")
        self.DMA("sp", sg[:], self.subln_d, W=sg.r)
        self.TS("dve", sg[:], sg[:], 1.0 - LAM_INIT, None, ALU.mult, None, sg.r, sg.r)
        wv = self.tile([128, 8, 512], BF16, "wv")
        self.load_w(wv, self.ab_w[:, 2816:3328])
        vtok = self.tile([128, NCH, 512], BF16, "vtok")
        for ch in range(NCH):
            pst = self.ps[ch % 4]
            self.proj_tok(pst, wv, 0, 512, ch)
            self.CP("act" if ch % 2 else "dve", vtok[:, ch, :], pst[:, :], pst.r, vtok.r)
        wq = [self.tile([128, 8, 128], BF16, f"wq{i}") for i in range(4)]
        qb = self.tile([128, T], BF16, "qb")
        kb = self.tile([128, T], BF16, "kb")
        pt = [self.tile([128, 512], BF16, f"pt{i}") for i in range(4)]
        ft = [self.tile([128, 512], F32, f"ft{i}") for i in range(4)]
        ob = [self.tile([128, 512], BF16, f"ob{i}") for i in range(2)]
        npt = 0
        nout = 0
        for h in range(4):
            self.load_w(wq[0], self.ab_w[:, 1792 + h * 128:1792 + (h + 1) * 128])
            self.load_w(wq[1], self.ab_perm[:, h * 128:(h + 1) * 128])
            self.load_w(wq[2], self.ab_w[:, 2304 + h * 128:2304 + (h + 1) * 128])
            self.load_w(wq[3], self.ab_perm[:, 512 + h * 128:512 + (h + 1) * 128])
            self.proj_rope(qb, wq[0], wq[1], 0, 0, rope)
            self.proj_rope(kb, wq[2], wq[3], 0, 0, rope)
            for (t0, n) in self.TILES:
                kts = [0, 1] if t0 < TC else list(range(NCH))
                items = [(m, ki, kt) for m in range(2) for ki, kt in enumerate(kts)]
                LA = 2
                slots = {}
                for i in range(len(items) + LA):
                    if i < len(items):
                        m, ki, kt = items[i]
                        S = self.ps[npt % 4]
                        p = pt[npt % 4]
                        npt += 1
                        slots[i] = p
                        self.MM(S[:, 0:n], kb[m * 64:(m + 1) * 64, kt * 128:(kt + 1) * 128], qb[m * 64:(m + 1) * 64, t0:t0 + n],
                                True, True, kb.r + qb.r, S.r)
                        self.ACT(p[:, 0:n], S[:, 0:n], AF.Exp, S.r, p.r, scale=0.125)
                    if i - LA >= 0:
                        m, ki, kt = items[i - LA]
                        p = slots.pop(i - LA)
                        Oa, Da = self.ps[4 + m], self.ps[6 + m]
                        self.MM(Oa[:, 0:n], vtok[:, kt, h * 128:(h + 1) * 128], p[:, 0:n], ki == 0, ki == len(kts) - 1,
                                vtok.r + p.r, Oa.r)
                        self.MM(Da[:, 0:n], self.ones_b[:], p[:, 0:n], ki == 0, ki == len(kts) - 1,
                                self.ones_b.r + p.r, Da.r)
                for m in range(2):
                    self.RECIP(ft[2 + m][:, 0:n], self.ps[6 + m][:, 0:n], self.ps[6 + m].r, ft[2 + m].r)
                    self.TT("dve", ft[m][:, 0:n], self.ps[4 + m][:, 0:n], ft[2 + m][:, 0:n], ALU.mult,
                            self.ps[4 + m].r + ft[2 + m].r, ft[m].r)
                self.STT("dve", ft[0][:, 0:n], ft[1][:, 0:n], nlam[:, 0:1], ft[0][:, 0:n], ALU.mult, ALU.add,
                         ft[1].r + nlam.r + ft[0].r, ft[0].r)
                self.ACT(ft[1][:, 0:n], ft[0][:, 0:n], AF.Square, ft[0].r, ft[1].r)
                st = self.ps[6]
                self.MM(st[:, 0:n], self.ones_f[:], ft[1][:, 0:n], True, True, self.ones_f.r + ft[1].r, st.r)
                self.ACT(ft[2][:, 0:n], st[:, 0:n], AF.Sqrt, st.r + self.cst.r, ft[2].r, bias=self.cst[:, 2:3], scale=1.0 / 128)
                self.RECIP(ft[2][:, 0:n], ft[2][:, 0:n], ft[2].r, ft[2].r)
                self.TT("pool", ft[0][:, 0:n], ft[0][:, 0:n], ft[2][:, 0:n], ALU.mult, ft[0].r + ft[2].r, ft[0].r)
                o = ob[nout % 2]
                nout += 1
                self.ACT(o[:, 0:n], ft[0][:, 0:n], AF.Copy, ft[0].r + sg.r, o.r, scale=sg[:, 0:1])
                self.DMA("sp", self.O[b, 4 + h, :, t0:t0 + n], o[:, 0:n], R=o.r)

    def layer_norm(self, y, N, gcol, bcol, hout, extra=None):
        st = self.ps[7]
        st2 = self.ps[6]
        sq = self.ln_sq
        for c in range(8):
            s = sq[c % 2]
            self.ACT(s[:, 0:N], y[:, c, :], AF.Square, y.r, s.r)
            self.MM(st[:, 0:N], self.ones_f[:], y[:, c, :], c == 0, c == 7, self.ones_f.r + y.r, st.r)
            self.MM(st2[:, 0:N], self.ones_b[:], s[:, 0:N], c == 0, c == 7, self.ones_b.r + s.r, st2.r)
        mean, rstd = self.ln_mean, self.ln_rstd
        self.ACT(mean[:, 0:N], st[:, 0:N], AF.Copy, st.r, mean.r, scale=1.0 / D)
        self.TT("dve", rstd[:, 0:N], mean[:, 0:N], mean[:, 0:N], ALU.mult, mean.r, rstd.r)
        self.STT("dve", rstd[:, 0:N], st2[:, 0:N], 1.0 / D, rstd[:, 0:N], ALU.mult, ALU.subtract, st2.r + rstd.r, rstd.r)
        self.ACT(rstd[:, 0:N], rstd[:, 0:N], AF.Sqrt, rstd.r + self.cst.r, rstd.r, bias=self.cst[:, 0:1])
        self.RECIP(rstd[:, 0:N], rstd[:, 0:N], rstd.r, rstd.r)
        for c in range(8):
            tmp = self.ln_tmp[c % 2]
            self.TT("dve", tmp[:, 0:N], y[:, c, :], mean[:, 0:N], ALU.subtract, y.r + mean.r, tmp.r)
            self.TT("pool", tmp[:, 0:N], tmp[:, 0:N], rstd[:, 0:N], ALU.mult, tmp.r + rstd.r, tmp.r)
            self.ACT(hout[:, c, :], tmp[:, 0:N], AF.Identity, tmp.r + self.lnp_t.r, hout.r,
                     bias=bcol(c), scale=gcol(c))
            if extra is not None:
                et, sfn, bfn = extra
                self.ACT(et[:, c, :], hout[:, c, :], AF.Identity, hout.r + self.S1.r + self.S1F.r + self.MOD.r, et.r,
                         bias=bfn(c), scale=sfn(c))

    def phase_ffn(self, layer, w_out, w1, w2):
        P = self.P
        last = layer == 1
        N = 256
        wo = self.tile([128, 8, D], BF16, "wo")
        w1t = self.tile([128, 8, 4 * D], BF16, "w1", nres=8)
        w2t = self.tile([128, 32, D], BF16, "w2", nres=32)
        for c in range(8):
            self.DMA("pool", wo[:, c, :], w_out[c * 128:(c + 1) * 128, :], W=wo.r)
        for c in range(8):
            self.DMA("pool", w1t[:, c, :], w1[c * 128:(c + 1) * 128, :], W=[w1t.r[c]])
        for c in range(32):
            self.DMA("pool", w2t[:, c, :], w2[c * 128:(c + 1) * 128, :], W=[w2t.r[c]])
        self.ln_sq = [self.tile([128, N], BF16, f"lnsq{i}") for i in range(2)]
        self.ln_tmp = [self.tile([128, N], F32, f"lntmp{i}") for i in range(2)]
        self.ln_mean = self.tile([128, N], F32, "lnmean")
        self.ln_rstd = self.tile([128, N], F32, "lnrstd")
        o_t = [self.tile([128, 8, N], BF16, f"o_t{i}") for i in range(1)]
        h_t = [self.tile([128, 8, N], F32, f"h_t{i}") for i in range(2)]
        hmods = [self.tile([128, 8, N], BF16, f"hmodf{i}") for i in range(2)]
        f_t = self.tile([128, 32, N], BF16, "f_t", nres=32)
        rl = [self.tile([128, N], F32, f"rl{i}") for i in range(2)]
        tok = [self.tile([128, D], F32, f"tok{i}") for i in range(1)] if last else None
        hm2 = self.tile([128, 8, N], BF16, "hm2") if not last else None
        tiles = []
        for b in range(NB):
            for t0 in range(0, T, N):
                if last and t0 < TC:
                    continue
                tiles.append((b, t0))
        import os
        if os.environ.get('FFN_NOTILES'):
            tiles = tiles[:1]
        lg = lambda k, c: self.lnp_t[:, layer, k, c:c + 1]
        st = {"npsum": 0, "ntok": 0}

        def nextps():
            p = self.ps[st["npsum"] % 6]
            st["npsum"] += 1
            return p

        def S1(n):
            b, t0 = tiles[n]
            w = self.who(b, t0)
            ot, ht, hmod = o_t[0], h_t[n % 2], hmods[n % 2]
            self.DMA("sp", ot[:], self.O[b, :, :, t0:t0 + N].rearrange("c p t -> p c t"), W=ot.r)
            self.DMA("sp", ht[:], self.H[b, :, :, t0:t0 + N].rearrange("c p t -> p c t"), W=ht.r)
            for oc in range(8):
                pst = nextps()
                for c in range(8):
                    self.MM(pst[:, 0:N], wo[:, c, oc * 128:(oc + 1) * 128], ot[:, c, :], c == 0, c == 7, wo.r + ot.r, pst.r)
                self.STT("dve", ht[:, oc, :], pst[:, 0:N], self.GA[:, layer, oc, w:w + 1], ht[:, oc, :], ALU.mult, ALU.add,
                         pst.r + self.GA.r + ht.r, ht.r)
            self.layer_norm(ht, N, lambda c: lg(0, c), lambda c: lg(1, c), ht,
                            extra=(hmod, lambda c: self.S1F[:, layer, c, w:w + 1], lambda c: self.SHF(layer, c, w)))

        def S23(n):
            b, t0 = tiles[n]
            w = self.who(b, t0)
            ht, hmod = h_t[n % 2], hmods[n % 2]
            for j in range(32):
                pst = nextps()
                for c in range(8):
                    self.MM(pst[:, 0:N], w1t[:, c, j * 128:(j + 1) * 128], hmod[:, c, :], c == 0, c == 7,
                            [w1t.r[c]] + hmod.r, pst.r)
                r = rl[j % 2]
                self.ACT(r[:, 0:N], pst[:, 0:N], AF.Relu, pst.r, r.r)
                self.TT("pool", f_t[:, j, :], r[:, 0:N], r[:, 0:N], ALU.mult, r.r, [f_t.r[j]])
            for oc in range(8):
                pst = nextps()
                for j in range(32):
                    self.MM(pst[:, 0:N], w2t[:, j, oc * 128:(oc + 1) * 128], f_t[:, j, :], j == 0, j == 31,
                            [w2t.r[j], f_t.r[j]], pst.r)
                self.STT("dve", ht[:, oc, :], pst[:, 0:N], self.GFA[:, layer, oc, w:w + 1], ht[:, oc, :], ALU.mult, ALU.add,
                         pst.r + self.GFA.r + ht.r, ht.r)
            if not last:
                self.layer_norm(ht, N, lambda c: lg(2, c), lambda c: lg(3, c), ht,
                                extra=(hm2, lambda c: self.S1[:, layer + 1, c, w:w + 1], lambda c: self.SH(layer + 1, c, w)))
                self.DMA("sp", self.H[b, :, :, t0:t0 + N].rearrange("c p t -> p c t"), ht[:], R=ht.r)
                self.DMA("sp", self.HM[b, :, :, t0:t0 + N].rearrange("c p t -> p c t"), hm2[:], R=hm2.r)
            else:
                self.layer_norm(ht, N, lambda c: lg(2, c), lambda c: lg(3, c), ht)
                for sub in range(N // 128):
                    tk = tok[0]
                    st["ntok"] += 1
                    for half in range(2):
                        pst = nextps()
                        for cc in range(4):
                            c = half * 4 + cc
                            self.TR(pst[:, cc * 128:(cc + 1) * 128], ht[:, c, sub * 128:(sub + 1) * 128], self.ident_f[:],
                                    ht.r + self.ident_f.r, pst.r)
                        self.CP("act", tk[:, half * 512:(half + 1) * 512], pst[:, :], pst.r, tk.r)
                    tl0 = t0 - TC + sub * 128
                    self.DMA("sp", self.out[b, tl0:tl0 + 128, :], tk[:], R=tk.r)

        S1(0)
        for n in range(len(tiles)):
            if n + 1 < len(tiles):
                S1(n + 1)
            S23(n)


def _per_core_inputs(inp, core):
    b0 = core * NB
    f = lambda a: np.ascontiguousarray(a, dtype=np.float32)
    cv = np.stack([inp["c"][b0], inp["c"][b0 + 1], inp["c_ctx"]], 0)
    m = {
        "x": f(inp["x"][b0:b0 + NB]),
        "ctx": f(inp["ctx"][b0:b0 + NB]),
        "cvec": f(cv.reshape(3, 8, 128).transpose(2, 1, 0)),
        "mod_w": f(inp["mod_w"]),
        "mod_b": f(inp["mod_b"].reshape(2, 48, 128).transpose(2, 0, 1)),
        "lnp": f(np.stack([inp["ln_mix_g"], inp["ln_mix_b"], inp["ln_ffn_g"], inp["ln_ffn_b"]], 1)
                 .reshape(2, 4, 8, 128).transpose(3, 0, 1, 2)),
        "ffn_w1": f(inp["ffn_w1"]),
        "ffn_w2": f(inp["ffn_w2"]),
        "w_out": f(inp["w_out"]),
        "c_ident": np.eye(128, dtype=np.float32),
        "ab_w": f(inp["ab_w_in"][0]),
        "ab_perm": f(inp["ab_w_in"][0][:, 1792:2816][:, _PERM1024]),
        "cd_w": f(inp["cd_w_in"][0]),
        "cd_perm": f(inp["cd_w_in"][0][:, 1552:2192][:, _PERM1024[:640]]),
        "c_rope": _ROPE,
        "c_swamask": _SWAMASK,
        "diff_l": f(np.stack([inp["diff_lq1"][0], inp["diff_lk1"][0], inp["diff_lq2"][0], inp["diff_lk2"][0]], 0)[None]),
        "subln": f(inp["diff_subln_g"][0].reshape(128, 1)),
        "sink": f(inp["swa_sink"]),
        "c_masks": _MASKS,
        "c_neg": _NEG,
        "convw": f(inp["ssd_conv_w"][0].reshape(5, 8, 128).transpose(2, 1, 0)),
        "convb": f(inp["ssd_conv_b"][0].reshape(8, 128).T),
        "dtb": f(inp["ssd_dt_bias"][0].reshape(1, 16)),
        "alog": f(inp["ssd_a_log"][0].reshape(1, 16)),
        "dskip": f(np.repeat(inp["ssd_d"][0], 64).reshape(4, 128).T),
        "ssdg": f(inp["ssd_norm_g"][0].reshape(4, 128).T),
        "rwkv_vec": f(np.stack([inp["rwkv_k_k"][0], inp["rwkv_k_a"][0], inp["rwkv_r_k"][0].reshape(512), inp["rwkv_gn_g"][0],
                                inp["rwkv_gn_b"][0], inp["rwkv_w0"][0, 0], inp["rwkv_w0"][0, 1], inp["rwkv_a0"][0, 0],
                                inp["rwkv_a0"][0, 1]], 0)),
        "rwkv_mu": f(inp["rwkv_mu"]),
        "rwkv_w2": f(inp["rwkv_w2"][0]),
        "rwkv_a2": f(inp["rwkv_a2"][0]),
        "rwkv_g2": f(inp["rwkv_g2"][0]),
    }
    return m


def _mk_consts():
    d = np.arange(64)
    partner = np.where((d % 32) < 16, d + 16, d - 16)
    perm = (np.arange(1024) // 64) * 64 + partner[np.arange(1024) % 64]
    t = np.arange(TL)
    rows, cols = t // 64, t % 64
    p = np.arange(128)
    dd = p % 64
    i = dd % 16
    inv = 10000.0 ** (-(i.astype(np.float64)) / 16.0)
    pos = np.where((dd // 32)[:, None] == 0, rows[None, :], cols[None, :]).astype(np.float64)
    ang = (pos.astype(np.float32) * inv.astype(np.float32)[:, None]).astype(np.float32)
    cos = np.cos(ang).astype(np.float32)
    sin = np.sin(ang).astype(np.float32)
    sgn = np.where((dd % 32) < 16, -1.0, 1.0).astype(np.float32)[:, None]
    rope = np.stack([cos, sin * sgn], 0).astype(np.float32)
    k = np.arange(128)[:, None]
    q = np.arange(512)[None, :]
    m = np.stack([(np.abs(q - k - 128 * (r - 1)) <= 128) for r in range(6)], 0).astype(np.float32)
    return perm, rope, m


_PERM1024, _ROPE, _SWAMASK = _mk_consts()
_i = np.arange(128)[:, None]
_t = np.arange(128)[None, :]
_MASKS = np.stack([_i <= _t, _i < _t, _i >= _t, _i > _t], 0).astype(np.float32)
_NEG = ((_MASKS[[0, 2]] - 1.0) * 1e30).astype(np.float32)


_CACHE = {}


def kernel(**inputs):
    inp = {k: np.asarray(v) for k, v in inputs.items()}
    if "nc" not in _CACHE:
        _CACHE["nc"] = Builder().build()
    nc = _CACHE["nc"]
    in_maps = [_per_core_inputs(inp, c) for c in range(8)]
    res = run_bass_kernel_spmd(nc, in_maps, core_ids=list(range(8)))
    return np.concatenate([r["out"] for r in res.results], axis=0).astype(np.float32)
```
